# Optimizing a Trainium2 kernel written in Bass

```python
import math
import jax, jax.numpy as jnp
from jax import lax
import numpy as np

D_MODEL = 2048
BATCH = 8
SEQ = 2048
DEPTH = 4

N_META = 16
EXPAND = 2
MIX_WIDTH = EXPAND * D_MODEL
A_HEADS = 16
A_DK = 128
A_DV = 128
A_WIDTH = A_HEADS * A_DV
A_CHUNK = 64
B_HEADS = 16
B_KV_HEADS = 4
B_GROUP = B_HEADS // B_KV_HEADS
B_HEAD_DIM = 128
B_WIDTH = B_HEADS * B_HEAD_DIM
WINDOW = 128
C_HEAD_DIM = 128
C_HEADS = MIX_WIDTH // (2 * C_HEAD_DIM)
C_WIDTH = C_HEADS * 2 * C_HEAD_DIM
Q_BLOCK = 128

N_EVEN = (DEPTH + 1) // 2
N_ODD = DEPTH // 2
EVEN_SIZES = (A_HEADS * A_DK, A_HEADS * A_DK, A_HEADS * A_DK, A_WIDTH, A_WIDTH,
              B_WIDTH, B_KV_HEADS * B_HEAD_DIM, B_KV_HEADS * B_HEAD_DIM, B_WIDTH)
EVEN_IN = 3 * A_HEADS * A_DK + 2 * A_WIDTH + 2 * B_WIDTH + 2 * B_KV_HEADS * B_HEAD_DIM
ODD_IN = 4 * C_WIDTH
EPS = 1e-6
NEG = -1e30

kernel_name = "hybrid_hgrn2_swa_diffattn_encoder"


def rms_norm(t, g):
    tf = t.astype(jnp.float32)
    y = tf * lax.rsqrt(jnp.mean(tf * tf, axis=-1, keepdims=True) + EPS) * g.astype(jnp.float32)
    return y.astype(t.dtype)


def alibi_slopes(n):
    return jnp.exp2(-8.0 * jnp.arange(1, n + 1, dtype=jnp.float32) / n)


def split_cols(z, sizes):
    out, start = [], 0
    for s in sizes:
        out.append(z[..., start:start + s])
        start += s
    return out


def hgrn2_chunk_scan(q, logf, v):
    Bn, H, Lp, dk = q.shape
    dv = v.shape[-1]
    n = Lp // A_CHUNK
    k = -jnp.expm1(logf)

    def chunks(t):
        return t.reshape(Bn, H, n, A_CHUNK, t.shape[-1]).transpose(2, 0, 1, 3, 4)

    tril = jnp.tril(jnp.ones((A_CHUNK, A_CHUNK), dtype=bool))

    def step(S, blk):
        qc, lc, kc, vc = blk
        b = jnp.cumsum(lc, axis=2)
        o = jnp.einsum('bhtk,bhkv->bhtv', qc * jnp.exp(b), S)
        decay = jnp.exp(jnp.minimum(b[:, :, :, None, :] - b[:, :, None, :, :], 0.0))
        scores = jnp.einsum('bhtk,bhsk,bhtsk->bhts', qc, kc, decay)
        scores = jnp.where(tril, scores, 0.0)
        o = o + jnp.einsum('bhts,bhsv->bhtv', scores, vc)
        b_last = b[:, :, -1:, :]
        S = S * jnp.exp(b_last).transpose(0, 1, 3, 2)
        S = S + jnp.einsum('bhsk,bhsv->bhkv', kc * jnp.exp(b_last - b), vc)
        return S, o

    S0 = jnp.zeros((Bn, H, dk, dv), jnp.float32)
    _, o = lax.scan(step, S0, (chunks(q), chunks(logf), chunks(k), chunks(v)))
    return o.transpose(1, 2, 0, 3, 4).reshape(Bn, H, Lp, dv)


def hgrn2_bidir(q, zf, zb, v, lb):
    pad = (-N_META) % A_CHUNK
    lb = lb.astype(jnp.float32)

    def log_forget(z, lbd):
        lbd = lbd[None, :, None, :]
        return jnp.logaddexp(jnp.log(lbd), jnp.log1p(-lbd) + jax.nn.log_sigmoid(z.astype(jnp.float32)))

    def padt(t):
        return jnp.pad(t, ((0, 0), (0, 0), (pad, 0), (0, 0)))

    def flip(t):
        return jnp.flip(t, axis=2)

    qp, vp = padt(q), padt(v)
    lf, lbk = padt(log_forget(zf, lb[0])), padt(log_forget(zb, lb[1]))
    o_f = hgrn2_chunk_scan(qp, lf, vp)
    o_b = flip(hgrn2_chunk_scan(flip(qp), flip(lbk), flip(vp)))
    return (o_f + o_b)[:, :, pad:]


def window_attention(q, k, v, sink):
    Bn, Hk, G, L, d = q.shape
    nb = -(-L // WINDOW)
    Lq = nb * WINDOW
    scale = d ** -0.5
    slopes = alibi_slopes(B_HEADS).reshape(1, Hk, G, 1, 1, 1)
    qb = jnp.pad(q, ((0, 0), (0, 0), (0, 0), (0, Lq - L), (0, 0))).reshape(Bn, Hk, G, nb, WINDOW, d)

    def band(t):
        tp = jnp.pad(t, ((0, 0), (0, 0), (WINDOW, Lq - L + WINDOW), (0, 0)))
        tp = tp.reshape(Bn, Hk, nb + 2, WINDOW, t.shape[-1])
        return jnp.concatenate([tp[:, :, :-2], tp[:, :, 1:-1], tp[:, :, 2:]], axis=3)

    kw, vw = band(k), band(v)
    tq = jnp.arange(Lq).reshape(nb, WINDOW)
    ts = (jnp.arange(nb)[:, None] - 1) * WINDOW + jnp.arange(3 * WINDOW)[None, :]
    dist = jnp.abs(tq[:, :, None] - ts[:, None, :])
    valid = (dist <= WINDOW) & (ts[:, None, :] >= N_META) & (ts[:, None, :] < L)
    s_w = jnp.einsum('bhgnqd,bhnkd->bhgnqk', qb, kw).astype(jnp.float32) * scale - slopes * dist
    s_w = jnp.where(valid, s_w, NEG)
    dist_m = jnp.minimum(jnp.abs(tq[:, :, None] - jnp.arange(N_META)[None, None, :]), WINDOW)
    km, vm = k[:, :, :N_META], v[:, :, :N_META]
    s_m = jnp.einsum('bhgnqd,bhmd->bhgnqm', qb, km).astype(jnp.float32) * scale - slopes * dist_m
    s_sink = jnp.broadcast_to(sink.astype(jnp.float32).reshape(1, Hk, G, 1, 1, 1), s_m.shape[:-1] + (1,))
    p = jax.nn.softmax(jnp.concatenate([s_sink, s_m, s_w], axis=-1), axis=-1)
    p_m = p[..., 1:1 + N_META].astype(v.dtype)
    p_w = p[..., 1 + N_META:].astype(v.dtype)
    o = jnp.einsum('bhgnqm,bhmd->bhgnqd', p_m, vm) + jnp.einsum('bhgnqk,bhnkd->bhgnqd', p_w, vw)
    return o.reshape(Bn, Hk, G, Lq, d)[:, :, :, :L]


def diff_attention(q, k, v, lam):
    Bn, H, _, L, d = q.shape
    nb = -(-L // Q_BLOCK)
    Lq = nb * Q_BLOCK
    scale = d ** -0.5
    slopes = alibi_slopes(H).reshape(1, H, 1, 1, 1)
    qb = jnp.pad(q, ((0, 0), (0, 0), (0, 0), (0, Lq - L), (0, 0)))
    qb = qb.reshape(Bn, H, 2, nb, Q_BLOCK, d).transpose(3, 0, 1, 2, 4, 5)
    tq = jnp.arange(Lq).reshape(nb, Q_BLOCK)
    ts = jnp.arange(L)

    def block(args):
        qblk, tblk = args
        s = jnp.einsum('bhjqd,bhjkd->bhjqk', qblk, k).astype(jnp.float32) * scale
        s = s - slopes * jnp.abs(tblk[:, None] - ts[None, :])
        p = jax.nn.softmax(s, axis=-1)
        w = (p[:, :, 0] - lam * p[:, :, 1]).astype(v.dtype)
        return jnp.einsum('bhqk,bhkv->bhqv', w, v)

    o = lax.map(block, (qb, tq))
    return o.transpose(1, 2, 0, 3, 4).reshape(Bn, H, Lq, v.shape[-1])[:, :, :L]


def even_mixer(h, w_in, w_out, lb, a_norm, sink):
    Bn, L, _ = h.shape
    z = jnp.einsum('bld,de->ble', h, w_in)
    aq, afw, abw, ai, ag, bq, bk, bv, bg = split_cols(z, EVEN_SIZES)

    def heads(t, n):
        return t.reshape(Bn, L, n, -1).transpose(0, 2, 1, 3)

    a = hgrn2_bidir(heads(aq, A_HEADS), heads(afw, A_HEADS), heads(abw, A_HEADS), heads(ai, A_HEADS), lb)
    a = rms_norm(a, a_norm)
    a = a.transpose(0, 2, 1, 3).reshape(Bn, L, A_WIDTH).astype(h.dtype) * jax.nn.silu(ag)
    q = bq.reshape(Bn, L, B_KV_HEADS, B_GROUP, B_HEAD_DIM).transpose(0, 2, 3, 1, 4)
    o = window_attention(q, heads(bk, B_KV_HEADS), heads(bv, B_KV_HEADS), sink)
    o = o.transpose(0, 3, 1, 2, 4).reshape(Bn, L, B_WIDTH) * jax.nn.silu(bg)
    return jnp.einsum('ble,ed->bld', jnp.concatenate([a, o], axis=-1), w_out)


def odd_mixer(h, w_in, w_out, lam_vecs, c_norm, layer_idx):
    Bn, L, _ = h.shape
    z = jnp.einsum('bld,de->ble', h, w_in)
    cq, ck, cv, cg = split_cols(z, (C_WIDTH, C_WIDTH, C_WIDTH, C_WIDTH))
    q = cq.reshape(Bn, L, C_HEADS, 2, C_HEAD_DIM).transpose(0, 2, 3, 1, 4)
    k = ck.reshape(Bn, L, C_HEADS, 2, C_HEAD_DIM).transpose(0, 2, 3, 1, 4)
    v = cv.reshape(Bn, L, C_HEADS, 2 * C_HEAD_DIM).transpose(0, 2, 1, 3)
    lam_init = 0.8 - 0.6 * math.exp(-0.3 * layer_idx)
    lv = lam_vecs.astype(jnp.float32)
    lam = jnp.exp(jnp.sum(lv[0] * lv[1])) - jnp.exp(jnp.sum(lv[2] * lv[3])) + lam_init
    o = diff_attention(q, k, v, lam)
    o = rms_norm(o, c_norm) * (1.0 - lam_init)
    o = o.transpose(0, 2, 1, 3).reshape(Bn, L, C_WIDTH) * jax.nn.silu(cg)
    return jnp.einsum('ble,ed->bld', o, w_out)


def setup_inputs(seed: int = 0) -> dict:
    key = jax.random.key(seed)
    ks = jax.random.split(key, 16)
    f32 = jnp.float32
    nrm = lambda k, s: jax.random.normal(k, s, f32)
    return {
        "x": nrm(ks[0], (BATCH, SEQ, D_MODEL)),
        "meta_tokens": nrm(ks[1], (N_META, D_MODEL)),
        "norm_a": 1.0 + 0.02 * nrm(ks[2], (N_EVEN, D_MODEL)),
        "w_in_a": nrm(ks[3], (N_EVEN, D_MODEL, EVEN_IN)) * D_MODEL ** -0.5,
        "w_out_a": nrm(ks[4], (N_EVEN, MIX_WIDTH, D_MODEL)) * MIX_WIDTH ** -0.5,
        "hgrn_lb": nrm(ks[5], (2, N_EVEN, A_HEADS * A_DK)),
        "hgrn_norm": 1.0 + 0.02 * nrm(ks[6], (N_EVEN, A_DV)),
        "sink_logits": nrm(ks[7], (N_EVEN, B_HEADS)),
        "norm_c": 1.0 + 0.02 * nrm(ks[8], (N_ODD, D_MODEL)),
        "w_in_c": nrm(ks[9], (N_ODD, D_MODEL, ODD_IN)) * D_MODEL ** -0.5,
        "w_out_c": nrm(ks[10], (N_ODD, MIX_WIDTH, D_MODEL)) * MIX_WIDTH ** -0.5,
        "diff_lambda": 0.1 * nrm(ks[11], (N_ODD, 4, C_HEAD_DIM)),
        "diff_norm": 1.0 + 0.02 * nrm(ks[12], (N_ODD, 2 * C_HEAD_DIM)),
        "final_norm": 1.0 + 0.02 * nrm(ks[13], (D_MODEL,)),
    }


def reference(x, meta_tokens, norm_a, w_in_a, w_out_a, hgrn_lb, hgrn_norm, sink_logits,
              norm_c, w_in_c, w_out_c, diff_lambda, diff_norm, final_norm):
    Bn = x.shape[0]
    meta = jnp.broadcast_to(meta_tokens.astype(x.dtype)[None], (Bn, N_META, D_MODEL))
    h = jnp.concatenate([meta, x], axis=1)
    lbs = jnp.cumsum(jax.nn.softmax(hgrn_lb.astype(jnp.float32), axis=1), axis=1)
    lbs = lbs - lbs[:, :1]
    for layer in range(DEPTH):
        i = layer // 2
        if layer % 2 == 0:
            h = h + even_mixer(rms_norm(h, norm_a[i]), w_in_a[i], w_out_a[i],
                               lbs[:, i].reshape(2, A_HEADS, A_DK), hgrn_norm[i], sink_logits[i])
        else:
            h = h + odd_mixer(rms_norm(h, norm_c[i]), w_in_c[i], w_out_c[i],
                              diff_lambda[i], diff_norm[i], layer)
    return rms_norm(h, final_norm)[:, N_META:]
```

```python
import math
from contextlib import ExitStack
import numpy as np
import concourse.bass as bass
import concourse.mybir as mybir
from concourse.bass_utils import run_bass_kernel_spmd

F32 = mybir.dt.float32
BF16 = mybir.dt.bfloat16
AF = mybir.ActivationFunctionType
ALU = mybir.AluOpType
AX = mybir.AxisListType

L = 2064
NX = 2048
NMETA = 16
D = 2048
EPS = 1e-6
TT = [(i * 128, 128) for i in range(16)] + [(2048, 16)]
QB = [(i * 512, 512) for i in range(4)] + [(2048, 16)]
DELTAS = [128 * m for m in range(1, 16)] + [16 + 128 * m for m in range(16)]
NDEL = len(DELTAS)
SAME_ENGINE_SYNC = True
import os
LOOK = int(os.environ.get('KLOOK', '1'))
ST3 = int(os.environ.get('KST3', '1'))


def vidx(s):
    return s if s < 2048 else s - 2048 - 16


def slopes16():
    return [2.0 ** (-8.0 * (i + 1) / 16) for i in range(16)]


class Eng:
    def __init__(self, nc, es, name, e, ndma):
        self.name = name
        self.e = e
        self.sem = es.enter_context(nc.semaphore("s_" + name))
        self.cnt = 0
        self.seen = {}
        self.ring = [[es.enter_context(nc.semaphore("d_%s%d" % (name, i))), 0] for i in range(ndma)]
        self.ri = 0


class Buf:
    def __init__(self, t):
        self.t = t
        self.w = None
        self.rs = {}

    def __getitem__(self, k):
        return self.t[k]


class K:
    def __init__(self, nc, es):
        self.nc = nc
        self.es = es
        self.E = {
            "pe": Eng(nc, es, "pe", nc.tensor, 0),
            "dve": Eng(nc, es, "dve", nc.vector, 0),
            "act": Eng(nc, es, "act", nc.scalar, 0),
            "pool": Eng(nc, es, "pool", nc.gpsimd, 16),
            "sp": Eng(nc, es, "sp", nc.sync, 24),
        }
        self.nbuf = 0
        self.log = []

    def sb(self, shape, dt, es=None, name=None):
        self.nbuf += 1
        t = (es or self.es).enter_context(self.nc.sbuf_tensor("%s_%d" % (name or "sb", self.nbuf), list(shape), dt))
        return Buf(t)

    def wait(self, eng, tok):
        if tok is None:
            return
        sid, sem, val = tok
        if eng.seen.get(sid, 0) >= val:
            return
        if sid == id(eng.sem) and (eng.name == "pe" or not SAME_ENGINE_SYNC):
            return
        eng.e.wait_ge(sem, val)
        eng.seen[sid] = val
        self.log.append((eng.name, "w", sid, val))

    def _deps(self, eng, reads, writes):
        for b in reads:
            self.wait(eng, b.w)
        for b in writes:
            self.wait(eng, b.w)
            for tok in list(b.rs.values()):
                self.wait(eng, tok)

    def _mark(self, tok, reads, writes):
        for b in reads:
            old = b.rs.get(tok[0])
            if old is None or old[2] < tok[2]:
                b.rs[tok[0]] = tok
        for b in writes:
            b.w = tok
            b.rs = {}

    def op(self, en, fn, reads=(), writes=()):
        eng = self.E[en]
        self._deps(eng, reads, writes)
        ins = fn(eng.e)
        eng.cnt += 1
        ins.then_inc(eng.sem, 1)
        self.log.append((eng.name, "i", id(eng.sem), 1))
        tok = (id(eng.sem), eng.sem, eng.cnt)
        self._mark(tok, reads, writes)
        return tok

    def mms(self, fns, reads=(), writes=()):
        eng = self.E["pe"]
        self._deps(eng, reads, writes)
        ins = None
        for fn in fns:
            ins = fn(eng.e)
        eng.cnt += 1
        ins.then_inc(eng.sem, 1)
        self.log.append((eng.name, "i", id(eng.sem), 1))
        tok = (id(eng.sem), eng.sem, eng.cnt)
        self._mark(tok, reads, writes)
        return tok

    def dma(self, qn, out, in_, reads=(), writes=()):
        eng = self.E[qn]
        self._deps(eng, reads, writes)
        slot = eng.ring[eng.ri % len(eng.ring)]
        eng.ri += 1
        if slot[1] > 0:
            self.wait(eng, (id(slot[0]), slot[0], slot[1]))
        ins = eng.e.dma_start(out=out, in_=in_)
        slot[1] += 16
        ins.then_inc(slot[0], 16)
        self.log.append((eng.name, "i", id(slot[0]), 16))
        tok = (id(slot[0]), slot[0], slot[1])
        self._mark(tok, reads, writes)
        return tok

    def check_deadlock(self):
        qs = {}
        for ev in self.log:
            qs.setdefault(ev[0], []).append(ev)
        ptr = {n: 0 for n in qs}
        sem = {}
        prog = True
        while prog:
            prog = False
            for n, q in qs.items():
                while ptr[n] < len(q):
                    _, kind, sid, val = q[ptr[n]]
                    if kind == "i":
                        sem[sid] = sem.get(sid, 0) + val
                    elif sem.get(sid, 0) < val:
                        break
                    ptr[n] += 1
                    prog = True
        stuck = {n: (ptr[n], len(q), q[ptr[n]]) for n, q in qs.items() if ptr[n] < len(q)}
        if stuck:
            names = {id(e.sem): e.name for e in self.E.values()}
            for e in self.E.values():
                for i, sl in enumerate(e.ring):
                    names[id(sl[0])] = "%s_dma%d" % (e.name, i)
            msg = "; ".join("%s at %d/%d waits %s>=%d (have %d)" % (n, p, t, names.get(ev[2]), ev[3], sem.get(ev[2], 0))
                            for n, (p, t, ev) in stuck.items())
            raise RuntimeError("DEADLOCK in emitted program: " + msg)

    def barrier(self):
        toks = []
        for e in self.E.values():
            if e.cnt > 0:
                toks.append((id(e.sem), e.sem, e.cnt))
            for s in e.ring:
                if s[1] > 0:
                    toks.append((id(s[0]), s[0], s[1]))
        for e in self.E.values():
            for t in toks:
                if t[0] == id(e.sem):
                    continue
                self.wait(e, t)


def build(layers, do_final, out_rows):
    nc = bass.Bass("TRN2", target_bir_lowering=False)
    dr = {}

    def din(name, shape):
        dr[name] = nc.dram_tensor(name, list(shape), F32, kind="ExternalInput").ap()
        return dr[name]

    x_d = din("x", [NX, D])
    meta_d = din("meta", [NMETA, D])
    nrm_d = din("norms", [5, D])
    ident_d = din("ident", [128, 128])
    dlin_d = din("dlin", [128, 512])
    dabs_d = din("dabs", [128, 896])
    cb_d = din("cbtab", [128, 16 * NDEL])
    n_odd = sum(1 for l in layers if l[0] == "O")
    n_even = sum(1 for l in layers if l[0] == "E")
    if n_odd:
        winc_d = din("winc", [2 * 16 * 4 * 2 * 128, 2048])
        woutc_d = din("woutc", [2 * 4 * 8 * 128, 2048])
        dlam_d = din("dlam", [2, 512])
        dnorm_d = din("dnorm", [2, 256])
    if n_even:
        wina_d = din("wina", [2 * 120 * 128, 2048])
        wouta_d = din("wouta", [2 * 4 * 8 * 128, 2048])
        smask_d = din("smask", [128, L])
        mab_d = din("mab", [128, 1024])
        mtri_d = din("mtri", [128, 256])
        lb_d = din("lbl", [128, 64])
        hnorm_d = din("hnorm", [2, 128])
        wabs_d = din("wabs", [128, 1152])
        mclip_d = din("mclip", [128, 1024])
        sink_d = din("sink", [2, 16])
    out_d = nc.dram_tensor("out", [out_rows, D], F32, kind="ExternalOutput").ap()
    hd = nc.dram_tensor("hd", [L, D], F32, kind="Internal").ap()
    yTd = nc.dram_tensor("yTd", [32, 128, L], BF16, kind="Internal").ap()

    with ExitStack() as es:
        k = K(nc, es)
        ident_f = k.sb([128, 128], F32)
        ident = k.sb([128, 128], BF16)
        wst = [k.sb([128, 2048], F32, name="wst") for _ in range(2)]
        ps = [Buf(es.enter_context(nc.psum_tensor("ps%d" % i, [128, 512], F32))) for i in range(7)]
        pst1 = es.enter_context(nc.psum_tensor("pst", [128, 1024], BF16))
        pstq = [Buf(pst1)]
        PSP = [ps]
        hd_b = [Buf(None) for _ in TT]
        yTd_b = [Buf(None) for _ in range(16)]
        st = {"wst": 0, "wb": 0, "ps": 0, "pst": 0, "uq": 0}

        def nxt(key, lst):
            i = st[key] % len(lst)
            st[key] += 1
            return lst[i]

        epsc = k.sb([128, 4], F32)
        k.op("pool", lambda e: e.memset(epsc[:, 0:1], EPS), writes=[epsc])
        for wi in range(2):
            k.op("pool", lambda e, wi=wi: e.memset(epsc[:, 1 + wi:2 + wi],
                                                   math.log(1.0 - (0.8 - 0.6 * math.exp(-0.3 * (2 * wi + 1))))),
                 writes=[epsc])
        k.dma("sp", ident_f[:], ident_d[:, :], writes=[ident_f])
        k.op("dve", lambda e: e.tensor_copy(out=ident[:], in_=ident_f[:]), reads=[ident_f], writes=[ident])

        if n_odd:
            lams = k.sb([128, 4], F32)
            lame = k.sb([128, 4], F32)
            neglam = k.sb([128, 2], F32)
            lam_es = ExitStack()
            lamt = k.sb([128, 2, 512], F32, lam_es)
            lamp = k.sb([128, 2, 2, 128], F32, lam_es)
            for i in range(2):
                k.dma("sp", lamt[:, i, :], dlam_d[i:i + 1, :].partition_broadcast(128), writes=[lamt])
            for i in range(2):
                for j in range(2):
                    k.op("dve", lambda e, i=i, j=j: e.tensor_tensor(
                        out=lamp[:, i, j, :], in0=lamt[:, i, 256 * j:256 * j + 128],
                        in1=lamt[:, i, 256 * j + 128:256 * j + 256], op=ALU.mult), reads=[lamt], writes=[lamp])
            for i in range(2):
                for j in range(2):
                    k.op("dve", lambda e, i=i, j=j: e.reduce_sum(
                        out=lams[:, 2 * i + j:2 * i + j + 1], in_=lamp[:, i, j, :], axis=AX.X),
                        reads=[lamp], writes=[lams])
            k.op("act", lambda e: e.activation(out=lame[:], in_=lams[:], func=AF.Exp), reads=[lams], writes=[lame])
            k.barrier()
            lam_es.close()

        def lam_init_of(layer_idx):
            return 0.8 - 0.6 * math.exp(-0.3 * layer_idx)

        def h_src(first, ti):
            s, n = TT[ti]
            if first:
                return (x_d[s:s + n, :] if s < 2048 else meta_d[0:n, :])
            return hd[s:s + n, :]

        def load_w_slice(row0, dst, ls):
            for half in range(2):
                stg = nxt("wst", wst)
                k.dma("sp", stg[:], ls[0][row0 + half * 128:row0 + half * 128 + 128, :], writes=[stg])
                k.op("pool", lambda e, stg=stg, half=half: e.tensor_copy(
                    out=dst[:, 8 * half:8 * half + 8, :],
                    in_=stg[:].rearrange("p (c n) -> p c n", c=8)), reads=[stg], writes=[dst])

        def phase_norm(first, nrow, xnT, ls):
            gt = k.sb([128, D], F32, ls, "gt")
            hb = [k.sb([128, D], F32, ls, "hb") for _ in range(2)]
            xb = [k.sb([128, D], BF16, ls, "xb") for _ in range(2)]
            junk = k.sb([128, D], BF16, ls, "junk")
            ssb = [k.sb([128, 2], F32, ls, "ss") for _ in range(2)]
            k.dma("sp", gt[:], nrm_d[nrow:nrow + 1, :].partition_broadcast(128), writes=[gt])
            for ti, (s, n) in enumerate(TT):
                h = hb[ti % 2]
                xn = xb[ti % 2]
                ss = ssb[ti % 2]
                k.dma("sp", h[0:n, :], h_src(first, ti), reads=[hd_b[ti]], writes=[h])
                k.op("act", lambda e: e.activation(out=junk[0:n, :], in_=h[0:n, :], func=AF.Square,
                                                   accum_out=ss[0:n, 0:1]), reads=[h], writes=[junk, ss])
                k.op("act", lambda e: e.activation(out=ss[0:n, 1:2], in_=ss[0:n, 0:1], func=AF.Ln, scale=1.0 / D,
                                                   bias=epsc[0:n, 0:1]), reads=[ss, epsc], writes=[ss])
                k.op("act", lambda e: e.activation(out=ss[0:n, 0:1], in_=ss[0:n, 1:2], func=AF.Exp, scale=-0.5),
                     reads=[ss], writes=[ss])
                k.op("dve", lambda e: e.scalar_tensor_tensor(out=xn[0:n, :], in0=h[0:n, :], scalar=ss[0:n, 0:1],
                                                             in1=gt[0:n, :], op0=ALU.mult, op1=ALU.mult),
                     reads=[h, ss, gt], writes=[xn])
                for g in range(2):
                    pt = nxt("pst", pstq)
                    k.mms([lambda e, c=c, g=g, pt=pt: e.transpose(
                        out=pt[:, c * 128:c * 128 + n], in_=xn[0:n, (8 * g + c) * 128:(8 * g + c + 1) * 128],
                        identity=ident[0:n, 0:n]) for c in range(8)], reads=[xn, ident], writes=[pt])
                    k.op("act" if g == 0 else "dve", lambda e, g=g, pt=pt: (e.copy if g == 0 else e.tensor_copy)(
                        out=xnT[:, 8 * g:8 * g + 8, s:s + n],
                        in_=pt[:, :].rearrange("p (c t) -> p c t", c=8)[:, :, 0:n]), reads=[pt], writes=[xnT])

        def proj_fm(xnT, w, c0, dst, dj, scale):
            for bi, (s, n) in enumerate(QB):
                p = nxt("ps", PSP[0])
                k.mms([lambda e, c=c, p=p: e.matmul(p[:, 0:n], lhsT=w[:, c, c0:c0 + 128], rhs=xnT[:, c, s:s + n],
                                                    start=(c == 0), stop=(c == 15)) for c in range(16)],
                      reads=[xnT, w], writes=[p])
                if scale is None:
                    k.op("act", lambda e, p=p: e.copy(out=dst[:, dj, s:s + n], in_=p[:, 0:n]), reads=[p], writes=[dst])
                else:
                    k.op("act", lambda e, p=p: e.mul(out=dst[:, dj, s:s + n], in_=p[:, 0:n], mul=scale),
                         reads=[p], writes=[dst])

        def proj_tm(xnT, w, ti, p, ncols=256, c0=0):
            s, n = TT[ti]
            k.mms([lambda e, c=c: e.matmul(p[0:n, 0:ncols], lhsT=xnT[:, c, s:s + n], rhs=w[:, c, c0:c0 + ncols],
                                           start=(c == 0), stop=(c == 15)) for c in range(16)],
                  reads=[xnT, w], writes=[p])

        def silu_from_psum(p, n, ncols, dst_ap, dstbuf, tmpa, tmpb):
            k.op("act", lambda e: e.activation(out=tmpa[0:n, 0:ncols], in_=p[0:n, 0:ncols], func=AF.Exp, scale=-1.0),
                 reads=[p], writes=[tmpa])
            k.op("dve", lambda e: e.tensor_scalar(out=tmpa[0:n, 0:ncols], in0=tmpa[0:n, 0:ncols], scalar1=1.0,
                                                  scalar2=None, op0=ALU.add), reads=[tmpa], writes=[tmpa])
            k.op("dve", lambda e: e.reciprocal(out=tmpb[0:n, 0:ncols], in_=tmpa[0:n, 0:ncols]), reads=[tmpa], writes=[tmpb])
            k.op("dve", lambda e: e.tensor_tensor(out=dst_ap, in0=p[0:n, 0:ncols], in1=tmpb[0:n, 0:ncols],
                                                  op=ALU.mult), reads=[p, tmpb], writes=[dstbuf])

        CT = {}

        def load_consts(ls):
            CT["dlin"] = k.sb([128, 512], F32, ls, "dlin")
            CT["dabs"] = k.sb([128, 896], F32, ls, "dabs")
            CT["cbt"] = k.sb([128, 16 * NDEL], F32, ls, "cbt")
            k.dma("sp", CT["dlin"][:], dlin_d[:, :], writes=[CT["dlin"]])
            k.dma("sp", CT["dabs"][:], dabs_d[:, :], writes=[CT["dabs"]])
            k.dma("sp", CT["cbt"][:], cb_d[:, :], writes=[CT["cbt"]])

        def bias_tile(t0v, N, s0v, kn, slope, h):
            dlin, dabs, cbt = CT["dlin"], CT["dabs"], CT["cbt"]
            if t0v < s0v + kn and s0v < t0v + N:
                off = t0v - s0v
                return dabs[0:kn, 384 + off:384 + off + N], dabs, -slope, None
            if t0v > s0v:
                dl = t0v - s0v
                return dlin[0:kn, 0:N], dlin, -slope, cbt[0:kn, h * NDEL + DELTAS.index(dl):h * NDEL + DELTAS.index(dl) + 1]
            dl = s0v - t0v
            return dlin[0:kn, 0:N], dlin, slope, cbt[0:kn, h * NDEL + DELTAS.index(dl):h * NDEL + DELTAS.index(dl) + 1]

        def odd_mixer(widx, layer_idx, xnT, ls):
            lam_init = lam_init_of(layer_idx)
            sl = slopes16()
            load_consts(ls)
            cbt = CT["cbt"]
            wb = [k.sb([128, 16, 256], BF16, ls, "wb") for _ in range(4)]
            QT = k.sb([128, 2, L], BF16, ls, "QT")
            KT = k.sb([128, 2, L], BF16, ls, "KT")
            Vx = k.sb([128, 17, 257], BF16, ls, "Vx")
            G = k.sb([128, 17, 256], BF16, ls, "G")
            yT = k.sb([128, 2, L], BF16, ls, "yT")
            tmpf = [k.sb([128, 512], F32, ls, "tmpf") for _ in range(4)]
            PT = [k.sb([128, 512], BF16, ls, "PT") for _ in range(4)]
            O = [[k.sb([128, 257], F32, ls, "O") for _ in range(4)] for _ in range(2)]
            ga = k.sb([128, 256], F32, ls, "ga")
            gb = k.sb([128, 256], F32, ls, "gb")
            gn = k.sb([128, 256], F32, ls, "gn")
            sm = [k.sb([128, 8], F32, ls, "sm") for _ in range(2)]
            u1 = [k.sb([128, 256], F32, ls, "u1") for _ in range(2)]
            u2 = [k.sb([128, 256], F32, ls, "u2") for _ in range(2)]
            u3 = [k.sb([128, 256], F32, ls, "u3") for _ in range(2)]
            jk = k.sb([128, 256], F32, ls, "jk")
            ybf = [k.sb([128, 256], BF16, ls, "ybf") for _ in range(2)]
            k.dma("sp", gn[:], dnorm_d[widx:widx + 1, :].partition_broadcast(128), writes=[gn])
            k.op("pool", lambda e: e.memset(Vx[:, :, 256:257], 1.0), writes=[Vx])
            k.op("dve", lambda e: e.tensor_tensor(out=neglam[:, widx:widx + 1], in0=lame[:, 2 * widx + 1:2 * widx + 2],
                                                  in1=lame[:, 2 * widx:2 * widx + 1], op=ALU.subtract),
                 reads=[lame], writes=[neglam])
            k.op("dve", lambda e: e.tensor_scalar(out=neglam[:, widx:widx + 1], in0=neglam[:, widx:widx + 1],
                                                  scalar1=-lam_init, scalar2=None, op0=ALU.add),
                 reads=[neglam], writes=[neglam])
            nl = neglam
            pp = 0
            for h in range(16):
                wq, wk, wv, wg = wb[0], wb[1], wb[2], wb[3]
                base = ((widx * 16 + h) * 4) * 256
                load_w_slice(base + 0 * 256, wq, [winc_d])
                load_w_slice(base + 1 * 256, wk, [winc_d])
                load_w_slice(base + 2 * 256, wv, [winc_d])
                load_w_slice(base + 3 * 256, wg, [winc_d])
                for j in range(2):
                    proj_fm(xnT, wq, j * 128, QT, j, 128 ** -0.5)
                for j in range(2):
                    proj_fm(xnT, wk, j * 128, KT, j, None)
                for ti, (s, n) in enumerate(TT):
                    p = nxt("ps", PSP[0])
                    proj_tm(xnT, wv, ti, p)
                    k.op("act", lambda e, p=p, ti=ti, n=n: e.copy(out=Vx[0:n, ti, 0:256], in_=p[0:n, 0:256]),
                         reads=[p], writes=[Vx])
                    p = nxt("ps", PSP[0])
                    proj_tm(xnT, wg, ti, p)
                    silu_from_psum(p, n, 256, G[0:n, ti, :], G, ga, gb)
                for (t0, N) in QB:
                    t0v = vidx(t0)
                    nqs = (N + 127) // 128
                    for j in range(2):
                        acc = ps[2:6]
                        pend = []
                        stb = [ps[0], ps[1], ps[6]] if ST3 else [ps[0], ps[1]]
                        for kt, (s0, kn) in enumerate(TT):
                            s0v = vidx(s0)
                            stp = stb[kt % len(stb)]
                            k.mms([lambda e: e.matmul(stp[0:kn, 0:N], lhsT=KT[:, j, s0:s0 + kn], rhs=QT[:, j, t0:t0 + N],
                                                      start=True, stop=True)], reads=[KT, QT], writes=[stp])
                            dt_ap, dbuf, coef, cb = bias_tile(t0v, N, s0v, kn, sl[h], h)
                            tf = tmpf[pp % 4]
                            ptile = PT[pp % 4]
                            pp += 1
                            k.op("dve", lambda e: e.scalar_tensor_tensor(out=tf[0:kn, 0:N], in0=dt_ap, scalar=coef,
                                                                         in1=stp[0:kn, 0:N], op0=ALU.mult, op1=ALU.add),
                                 reads=[dbuf, stp], writes=[tf])
                            if cb is None:
                                k.op("act", lambda e: e.activation(out=ptile[0:kn, 0:N], in_=tf[0:kn, 0:N], func=AF.Exp),
                                     reads=[tf], writes=[ptile])
                            else:
                                k.op("act", lambda e: e.activation(out=ptile[0:kn, 0:N], in_=tf[0:kn, 0:N], func=AF.Exp,
                                                                   bias=cb), reads=[tf, cbt], writes=[ptile])
                            if len(pend) >= LOOK:
                                pend.pop(0)()
                            def pv(kt=kt, kn=kn, ptile=ptile):
                                for qs in range(nqs):
                                    qn = min(128, N - qs * 128)
                                    a = acc[qs]
                                    k.mms([lambda e: e.matmul(a[0:qn, 0:257], lhsT=ptile[0:kn, qs * 128:qs * 128 + qn],
                                                              rhs=Vx[0:kn, kt, :], start=(kt == 0), stop=(kt == 16))],
                                          reads=[ptile, Vx], writes=[a])
                            pend.append(pv)
                        while pend:
                            pend.pop(0)()
                        for qs in range(nqs):
                            qn = min(128, N - qs * 128)
                            k.op("act" if qs % 2 == 0 else "dve",
                                 lambda e, qs=qs, qn=qn: (e.copy if qs % 2 == 0 else e.tensor_copy)(
                                     out=O[j][qs][0:qn, :], in_=acc[qs][0:qn, 0:257]),
                                 reads=[acc[qs]], writes=[O[j][qs]])
                    for qs in range(nqs):
                        qn = min(128, N - qs * 128)
                        tok0 = t0 + qs * 128
                        ti = tok0 // 128
                        o1, o2 = O[0][qs], O[1][qs]
                        s_ = sm[qs % 2]
                        a1, a2, a3 = u1[qs % 2], u2[qs % 2], u3[qs % 2]
                        yb = ybf[qs % 2]
                        k.op("dve", lambda e: e.reciprocal(out=s_[0:qn, 0:1], in_=o1[0:qn, 256:257]), reads=[o1], writes=[s_])
                        k.op("dve", lambda e: e.reciprocal(out=s_[0:qn, 1:2], in_=o2[0:qn, 256:257]), reads=[o2], writes=[s_])
                        k.op("dve", lambda e: e.tensor_tensor(out=s_[0:qn, 2:3], in0=s_[0:qn, 1:2],
                                                              in1=nl[0:qn, widx:widx + 1], op=ALU.mult),
                             reads=[s_, nl], writes=[s_])
                        k.op("dve", lambda e: e.tensor_scalar(out=a1[0:qn, :], in0=o1[0:qn, 0:256], scalar1=s_[0:qn, 0:1],
                                                              scalar2=None, op0=ALU.mult), reads=[o1, s_], writes=[a1])
                        k.op("dve", lambda e: e.scalar_tensor_tensor(out=a2[0:qn, :], in0=o2[0:qn, 0:256],
                                                                     scalar=s_[0:qn, 2:3], in1=a1[0:qn, :],
                                                                     op0=ALU.mult, op1=ALU.add),
                             reads=[o2, s_, a1], writes=[a2])
                        k.op("act", lambda e: e.activation(out=jk[0:qn, :], in_=a2[0:qn, :], func=AF.Square,
                                                           accum_out=s_[0:qn, 3:4]), reads=[a2], writes=[jk, s_])
                        k.op("act", lambda e: e.activation(out=s_[0:qn, 4:5], in_=s_[0:qn, 3:4], func=AF.Ln,
                                                           scale=1.0 / 256, bias=epsc[0:qn, 0:1]),
                             reads=[s_, epsc], writes=[s_])
                        k.op("act", lambda e: e.activation(out=s_[0:qn, 5:6], in_=s_[0:qn, 4:5], func=AF.Exp,
                                                           scale=-0.5, bias=epsc[0:qn, 1 + widx:2 + widx]),
                             reads=[s_, epsc], writes=[s_])
                        k.op("dve", lambda e: e.scalar_tensor_tensor(out=a3[0:qn, :], in0=a2[0:qn, :], scalar=s_[0:qn, 5:6],
                                                                     in1=gn[0:qn, :], op0=ALU.mult, op1=ALU.mult),
                             reads=[a2, s_, gn], writes=[a3])
                        k.op("pool", lambda e: e.tensor_tensor(out=yb[0:qn, :], in0=a3[0:qn, :], in1=G[0:qn, ti, :],
                                                               op=ALU.mult), reads=[a3, G], writes=[yb])
                        pt = nxt("pst", pstq)
                        k.mms([lambda e, c=c: e.transpose(out=pt[:, c * 128:c * 128 + qn], in_=yb[0:qn, c * 128:(c + 1) * 128],
                                                          identity=ident[0:qn, 0:qn]) for c in range(2)],
                              reads=[yb, ident], writes=[pt])
                        k.op("act", lambda e: e.copy(out=yT[:, :, tok0:tok0 + qn],
                                                     in_=pt[:, 0:256].rearrange("p (c t) -> p c t", c=2)[:, :, 0:qn]),
                             reads=[pt], writes=[yT])
                k.dma("pool", yTd[2 * h:2 * h + 2, :, :].rearrange("c p t -> p c t"), yT[:], reads=[yT], writes=[yTd_b[h]])

        def proj_fm_cb(xnT, w, c0, cb):
            for bi, (s, n) in enumerate(QB):
                p = nxt("ps", PSP[0])
                k.mms([lambda e, c=c, p=p: e.matmul(p[:, 0:n], lhsT=w[:, c, c0:c0 + 128], rhs=xnT[:, c, s:s + n],
                                                    start=(c == 0), stop=(c == 15)) for c in range(16)],
                      reads=[xnT, w], writes=[p])
                cb(p, s, n)

        def load_w128(sidx_row0, dst, c0):
            stg = nxt("wst", wst)
            k.dma("sp", stg[:], wina_d[sidx_row0:sidx_row0 + 128, :], writes=[stg])
            k.op("pool", lambda e: e.tensor_copy(out=dst[:, :, c0:c0 + 128],
                                                 in_=stg[:].rearrange("p (c n) -> p c n", c=16)),
                 reads=[stg], writes=[dst])

        def even_mixer_A(widx, xnT, ls):
            wb = [k.sb([128, 16, 256], BF16, ls, "wb") for _ in range(3)]
            qA = k.sb([128, L], BF16, ls, "qA")
            qB = k.sb([128, L], BF16, ls, "qB")
            T1 = k.sb([128, L], F32, ls, "T1")
            T2 = k.sb([128, L], F32, ls, "T2")
            T3 = k.sb([128, L], F32, ls, "T3")
            QEA = k.sb([128, L], BF16, ls, "QEA")
            QEB = k.sb([128, L], BF16, ls, "QEB")
            KE = k.sb([128, L], BF16, ls, "KE")
            KL = k.sb([128, L], BF16, ls, "KL")
            V = k.sb([128, 17, 128], BF16, ls, "V")
            G = k.sb([128, 17, 128], BF16, ls, "G")
            OF = k.sb([128, 17, 128], F32, ls, "OF")
            yT = k.sb([128, L], BF16, ls, "yT")
            smk = k.sb([128, L], BF16, ls, "smk")
            mAB = k.sb([128, 2, 512], F32, ls, "mAB")
            mtri = k.sb([128, 2, 128], F32, ls, "mtri")
            lbt = k.sb([128, 64], F32, ls, "lbt")
            lbc = k.sb([128, 2, 16], F32, ls, "lbc")
            omc = k.sb([128, 2, 16], F32, ls, "omc")
            gna = k.sb([128, 128], F32, ls, "gna")
            ATm = [k.sb([128, 128], BF16, ls, "ATm") for _ in range(3)]
            KLt = [[k.sb([128, 128], BF16, ls, "KLt") for _ in range(2)] for _ in range(2)]
            S = [k.sb([128, 128], F32, ls, "S") for _ in range(4)]
            SBALL = k.sb([128, 33, 128], BF16, ls, "SBALL")
            ECs = [k.sb([128, 34], F32, ls, "EC") for _ in range(2)]
            k.op("pool", lambda e: e.memset(SBALL[:, 0, :], 0.0), writes=[SBALL])
            ubanks = [ps[4], ps[5], ps[6]]
            PSP[0] = ps[0:4]
            ga = k.sb([128, 128], F32, ls, "ga")
            gb = k.sb([128, 128], F32, ls, "gb")
            sm = [k.sb([128, 8], F32, ls, "sm") for _ in range(2)]
            a1 = [k.sb([128, 128], F32, ls, "a1") for _ in range(2)]
            a2 = [k.sb([128, 128], F32, ls, "a2") for _ in range(2)]
            jk = k.sb([128, 128], F32, ls, "jk")
            ybf = [k.sb([128, 128], BF16, ls, "ybf") for _ in range(2)]
            k.dma("pool", smk[:], smask_d[:, :], writes=[smk])
            k.dma("sp", mAB[:], mab_d[:, :].rearrange("p (a n) -> p a n", a=2), writes=[mAB])
            k.dma("sp", mtri[:], mtri_d[:, :].rearrange("p (a n) -> p a n", a=2), writes=[mtri])
            k.dma("sp", lbt[:], lb_d[:, :], writes=[lbt])
            k.dma("sp", gna[:], hnorm_d[widx:widx + 1, :].partition_broadcast(128), writes=[gna])
            for par in range(2):
                for ab in range(2):
                    k.op("pool", lambda e, par=par, ab=ab: e.memset(KLt[par][ab][:], 0.0), writes=[KLt[par][ab]])
            lb4 = lbt[:].rearrange("p (r l h) -> p r l h", r=2, l=2)
            if widx == 0:
                k.op("pool", lambda e: e.memset(lbc[:], 0.0), writes=[lbc])
                k.op("pool", lambda e: e.memset(omc[:], 1.0), writes=[omc])
            else:
                k.op("dve", lambda e: e.tensor_tensor(out=omc[:], in0=lb4[:, :, 0, :], in1=lb4[:, :, 1, :], op=ALU.subtract),
                     reads=[lbt], writes=[omc])
                k.op("act", lambda e: e.activation(out=omc[:], in_=omc[:], func=AF.Exp), reads=[omc], writes=[omc])
                k.op("dve", lambda e: e.tensor_scalar(out=omc[:], in0=omc[:], scalar1=1.0, scalar2=None, op0=ALU.add),
                     reads=[omc], writes=[omc])
                k.op("dve", lambda e: e.reciprocal(out=lbc[:], in_=omc[:]), reads=[omc], writes=[lbc])
                k.op("dve", lambda e: e.tensor_scalar(out=omc[:], in0=lbc[:], scalar1=-1.0, scalar2=1.0, op0=ALU.mult,
                                                      op1=ALU.add), reads=[lbc], writes=[omc])
            for h in range(16):
                wq, wzf, wzb, wv, wg = (wb[0], 0), (wb[0], 128), (wb[1], 0), (wb[1], 128), (wb[2], 0)
                for (wt, c0), si in ((wq, h), (wzf, 16 + h), (wzb, 32 + h), (wv, 48 + h), (wg, 64 + h)):
                    load_w128((widx * 120 + si) * 128, wt, c0)
                def cbq(p, s, n):
                    k.op("dve", lambda e: e.tensor_tensor(out=qA[:, s:s + n], in0=p[:, 0:n], in1=mAB[:, 0, 0:n], op=ALU.mult),
                         reads=[p, mAB], writes=[qA])
                    k.op("dve", lambda e: e.tensor_tensor(out=qB[:, s:s + n], in0=p[:, 0:n], in1=mAB[:, 1, 0:n], op=ALU.mult),
                         reads=[p, mAB], writes=[qB])
                proj_fm_cb(xnT, wq[0], wq[1], cbq)
                for ti, (s, n) in enumerate(TT):
                    p = nxt("ps", PSP[0])
                    proj_tm(xnT, wv[0], ti, p, 128, wv[1])
                    k.op("act", lambda e: e.copy(out=V[0:n, ti, :], in_=p[0:n, 0:128]), reads=[p], writes=[V])
                    p = nxt("ps", PSP[0])
                    proj_tm(xnT, wg[0], ti, p, 128, wg[1])
                    silu_from_psum(p, n, 128, G[0:n, ti, :], G, ga, gb)
                for di in range(2):
                    wz = wzf if di == 0 else wzb
                    def cbz(p, s, n):
                        k.op("act", lambda e: e.activation(out=T1[:, s:s + n], in_=p[:, 0:n], func=AF.Exp, scale=-1.0),
                             reads=[p], writes=[T1])
                    proj_fm_cb(xnT, wz[0], wz[1], cbz)
                    k.op("pool", lambda e: e.tensor_scalar(out=T1[:], in0=T1[:], scalar1=1.0, scalar2=None, op0=ALU.add),
                         reads=[T1], writes=[T1])
                    k.op("dve", lambda e: e.reciprocal(out=T2[:], in_=T1[:]), reads=[T1], writes=[T2])
                    k.op("dve", lambda e: e.tensor_scalar(out=T1[:], in0=T2[:], scalar1=omc[:, di, h:h + 1],
                                                          scalar2=lbc[:, di, h:h + 1], op0=ALU.mult, op1=ALU.add),
                         reads=[T2, omc, lbc], writes=[T1])
                    k.op("act", lambda e: e.activation(out=T2[:], in_=T1[:], func=AF.Ln), reads=[T1], writes=[T2])
                    k.op("pool", lambda e: e.tensor_scalar(out=T1[:], in0=T1[:], scalar1=-1.0, scalar2=1.0, op0=ALU.mult,
                                                           op1=ALU.add), reads=[T1], writes=[T1])
                    k.op("dve", lambda e: e.tensor_tensor_scan(out=T3[:], data0=smk[:], data1=T2[:], initial=0.0,
                                                               op0=ALU.mult, op1=ALU.add),
                         reads=[smk, T2], writes=[T3])
                    if di == 0:
                        k.op("act", lambda e: e.activation(out=T2[:], in_=T3[:], func=AF.Exp), reads=[T3], writes=[T2])
                        Eb, Fb = T2, T3
                    else:
                        k.op("pool", lambda e: e.tensor_tensor(out=T2[:], in0=T2[:], in1=T3[:], op=ALU.subtract),
                             reads=[T2, T3], writes=[T2])
                        k.op("dve", lambda e: e.tensor_tensor(
                            out=T2[:, 0:2048].rearrange("p (c t) -> p c t", t=64),
                            in0=T2[:, 0:2048].rearrange("p (c t) -> p c t", t=64),
                            in1=T3[:, 0:2048].rearrange("p (c t) -> p c t", t=64)[:, :, 63:64].to_broadcast([128, 32, 64]),
                            op=ALU.add), reads=[T2, T3], writes=[T2])
                        k.op("dve", lambda e: e.tensor_tensor(out=T2[:, 2048:2064], in0=T2[:, 2048:2064],
                                                              in1=T3[:, 2063:2064].to_broadcast([128, 16]), op=ALU.add),
                             reads=[T2, T3], writes=[T2])
                        k.op("act", lambda e: e.activation(out=T3[:], in_=T2[:], func=AF.Exp), reads=[T2], writes=[T3])
                        Eb, Fb = T3, T2
                    k.op("dve", lambda e: e.reciprocal(out=Fb[:], in_=Eb[:]), reads=[Eb], writes=[Fb])
                    k.op("pool", lambda e: e.tensor_tensor(out=QEA[:], in0=qA[:], in1=Eb[:], op=ALU.mult),
                         reads=[qA, Eb], writes=[QEA])
                    k.op("pool", lambda e: e.tensor_tensor(out=QEB[:], in0=qB[:], in1=Eb[:], op=ALU.mult),
                         reads=[qB, Eb], writes=[QEB])
                    k.op("dve", lambda e: e.tensor_tensor(out=KE[:], in0=T1[:], in1=Fb[:], op=ALU.mult),
                         reads=[T1, Fb], writes=[KE])
                    ecol = 63 if di == 0 else 0
                    k.op("dve", lambda e: e.tensor_tensor(
                        out=KL[:, 0:2048].rearrange("p (c t) -> p c t", t=64),
                        in0=KE[:, 0:2048].rearrange("p (c t) -> p c t", t=64),
                        in1=Eb[:, 0:2048].rearrange("p (c t) -> p c t", t=64)[:, :, ecol:ecol + 1].to_broadcast([128, 32, 64]),
                        op=ALU.mult), reads=[KE, Eb], writes=[KL])
                    mcol = 2063 if di == 0 else 2048
                    k.op("dve", lambda e: e.tensor_tensor(out=KL[:, 2048:2064], in0=KE[:, 2048:2064],
                                                          in1=Eb[:, mcol:mcol + 1].to_broadcast([128, 16]), op=ALU.mult),
                         reads=[KE, Eb], writes=[KL])
                    EC = ECs[di]
                    k.op("pool", lambda e: e.tensor_copy(
                        out=EC[:, 0:32], in_=Eb[:, 0:2048].rearrange("p (c t) -> p c t", t=64)[:, :, ecol]),
                        reads=[Eb], writes=[EC])
                    k.op("pool", lambda e: e.tensor_copy(out=EC[:, 32:33], in_=Eb[:, mcol:mcol + 1]), reads=[Eb], writes=[EC])
                    order = [16] + list(range(16)) if di == 0 else list(range(15, -1, -1)) + [16]
                    seqidx = {}
                    k.op("pool", lambda e: e.memset(S[0][:], 0.0), writes=[S[0]])
                    if di == 0:
                        groups = [[16]] + [[2 * g_, 2 * g_ + 1] for g_ in range(8)]
                    else:
                        groups = [[15 - 2 * g_, 14 - 2 * g_] for g_ in range(8)] + [[16]]
                    step = 0
                    tcount = 0
                    for gt in groups:
                        bank = ubanks[st["uq"] % 3]
                        st["uq"] += 1
                        slots = []
                        for ti in gt:
                            s, n = TT[ti]
                            par = tcount % 2
                            tcount += 1
                            pt = nxt("pst", pstq)
                            k.mms([lambda e: e.transpose(out=pt[0:n, 0:128], in_=KL[:, s:s + n], identity=ident[:, :])],
                                  reads=[KL, ident], writes=[pt])
                            nA = min(n, 64)
                            k.op("act", lambda e: e.copy(out=KLt[par][0][0:nA, :], in_=pt[0:nA, 0:128]), reads=[pt],
                                 writes=[KLt[par][0]])
                            if n == 128:
                                k.op("act", lambda e: e.copy(out=KLt[par][1][64:128, :], in_=pt[64:128, 0:128]), reads=[pt],
                                     writes=[KLt[par][1]])
                            chunks = [0] if n == 16 else ([0, 1] if di == 0 else [1, 0])
                            for ab in chunks:
                                cid = 32 if ti == 16 else 2 * ti + ab
                                q_ = len(slots)
                                seqidx[(ti, ab)] = step + q_
                                k.mms([lambda e: e.matmul(bank[:, q_ * 128:(q_ + 1) * 128], lhsT=KLt[par][ab][0:n, :],
                                                          rhs=V[0:n, ti, :], start=True, stop=True)],
                                      reads=[KLt[par][ab], V], writes=[bank])
                                slots.append((q_, cid))
                        for (q_, cid) in slots:
                            if step < 32:
                                so, sn = S[step % 4], S[(step + 1) % 4]
                                k.op("dve", lambda e: e.scalar_tensor_tensor(out=sn[:], in0=so[:], scalar=EC[:, cid:cid + 1],
                                                                             in1=bank[:, q_ * 128:(q_ + 1) * 128],
                                                                             op0=ALU.mult, op1=ALU.add),
                                     reads=[so, EC, bank], writes=[sn])
                                k.op("act", lambda e: e.copy(out=SBALL[:, step + 1, :], in_=sn[:]), reads=[sn], writes=[SBALL])
                            step += 1
                    def at_stage(oi, ti):
                        s, n = TT[ti]
                        pa = nxt("ps", PSP[0])
                        k.mms([lambda e: e.matmul(pa[0:n, 0:n], lhsT=KE[:, s:s + n], rhs=QEA[:, s:s + n], start=True, stop=False),
                               lambda e: e.matmul(pa[0:n, 0:n], lhsT=KE[:, s:s + n], rhs=QEB[:, s:s + n], start=False, stop=True)],
                              reads=[KE, QEA, QEB], writes=[pa])
                        at = ATm[oi % 3]
                        k.op("dve", lambda e: e.tensor_tensor(out=at[0:n, 0:n], in0=pa[0:n, 0:n], in1=mtri[0:n, di, 0:n],
                                                              op=ALU.mult), reads=[pa, mtri], writes=[at])
                    at_stage(0, order[0])
                    for oi, ti in enumerate(order):
                        s, n = TT[ti]
                        if oi + 1 < len(order):
                            at_stage(oi + 1, order[oi + 1])
                        at = ATm[oi % 3]
                        chunks = [0] if n == 16 else ([0, 1] if di == 0 else [1, 0])
                        po = nxt("ps", PSP[0])
                        fns = [lambda e: e.matmul(po[0:n, 0:128], lhsT=at[0:n, 0:n], rhs=V[0:n, ti, :], start=True, stop=False)]
                        for ci, ab in enumerate(chunks):
                            qe = QEA if ab == 0 else QEB
                            sbi = seqidx[(ti, ab)]
                            fns.append(lambda e, qe=qe, sbi=sbi, last=(ci == len(chunks) - 1): e.matmul(
                                po[0:n, 0:128], lhsT=qe[:, s:s + n], rhs=SBALL[:, sbi, :], start=False, stop=last))
                        k.mms(fns, reads=[at, V, QEA, QEB, SBALL], writes=[po])
                        if di == 0:
                            k.op("act", lambda e: e.copy(out=OF[0:n, ti, :], in_=po[0:n, 0:128]), reads=[po], writes=[OF])
                        else:
                            s_ = sm[oi % 2]
                            x1, x2, yb = a1[oi % 2], a2[oi % 2], ybf[oi % 2]
                            k.op("dve", lambda e: e.tensor_tensor(out=x1[0:n, :], in0=po[0:n, 0:128], in1=OF[0:n, ti, :],
                                                                  op=ALU.add), reads=[po, OF], writes=[x1])
                            k.op("act", lambda e: e.activation(out=jk[0:n, :], in_=x1[0:n, :], func=AF.Square,
                                                               accum_out=s_[0:n, 0:1]), reads=[x1], writes=[jk, s_])
                            k.op("act", lambda e: e.activation(out=s_[0:n, 1:2], in_=s_[0:n, 0:1], func=AF.Ln,
                                                               scale=1.0 / 128, bias=epsc[0:n, 0:1]),
                                 reads=[s_, epsc], writes=[s_])
                            k.op("act", lambda e: e.activation(out=s_[0:n, 2:3], in_=s_[0:n, 1:2], func=AF.Exp, scale=-0.5),
                                 reads=[s_], writes=[s_])
                            k.op("dve", lambda e: e.scalar_tensor_tensor(out=x2[0:n, :], in0=x1[0:n, :], scalar=s_[0:n, 2:3],
                                                                         in1=gna[0:n, :], op0=ALU.mult, op1=ALU.mult),
                                 reads=[x1, s_, gna], writes=[x2])
                            k.op("pool", lambda e: e.tensor_tensor(out=yb[0:n, :], in0=x2[0:n, :], in1=G[0:n, ti, :],
                                                                   op=ALU.mult), reads=[x2, G], writes=[yb])
                            pt2 = nxt("pst", pstq)
                            k.mms([lambda e: e.transpose(out=pt2[:, 0:n], in_=yb[0:n, :], identity=ident[0:n, 0:n])],
                                  reads=[yb, ident], writes=[pt2])
                            k.op("act", lambda e: e.copy(out=yT[:, s:s + n], in_=pt2[:, 0:n]), reads=[pt2], writes=[yT])
                k.dma("pool", yTd[h, :, :], yT[:], reads=[yT], writes=[yTd_b[h]])
            PSP[0] = ps

        def even_mixer_B(widx, xnT, ls):
            sl = slopes16()
            load_consts(ls)
            dabs = CT["dabs"]
            wb = [k.sb([128, 16, 256], BF16, ls, "wb") for _ in range(3)]
            QT = k.sb([128, 1, L], BF16, ls, "QT")
            KT = k.sb([128, 1, L], BF16, ls, "KT")
            Vx = k.sb([128, 17, 129], BF16, ls, "Vx")
            G = k.sb([128, 17, 128], BF16, ls, "G")
            yT = k.sb([128, L], BF16, ls, "yT")
            wabs = k.sb([128, 1152], F32, ls, "wabs")
            mclip = k.sb([128, 1024], F32, ls, "mclip")
            sink = k.sb([128, 16], F32, ls, "sink")
            esink = k.sb([128, 16], F32, ls, "esink")
            tmpf = [k.sb([128, 512], F32, ls, "tmpf") for _ in range(4)]
            PT = [k.sb([128, 512], BF16, ls, "PT") for _ in range(4)]
            O = [k.sb([128, 129], F32, ls, "O") for _ in range(4)]
            ga = k.sb([128, 128], F32, ls, "ga")
            gb = k.sb([128, 128], F32, ls, "gb")
            sm = [k.sb([128, 4], F32, ls, "sm") for _ in range(2)]
            a1 = [k.sb([128, 128], F32, ls, "a1") for _ in range(2)]
            ybf = [k.sb([128, 128], BF16, ls, "ybf") for _ in range(2)]
            k.dma("sp", wabs[:], wabs_d[:, :], writes=[wabs])
            k.dma("sp", mclip[:], mclip_d[:, :], writes=[mclip])
            k.dma("sp", sink[:], sink_d[widx:widx + 1, :].partition_broadcast(128), writes=[sink])
            k.op("act", lambda e: e.activation(out=esink[:], in_=sink[:], func=AF.Exp), reads=[sink], writes=[esink])
            k.op("pool", lambda e: e.memset(Vx[:, :, 128:129], 1.0), writes=[Vx])
            pp = 0
            for hq in range(16):
                kv = hq // 4
                if hq % 4 == 0:
                    wk, wv = (wb[0], 0), (wb[0], 128)
                    load_w128((widx * 120 + 96 + kv) * 128, wk[0], wk[1])
                    load_w128((widx * 120 + 100 + kv) * 128, wv[0], wv[1])
                    proj_fm(xnT, wk[0], wk[1], KT, 0, None)
                    for ti, (s, n) in enumerate(TT):
                        p = nxt("ps", PSP[0])
                        proj_tm(xnT, wv[0], ti, p, 128, wv[1])
                        k.op("act", lambda e: e.copy(out=Vx[0:n, ti, 0:128], in_=p[0:n, 0:128]), reads=[p], writes=[Vx])
                wq, wg = (wb[1 + hq % 2], 0), (wb[1 + hq % 2], 128)
                load_w128((widx * 120 + 80 + hq) * 128, wq[0], wq[1])
                load_w128((widx * 120 + 104 + hq) * 128, wg[0], wg[1])
                proj_fm(xnT, wq[0], wq[1], QT, 0, 128 ** -0.5)
                for ti, (s, n) in enumerate(TT):
                    p = nxt("ps", PSP[0])
                    proj_tm(xnT, wg[0], ti, p, 128, wg[1])
                    silu_from_psum(p, n, 128, G[0:n, ti, :], G, ga, gb)
                for (t0, N) in QB:
                    t0v = vidx(t0)
                    nqs = (N + 127) // 128
                    acc = ps[2:6]
                    if t0 < 2048:
                        xt = [s0 for s0 in range(t0 - 128, t0 + N + 1, 128) if 0 <= s0 <= 1920]
                    else:
                        xt = [0]
                    ktl = [(2048, 16)] + [(s0, 128) for s0 in xt]
                    pend = []
                    stb = [ps[0], ps[1], ps[6]] if ST3 else [ps[0], ps[1]]
                    for ki, (s0, kn) in enumerate(ktl):
                        s0v = vidx(s0)
                        kt = s0 // 128
                        stp = stb[ki % len(stb)]
                        k.mms([lambda e: e.matmul(stp[0:kn, 0:N], lhsT=KT[:, 0, s0:s0 + kn], rhs=QT[:, 0, t0:t0 + N],
                                                  start=True, stop=True)], reads=[KT, QT], writes=[stp])
                        if s0 == 2048 and t0 < 2048:
                            c0 = min(t0, 512)
                            dt_ap, dbuf = mclip[0:16, c0:c0 + N], mclip
                        elif s0 == 2048:
                            dt_ap, dbuf = dabs[0:16, 384:384 + N], dabs
                        else:
                            off = t0v - s0v
                            dt_ap, dbuf = wabs[0:kn, 512 + off:512 + off + N], wabs
                        tf = tmpf[pp % 4]
                        ptile = PT[pp % 4]
                        pp += 1
                        k.op("dve", lambda e: e.scalar_tensor_tensor(out=tf[0:kn, 0:N], in0=dt_ap, scalar=-sl[hq],
                                                                     in1=stp[0:kn, 0:N], op0=ALU.mult, op1=ALU.add),
                             reads=[dbuf, stp], writes=[tf])
                        k.op("act", lambda e: e.activation(out=ptile[0:kn, 0:N], in_=tf[0:kn, 0:N], func=AF.Exp),
                             reads=[tf], writes=[ptile])
                        if len(pend) >= LOOK:
                            pend.pop(0)()
                        def pv(s0=s0, kn=kn, kt=kt, ptile=ptile):
                            for qs in range(nqs):
                                qn = min(128, N - qs * 128)
                                tok0 = t0 + qs * 128
                                if t0 < 2048:
                                    rel = [s_ for s_ in (tok0 - 128, tok0, tok0 + 128) if 0 <= s_ <= 1920]
                                else:
                                    rel = [0]
                                if s0 != 2048 and s0 not in rel:
                                    continue
                                a = acc[qs]
                                k.mms([lambda e: e.matmul(a[0:qn, 0:129], lhsT=ptile[0:kn, qs * 128:qs * 128 + qn],
                                                          rhs=Vx[0:kn, kt, :], start=(s0 == 2048), stop=(s0 == rel[-1]))],
                                      reads=[ptile, Vx], writes=[a])
                        pend.append(pv)
                    while pend:
                        pend.pop(0)()
                    for qs in range(nqs):
                        qn = min(128, N - qs * 128)
                        tok0 = t0 + qs * 128
                        ti = tok0 // 128
                        o = O[qs]
                        s_ = sm[qs % 2]
                        x1, yb = a1[qs % 2], ybf[qs % 2]
                        k.op("act", lambda e: e.copy(out=o[0:qn, :], in_=acc[qs][0:qn, 0:129]), reads=[acc[qs]], writes=[o])
                        k.op("dve", lambda e: e.tensor_tensor(out=s_[0:qn, 0:1], in0=o[0:qn, 128:129],
                                                              in1=esink[0:qn, hq:hq + 1], op=ALU.add),
                             reads=[o, esink], writes=[s_])
                        k.op("dve", lambda e: e.reciprocal(out=s_[0:qn, 1:2], in_=s_[0:qn, 0:1]), reads=[s_], writes=[s_])
                        k.op("dve", lambda e: e.scalar_tensor_tensor(out=yb[0:qn, :], in0=o[0:qn, 0:128], scalar=s_[0:qn, 1:2],
                                                                     in1=G[0:qn, ti, :], op0=ALU.mult, op1=ALU.mult),
                             reads=[o, s_, G], writes=[yb])
                        pt = nxt("pst", pstq)
                        k.mms([lambda e: e.transpose(out=pt[:, 0:qn], in_=yb[0:qn, :], identity=ident[0:qn, 0:qn])],
                              reads=[yb, ident], writes=[pt])
                        k.op("act", lambda e: e.copy(out=yT[:, tok0:tok0 + qn], in_=pt[:, 0:qn]), reads=[pt], writes=[yT])
                k.dma("pool", yTd[16 + hq, :, :], yT[:], reads=[yT], writes=[yTd_b[hq]])

        def phase_out(first, wout_d, widx, ls):
            wob = [k.sb([128, 32, 512], BF16, ls, "wob") for _ in range(2)]
            yb = [k.sb([128, 32, 512], BF16, ls, "ytb") for _ in range(2)]
            hb = [k.sb([128, 512], F32, ls, "hb") for _ in range(4)]
            ho = [k.sb([128, 512], F32, ls, "ho") for _ in range(4)]
            cnt = 0
            for nb in range(4):
                wo = wob[nb % 2]
                for pc in range(8):
                    stg = nxt("wst", wst)
                    r0 = ((widx * 4 + nb) * 8 + pc) * 128
                    k.dma("sp", stg[:], wout_d[r0:r0 + 128, :], writes=[stg])
                    k.op("pool", lambda e, stg=stg, pc=pc: e.tensor_copy(
                        out=wo[:, 4 * pc:4 * pc + 4, :], in_=stg[:].rearrange("p (c n) -> p c n", c=4)),
                        reads=[stg], writes=[wo])
                for bi, (t0, N) in enumerate(QB):
                    y = yb[bi % 2]
                    k.dma("sp", y[:, :, 0:N], yTd[:, :, t0:t0 + N].rearrange("c p t -> p c t"),
                          reads=yTd_b, writes=[y])
                    for qs in range((N + 127) // 128):
                        qn = min(128, N - qs * 128)
                        tok0 = t0 + qs * 128
                        ti = tok0 // 128
                        p = nxt("ps", PSP[0])
                        k.mms([lambda e, c=c: e.matmul(p[0:qn, :], lhsT=y[:, c, qs * 128:qs * 128 + qn], rhs=wo[:, c, :],
                                                       start=(c == 0), stop=(c == 31)) for c in range(32)],
                              reads=[y, wo], writes=[p])
                        hi = hb[cnt % 4]
                        hn = ho[cnt % 4]
                        cnt += 1
                        src = h_src(first, ti)
                        k.dma("sp", hi[0:qn, :], src[:, nb * 512:(nb + 1) * 512], reads=[hd_b[ti]], writes=[hi])
                        k.op("dve", lambda e: e.tensor_tensor(out=hn[0:qn, :], in0=p[0:qn, :], in1=hi[0:qn, :], op=ALU.add),
                             reads=[p, hi], writes=[hn])
                        k.dma("pool", hd[tok0:tok0 + qn, nb * 512:(nb + 1) * 512], hn[0:qn, :], reads=[hn],
                              writes=[hd_b[ti]])

        def phase_final(first, ls):
            gt = k.sb([128, D], F32, ls, "gt")
            hb = [k.sb([128, D], F32, ls, "hb") for _ in range(2)]
            ob = [k.sb([128, D], F32, ls, "ob") for _ in range(2)]
            junk = k.sb([128, D], BF16, ls, "junk")
            ssb = [k.sb([128, 2], F32, ls, "ss") for _ in range(2)]
            k.dma("sp", gt[:], nrm_d[4:5, :].partition_broadcast(128), writes=[gt])
            toks = []
            for ti, (s, n) in enumerate(TT[:16]):
                h = hb[ti % 2]
                o = ob[ti % 2]
                ss = ssb[ti % 2]
                k.dma("sp", h[0:n, :], h_src(first, ti), reads=[hd_b[ti]], writes=[h])
                k.op("act", lambda e: e.activation(out=junk[0:n, :], in_=h[0:n, :], func=AF.Square,
                                                   accum_out=ss[0:n, 0:1]), reads=[h], writes=[junk, ss])
                k.op("act", lambda e: e.activation(out=ss[0:n, 1:2], in_=ss[0:n, 0:1], func=AF.Ln, scale=1.0 / D,
                                                   bias=epsc[0:n, 0:1]), reads=[ss, epsc], writes=[ss])
                k.op("act", lambda e: e.activation(out=ss[0:n, 0:1], in_=ss[0:n, 1:2], func=AF.Exp, scale=-0.5),
                     reads=[ss], writes=[ss])
                k.op("dve", lambda e: e.scalar_tensor_tensor(out=o[0:n, :], in0=h[0:n, :], scalar=ss[0:n, 0:1],
                                                             in1=gt[0:n, :], op0=ALU.mult, op1=ALU.mult),
                     reads=[h, ss, gt], writes=[o])
                toks.append(k.dma("pool", out_d[s:s + n, :], o[0:n, :], reads=[o]))
            return toks

        first = True
        for (kind, widx, layer_idx) in layers:
            with ExitStack() as ls:
                xnT = k.sb([128, 16, L], BF16, ls, "xnT")
                with ExitStack() as ls2:
                    phase_norm(first, (0 if kind == "E" else 2) + widx, xnT, ls2)
                    k.barrier()
                with ExitStack() as ls2:
                    if kind == "O":
                        odd_mixer(widx, layer_idx, xnT, ls2)
                    else:
                        even_mixer_A(widx, xnT, ls2)
                    k.barrier()
                if kind == "E":
                    with ExitStack() as ls2:
                        even_mixer_B(widx, xnT, ls2)
                        k.barrier()
            with ExitStack() as ls:
                phase_out(first, woutc_d if kind == "O" else wouta_d, widx, ls)
                k.barrier()
            first = False
        out_toks = []
        with ExitStack() as ls:
            if do_final:
                out_toks = phase_final(first, ls)
            else:
                hb = [k.sb([128, D], F32, ls, "hb") for _ in range(2)]
                for ti, (s, n) in enumerate(TT):
                    h = hb[ti % 2]
                    k.dma("sp", h[0:n, :], h_src(first, ti), reads=[hd_b[ti]], writes=[h])
                    out_toks.append(k.dma("pool", out_d[s:s + n, :], h[0:n, :], reads=[h]))
            k.barrier()
        k.check_deadlock()
    return nc


def const_tables():
    i = np.arange(128, dtype=np.float32)[:, None]
    ident = np.eye(128, dtype=np.float32)
    dlin = (np.arange(512, dtype=np.float32)[None, :] - i).astype(np.float32)
    dabs = np.abs(np.arange(896, dtype=np.float32)[None, :] - i - 384).astype(np.float32)
    sl = np.array(slopes16(), dtype=np.float64)
    cb = (-(sl[:, None] * np.array(DELTAS, dtype=np.float64)[None, :])).reshape(1, -1)
    cb = np.repeat(cb, 128, axis=0).astype(np.float32)
    return {"ident": ident, "dlin": dlin, "dabs": dabs, "cbtab": np.ascontiguousarray(cb)}


def layout_winc(w):
    a = w.reshape(2, 2, 8, 128, 4, 16, 256)
    a = a.transpose(0, 5, 4, 1, 3, 2, 6)
    return np.ascontiguousarray(a).reshape(2 * 16 * 4 * 2 * 128, 2048)


def layout_wina(w):
    a = w.reshape(2, 16, 128, 120, 128)
    a = a.transpose(0, 3, 2, 1, 4)
    return np.ascontiguousarray(a).reshape(2 * 120 * 128, 2048)


def even_tables():
    i = np.arange(128, dtype=np.float32)[:, None]
    smask = np.ones((128, L), np.float32)
    smask[:, 0:2048:64] = 0.0
    smask[:, 2048] = 0.0
    j = np.arange(512)
    mA = ((j % 128) < 64).astype(np.float32)
    mab = np.concatenate([np.tile(mA[None], (128, 1)), np.tile((1 - mA)[None], (128, 1))], axis=1)
    s_ = np.arange(128)[:, None]; t_ = np.arange(128)[None, :]
    same = (s_ // 64) == (t_ // 64)
    mf = (same & (s_ <= t_)).astype(np.float32)
    mb_ = (same & (s_ >= t_)).astype(np.float32)
    mtri = np.concatenate([mf, mb_], axis=1)
    dw = np.abs(np.arange(1152, dtype=np.float32)[None, :] - i - 512)
    wabs = np.where(dw <= 128, dw, 1e9).astype(np.float32)
    mclip = np.minimum(np.arange(1024, dtype=np.float32)[None, :] - i + 16, 128.0).astype(np.float32)
    return {"smask": smask, "mab": np.ascontiguousarray(mab), "mtri": np.ascontiguousarray(mtri),
            "wabs": wabs, "mclip": mclip}


def layout_wout(w):
    a = w.reshape(2, 8, 4, 128, 4, 512)
    a = a.transpose(0, 4, 1, 3, 2, 5)
    return np.ascontiguousarray(a).reshape(2 * 4 * 8 * 128, 2048)


def run_layers(layers, do_final, inputs, ncores=8):
    out_rows = NX if do_final else L
    nc = build(layers, do_final, out_rows)
    f = lambda a: np.ascontiguousarray(np.asarray(a, dtype=np.float32))
    shared = dict(const_tables())
    shared["meta"] = f(inputs["meta_tokens"])
    shared["norms"] = np.concatenate([f(inputs["norm_a"]), f(inputs["norm_c"]), f(inputs["final_norm"])[None]], axis=0)
    if any(l[0] == "E" for l in layers):
        shared.update(even_tables())
        shared["wina"] = layout_wina(f(inputs["w_in_a"]))
        shared["wouta"] = layout_wout(f(inputs["w_out_a"]))
        shared["lbl"] = np.ascontiguousarray(f(inputs["hgrn_lb"]).reshape(2, 2, 16, 128).transpose(3, 0, 1, 2)).reshape(128, 64)
        shared["hnorm"] = f(inputs["hgrn_norm"])
        shared["sink"] = f(inputs["sink_logits"])
    if any(l[0] == "O" for l in layers):
        shared["winc"] = layout_winc(f(inputs["w_in_c"]))
        shared["woutc"] = layout_wout(f(inputs["w_out_c"]))
        shared["dlam"] = f(inputs["diff_lambda"]).reshape(2, 512)
        shared["dnorm"] = f(inputs["diff_norm"])
    x = f(inputs["x"])
    in_maps = []
    for b in range(ncores):
        m = dict(shared)
        m["x"] = x[b]
        in_maps.append(m)
    res = run_bass_kernel_spmd(nc, in_maps, core_ids=list(range(ncores)))
    return np.stack([r["out"] for r in res.results], axis=0)


def kernel(**inputs):
    layers = [("E", 0, 0), ("O", 0, 1), ("E", 1, 2), ("O", 1, 3)]
    return run_layers(layers, True, inputs)
```

```python
import math
from contextlib import ExitStack
import numpy as np
import concourse.bass as bass
import concourse.mybir as mybir
from concourse.bass_utils import run_bass_kernel_spmd

F32 = mybir.dt.float32
BF16 = mybir.dt.bfloat16
AF = mybir.ActivationFunctionType
ALU = mybir.AluOpType
AX = mybir.AxisListType

L = 2064
NX = 2048
NMETA = 16
D = 2048
EPS = 1e-6
TT = [(i * 128, 128) for i in range(16)] + [(2048, 16)]
QB = [(i * 512, 512) for i in range(4)] + [(2048, 16)]
DELTAS = [128 * m for m in range(1, 16)] + [16 + 128 * m for m in range(16)]
NDEL = len(DELTAS)
SAME_ENGINE_SYNC = True
import os
LOOK = int(os.environ.get('KLOOK', '1'))
ST3 = int(os.environ.get('KST3', '1'))


def vidx(s):
    return s if s < 2048 else s - 2048 - 16


def slopes16():
    return [2.0 ** (-8.0 * (i + 1) / 16) for i in range(16)]


class Eng:
    def __init__(self, nc, es, name, e, ndma):
        self.name = name
        self.e = e
        self.sem = es.enter_context(nc.semaphore("s_" + name))
        self.cnt = 0
        self.seen = {}
        self.ring = [[es.enter_context(nc.semaphore("d_%s%d" % (name, i))), 0] for i in range(ndma)]
        self.ri = 0


class Buf:
    def __init__(self, t):
        self.t = t
        self.w = None
        self.rs = {}

    def __getitem__(self, k):
        return self.t[k]


class K:
    def __init__(self, nc, es):
        self.nc = nc
        self.es = es
        self.E = {
            "pe": Eng(nc, es, "pe", nc.tensor, 0),
            "dve": Eng(nc, es, "dve", nc.vector, 0),
            "act": Eng(nc, es, "act", nc.scalar, 0),
            "pool": Eng(nc, es, "pool", nc.gpsimd, 16),
            "sp": Eng(nc, es, "sp", nc.sync, 24),
        }
        self.nbuf = 0
        self.log = []

    def sb(self, shape, dt, es=None, name=None):
        self.nbuf += 1
        t = (es or self.es).enter_context(self.nc.sbuf_tensor("%s_%d" % (name or "sb", self.nbuf), list(shape), dt))
        return Buf(t)

    def wait(self, eng, tok):
        if tok is None:
            return
        sid, sem, val = tok
        if eng.seen.get(sid, 0) >= val:
            return
        if sid == id(eng.sem) and (eng.name == "pe" or not SAME_ENGINE_SYNC):
            return
        eng.e.wait_ge(sem, val)
        eng.seen[sid] = val
        self.log.append((eng.name, "w", sid, val))

    def _deps(self, eng, reads, writes):
        for b in reads:
            self.wait(eng, b.w)
        for b in writes:
            self.wait(eng, b.w)
            for tok in list(b.rs.values()):
                self.wait(eng, tok)

    def _mark(self, tok, reads, writes):
        for b in reads:
            old = b.rs.get(tok[0])
            if old is None or old[2] < tok[2]:
                b.rs[tok[0]] = tok
        for b in writes:
            b.w = tok
            b.rs = {}

    def op(self, en, fn, reads=(), writes=()):
        eng = self.E[en]
        self._deps(eng, reads, writes)
        ins = fn(eng.e)
        eng.cnt += 1
        ins.then_inc(eng.sem, 1)
        self.log.append((eng.name, "i", id(eng.sem), 1))
        tok = (id(eng.sem), eng.sem, eng.cnt)
        self._mark(tok, reads, writes)
        return tok

    def mms(self, fns, reads=(), writes=()):
        eng = self.E["pe"]
        self._deps(eng, reads, writes)
        ins = None
        for fn in fns:
            ins = fn(eng.e)
        eng.cnt += 1
        ins.then_inc(eng.sem, 1)
        self.log.append((eng.name, "i", id(eng.sem), 1))
        tok = (id(eng.sem), eng.sem, eng.cnt)
        self._mark(tok, reads, writes)
        return tok

    def dma(self, qn, out, in_, reads=(), writes=()):
        eng = self.E[qn]
        self._deps(eng, reads, writes)
        slot = eng.ring[eng.ri % len(eng.ring)]
        eng.ri += 1
        if slot[1] > 0:
            self.wait(eng, (id(slot[0]), slot[0], slot[1]))
        ins = eng.e.dma_start(out=out, in_=in_)
        slot[1] += 16
        ins.then_inc(slot[0], 16)
        self.log.append((eng.name, "i", id(slot[0]), 16))
        tok = (id(slot[0]), slot[0], slot[1])
        self._mark(tok, reads, writes)
        return tok

    def check_deadlock(self):
        qs = {}
        for ev in self.log:
            qs.setdefault(ev[0], []).append(ev)
        ptr = {n: 0 for n in qs}
        sem = {}
        prog = True
        while prog:
            prog = False
            for n, q in qs.items():
                while ptr[n] < len(q):
                    _, kind, sid, val = q[ptr[n]]
                    if kind == "i":
                        sem[sid] = sem.get(sid, 0) + val
                    elif sem.get(sid, 0) < val:
                        break
                    ptr[n] += 1
                    prog = True
        stuck = {n: (ptr[n], len(q), q[ptr[n]]) for n, q in qs.items() if ptr[n] < len(q)}
        if stuck:
            names = {id(e.sem): e.name for e in self.E.values()}
            for e in self.E.values():
                for i, sl in enumerate(e.ring):
                    names[id(sl[0])] = "%s_dma%d" % (e.name, i)
            msg = "; ".join("%s at %d/%d waits %s>=%d (have %d)" % (n, p, t, names.get(ev[2]), ev[3], sem.get(ev[2], 0))
                            for n, (p, t, ev) in stuck.items())
            raise RuntimeError("DEADLOCK in emitted program: " + msg)

    def barrier(self):
        toks = []
        for e in self.E.values():
            if e.cnt > 0:
                toks.append((id(e.sem), e.sem, e.cnt))
            for s in e.ring:
                if s[1] > 0:
                    toks.append((id(s[0]), s[0], s[1]))
        for e in self.E.values():
            for t in toks:
                if t[0] == id(e.sem):
                    continue
                self.wait(e, t)


def build(layers, do_final, out_rows):
    nc = bass.Bass("TRN2", target_bir_lowering=False)
    dr = {}

    def din(name, shape):
        dr[name] = nc.dram_tensor(name, list(shape), F32, kind="ExternalInput").ap()
        return dr[name]

    x_d = din("x", [NX, D])
    meta_d = din("meta", [NMETA, D])
    nrm_d = din("norms", [5, D])
    ident_d = din("ident", [128, 128])
    dlin_d = din("dlin", [128, 512])
    dabs_d = din("dabs", [128, 896])
    cb_d = din("cbtab", [128, 16 * NDEL])
    n_odd = sum(1 for l in layers if l[0] == "O")
    n_even = sum(1 for l in layers if l[0] == "E")
    if n_odd:
        winc_d = din("winc", [2 * 16 * 4 * 2 * 128, 2048])
        woutc_d = din("woutc", [2 * 4 * 8 * 128, 2048])
        dlam_d = din("dlam", [2, 512])
        dnorm_d = din("dnorm", [2, 256])
    if n_even:
        wina_d = din("wina", [2 * 120 * 128, 2048])
        wouta_d = din("wouta", [2 * 4 * 8 * 128, 2048])
        smask_d = din("smask", [128, L])
        mab_d = din("mab", [128, 1024])
        mtri_d = din("mtri", [128, 256])
        lb_d = din("lbl", [128, 64])
        hnorm_d = din("hnorm", [2, 128])
        wabs_d = din("wabs", [128, 1152])
        mclip_d = din("mclip", [128, 1024])
        sink_d = din("sink", [2, 16])
    out_d = nc.dram_tensor("out", [out_rows, D], F32, kind="ExternalOutput").ap()
    hd = nc.dram_tensor("hd", [L, D], F32, kind="Internal").ap()
    yTd = nc.dram_tensor("yTd", [32, 128, L], BF16, kind="Internal").ap()

    with ExitStack() as es:
        k = K(nc, es)
        ident_f = k.sb([128, 128], F32)
        ident = k.sb([128, 128], BF16)
        wst = [k.sb([128, 2048], F32, name="wst") for _ in range(2)]
        ps = [Buf(es.enter_context(nc.psum_tensor("ps%d" % i, [128, 512], F32))) for i in range(7)]
        pst1 = es.enter_context(nc.psum_tensor("pst", [128, 1024], BF16))
        pstq = [Buf(pst1)]
        PSP = [ps]
        hd_b = [Buf(None) for _ in TT]
        yTd_b = [Buf(None) for _ in range(16)]
        st = {"wst": 0, "wb": 0, "ps": 0, "pst": 0, "uq": 0}

        def nxt(key, lst):
            i = st[key] % len(lst)
            st[key] += 1
            return lst[i]

        epsc = k.sb([128, 4], F32)
        k.op("pool", lambda e: e.memset(epsc[:, 0:1], EPS), writes=[epsc])
        k.op("pool", lambda e: e.memset(epsc[:, 3:4], 1.0), writes=[epsc])
        for wi in range(2):
            k.op("pool", lambda e, wi=wi: e.memset(epsc[:, 1 + wi:2 + wi],
                                                   math.log(1.0 - (0.8 - 0.6 * math.exp(-0.3 * (2 * wi + 1))))),
                 writes=[epsc])
        k.dma("sp", ident_f[:], ident_d[:, :], writes=[ident_f])
        k.op("dve", lambda e: e.tensor_copy(out=ident[:], in_=ident_f[:]), reads=[ident_f], writes=[ident])

        if n_odd:
            lams = k.sb([128, 4], F32)
            lame = k.sb([128, 4], F32)
            neglam = k.sb([128, 2], F32)
            lam_es = ExitStack()
            lamt = k.sb([128, 2, 512], F32, lam_es)
            lamp = k.sb([128, 2, 2, 128], F32, lam_es)
            for i in range(2):
                k.dma("sp", lamt[:, i, :], dlam_d[i:i + 1, :].partition_broadcast(128), writes=[lamt])
            for i in range(2):
                for j in range(2):
                    k.op("dve", lambda e, i=i, j=j: e.tensor_tensor(
                        out=lamp[:, i, j, :], in0=lamt[:, i, 256 * j:256 * j + 128],
                        in1=lamt[:, i, 256 * j + 128:256 * j + 256], op=ALU.mult), reads=[lamt], writes=[lamp])
            for i in range(2):
                for j in range(2):
                    k.op("dve", lambda e, i=i, j=j: e.reduce_sum(
                        out=lams[:, 2 * i + j:2 * i + j + 1], in_=lamp[:, i, j, :], axis=AX.X),
                        reads=[lamp], writes=[lams])
            k.op("act", lambda e: e.activation(out=lame[:], in_=lams[:], func=AF.Exp), reads=[lams], writes=[lame])
            k.barrier()
            lam_es.close()

        def lam_init_of(layer_idx):
            return 0.8 - 0.6 * math.exp(-0.3 * layer_idx)

        def h_src(first, ti):
            s, n = TT[ti]
            if first:
                return (x_d[s:s + n, :] if s < 2048 else meta_d[0:n, :])
            return hd[s:s + n, :]

        def load_w_slice(row0, dst, ls):
            for half in range(2):
                stg = nxt("wst", wst)
                k.dma("sp", stg[:], ls[0][row0 + half * 128:row0 + half * 128 + 128, :], writes=[stg])
                k.op("pool", lambda e, stg=stg, half=half: e.tensor_copy(
                    out=dst[:, 8 * half:8 * half + 8, :],
                    in_=stg[:].rearrange("p (c n) -> p c n", c=8)), reads=[stg], writes=[dst])

        def phase_norm(first, nrow, xnT, ls):
            gt = k.sb([128, D], F32, ls, "gt")
            hb = [k.sb([128, D], F32, ls, "hb") for _ in range(2)]
            xb = [k.sb([128, D], BF16, ls, "xb") for _ in range(2)]
            junk = k.sb([128, D], BF16, ls, "junk")
            ssb = [k.sb([128, 2], F32, ls, "ss") for _ in range(2)]
            k.dma("sp", gt[:], nrm_d[nrow:nrow + 1, :].partition_broadcast(128), writes=[gt])
            for ti, (s, n) in enumerate(TT):
                h = hb[ti % 2]
                xn = xb[ti % 2]
                ss = ssb[ti % 2]
                k.dma("sp", h[0:n, :], h_src(first, ti), reads=[hd_b[ti]], writes=[h])
                k.op("act", lambda e: e.activation(out=junk[0:n, :], in_=h[0:n, :], func=AF.Square,
                                                   accum_out=ss[0:n, 0:1]), reads=[h], writes=[junk, ss])
                k.op("act", lambda e: e.activation(out=ss[0:n, 1:2], in_=ss[0:n, 0:1], func=AF.Ln, scale=1.0 / D,
                                                   bias=epsc[0:n, 0:1]), reads=[ss, epsc], writes=[ss])
                k.op("act", lambda e: e.activation(out=ss[0:n, 0:1], in_=ss[0:n, 1:2], func=AF.Exp, scale=-0.5),
                     reads=[ss], writes=[ss])
                k.op("dve", lambda e: e.scalar_tensor_tensor(out=xn[0:n, :], in0=h[0:n, :], scalar=ss[0:n, 0:1],
                                                             in1=gt[0:n, :], op0=ALU.mult, op1=ALU.mult),
                     reads=[h, ss, gt], writes=[xn])
                for g in range(2):
                    pt = nxt("pst", pstq)
                    k.mms([lambda e, c=c, g=g, pt=pt: e.transpose(
                        out=pt[:, c * 128:c * 128 + n], in_=xn[0:n, (8 * g + c) * 128:(8 * g + c + 1) * 128],
                        identity=ident[0:n, 0:n]) for c in range(8)], reads=[xn, ident], writes=[pt])
                    k.op("act" if g == 0 else "dve", lambda e, g=g, pt=pt: (e.copy if g == 0 else e.tensor_copy)(
                        out=xnT[:, 8 * g:8 * g + 8, s:s + n],
                        in_=pt[:, :].rearrange("p (c t) -> p c t", c=8)[:, :, 0:n]), reads=[pt], writes=[xnT])

        def proj_fm(xnT, w, c0, dst, dj, scale):
            for bi, (s, n) in enumerate(QB):
                p = nxt("ps", PSP[0])
                k.mms([lambda e, c=c, p=p: e.matmul(p[:, 0:n], lhsT=w[:, c, c0:c0 + 128], rhs=xnT[:, c, s:s + n],
                                                    start=(c == 0), stop=(c == 15)) for c in range(16)],
                      reads=[xnT, w], writes=[p])
                if scale is None:
                    k.op("act", lambda e, p=p: e.copy(out=dst[:, dj, s:s + n], in_=p[:, 0:n]), reads=[p], writes=[dst])
                else:
                    k.op("act", lambda e, p=p: e.mul(out=dst[:, dj, s:s + n], in_=p[:, 0:n], mul=scale),
                         reads=[p], writes=[dst])

        def proj_tm(xnT, w, ti, p, ncols=256, c0=0):
            s, n = TT[ti]
            k.mms([lambda e, c=c: e.matmul(p[0:n, 0:ncols], lhsT=xnT[:, c, s:s + n], rhs=w[:, c, c0:c0 + ncols],
                                           start=(c == 0), stop=(c == 15)) for c in range(16)],
                  reads=[xnT, w], writes=[p])

        def silu_from_psum(p, n, ncols, dst_ap, dstbuf, tmpa, tmpb, pc0=0):
            k.op("act", lambda e: e.activation(out=tmpa[0:n, 0:ncols], in_=p[0:n, pc0:pc0 + ncols], func=AF.Exp, scale=-1.0),
                 reads=[p], writes=[tmpa])
            k.op("dve", lambda e: e.tensor_scalar(out=tmpa[0:n, 0:ncols], in0=tmpa[0:n, 0:ncols], scalar1=1.0,
                                                  scalar2=None, op0=ALU.add), reads=[tmpa], writes=[tmpa])
            k.op("dve", lambda e: e.reciprocal(out=tmpb[0:n, 0:ncols], in_=tmpa[0:n, 0:ncols]), reads=[tmpa], writes=[tmpb])
            k.op("dve", lambda e: e.tensor_tensor(out=dst_ap, in0=p[0:n, pc0:pc0 + ncols], in1=tmpb[0:n, 0:ncols],
                                                  op=ALU.mult), reads=[p, tmpb], writes=[dstbuf])

        CT = {}

        def load_consts(ls):
            CT["dlin"] = k.sb([128, 512], F32, ls, "dlin")
            CT["dabs"] = k.sb([128, 896], F32, ls, "dabs")
            CT["cbt"] = k.sb([128, 16 * NDEL], F32, ls, "cbt")
            k.dma("sp", CT["dlin"][:], dlin_d[:, :], writes=[CT["dlin"]])
            k.dma("sp", CT["dabs"][:], dabs_d[:, :], writes=[CT["dabs"]])
            k.dma("sp", CT["cbt"][:], cb_d[:, :], writes=[CT["cbt"]])

        def bias_tile(t0v, N, s0v, kn, slope, h):
            dlin, dabs, cbt = CT["dlin"], CT["dabs"], CT["cbt"]
            if t0v < s0v + kn and s0v < t0v + N:
                off = t0v - s0v
                return dabs[0:kn, 384 + off:384 + off + N], dabs, -slope, None
            if t0v > s0v:
                dl = t0v - s0v
                return dlin[0:kn, 0:N], dlin, -slope, cbt[0:kn, h * NDEL + DELTAS.index(dl):h * NDEL + DELTAS.index(dl) + 1]
            dl = s0v - t0v
            return dlin[0:kn, 0:N], dlin, slope, cbt[0:kn, h * NDEL + DELTAS.index(dl):h * NDEL + DELTAS.index(dl) + 1]

        def odd_mixer(widx, layer_idx, xnT, ls):
            lam_init = lam_init_of(layer_idx)
            sl = slopes16()
            load_consts(ls)
            cbt = CT["cbt"]
            wb = [k.sb([128, 16, 256], BF16, ls, "wb") for _ in range(4)]
            QT = k.sb([128, 2, L], BF16, ls, "QT")
            KT = k.sb([128, 2, L], BF16, ls, "KT")
            Vx = k.sb([128, 17, 257], BF16, ls, "Vx")
            G = k.sb([128, 17, 256], BF16, ls, "G")
            yT = k.sb([128, 2, L], BF16, ls, "yT")
            tmpf = [k.sb([128, 512], F32, ls, "tmpf") for _ in range(4)]
            PT = [k.sb([128, 512], BF16, ls, "PT") for _ in range(4)]
            O = [[k.sb([128, 257], F32, ls, "O") for _ in range(4)] for _ in range(2)]
            ga = k.sb([128, 256], F32, ls, "ga")
            gb = k.sb([128, 256], F32, ls, "gb")
            gn = k.sb([128, 256], F32, ls, "gn")
            sm = [k.sb([128, 8], F32, ls, "sm") for _ in range(2)]
            u1 = [k.sb([128, 256], F32, ls, "u1") for _ in range(2)]
            u2 = [k.sb([128, 256], F32, ls, "u2") for _ in range(2)]
            u3 = [k.sb([128, 256], F32, ls, "u3") for _ in range(2)]
            jk = k.sb([128, 256], F32, ls, "jk")
            ybf = [k.sb([128, 256], BF16, ls, "ybf") for _ in range(2)]
            k.dma("sp", gn[:], dnorm_d[widx:widx + 1, :].partition_broadcast(128), writes=[gn])
            k.op("pool", lambda e: e.memset(Vx[:, :, 256:257], 1.0), writes=[Vx])
            k.op("dve", lambda e: e.tensor_tensor(out=neglam[:, widx:widx + 1], in0=lame[:, 2 * widx + 1:2 * widx + 2],
                                                  in1=lame[:, 2 * widx:2 * widx + 1], op=ALU.subtract),
                 reads=[lame], writes=[neglam])
            k.op("dve", lambda e: e.tensor_scalar(out=neglam[:, widx:widx + 1], in0=neglam[:, widx:widx + 1],
                                                  scalar1=-lam_init, scalar2=None, op0=ALU.add),
                 reads=[neglam], writes=[neglam])
            nl = neglam
            pp = 0
            for h in range(16):
                wq, wk, wv, wg = wb[0], wb[1], wb[2], wb[3]
                base = ((widx * 16 + h) * 4) * 256
                load_w_slice(base + 0 * 256, wq, [winc_d])
                load_w_slice(base + 1 * 256, wk, [winc_d])
                load_w_slice(base + 2 * 256, wv, [winc_d])
                load_w_slice(base + 3 * 256, wg, [winc_d])
                for j in range(2):
                    proj_fm(xnT, wq, j * 128, QT, j, 128 ** -0.5)
                for j in range(2):
                    proj_fm(xnT, wk, j * 128, KT, j, None)
                for ti, (s, n) in enumerate(TT):
                    p = nxt("ps", PSP[0])
                    proj_tm(xnT, wv, ti, p)
                    k.op("act", lambda e, p=p, ti=ti, n=n: e.copy(out=Vx[0:n, ti, 0:256], in_=p[0:n, 0:256]),
                         reads=[p], writes=[Vx])
                    p = nxt("ps", PSP[0])
                    proj_tm(xnT, wg, ti, p)
                    silu_from_psum(p, n, 256, G[0:n, ti, :], G, ga, gb)
                for (t0, N) in QB:
                    t0v = vidx(t0)
                    nqs = (N + 127) // 128
                    for j in range(2):
                        acc = ps[2:6]
                        pend = []
                        stb = [ps[0], ps[1], ps[6]] if ST3 else [ps[0], ps[1]]
                        for kt, (s0, kn) in enumerate(TT):
                            s0v = vidx(s0)
                            stp = stb[kt % len(stb)]
                            k.mms([lambda e: e.matmul(stp[0:kn, 0:N], lhsT=KT[:, j, s0:s0 + kn], rhs=QT[:, j, t0:t0 + N],
                                                      start=True, stop=True)], reads=[KT, QT], writes=[stp])
                            dt_ap, dbuf, coef, cb = bias_tile(t0v, N, s0v, kn, sl[h], h)
                            tf = tmpf[pp % 4]
                            ptile = PT[pp % 4]
                            pp += 1
                            k.op("dve", lambda e: e.scalar_tensor_tensor(out=tf[0:kn, 0:N], in0=dt_ap, scalar=coef,
                                                                         in1=stp[0:kn, 0:N], op0=ALU.mult, op1=ALU.add),
                                 reads=[dbuf, stp], writes=[tf])
                            if cb is None:
                                k.op("act", lambda e: e.activation(out=ptile[0:kn, 0:N], in_=tf[0:kn, 0:N], func=AF.Exp),
                                     reads=[tf], writes=[ptile])
                            else:
                                k.op("act", lambda e: e.activation(out=ptile[0:kn, 0:N], in_=tf[0:kn, 0:N], func=AF.Exp,
                                                                   bias=cb), reads=[tf, cbt], writes=[ptile])
                            if len(pend) >= LOOK:
                                pend.pop(0)()
                            def pv(kt=kt, kn=kn, ptile=ptile):
                                for qs in range(nqs):
                                    qn = min(128, N - qs * 128)
                                    a = acc[qs]
                                    k.mms([lambda e: e.matmul(a[0:qn, 0:257], lhsT=ptile[0:kn, qs * 128:qs * 128 + qn],
                                                              rhs=Vx[0:kn, kt, :], start=(kt == 0), stop=(kt == 16))],
                                          reads=[ptile, Vx], writes=[a])
                            pend.append(pv)
                        while pend:
                            pend.pop(0)()
                        for qs in range(nqs):
                            qn = min(128, N - qs * 128)
                            k.op("act" if qs % 2 == 0 else "dve",
                                 lambda e, qs=qs, qn=qn: (e.copy if qs % 2 == 0 else e.tensor_copy)(
                                     out=O[j][qs][0:qn, :], in_=acc[qs][0:qn, 0:257]),
                                 reads=[acc[qs]], writes=[O[j][qs]])
                    for qs in range(nqs):
                        qn = min(128, N - qs * 128)
                        tok0 = t0 + qs * 128
                        ti = tok0 // 128
                        o1, o2 = O[0][qs], O[1][qs]
                        s_ = sm[qs % 2]
                        a1, a2, a3 = u1[qs % 2], u2[qs % 2], u3[qs % 2]
                        yb = ybf[qs % 2]
                        k.op("dve", lambda e: e.reciprocal(out=s_[0:qn, 0:1], in_=o1[0:qn, 256:257]), reads=[o1], writes=[s_])
                        k.op("dve", lambda e: e.reciprocal(out=s_[0:qn, 1:2], in_=o2[0:qn, 256:257]), reads=[o2], writes=[s_])
                        k.op("dve", lambda e: e.tensor_tensor(out=s_[0:qn, 2:3], in0=s_[0:qn, 1:2],
                                                              in1=nl[0:qn, widx:widx + 1], op=ALU.mult),
                             reads=[s_, nl], writes=[s_])
                        k.op("dve", lambda e: e.tensor_scalar(out=a1[0:qn, :], in0=o1[0:qn, 0:256], scalar1=s_[0:qn, 0:1],
                                                              scalar2=None, op0=ALU.mult), reads=[o1, s_], writes=[a1])
                        k.op("dve", lambda e: e.scalar_tensor_tensor(out=a2[0:qn, :], in0=o2[0:qn, 0:256],
                                                                     scalar=s_[0:qn, 2:3], in1=a1[0:qn, :],
                                                                     op0=ALU.mult, op1=ALU.add),
                             reads=[o2, s_, a1], writes=[a2])
                        k.op("act", lambda e: e.activation(out=jk[0:qn, :], in_=a2[0:qn, :], func=AF.Square,
                                                           accum_out=s_[0:qn, 3:4]), reads=[a2], writes=[jk, s_])
                        k.op("act", lambda e: e.activation(out=s_[0:qn, 4:5], in_=s_[0:qn, 3:4], func=AF.Ln,
                                                           scale=1.0 / 256, bias=epsc[0:qn, 0:1]),
                             reads=[s_, epsc], writes=[s_])
                        k.op("act", lambda e: e.activation(out=s_[0:qn, 5:6], in_=s_[0:qn, 4:5], func=AF.Exp,
                                                           scale=-0.5, bias=epsc[0:qn, 1 + widx:2 + widx]),
                             reads=[s_, epsc], writes=[s_])
                        k.op("dve", lambda e: e.scalar_tensor_tensor(out=a3[0:qn, :], in0=a2[0:qn, :], scalar=s_[0:qn, 5:6],
                                                                     in1=gn[0:qn, :], op0=ALU.mult, op1=ALU.mult),
                             reads=[a2, s_, gn], writes=[a3])
                        k.op("pool", lambda e: e.tensor_tensor(out=yb[0:qn, :], in0=a3[0:qn, :], in1=G[0:qn, ti, :],
                                                               op=ALU.mult), reads=[a3, G], writes=[yb])
                        pt = nxt("pst", pstq)
                        k.mms([lambda e, c=c: e.transpose(out=pt[:, c * 128:c * 128 + qn], in_=yb[0:qn, c * 128:(c + 1) * 128],
                                                          identity=ident[0:qn, 0:qn]) for c in range(2)],
                              reads=[yb, ident], writes=[pt])
                        k.op("act", lambda e: e.copy(out=yT[:, :, tok0:tok0 + qn],
                                                     in_=pt[:, 0:256].rearrange("p (c t) -> p c t", c=2)[:, :, 0:qn]),
                             reads=[pt], writes=[yT])
                k.dma("pool", yTd[2 * h:2 * h + 2, :, :].rearrange("c p t -> p c t"), yT[:], reads=[yT], writes=[yTd_b[h]])

        def proj_fm_cb(xnT, w, c0, cb):
            for bi, (s, n) in enumerate(QB):
                p = nxt("ps", PSP[0])
                k.mms([lambda e, c=c, p=p: e.matmul(p[:, 0:n], lhsT=w[:, c, c0:c0 + 128], rhs=xnT[:, c, s:s + n],
                                                    start=(c == 0), stop=(c == 15)) for c in range(16)],
                      reads=[xnT, w], writes=[p])
                cb(p, s, n)

        def load_w128(sidx_row0, dst, c0):
            stg = nxt("wst", wst)
            k.dma("sp", stg[:], wina_d[sidx_row0:sidx_row0 + 128, :], writes=[stg])
            k.op("pool", lambda e: e.tensor_copy(out=dst[:, :, c0:c0 + 128],
                                                 in_=stg[:].rearrange("p (c n) -> p c n", c=16)),
                 reads=[stg], writes=[dst])

        def even_mixer_A(widx, xnT, ls):
            wb = [k.sb([128, 16, 256], BF16, ls, "wb") for _ in range(3)]
            qA = k.sb([128, L], BF16, ls, "qA")
            qB = k.sb([128, L], BF16, ls, "qB")
            T1 = k.sb([128, L], F32, ls, "T1")
            T2 = k.sb([128, L], F32, ls, "T2")
            T3 = k.sb([128, L], F32, ls, "T3")
            QEA = k.sb([128, L], BF16, ls, "QEA")
            QEB = k.sb([128, L], BF16, ls, "QEB")
            KE = k.sb([128, L], BF16, ls, "KE")
            KL = k.sb([128, L], BF16, ls, "KL")
            V = k.sb([128, 17, 128], BF16, ls, "V")
            G = k.sb([128, 17, 128], BF16, ls, "G")
            OF = k.sb([128, 17, 128], F32, ls, "OF")
            yT = k.sb([128, L], BF16, ls, "yT")
            smk = k.sb([128, L], BF16, ls, "smk")
            mAB = k.sb([128, 2, 512], F32, ls, "mAB")
            mtri = k.sb([128, 2, 128], F32, ls, "mtri")
            lbt = k.sb([128, 64], F32, ls, "lbt")
            lbc = k.sb([128, 2, 16], F32, ls, "lbc")
            omc = k.sb([128, 2, 16], F32, ls, "omc")
            gna = k.sb([128, 128], F32, ls, "gna")
            ATm = [k.sb([128, 128], BF16, ls, "ATm") for _ in range(3)]
            KLt = [[k.sb([128, 128], BF16, ls, "KLt") for _ in range(2)] for _ in range(2)]
            S = [k.sb([128, 128], F32, ls, "S") for _ in range(4)]
            SBALL = k.sb([128, 33, 128], BF16, ls, "SBALL")
            ECs = [k.sb([128, 34], F32, ls, "EC") for _ in range(2)]
            k.op("pool", lambda e: e.memset(SBALL[:, 0, :], 0.0), writes=[SBALL])
            ubanks = [ps[4], ps[5], ps[6]]
            PSP[0] = ps[0:4]
            ga = k.sb([128, 128], F32, ls, "ga")
            gb = k.sb([128, 128], F32, ls, "gb")
            sm = [k.sb([128, 8], F32, ls, "sm") for _ in range(2)]
            a1 = [k.sb([128, 128], F32, ls, "a1") for _ in range(2)]
            a2 = [k.sb([128, 128], F32, ls, "a2") for _ in range(2)]
            jk = k.sb([128, 128], F32, ls, "jk")
            ybf = [k.sb([128, 128], BF16, ls, "ybf") for _ in range(2)]
            k.dma("pool", smk[:], smask_d[:, :], writes=[smk])
            k.dma("sp", mAB[:], mab_d[:, :].rearrange("p (a n) -> p a n", a=2), writes=[mAB])
            k.dma("sp", mtri[:], mtri_d[:, :].rearrange("p (a n) -> p a n", a=2), writes=[mtri])
            k.dma("sp", lbt[:], lb_d[:, :], writes=[lbt])
            k.dma("sp", gna[:], hnorm_d[widx:widx + 1, :].partition_broadcast(128), writes=[gna])
            for par in range(2):
                for ab in range(2):
                    k.op("pool", lambda e, par=par, ab=ab: e.memset(KLt[par][ab][:], 0.0), writes=[KLt[par][ab]])
            lb4 = lbt[:].rearrange("p (r l h) -> p r l h", r=2, l=2)
            if widx == 0:
                k.op("pool", lambda e: e.memset(lbc[:], 0.0), writes=[lbc])
                k.op("pool", lambda e: e.memset(omc[:], 1.0), writes=[omc])
            else:
                k.op("dve", lambda e: e.tensor_tensor(out=omc[:], in0=lb4[:, :, 0, :], in1=lb4[:, :, 1, :], op=ALU.subtract),
                     reads=[lbt], writes=[omc])
                k.op("act", lambda e: e.activation(out=omc[:], in_=omc[:], func=AF.Exp), reads=[omc], writes=[omc])
                k.op("dve", lambda e: e.tensor_scalar(out=omc[:], in0=omc[:], scalar1=1.0, scalar2=None, op0=ALU.add),
                     reads=[omc], writes=[omc])
                k.op("dve", lambda e: e.reciprocal(out=lbc[:], in_=omc[:]), reads=[omc], writes=[lbc])
                k.op("dve", lambda e: e.tensor_scalar(out=omc[:], in0=lbc[:], scalar1=-1.0, scalar2=1.0, op0=ALU.mult,
                                                      op1=ALU.add), reads=[lbc], writes=[omc])
            for h in range(16):
                wq, wzf, wzb, wv, wg = (wb[0], 0), (wb[0], 128), (wb[2], 0), (wb[1], 0), (wb[1], 128)
                for (wt, c0), si in ((wq, h), (wzf, 16 + h), (wzb, 32 + h), (wv, 48 + h), (wg, 64 + h)):
                    load_w128((widx * 120 + si) * 128, wt, c0)
                def cbq(p, s, n):
                    k.op("dve", lambda e: e.tensor_tensor(out=qA[:, s:s + n], in0=p[:, 0:n], in1=mAB[:, 0, 0:n], op=ALU.mult),
                         reads=[p, mAB], writes=[qA])
                    k.op("dve", lambda e: e.tensor_tensor(out=qB[:, s:s + n], in0=p[:, 0:n], in1=mAB[:, 1, 0:n], op=ALU.mult),
                         reads=[p, mAB], writes=[qB])
                proj_fm_cb(xnT, wq[0], wq[1], cbq)
                for ti, (s, n) in enumerate(TT):
                    p = nxt("ps", PSP[0])
                    proj_tm(xnT, wb[1], ti, p, 256, 0)
                    k.op("act", lambda e: e.copy(out=V[0:n, ti, :], in_=p[0:n, 0:128]), reads=[p], writes=[V])
                    silu_from_psum(p, n, 128, G[0:n, ti, :], G, ga, gb, 128)
                for di in range(2):
                    wz = wzf if di == 0 else wzb
                    def cbz(p, s, n):
                        k.op("act", lambda e: e.activation(out=T1[:, s:s + n], in_=p[:, 0:n], func=AF.Exp, scale=-1.0),
                             reads=[p], writes=[T1])
                    proj_fm_cb(xnT, wz[0], wz[1], cbz)
                    k.op("act", lambda e: e.activation(out=T2[:], in_=T1[:], func=AF.Ln, bias=epsc[:, 3:4]),
                         reads=[T1, epsc], writes=[T2])
                    if widx == 0:
                        k.op("dve", lambda e: e.tensor_scalar(out=T3[:], in0=T2[:], scalar1=-1.0, scalar2=None, op0=ALU.mult),
                             reads=[T2], writes=[T3])
                    else:
                        k.op("act", lambda e: e.activation(out=T3[:], in_=T1[:], func=AF.Ln, scale=lbc[:, di, h:h + 1],
                                                           bias=epsc[:, 3:4]), reads=[T1, lbc, epsc], writes=[T3])
                        k.op("dve", lambda e: e.tensor_tensor(out=T3[:], in0=T3[:], in1=T2[:], op=ALU.subtract),
                             reads=[T3, T2], writes=[T3])
                    k.op("act", lambda e: e.activation(out=T2[:], in_=T2[:], func=AF.Exp, scale=-1.0), reads=[T2], writes=[T2])
                    k.op("dve", lambda e: e.scalar_tensor_tensor(out=T1[:], in0=T1[:], scalar=omc[:, di, h:h + 1], in1=T2[:],
                                                                 op0=ALU.mult, op1=ALU.mult), reads=[T1, omc, T2], writes=[T1])
                    EC = ECs[di]
                    xv = lambda t_: t_[:, 0:2048].rearrange("p (c t) -> p c t", t=64)
                    if di == 0:
                        k.op("dve", lambda e: e.tensor_tensor_scan(out=T2[:], data0=smk[:], data1=T3[:], initial=0.0,
                                                                   op0=ALU.mult, op1=ALU.add), reads=[smk, T3], writes=[T2])
                        k.op("act", lambda e: e.activation(out=T3[:], in_=T2[:], func=AF.Exp), reads=[T2], writes=[T3])
                        k.op("pool", lambda e: e.tensor_copy(out=EC[:, 0:32], in_=xv(T3)[:, :, 63]), reads=[T3], writes=[EC])
                        k.op("pool", lambda e: e.tensor_copy(out=EC[:, 32:33], in_=T3[:, 2063:2064]), reads=[T3], writes=[EC])
                        k.op("act", lambda e: e.activation(out=T2[:], in_=T2[:], func=AF.Exp, scale=-1.0),
                             reads=[T2], writes=[T2])
                        k.op("dve", lambda e: e.tensor_tensor(out=QEA[:], in0=qA[:], in1=T3[:], op=ALU.mult),
                             reads=[qA, T3], writes=[QEA])
                        k.op("dve", lambda e: e.tensor_tensor(out=QEB[:], in0=qB[:], in1=T3[:], op=ALU.mult),
                             reads=[qB, T3], writes=[QEB])
                        k.op("dve", lambda e: e.tensor_tensor(out=KE[:], in0=T1[:], in1=T2[:], op=ALU.mult),
                             reads=[T1, T2], writes=[KE])
                        k.op("dve", lambda e: e.tensor_tensor(
                            out=xv(KL), in0=xv(KE), in1=xv(T3)[:, :, 63:64].to_broadcast([128, 32, 64]), op=ALU.mult),
                            reads=[KE, T3], writes=[KL])
                        k.op("dve", lambda e: e.tensor_tensor(out=KL[:, 2048:2064], in0=KE[:, 2048:2064],
                                                              in1=T3[:, 2063:2064].to_broadcast([128, 16]), op=ALU.mult),
                             reads=[KE, T3], writes=[KL])
                        KLs = KL
                    else:
                        k.op("pool", lambda e: e.memset(T2[:, 0:1], 0.0), writes=[T2])
                        k.op("dve", lambda e: e.tensor_tensor_scan(out=T2[:, 1:L], data0=T3[:, 0:L - 1], data1=smk[:, 1:L],
                                                                   initial=0.0, op0=ALU.add, op1=ALU.mult),
                             reads=[smk, T3], writes=[T2])
                        k.op("dve", lambda e: e.tensor_tensor(out=EC[:, 0:32], in0=xv(T2)[:, :, 63], in1=xv(T3)[:, :, 63],
                                                              op=ALU.add), reads=[T2, T3], writes=[EC])
                        k.op("dve", lambda e: e.tensor_tensor(out=EC[:, 32:33], in0=T2[:, 2063:2064], in1=T3[:, 2063:2064],
                                                              op=ALU.add), reads=[T2, T3], writes=[EC])
                        k.op("act", lambda e: e.activation(out=EC[:, 0:33], in_=EC[:, 0:33], func=AF.Exp), reads=[EC], writes=[EC])
                        k.op("act", lambda e: e.activation(out=T3[:], in_=T2[:], func=AF.Exp, scale=-1.0),
                             reads=[T2], writes=[T3])
                        k.op("act", lambda e: e.activation(out=T2[:], in_=T2[:], func=AF.Exp), reads=[T2], writes=[T2])
                        k.op("dve", lambda e: e.tensor_tensor(out=QEA[:], in0=qA[:], in1=T3[:], op=ALU.mult),
                             reads=[qA, T3], writes=[QEA])
                        k.op("dve", lambda e: e.tensor_tensor(out=QEB[:], in0=qB[:], in1=T3[:], op=ALU.mult),
                             reads=[qB, T3], writes=[QEB])
                        k.op("dve", lambda e: e.tensor_tensor(out=KE[:], in0=T1[:], in1=T2[:], op=ALU.mult),
                             reads=[T1, T2], writes=[KE])
                        KLs = KE
                    if di == 0:
                        seq = [(16, 0)] + [(t_, ab_) for t_ in range(16) for ab_ in (0, 1)]
                    else:
                        seq = [(t_, ab_) for t_ in range(15, -1, -1) for ab_ in (1, 0)] + [(16, 0)]
                    cids = [32 if t_ == 16 else 2 * t_ + ab_ for (t_, ab_) in seq]
                    order = [16] + list(range(16)) if di == 0 else list(range(15, -1, -1)) + [16]
                    seqidx = {}
                    k.op("pool", lambda e: e.memset(S[0][:], 0.0), writes=[S[0]])
                    if di == 0:
                        groups = [[16]] + [[2 * g_, 2 * g_ + 1] for g_ in range(8)]
                    else:
                        groups = [[15 - 2 * g_, 14 - 2 * g_] for g_ in range(8)] + [[16]]
                    step = 0
                    tcount = 0
                    for gt in groups:
                        bank = ubanks[st["uq"] % 3]
                        st["uq"] += 1
                        slots = []
                        for ti in gt:
                            s, n = TT[ti]
                            par = tcount % 2
                            tcount += 1
                            pt = nxt("pst", pstq)
                            k.mms([lambda e: e.transpose(out=pt[0:n, 0:128], in_=KLs[:, s:s + n], identity=ident[:, :])],
                                  reads=[KLs, ident], writes=[pt])
                            nA = min(n, 64)
                            k.op("act", lambda e: e.copy(out=KLt[par][0][0:nA, :], in_=pt[0:nA, 0:128]), reads=[pt],
                                 writes=[KLt[par][0]])
                            if n == 128:
                                k.op("act", lambda e: e.copy(out=KLt[par][1][64:128, :], in_=pt[64:128, 0:128]), reads=[pt],
                                     writes=[KLt[par][1]])
                            chunks = [0] if n == 16 else ([0, 1] if di == 0 else [1, 0])
                            for ab in chunks:
                                cid = 32 if ti == 16 else 2 * ti + ab
                                q_ = len(slots)
                                seqidx[(ti, ab)] = step + q_
                                k.mms([lambda e: e.matmul(bank[:, q_ * 128:(q_ + 1) * 128], lhsT=KLt[par][ab][0:n, :],
                                                          rhs=V[0:n, ti, :], start=True, stop=True)],
                                      reads=[KLt[par][ab], V], writes=[bank])
                                slots.append((q_, cid))
                        for (q_, cid) in slots:
                            if step < 32:
                                so, sn = S[step % 4], S[(step + 1) % 4]
                                k.op("dve", lambda e: e.scalar_tensor_tensor(out=sn[:], in0=so[:], scalar=EC[:, cid:cid + 1],
                                                                             in1=bank[:, q_ * 128:(q_ + 1) * 128],
                                                                             op0=ALU.mult, op1=ALU.add),
                                     reads=[so, EC, bank], writes=[sn])
                                if di == 0:
                                    k.op("act", lambda e: e.copy(out=SBALL[:, step + 1, :], in_=sn[:]), reads=[sn], writes=[SBALL])
                                else:
                                    cn = cids[step + 1]
                                    k.op("act", lambda e: e.mul(out=SBALL[:, step + 1, :], in_=sn[:], mul=EC[:, cn:cn + 1]),
                                         reads=[sn, EC], writes=[SBALL])
                            step += 1
                    def at_stage(oi, ti):
                        s, n = TT[ti]
                        pa = nxt("ps", PSP[0])
                        k.mms([lambda e: e.matmul(pa[0:n, 0:n], lhsT=KE[:, s:s + n], rhs=QEA[:, s:s + n], start=True, stop=False),
                               lambda e: e.matmul(pa[0:n, 0:n], lhsT=KE[:, s:s + n], rhs=QEB[:, s:s + n], start=False, stop=True)],
                              reads=[KE, QEA, QEB], writes=[pa])
                        at = ATm[oi % 3]
                        k.op("dve", lambda e: e.tensor_tensor(out=at[0:n, 0:n], in0=pa[0:n, 0:n], in1=mtri[0:n, di, 0:n],
                                                              op=ALU.mult), reads=[pa, mtri], writes=[at])
                    at_stage(0, order[0])
                    for oi, ti in enumerate(order):
                        s, n = TT[ti]
                        if oi + 1 < len(order):
                            at_stage(oi + 1, order[oi + 1])
                        at = ATm[oi % 3]
                        chunks = [0] if n == 16 else ([0, 1] if di == 0 else [1, 0])
                        po = nxt("ps", PSP[0])
                        fns = [lambda e: e.matmul(po[0:n, 0:128], lhsT=at[0:n, 0:n], rhs=V[0:n, ti, :], start=True, stop=False)]
                        for ci, ab in enumerate(chunks):
                            qe = QEA if ab == 0 else QEB
                            sbi = seqidx[(ti, ab)]
                            fns.append(lambda e, qe=qe, sbi=sbi, last=(ci == len(chunks) - 1): e.matmul(
                                po[0:n, 0:128], lhsT=qe[:, s:s + n], rhs=SBALL[:, sbi, :], start=False, stop=last))
                        k.mms(fns, reads=[at, V, QEA, QEB, SBALL], writes=[po])
                        if di == 0:
                            k.op("act", lambda e: e.copy(out=OF[0:n, ti, :], in_=po[0:n, 0:128]), reads=[po], writes=[OF])
                        else:
                            s_ = sm[oi % 2]
                            x1, x2, yb = a1[oi % 2], a2[oi % 2], ybf[oi % 2]
                            k.op("dve", lambda e: e.tensor_tensor(out=x1[0:n, :], in0=po[0:n, 0:128], in1=OF[0:n, ti, :],
                                                                  op=ALU.add), reads=[po, OF], writes=[x1])
                            k.op("act", lambda e: e.activation(out=jk[0:n, :], in_=x1[0:n, :], func=AF.Square,
                                                               accum_out=s_[0:n, 0:1]), reads=[x1], writes=[jk, s_])
                            k.op("act", lambda e: e.activation(out=s_[0:n, 1:2], in_=s_[0:n, 0:1], func=AF.Ln,
                                                               scale=1.0 / 128, bias=epsc[0:n, 0:1]),
                                 reads=[s_, epsc], writes=[s_])
                            k.op("act", lambda e: e.activation(out=s_[0:n, 2:3], in_=s_[0:n, 1:2], func=AF.Exp, scale=-0.5),
                                 reads=[s_], writes=[s_])
                            k.op("dve", lambda e: e.scalar_tensor_tensor(out=x2[0:n, :], in0=x1[0:n, :], scalar=s_[0:n, 2:3],
                                                                         in1=gna[0:n, :], op0=ALU.mult, op1=ALU.mult),
                                 reads=[x1, s_, gna], writes=[x2])
                            k.op("pool", lambda e: e.tensor_tensor(out=yb[0:n, :], in0=x2[0:n, :], in1=G[0:n, ti, :],
                                                                   op=ALU.mult), reads=[x2, G], writes=[yb])
                            pt2 = nxt("pst", pstq)
                            k.mms([lambda e: e.transpose(out=pt2[:, 0:n], in_=yb[0:n, :], identity=ident[0:n, 0:n])],
                                  reads=[yb, ident], writes=[pt2])
                            k.op("act", lambda e: e.copy(out=yT[:, s:s + n], in_=pt2[:, 0:n]), reads=[pt2], writes=[yT])
                k.dma("pool", yTd[h, :, :], yT[:], reads=[yT], writes=[yTd_b[h]])
            PSP[0] = ps

        def even_mixer_B(widx, xnT, ls):
            sl = slopes16()
            load_consts(ls)
            dabs = CT["dabs"]
            wb = [k.sb([128, 16, 256], BF16, ls, "wb") for _ in range(3)]
            QT = k.sb([128, 1, L], BF16, ls, "QT")
            KT = k.sb([128, 1, L], BF16, ls, "KT")
            Vx = k.sb([128, 17, 129], BF16, ls, "Vx")
            G = k.sb([128, 17, 128], BF16, ls, "G")
            yT = k.sb([128, L], BF16, ls, "yT")
            wabs = k.sb([128, 1152], F32, ls, "wabs")
            mclip = k.sb([128, 1024], F32, ls, "mclip")
            sink = k.sb([128, 16], F32, ls, "sink")
            esink = k.sb([128, 16], F32, ls, "esink")
            tmpf = [k.sb([128, 512], F32, ls, "tmpf") for _ in range(4)]
            PT = [k.sb([128, 512], BF16, ls, "PT") for _ in range(4)]
            O = [k.sb([128, 129], F32, ls, "O") for _ in range(4)]
            ga = k.sb([128, 128], F32, ls, "ga")
            gb = k.sb([128, 128], F32, ls, "gb")
            sm = [k.sb([128, 4], F32, ls, "sm") for _ in range(2)]
            a1 = [k.sb([128, 128], F32, ls, "a1") for _ in range(2)]
            ybf = [k.sb([128, 128], BF16, ls, "ybf") for _ in range(2)]
            k.dma("sp", wabs[:], wabs_d[:, :], writes=[wabs])
            k.dma("sp", mclip[:], mclip_d[:, :], writes=[mclip])
            k.dma("sp", sink[:], sink_d[widx:widx + 1, :].partition_broadcast(128), writes=[sink])
            k.op("act", lambda e: e.activation(out=esink[:], in_=sink[:], func=AF.Exp), reads=[sink], writes=[esink])
            k.op("pool", lambda e: e.memset(Vx[:, :, 128:129], 1.0), writes=[Vx])
            pp = 0
            for hq in range(16):
                kv = hq // 4
                if hq % 4 == 0:
                    wk, wv = (wb[0], 0), (wb[0], 128)
                    load_w128((widx * 120 + 96 + kv) * 128, wk[0], wk[1])
                    load_w128((widx * 120 + 100 + kv) * 128, wv[0], wv[1])
                    proj_fm(xnT, wk[0], wk[1], KT, 0, None)
                    for ti, (s, n) in enumerate(TT):
                        p = nxt("ps", PSP[0])
                        proj_tm(xnT, wv[0], ti, p, 128, wv[1])
                        k.op("act", lambda e: e.copy(out=Vx[0:n, ti, 0:128], in_=p[0:n, 0:128]), reads=[p], writes=[Vx])
                wq, wg = (wb[1 + hq % 2], 0), (wb[1 + hq % 2], 128)
                load_w128((widx * 120 + 80 + hq) * 128, wq[0], wq[1])
                load_w128((widx * 120 + 104 + hq) * 128, wg[0], wg[1])
                proj_fm(xnT, wq[0], wq[1], QT, 0, 128 ** -0.5)
                for ti, (s, n) in enumerate(TT):
                    p = nxt("ps", PSP[0])
                    proj_tm(xnT, wg[0], ti, p, 128, wg[1])
                    silu_from_psum(p, n, 128, G[0:n, ti, :], G, ga, gb)
                for (t0, N) in QB:
                    t0v = vidx(t0)
                    nqs = (N + 127) // 128
                    acc = ps[2:6]
                    if t0 < 2048:
                        xt = [s0 for s0 in range(t0 - 128, t0 + N + 1, 128) if 0 <= s0 <= 1920]
                    else:
                        xt = [0]
                    ktl = [(2048, 16)] + [(s0, 128) for s0 in xt]
                    pend = []
                    stb = [ps[0], ps[1], ps[6]] if ST3 else [ps[0], ps[1]]
                    for ki, (s0, kn) in enumerate(ktl):
                        s0v = vidx(s0)
                        kt = s0 // 128
                        stp = stb[ki % len(stb)]
                        k.mms([lambda e: e.matmul(stp[0:kn, 0:N], lhsT=KT[:, 0, s0:s0 + kn], rhs=QT[:, 0, t0:t0 + N],
                                                  start=True, stop=True)], reads=[KT, QT], writes=[stp])
                        if s0 == 2048 and t0 < 2048:
                            c0 = min(t0, 512)
                            dt_ap, dbuf = mclip[0:16, c0:c0 + N], mclip
                        elif s0 == 2048:
                            dt_ap, dbuf = dabs[0:16, 384:384 + N], dabs
                        else:
                            off = t0v - s0v
                            dt_ap, dbuf = wabs[0:kn, 512 + off:512 + off + N], wabs
                        tf = tmpf[pp % 4]
                        ptile = PT[pp % 4]
                        pp += 1
                        k.op("dve", lambda e: e.scalar_tensor_tensor(out=tf[0:kn, 0:N], in0=dt_ap, scalar=-sl[hq],
                                                                     in1=stp[0:kn, 0:N], op0=ALU.mult, op1=ALU.add),
                             reads=[dbuf, stp], writes=[tf])
                        k.op("act", lambda e: e.activation(out=ptile[0:kn, 0:N], in_=tf[0:kn, 0:N], func=AF.Exp),
                             reads=[tf], writes=[ptile])
                        if len(pend) >= LOOK:
                            pend.pop(0)()
                        def pv(s0=s0, kn=kn, kt=kt, ptile=ptile):
                            for qs in range(nqs):
                                qn = min(128, N - qs * 128)
                                tok0 = t0 + qs * 128
                                if t0 < 2048:
                                    rel = [s_ for s_ in (tok0 - 128, tok0, tok0 + 128) if 0 <= s_ <= 1920]
                                else:
                                    rel = [0]
                                if s0 != 2048 and s0 not in rel:
                                    continue
                                a = acc[qs]
                                k.mms([lambda e: e.matmul(a[0:qn, 0:129], lhsT=ptile[0:kn, qs * 128:qs * 128 + qn],
                                                          rhs=Vx[0:kn, kt, :], start=(s0 == 2048), stop=(s0 == rel[-1]))],
                                      reads=[ptile, Vx], writes=[a])
                        pend.append(pv)
                    while pend:
                        pend.pop(0)()
                    for qs in range(nqs):
                        qn = min(128, N - qs * 128)
                        tok0 = t0 + qs * 128
                        ti = tok0 // 128
                        o = O[qs]
                        s_ = sm[qs % 2]
                        x1, yb = a1[qs % 2], ybf[qs % 2]
                        k.op("act", lambda e: e.copy(out=o[0:qn, :], in_=acc[qs][0:qn, 0:129]), reads=[acc[qs]], writes=[o])
                        k.op("dve", lambda e: e.tensor_tensor(out=s_[0:qn, 0:1], in0=o[0:qn, 128:129],
                                                              in1=esink[0:qn, hq:hq + 1], op=ALU.add),
                             reads=[o, esink], writes=[s_])
                        k.op("dve", lambda e: e.reciprocal(out=s_[0:qn, 1:2], in_=s_[0:qn, 0:1]), reads=[s_], writes=[s_])
                        k.op("dve", lambda e: e.scalar_tensor_tensor(out=yb[0:qn, :], in0=o[0:qn, 0:128], scalar=s_[0:qn, 1:2],
                                                                     in1=G[0:qn, ti, :], op0=ALU.mult, op1=ALU.mult),
                             reads=[o, s_, G], writes=[yb])
                        pt = nxt("pst", pstq)
                        k.mms([lambda e: e.transpose(out=pt[:, 0:qn], in_=yb[0:qn, :], identity=ident[0:qn, 0:qn])],
                              reads=[yb, ident], writes=[pt])
                        k.op("act", lambda e: e.copy(out=yT[:, tok0:tok0 + qn], in_=pt[:, 0:qn]), reads=[pt], writes=[yT])
                k.dma("pool", yTd[16 + hq, :, :], yT[:], reads=[yT], writes=[yTd_b[hq]])

        def phase_out(first, wout_d, widx, ls):
            wob = [k.sb([128, 32, 512], BF16, ls, "wob") for _ in range(2)]
            yb = [k.sb([128, 32, 512], BF16, ls, "ytb") for _ in range(2)]
            hb = [k.sb([128, 512], F32, ls, "hb") for _ in range(4)]
            ho = [k.sb([128, 512], F32, ls, "ho") for _ in range(4)]
            cnt = 0
            for nb in range(4):
                wo = wob[nb % 2]
                for pc in range(8):
                    stg = nxt("wst", wst)
                    r0 = ((widx * 4 + nb) * 8 + pc) * 128
                    k.dma("sp", stg[:], wout_d[r0:r0 + 128, :], writes=[stg])
                    k.op("pool", lambda e, stg=stg, pc=pc: e.tensor_copy(
                        out=wo[:, 4 * pc:4 * pc + 4, :], in_=stg[:].rearrange("p (c n) -> p c n", c=4)),
                        reads=[stg], writes=[wo])
                for bi, (t0, N) in enumerate(QB):
                    y = yb[bi % 2]
                    k.dma("sp", y[:, :, 0:N], yTd[:, :, t0:t0 + N].rearrange("c p t -> p c t"),
                          reads=yTd_b, writes=[y])
                    for qs in range((N + 127) // 128):
                        qn = min(128, N - qs * 128)
                        tok0 = t0 + qs * 128
                        ti = tok0 // 128
                        p = nxt("ps", PSP[0])
                        k.mms([lambda e, c=c: e.matmul(p[0:qn, :], lhsT=y[:, c, qs * 128:qs * 128 + qn], rhs=wo[:, c, :],
                                                       start=(c == 0), stop=(c == 31)) for c in range(32)],
                              reads=[y, wo], writes=[p])
                        hi = hb[cnt % 4]
                        hn = ho[cnt % 4]
                        cnt += 1
                        src = h_src(first, ti)
                        k.dma("sp", hi[0:qn, :], src[:, nb * 512:(nb + 1) * 512], reads=[hd_b[ti]], writes=[hi])
                        k.op("dve", lambda e: e.tensor_tensor(out=hn[0:qn, :], in0=p[0:qn, :], in1=hi[0:qn, :], op=ALU.add),
                             reads=[p, hi], writes=[hn])
                        k.dma("pool", hd[tok0:tok0 + qn, nb * 512:(nb + 1) * 512], hn[0:qn, :], reads=[hn],
                              writes=[hd_b[ti]])

        def phase_final(first, ls):
            gt = k.sb([128, D], F32, ls, "gt")
            hb = [k.sb([128, D], F32, ls, "hb") for _ in range(2)]
            ob = [k.sb([128, D], F32, ls, "ob") for _ in range(2)]
            junk = k.sb([128, D], BF16, ls, "junk")
            ssb = [k.sb([128, 2], F32, ls, "ss") for _ in range(2)]
            k.dma("sp", gt[:], nrm_d[4:5, :].partition_broadcast(128), writes=[gt])
            toks = []
            for ti, (s, n) in enumerate(TT[:16]):
                h = hb[ti % 2]
                o = ob[ti % 2]
                ss = ssb[ti % 2]
                k.dma("sp", h[0:n, :], h_src(first, ti), reads=[hd_b[ti]], writes=[h])
                k.op("act", lambda e: e.activation(out=junk[0:n, :], in_=h[0:n, :], func=AF.Square,
                                                   accum_out=ss[0:n, 0:1]), reads=[h], writes=[junk, ss])
                k.op("act", lambda e: e.activation(out=ss[0:n, 1:2], in_=ss[0:n, 0:1], func=AF.Ln, scale=1.0 / D,
                                                   bias=epsc[0:n, 0:1]), reads=[ss, epsc], writes=[ss])
                k.op("act", lambda e: e.activation(out=ss[0:n, 0:1], in_=ss[0:n, 1:2], func=AF.Exp, scale=-0.5),
                     reads=[ss], writes=[ss])
                k.op("dve", lambda e: e.scalar_tensor_tensor(out=o[0:n, :], in0=h[0:n, :], scalar=ss[0:n, 0:1],
                                                             in1=gt[0:n, :], op0=ALU.mult, op1=ALU.mult),
                     reads=[h, ss, gt], writes=[o])
                toks.append(k.dma("pool", out_d[s:s + n, :], o[0:n, :], reads=[o]))
            return toks

        first = True
        for (kind, widx, layer_idx) in layers:
            with ExitStack() as ls:
                xnT = k.sb([128, 16, L], BF16, ls, "xnT")
                with ExitStack() as ls2:
                    phase_norm(first, (0 if kind == "E" else 2) + widx, xnT, ls2)
                    k.barrier()
                with ExitStack() as ls2:
                    if kind == "O":
                        odd_mixer(widx, layer_idx, xnT, ls2)
                    else:
                        even_mixer_A(widx, xnT, ls2)
                    k.barrier()
                if kind == "E":
                    with ExitStack() as ls2:
                        even_mixer_B(widx, xnT, ls2)
                        k.barrier()
            with ExitStack() as ls:
                phase_out(first, woutc_d if kind == "O" else wouta_d, widx, ls)
                k.barrier()
            first = False
        out_toks = []
        with ExitStack() as ls:
            if do_final:
                out_toks = phase_final(first, ls)
            else:
                hb = [k.sb([128, D], F32, ls, "hb") for _ in range(2)]
                for ti, (s, n) in enumerate(TT):
                    h = hb[ti % 2]
                    k.dma("sp", h[0:n, :], h_src(first, ti), reads=[hd_b[ti]], writes=[h])
                    out_toks.append(k.dma("pool", out_d[s:s + n, :], h[0:n, :], reads=[h]))
            k.barrier()
        k.check_deadlock()
    return nc


def const_tables():
    i = np.arange(128, dtype=np.float32)[:, None]
    ident = np.eye(128, dtype=np.float32)
    dlin = (np.arange(512, dtype=np.float32)[None, :] - i).astype(np.float32)
    dabs = np.abs(np.arange(896, dtype=np.float32)[None, :] - i - 384).astype(np.float32)
    sl = np.array(slopes16(), dtype=np.float64)
    cb = (-(sl[:, None] * np.array(DELTAS, dtype=np.float64)[None, :])).reshape(1, -1)
    cb = np.repeat(cb, 128, axis=0).astype(np.float32)
    return {"ident": ident, "dlin": dlin, "dabs": dabs, "cbtab": np.ascontiguousarray(cb)}


def layout_winc(w):
    a = w.reshape(2, 2, 8, 128, 4, 16, 256)
    a = a.transpose(0, 5, 4, 1, 3, 2, 6)
    return np.ascontiguousarray(a).reshape(2 * 16 * 4 * 2 * 128, 2048)


def layout_wina(w):
    a = w.reshape(2, 16, 128, 120, 128)
    a = a.transpose(0, 3, 2, 1, 4)
    return np.ascontiguousarray(a).reshape(2 * 120 * 128, 2048)


def even_tables():
    i = np.arange(128, dtype=np.float32)[:, None]
    smask = np.ones((128, L), np.float32)
    smask[:, 0:2048:64] = 0.0
    smask[:, 2048] = 0.0
    j = np.arange(512)
    mA = ((j % 128) < 64).astype(np.float32)
    mab = np.concatenate([np.tile(mA[None], (128, 1)), np.tile((1 - mA)[None], (128, 1))], axis=1)
    s_ = np.arange(128)[:, None]; t_ = np.arange(128)[None, :]
    same = (s_ // 64) == (t_ // 64)
    mf = (same & (s_ <= t_)).astype(np.float32)
    mb_ = (same & (s_ >= t_)).astype(np.float32)
    mtri = np.concatenate([mf, mb_], axis=1)
    dw = np.abs(np.arange(1152, dtype=np.float32)[None, :] - i - 512)
    wabs = np.where(dw <= 128, dw, 1e9).astype(np.float32)
    mclip = np.minimum(np.arange(1024, dtype=np.float32)[None, :] - i + 16, 128.0).astype(np.float32)
    return {"smask": smask, "mab": np.ascontiguousarray(mab), "mtri": np.ascontiguousarray(mtri),
            "wabs": wabs, "mclip": mclip}


def layout_wout(w):
    a = w.reshape(2, 8, 4, 128, 4, 512)
    a = a.transpose(0, 4, 1, 3, 2, 5)
    return np.ascontiguousarray(a).reshape(2 * 4 * 8 * 128, 2048)


def run_layers(layers, do_final, inputs, ncores=8):
    out_rows = NX if do_final else L
    nc = build(layers, do_final, out_rows)
    f = lambda a: np.ascontiguousarray(np.asarray(a, dtype=np.float32))
    shared = dict(const_tables())
    shared["meta"] = f(inputs["meta_tokens"])
    shared["norms"] = np.concatenate([f(inputs["norm_a"]), f(inputs["norm_c"]), f(inputs["final_norm"])[None]], axis=0)
    if any(l[0] == "E" for l in layers):
        shared.update(even_tables())
        shared["wina"] = layout_wina(f(inputs["w_in_a"]))
        shared["wouta"] = layout_wout(f(inputs["w_out_a"]))
        shared["lbl"] = np.ascontiguousarray(f(inputs["hgrn_lb"]).reshape(2, 2, 16, 128).transpose(3, 0, 1, 2)).reshape(128, 64)
        shared["hnorm"] = f(inputs["hgrn_norm"])
        shared["sink"] = f(inputs["sink_logits"])
    if any(l[0] == "O" for l in layers):
        shared["winc"] = layout_winc(f(inputs["w_in_c"]))
        shared["woutc"] = layout_wout(f(inputs["w_out_c"]))
        shared["dlam"] = f(inputs["diff_lambda"]).reshape(2, 512)
        shared["dnorm"] = f(inputs["diff_norm"])
    x = f(inputs["x"])
    in_maps = []
    for b in range(ncores):
        m = dict(shared)
        m["x"] = x[b]
        in_maps.append(m)
    res = run_bass_kernel_spmd(nc, in_maps, core_ids=list(range(ncores)))
    return np.stack([r["out"] for r in res.results], axis=0)


def kernel(**inputs):
    layers = [("E", 0, 0), ("O", 0, 1), ("E", 1, 2), ("O", 1, 3)]
    return run_layers(layers, True, inputs)
```

```python
import math
from contextlib import ExitStack
import numpy as np
import concourse.bass as bass
import concourse.mybir as mybir
from concourse.bass_utils import run_bass_kernel_spmd

F32 = mybir.dt.float32
BF16 = mybir.dt.bfloat16
AF = mybir.ActivationFunctionType
ALU = mybir.AluOpType
AX = mybir.AxisListType

L = 2064
NX = 2048
NMETA = 16
D = 2048
EPS = 1e-6
TT = [(i * 128, 128) for i in range(16)] + [(2048, 16)]
QB = [(i * 512, 512) for i in range(4)] + [(2048, 16)]
DELTAS = [128 * m for m in range(1, 16)] + [16 + 128 * m for m in range(16)]
NDEL = len(DELTAS)
SAME_ENGINE_SYNC = True
import os
LOOK = int(os.environ.get('KLOOK', '1'))
ST3 = int(os.environ.get('KST3', '1'))


def vidx(s):
    return s if s < 2048 else s - 2048 - 16


def slopes16():
    return [2.0 ** (-8.0 * (i + 1) / 16) for i in range(16)]


class Eng:
    def __init__(self, nc, es, name, e, ndma):
        self.name = name
        self.e = e
        self.sem = es.enter_context(nc.semaphore("s_" + name))
        self.cnt = 0
        self.seen = {}
        self.ring = [[es.enter_context(nc.semaphore("d_%s%d" % (name, i))), 0] for i in range(ndma)]
        self.ri = 0


class Buf:
    def __init__(self, t):
        self.t = t
        self.w = None
        self.rs = {}

    def __getitem__(self, k):
        return self.t[k]


class K:
    def __init__(self, nc, es):
        self.nc = nc
        self.es = es
        self.E = {
            "pe": Eng(nc, es, "pe", nc.tensor, 0),
            "dve": Eng(nc, es, "dve", nc.vector, 0),
            "act": Eng(nc, es, "act", nc.scalar, 0),
            "pool": Eng(nc, es, "pool", nc.gpsimd, 16),
            "sp": Eng(nc, es, "sp", nc.sync, 24),
        }
        self.nbuf = 0
        self.log = []

    def sb(self, shape, dt, es=None, name=None):
        self.nbuf += 1
        t = (es or self.es).enter_context(self.nc.sbuf_tensor("%s_%d" % (name or "sb", self.nbuf), list(shape), dt))
        return Buf(t)

    def wait(self, eng, tok):
        if tok is None:
            return
        sid, sem, val = tok
        if eng.seen.get(sid, 0) >= val:
            return
        if sid == id(eng.sem) and (eng.name == "pe" or not SAME_ENGINE_SYNC):
            return
        eng.e.wait_ge(sem, val)
        eng.seen[sid] = val
        self.log.append((eng.name, "w", sid, val))

    def _deps(self, eng, reads, writes):
        for b in reads:
            self.wait(eng, b.w)
        for b in writes:
            self.wait(eng, b.w)
            for tok in list(b.rs.values()):
                self.wait(eng, tok)

    def _mark(self, tok, reads, writes):
        for b in reads:
            old = b.rs.get(tok[0])
            if old is None or old[2] < tok[2]:
                b.rs[tok[0]] = tok
        for b in writes:
            b.w = tok
            b.rs = {}

    def op(self, en, fn, reads=(), writes=()):
        eng = self.E[en]
        self._deps(eng, reads, writes)
        ins = fn(eng.e)
        eng.cnt += 1
        ins.then_inc(eng.sem, 1)
        self.log.append((eng.name, "i", id(eng.sem), 1))
        tok = (id(eng.sem), eng.sem, eng.cnt)
        self._mark(tok, reads, writes)
        return tok

    def mms(self, fns, reads=(), writes=()):
        eng = self.E["pe"]
        self._deps(eng, reads, writes)
        ins = None
        for fn in fns:
            ins = fn(eng.e)
        eng.cnt += 1
        ins.then_inc(eng.sem, 1)
        self.log.append((eng.name, "i", id(eng.sem), 1))
        tok = (id(eng.sem), eng.sem, eng.cnt)
        self._mark(tok, reads, writes)
        return tok

    def dma(self, qn, out, in_, reads=(), writes=()):
        eng = self.E[qn]
        self._deps(eng, reads, writes)
        slot = eng.ring[eng.ri % len(eng.ring)]
        eng.ri += 1
        if slot[1] > 0:
            self.wait(eng, (id(slot[0]), slot[0], slot[1]))
        ins = eng.e.dma_start(out=out, in_=in_)
        slot[1] += 16
        ins.then_inc(slot[0], 16)
        self.log.append((eng.name, "i", id(slot[0]), 16))
        tok = (id(slot[0]), slot[0], slot[1])
        self._mark(tok, reads, writes)
        return tok

    def check_deadlock(self):
        qs = {}
        for ev in self.log:
            qs.setdefault(ev[0], []).append(ev)
        ptr = {n: 0 for n in qs}
        sem = {}
        prog = True
        while prog:
            prog = False
            for n, q in qs.items():
                while ptr[n] < len(q):
                    _, kind, sid, val = q[ptr[n]]
                    if kind == "i":
                        sem[sid] = sem.get(sid, 0) + val
                    elif sem.get(sid, 0) < val:
                        break
                    ptr[n] += 1
                    prog = True
        stuck = {n: (ptr[n], len(q), q[ptr[n]]) for n, q in qs.items() if ptr[n] < len(q)}
        if stuck:
            names = {id(e.sem): e.name for e in self.E.values()}
            for e in self.E.values():
                for i, sl in enumerate(e.ring):
                    names[id(sl[0])] = "%s_dma%d" % (e.name, i)
            msg = "; ".join("%s at %d/%d waits %s>=%d (have %d)" % (n, p, t, names.get(ev[2]), ev[3], sem.get(ev[2], 0))
                            for n, (p, t, ev) in stuck.items())
            raise RuntimeError("DEADLOCK in emitted program: " + msg)

    def barrier(self):
        toks = []
        for e in self.E.values():
            if e.cnt > 0:
                toks.append((id(e.sem), e.sem, e.cnt))
            for s in e.ring:
                if s[1] > 0:
                    toks.append((id(s[0]), s[0], s[1]))
        for e in self.E.values():
            for t in toks:
                if t[0] == id(e.sem):
                    continue
                self.wait(e, t)


def build(layers, do_final, out_rows):
    nc = bass.Bass("TRN2", target_bir_lowering=False)
    dr = {}

    def din(name, shape):
        dr[name] = nc.dram_tensor(name, list(shape), F32, kind="ExternalInput").ap()
        return dr[name]

    x_d = din("x", [NX, D])
    meta_d = din("meta", [NMETA, D])
    nrm_d = din("norms", [5, D])
    ident_d = din("ident", [128, 128])
    dlin_d = din("dlin", [128, 512])
    dabs_d = din("dabs", [128, 896])
    cb_d = din("cbtab", [128, 16 * NDEL])
    n_odd = sum(1 for l in layers if l[0] == "O")
    n_even = sum(1 for l in layers if l[0] == "E")
    if n_odd:
        winc_d = din("winc", [2 * 16 * 4 * 2 * 128, 2048])
        woutc_d = din("woutc", [2 * 4 * 8 * 128, 2048])
        dlam_d = din("dlam", [2, 512])
        dnorm_d = din("dnorm", [2, 256])
    if n_even:
        wina_d = din("wina", [2 * 120 * 128, 2048])
        wouta_d = din("wouta", [2 * 4 * 8 * 128, 2048])
        smask_d = din("smask", [128, L])
        mab_d = din("mab", [128, 1024])
        mtri_d = din("mtri", [128, 256])
        lb_d = din("lbl", [128, 64])
        hnorm_d = din("hnorm", [2, 128])
        wabs_d = din("wabs", [128, 1152])
        mclip_d = din("mclip", [128, 1024])
        sink_d = din("sink", [2, 16])
    out_d = nc.dram_tensor("out", [out_rows, D], F32, kind="ExternalOutput").ap()
    hd = nc.dram_tensor("hd", [L, D], F32, kind="Internal").ap()
    yTd = nc.dram_tensor("yTd", [32, 128, L], BF16, kind="Internal").ap()

    with ExitStack() as es:
        k = K(nc, es)
        ident_f = k.sb([128, 128], F32)
        ident = k.sb([128, 128], BF16)
        wst = [k.sb([128, 2048], F32, name="wst") for _ in range(2)]
        ps = [Buf(es.enter_context(nc.psum_tensor("ps%d" % i, [128, 512], F32))) for i in range(7)]
        pst1 = es.enter_context(nc.psum_tensor("pst", [128, 1024], BF16))
        pstq = [Buf(pst1)]
        PSP = [ps]
        hd_b = [Buf(None) for _ in TT]
        yTd_b = [Buf(None) for _ in range(16)]
        st = {"wst": 0, "wb": 0, "ps": 0, "pst": 0, "uq": 0}

        def nxt(key, lst):
            i = st[key] % len(lst)
            st[key] += 1
            return lst[i]

        epsc = k.sb([128, 4], F32)
        k.op("pool", lambda e: e.memset(epsc[:, 0:1], EPS), writes=[epsc])
        k.op("pool", lambda e: e.memset(epsc[:, 3:4], 1.0), writes=[epsc])
        for wi in range(2):
            k.op("pool", lambda e, wi=wi: e.memset(epsc[:, 1 + wi:2 + wi],
                                                   math.log(1.0 - (0.8 - 0.6 * math.exp(-0.3 * (2 * wi + 1))))),
                 writes=[epsc])
        k.dma("sp", ident_f[:], ident_d[:, :], writes=[ident_f])
        k.op("dve", lambda e: e.tensor_copy(out=ident[:], in_=ident_f[:]), reads=[ident_f], writes=[ident])

        if n_odd:
            lams = k.sb([128, 4], F32)
            lame = k.sb([128, 4], F32)
            neglam = k.sb([128, 2], F32)
            lam_es = ExitStack()
            lamt = k.sb([128, 2, 512], F32, lam_es)
            lamp = k.sb([128, 2, 2, 128], F32, lam_es)
            for i in range(2):
                k.dma("sp", lamt[:, i, :], dlam_d[i:i + 1, :].partition_broadcast(128), writes=[lamt])
            for i in range(2):
                for j in range(2):
                    k.op("dve", lambda e, i=i, j=j: e.tensor_tensor(
                        out=lamp[:, i, j, :], in0=lamt[:, i, 256 * j:256 * j + 128],
                        in1=lamt[:, i, 256 * j + 128:256 * j + 256], op=ALU.mult), reads=[lamt], writes=[lamp])
            for i in range(2):
                for j in range(2):
                    k.op("dve", lambda e, i=i, j=j: e.reduce_sum(
                        out=lams[:, 2 * i + j:2 * i + j + 1], in_=lamp[:, i, j, :], axis=AX.X),
                        reads=[lamp], writes=[lams])
            k.op("act", lambda e: e.activation(out=lame[:], in_=lams[:], func=AF.Exp), reads=[lams], writes=[lame])
            k.barrier()
            lam_es.close()

        def lam_init_of(layer_idx):
            return 0.8 - 0.6 * math.exp(-0.3 * layer_idx)

        def h_src(first, ti):
            s, n = TT[ti]
            if first:
                return (x_d[s:s + n, :] if s < 2048 else meta_d[0:n, :])
            return hd[s:s + n, :]

        def load_w_slice(row0, dst, ls):
            for half in range(2):
                stg = nxt("wst", wst)
                k.dma("sp", stg[:], ls[0][row0 + half * 128:row0 + half * 128 + 128, :], writes=[stg])
                k.op("pool", lambda e, stg=stg, half=half: e.tensor_copy(
                    out=dst[:, 8 * half:8 * half + 8, :],
                    in_=stg[:].rearrange("p (c n) -> p c n", c=8)), reads=[stg], writes=[dst])

        def phase_norm(first, nrow, xnT, ls):
            gt = k.sb([128, D], F32, ls, "gt")
            hb = [k.sb([128, D], F32, ls, "hb") for _ in range(2)]
            xb = [k.sb([128, D], BF16, ls, "xb") for _ in range(2)]
            junk = k.sb([128, D], BF16, ls, "junk")
            ssb = [k.sb([128, 2], F32, ls, "ss") for _ in range(2)]
            k.dma("sp", gt[:], nrm_d[nrow:nrow + 1, :].partition_broadcast(128), writes=[gt])
            for ti, (s, n) in enumerate(TT):
                h = hb[ti % 2]
                xn = xb[ti % 2]
                ss = ssb[ti % 2]
                k.dma("sp", h[0:n, :], h_src(first, ti), reads=[hd_b[ti]], writes=[h])
                k.op("act", lambda e: e.activation(out=junk[0:n, :], in_=h[0:n, :], func=AF.Square,
                                                   accum_out=ss[0:n, 0:1]), reads=[h], writes=[junk, ss])
                k.op("act", lambda e: e.activation(out=ss[0:n, 1:2], in_=ss[0:n, 0:1], func=AF.Ln, scale=1.0 / D,
                                                   bias=epsc[0:n, 0:1]), reads=[ss, epsc], writes=[ss])
                k.op("act", lambda e: e.activation(out=ss[0:n, 0:1], in_=ss[0:n, 1:2], func=AF.Exp, scale=-0.5),
                     reads=[ss], writes=[ss])
                k.op("dve", lambda e: e.scalar_tensor_tensor(out=xn[0:n, :], in0=h[0:n, :], scalar=ss[0:n, 0:1],
                                                             in1=gt[0:n, :], op0=ALU.mult, op1=ALU.mult),
                     reads=[h, ss, gt], writes=[xn])
                for g in range(2):
                    pt = nxt("pst", pstq)
                    k.mms([lambda e, c=c, g=g, pt=pt: e.transpose(
                        out=pt[:, c * 128:c * 128 + n], in_=xn[0:n, (8 * g + c) * 128:(8 * g + c + 1) * 128],
                        identity=ident[0:n, 0:n]) for c in range(8)], reads=[xn, ident], writes=[pt])
                    k.op("act" if g == 0 else "dve", lambda e, g=g, pt=pt: (e.copy if g == 0 else e.tensor_copy)(
                        out=xnT[:, 8 * g:8 * g + 8, s:s + n],
                        in_=pt[:, :].rearrange("p (c t) -> p c t", c=8)[:, :, 0:n]), reads=[pt], writes=[xnT])

        def proj_fm(xnT, w, c0, dst, dj, scale):
            for bi, (s, n) in enumerate(QB):
                p = nxt("ps", PSP[0])
                k.mms([lambda e, c=c, p=p: e.matmul(p[:, 0:n], lhsT=w[:, c, c0:c0 + 128], rhs=xnT[:, c, s:s + n],
                                                    start=(c == 0), stop=(c == 15)) for c in range(16)],
                      reads=[xnT, w], writes=[p])
                if scale is None:
                    k.op("act", lambda e, p=p: e.copy(out=dst[:, dj, s:s + n], in_=p[:, 0:n]), reads=[p], writes=[dst])
                else:
                    k.op("act", lambda e, p=p: e.mul(out=dst[:, dj, s:s + n], in_=p[:, 0:n], mul=scale),
                         reads=[p], writes=[dst])

        def proj_tm(xnT, w, ti, p, ncols=256, c0=0):
            s, n = TT[ti]
            k.mms([lambda e, c=c: e.matmul(p[0:n, 0:ncols], lhsT=xnT[:, c, s:s + n], rhs=w[:, c, c0:c0 + ncols],
                                           start=(c == 0), stop=(c == 15)) for c in range(16)],
                  reads=[xnT, w], writes=[p])

        def silu_from_psum(p, n, ncols, dst_ap, dstbuf, tmpa, tmpb, pc0=0):
            k.op("act", lambda e: e.activation(out=tmpa[0:n, 0:ncols], in_=p[0:n, pc0:pc0 + ncols], func=AF.Exp, scale=-1.0),
                 reads=[p], writes=[tmpa])
            k.op("dve", lambda e: e.tensor_scalar(out=tmpa[0:n, 0:ncols], in0=tmpa[0:n, 0:ncols], scalar1=1.0,
                                                  scalar2=None, op0=ALU.add), reads=[tmpa], writes=[tmpa])
            k.op("dve", lambda e: e.reciprocal(out=tmpb[0:n, 0:ncols], in_=tmpa[0:n, 0:ncols]), reads=[tmpa], writes=[tmpb])
            k.op("dve", lambda e: e.tensor_tensor(out=dst_ap, in0=p[0:n, pc0:pc0 + ncols], in1=tmpb[0:n, 0:ncols],
                                                  op=ALU.mult), reads=[p, tmpb], writes=[dstbuf])

        CT = {}

        def load_consts(ls):
            CT["dlin"] = k.sb([128, 512], F32, ls, "dlin")
            CT["dabs"] = k.sb([128, 896], F32, ls, "dabs")
            CT["cbt"] = k.sb([128, 16 * NDEL], F32, ls, "cbt")
            k.dma("sp", CT["dlin"][:], dlin_d[:, :], writes=[CT["dlin"]])
            k.dma("sp", CT["dabs"][:], dabs_d[:, :], writes=[CT["dabs"]])
            k.dma("sp", CT["cbt"][:], cb_d[:, :], writes=[CT["cbt"]])

        def bias_tile(t0v, N, s0v, kn, slope, h):
            dlin, dabs, cbt = CT["dlin"], CT["dabs"], CT["cbt"]
            if t0v < s0v + kn and s0v < t0v + N:
                off = t0v - s0v
                return dabs[0:kn, 384 + off:384 + off + N], dabs, -slope, None
            if t0v > s0v:
                dl = t0v - s0v
                return dlin[0:kn, 0:N], dlin, -slope, cbt[0:kn, h * NDEL + DELTAS.index(dl):h * NDEL + DELTAS.index(dl) + 1]
            dl = s0v - t0v
            return dlin[0:kn, 0:N], dlin, slope, cbt[0:kn, h * NDEL + DELTAS.index(dl):h * NDEL + DELTAS.index(dl) + 1]

        def odd_mixer(widx, layer_idx, xnT, ls):
            lam_init = lam_init_of(layer_idx)
            sl = slopes16()
            load_consts(ls)
            cbt = CT["cbt"]
            wb = [k.sb([128, 16, 256], BF16, ls, "wb") for _ in range(4)]
            QT = k.sb([128, 2, L], BF16, ls, "QT")
            KT = k.sb([128, 2, L], BF16, ls, "KT")
            Vx = k.sb([128, 17, 257], BF16, ls, "Vx")
            G = k.sb([128, 17, 256], BF16, ls, "G")
            yT = k.sb([128, 2, L], BF16, ls, "yT")
            tmpf = [k.sb([128, 512], F32, ls, "tmpf") for _ in range(4)]
            PT = [k.sb([128, 512], BF16, ls, "PT") for _ in range(4)]
            O = [[k.sb([128, 257], F32, ls, "O") for _ in range(4)] for _ in range(2)]
            ga = k.sb([128, 256], F32, ls, "ga")
            gb = k.sb([128, 256], F32, ls, "gb")
            gn = k.sb([128, 256], F32, ls, "gn")
            sm = [k.sb([128, 8], F32, ls, "sm") for _ in range(2)]
            u1 = [k.sb([128, 256], F32, ls, "u1") for _ in range(2)]
            u2 = [k.sb([128, 256], F32, ls, "u2") for _ in range(2)]
            u3 = [k.sb([128, 256], F32, ls, "u3") for _ in range(2)]
            jk = k.sb([128, 256], F32, ls, "jk")
            ybf = [k.sb([128, 256], BF16, ls, "ybf") for _ in range(2)]
            k.dma("sp", gn[:], dnorm_d[widx:widx + 1, :].partition_broadcast(128), writes=[gn])
            k.op("pool", lambda e: e.memset(Vx[:, :, 256:257], 1.0), writes=[Vx])
            k.op("dve", lambda e: e.tensor_tensor(out=neglam[:, widx:widx + 1], in0=lame[:, 2 * widx + 1:2 * widx + 2],
                                                  in1=lame[:, 2 * widx:2 * widx + 1], op=ALU.subtract),
                 reads=[lame], writes=[neglam])
            k.op("dve", lambda e: e.tensor_scalar(out=neglam[:, widx:widx + 1], in0=neglam[:, widx:widx + 1],
                                                  scalar1=-lam_init, scalar2=None, op0=ALU.add),
                 reads=[neglam], writes=[neglam])
            nl = neglam
            pp = 0
            def load_head(h_):
                base_ = ((widx * 16 + h_) * 4) * 256
                for i_ in range(4):
                    load_w_slice(base_ + i_ * 256, wb[i_], [winc_d])
            load_head(0)
            for h in range(16):
                wq, wk, wv, wg = wb[0], wb[1], wb[2], wb[3]
                for j in range(2):
                    proj_fm(xnT, wq, j * 128, QT, j, 128 ** -0.5)
                for j in range(2):
                    proj_fm(xnT, wk, j * 128, KT, j, None)
                for ti, (s, n) in enumerate(TT):
                    p = nxt("ps", PSP[0])
                    proj_tm(xnT, wv, ti, p)
                    k.op("act", lambda e, p=p, ti=ti, n=n: e.copy(out=Vx[0:n, ti, 0:256], in_=p[0:n, 0:256]),
                         reads=[p], writes=[Vx])
                    p = nxt("ps", PSP[0])
                    proj_tm(xnT, wg, ti, p)
                    silu_from_psum(p, n, 256, G[0:n, ti, :], G, ga, gb)
                if h + 1 < 16:
                    load_head(h + 1)
                for (t0, N) in QB:
                    t0v = vidx(t0)
                    nqs = (N + 127) // 128
                    for j in range(2):
                        acc = ps[2:6]
                        pend = []
                        stb = [ps[0], ps[1], ps[6]] if ST3 else [ps[0], ps[1]]
                        for kt, (s0, kn) in enumerate(TT):
                            s0v = vidx(s0)
                            stp = stb[kt % len(stb)]
                            k.mms([lambda e: e.matmul(stp[0:kn, 0:N], lhsT=KT[:, j, s0:s0 + kn], rhs=QT[:, j, t0:t0 + N],
                                                      start=True, stop=True)], reads=[KT, QT], writes=[stp])
                            dt_ap, dbuf, coef, cb = bias_tile(t0v, N, s0v, kn, sl[h], h)
                            tf = tmpf[pp % 4]
                            ptile = PT[pp % 4]
                            pp += 1
                            k.op("dve", lambda e: e.scalar_tensor_tensor(out=tf[0:kn, 0:N], in0=dt_ap, scalar=coef,
                                                                         in1=stp[0:kn, 0:N], op0=ALU.mult, op1=ALU.add),
                                 reads=[dbuf, stp], writes=[tf])
                            if cb is None:
                                k.op("act", lambda e: e.activation(out=ptile[0:kn, 0:N], in_=tf[0:kn, 0:N], func=AF.Exp),
                                     reads=[tf], writes=[ptile])
                            else:
                                k.op("act", lambda e: e.activation(out=ptile[0:kn, 0:N], in_=tf[0:kn, 0:N], func=AF.Exp,
                                                                   bias=cb), reads=[tf, cbt], writes=[ptile])
                            if len(pend) >= LOOK:
                                pend.pop(0)()
                            def pv(kt=kt, kn=kn, ptile=ptile):
                                for qs in range(nqs):
                                    qn = min(128, N - qs * 128)
                                    a = acc[qs]
                                    k.mms([lambda e: e.matmul(a[0:qn, 0:257], lhsT=ptile[0:kn, qs * 128:qs * 128 + qn],
                                                              rhs=Vx[0:kn, kt, :], start=(kt == 0), stop=(kt == 16))],
                                          reads=[ptile, Vx], writes=[a])
                            pend.append(pv)
                        while pend:
                            pend.pop(0)()
                        for qs in range(nqs):
                            qn = min(128, N - qs * 128)
                            k.op("act" if qs % 2 == 0 else "dve",
                                 lambda e, qs=qs, qn=qn: (e.copy if qs % 2 == 0 else e.tensor_copy)(
                                     out=O[j][qs][0:qn, :], in_=acc[qs][0:qn, 0:257]),
                                 reads=[acc[qs]], writes=[O[j][qs]])
                    for qs in range(nqs):
                        qn = min(128, N - qs * 128)
                        tok0 = t0 + qs * 128
                        ti = tok0 // 128
                        o1, o2 = O[0][qs], O[1][qs]
                        s_ = sm[qs % 2]
                        a1, a2, a3 = u1[qs % 2], u2[qs % 2], u3[qs % 2]
                        yb = ybf[qs % 2]
                        k.op("dve", lambda e: e.reciprocal(out=s_[0:qn, 0:1], in_=o1[0:qn, 256:257]), reads=[o1], writes=[s_])
                        k.op("dve", lambda e: e.reciprocal(out=s_[0:qn, 1:2], in_=o2[0:qn, 256:257]), reads=[o2], writes=[s_])
                        k.op("dve", lambda e: e.tensor_tensor(out=s_[0:qn, 2:3], in0=s_[0:qn, 1:2],
                                                              in1=nl[0:qn, widx:widx + 1], op=ALU.mult),
                             reads=[s_, nl], writes=[s_])
                        k.op("dve", lambda e: e.tensor_scalar(out=a1[0:qn, :], in0=o1[0:qn, 0:256], scalar1=s_[0:qn, 0:1],
                                                              scalar2=None, op0=ALU.mult), reads=[o1, s_], writes=[a1])
                        k.op("dve", lambda e: e.scalar_tensor_tensor(out=a2[0:qn, :], in0=o2[0:qn, 0:256],
                                                                     scalar=s_[0:qn, 2:3], in1=a1[0:qn, :],
                                                                     op0=ALU.mult, op1=ALU.add),
                             reads=[o2, s_, a1], writes=[a2])
                        k.op("act", lambda e: e.activation(out=jk[0:qn, :], in_=a2[0:qn, :], func=AF.Square,
                                                           accum_out=s_[0:qn, 3:4]), reads=[a2], writes=[jk, s_])
                        k.op("act", lambda e: e.activation(out=s_[0:qn, 4:5], in_=s_[0:qn, 3:4], func=AF.Ln,
                                                           scale=1.0 / 256, bias=epsc[0:qn, 0:1]),
                             reads=[s_, epsc], writes=[s_])
                        k.op("act", lambda e: e.activation(out=s_[0:qn, 5:6], in_=s_[0:qn, 4:5], func=AF.Exp,
                                                           scale=-0.5, bias=epsc[0:qn, 1 + widx:2 + widx]),
                             reads=[s_, epsc], writes=[s_])
                        k.op("dve", lambda e: e.scalar_tensor_tensor(out=a3[0:qn, :], in0=a2[0:qn, :], scalar=s_[0:qn, 5:6],
                                                                     in1=gn[0:qn, :], op0=ALU.mult, op1=ALU.mult),
                             reads=[a2, s_, gn], writes=[a3])
                        k.op("dve", lambda e: e.tensor_tensor(out=yb[0:qn, :], in0=a3[0:qn, :], in1=G[0:qn, ti, :],
                                                               op=ALU.mult), reads=[a3, G], writes=[yb])
                        pt = nxt("pst", pstq)
                        k.mms([lambda e, c=c: e.transpose(out=pt[:, c * 128:c * 128 + qn], in_=yb[0:qn, c * 128:(c + 1) * 128],
                                                          identity=ident[0:qn, 0:qn]) for c in range(2)],
                              reads=[yb, ident], writes=[pt])
                        k.op("act", lambda e: e.copy(out=yT[:, :, tok0:tok0 + qn],
                                                     in_=pt[:, 0:256].rearrange("p (c t) -> p c t", c=2)[:, :, 0:qn]),
                             reads=[pt], writes=[yT])
                k.dma("pool", yTd[2 * h:2 * h + 2, :, :].rearrange("c p t -> p c t"), yT[:], reads=[yT], writes=[yTd_b[h]])

        def proj_fm_cb(xnT, w, c0, cb):
            for bi, (s, n) in enumerate(QB):
                p = nxt("ps", PSP[0])
                k.mms([lambda e, c=c, p=p: e.matmul(p[:, 0:n], lhsT=w[:, c, c0:c0 + 128], rhs=xnT[:, c, s:s + n],
                                                    start=(c == 0), stop=(c == 15)) for c in range(16)],
                      reads=[xnT, w], writes=[p])
                cb(p, s, n)

        def load_w128(sidx_row0, dst, c0):
            stg = nxt("wst", wst)
            k.dma("sp", stg[:], wina_d[sidx_row0:sidx_row0 + 128, :], writes=[stg])
            k.op("pool", lambda e: e.tensor_copy(out=dst[:, :, c0:c0 + 128],
                                                 in_=stg[:].rearrange("p (c n) -> p c n", c=16)),
                 reads=[stg], writes=[dst])

        def even_mixer_A(widx, xnT, ls):
            wb = [k.sb([128, 16, 256], BF16, ls, "wb") for _ in range(2)] + [k.sb([128, 16, 128], BF16, ls, "wb")]
            qA = k.sb([128, L], BF16, ls, "qA")
            qB = k.sb([128, L], BF16, ls, "qB")
            T1 = k.sb([128, L], F32, ls, "T1")
            T1b = k.sb([128, L], F32, ls, "T1b")
            T2 = k.sb([128, L], F32, ls, "T2")
            T3 = k.sb([128, L], F32, ls, "T3")
            QEA = k.sb([128, L], BF16, ls, "QEA")
            QEB = k.sb([128, L], BF16, ls, "QEB")
            KE = k.sb([128, L], BF16, ls, "KE")
            KL = k.sb([128, L], BF16, ls, "KL")
            V = k.sb([128, 17, 128], BF16, ls, "V")
            G = k.sb([128, 17, 128], BF16, ls, "G")
            OF = k.sb([128, 17, 128], F32, ls, "OF")
            yT = k.sb([128, L], BF16, ls, "yT")
            smk = k.sb([128, L], BF16, ls, "smk")
            mAB = k.sb([128, 2, 512], BF16, ls, "mAB")
            mtri = k.sb([128, 2, 128], F32, ls, "mtri")
            lbt = k.sb([128, 64], F32, ls, "lbt")
            lbc = k.sb([128, 2, 16], F32, ls, "lbc")
            omc = k.sb([128, 2, 16], F32, ls, "omc")
            gna = k.sb([128, 128], F32, ls, "gna")
            ATm = [k.sb([128, 128], BF16, ls, "ATm") for _ in range(3)]
            KLt = [[k.sb([128, 128], BF16, ls, "KLt") for _ in range(2)] for _ in range(2)]
            S = [k.sb([128, 128], F32, ls, "S") for _ in range(4)]
            SBALL = k.sb([128, 33, 128], BF16, ls, "SBALL")
            ECs = [k.sb([128, 34], F32, ls, "EC") for _ in range(2)]
            k.op("pool", lambda e: e.memset(SBALL[:, 0, :], 0.0), writes=[SBALL])
            ubanks = [ps[4], ps[5], ps[6]]
            PSP[0] = ps[0:4]
            ga = k.sb([128, 128], F32, ls, "ga")
            gb = k.sb([128, 128], F32, ls, "gb")
            sm = [k.sb([128, 8], F32, ls, "sm") for _ in range(2)]
            a1 = [k.sb([128, 128], F32, ls, "a1") for _ in range(2)]
            a2 = [k.sb([128, 128], F32, ls, "a2") for _ in range(2)]
            jk = k.sb([128, 128], F32, ls, "jk")
            ybf = [k.sb([128, 128], BF16, ls, "ybf") for _ in range(2)]
            k.dma("pool", smk[:], smask_d[:, :], writes=[smk])
            k.dma("pool", mAB[:], mab_d[:, :].rearrange("p (a n) -> p a n", a=2), writes=[mAB])
            k.dma("sp", mtri[:], mtri_d[:, :].rearrange("p (a n) -> p a n", a=2), writes=[mtri])
            k.dma("sp", lbt[:], lb_d[:, :], writes=[lbt])
            k.dma("sp", gna[:], hnorm_d[widx:widx + 1, :].partition_broadcast(128), writes=[gna])
            for par in range(2):
                for ab in range(2):
                    k.op("pool", lambda e, par=par, ab=ab: e.memset(KLt[par][ab][:], 0.0), writes=[KLt[par][ab]])
            lb4 = lbt[:].rearrange("p (r l h) -> p r l h", r=2, l=2)
            if widx == 0:
                k.op("pool", lambda e: e.memset(lbc[:], 0.0), writes=[lbc])
                k.op("pool", lambda e: e.memset(omc[:], 1.0), writes=[omc])
            else:
                k.op("dve", lambda e: e.tensor_tensor(out=omc[:], in0=lb4[:, :, 0, :], in1=lb4[:, :, 1, :], op=ALU.subtract),
                     reads=[lbt], writes=[omc])
                k.op("act", lambda e: e.activation(out=omc[:], in_=omc[:], func=AF.Exp), reads=[omc], writes=[omc])
                k.op("dve", lambda e: e.tensor_scalar(out=omc[:], in0=omc[:], scalar1=1.0, scalar2=None, op0=ALU.add),
                     reads=[omc], writes=[omc])
                k.op("dve", lambda e: e.reciprocal(out=lbc[:], in_=omc[:]), reads=[omc], writes=[lbc])
                k.op("dve", lambda e: e.tensor_scalar(out=omc[:], in0=lbc[:], scalar1=-1.0, scalar2=1.0, op0=ALU.mult,
                                                      op1=ALU.add), reads=[lbc], writes=[omc])
            for h in range(16):
                wq, wzf, wzb, wv, wg = (wb[0], 0), (wb[0], 128), (wb[2], 0), (wb[1], 0), (wb[1], 128)

                def load_head(h_):
                    for (wt, c0), si in ((wzf, 16 + h_), (wzb, 32 + h_), (wq, h_), (wv, 48 + h_), (wg, 64 + h_)):
                        load_w128((widx * 120 + si) * 128, wt, c0)
                if h == 0:
                    load_head(0)
                for di_ in range(2):
                    wz_ = wzf if di_ == 0 else wzb
                    Te = T1 if di_ == 0 else T1b

                    def cbz(p, s, n, Te=Te):
                        k.op("act", lambda e: e.activation(out=Te[:, s:s + n], in_=p[:, 0:n], func=AF.Exp, scale=-1.0),
                             reads=[p], writes=[Te])
                    proj_fm_cb(xnT, wz_[0], wz_[1], cbz)
                def cbq(p, s, n):
                    k.op("dve", lambda e: e.tensor_tensor(out=qA[:, s:s + n], in0=p[:, 0:n], in1=mAB[:, 0, 0:n], op=ALU.mult),
                         reads=[p, mAB], writes=[qA])
                    k.op("dve", lambda e: e.tensor_tensor(out=qB[:, s:s + n], in0=p[:, 0:n], in1=mAB[:, 1, 0:n], op=ALU.mult),
                         reads=[p, mAB], writes=[qB])
                proj_fm_cb(xnT, wq[0], wq[1], cbq)
                for ti, (s, n) in enumerate(TT):
                    p = nxt("ps", PSP[0])
                    proj_tm(xnT, wb[1], ti, p, 256, 0)
                    k.op("act", lambda e: e.copy(out=V[0:n, ti, :], in_=p[0:n, 0:128]), reads=[p], writes=[V])
                    silu_from_psum(p, n, 128, G[0:n, ti, :], G, ga, gb, 128)
                if h + 1 < 16:
                    load_head(h + 1)
                T1f = T1
                for di in range(2):
                    T1 = T1f if di == 0 else T1b
                    k.op("act", lambda e: e.activation(out=T2[:], in_=T1[:], func=AF.Ln, bias=epsc[:, 3:4]),
                         reads=[T1, epsc], writes=[T2])
                    if widx == 0:
                        k.op("dve", lambda e: e.tensor_scalar(out=T3[:], in0=T2[:], scalar1=-1.0, scalar2=None, op0=ALU.mult),
                             reads=[T2], writes=[T3])
                    else:
                        k.op("act", lambda e: e.activation(out=T3[:], in_=T1[:], func=AF.Ln, scale=lbc[:, di, h:h + 1],
                                                           bias=epsc[:, 3:4]), reads=[T1, lbc, epsc], writes=[T3])
                        k.op("dve", lambda e: e.tensor_tensor(out=T3[:], in0=T3[:], in1=T2[:], op=ALU.subtract),
                             reads=[T3, T2], writes=[T3])
                    k.op("act", lambda e: e.activation(out=T2[:], in_=T2[:], func=AF.Exp, scale=-1.0), reads=[T2], writes=[T2])
                    k.op("dve", lambda e: e.scalar_tensor_tensor(out=T1[:], in0=T1[:], scalar=omc[:, di, h:h + 1], in1=T2[:],
                                                                 op0=ALU.mult, op1=ALU.mult), reads=[T1, omc, T2], writes=[T1])
                    EC = ECs[di]
                    xv = lambda t_: t_[:, 0:2048].rearrange("p (c t) -> p c t", t=64)
                    if di == 0:
                        k.op("dve", lambda e: e.tensor_tensor_scan(out=T2[:], data0=smk[:], data1=T3[:], initial=0.0,
                                                                   op0=ALU.mult, op1=ALU.add), reads=[smk, T3], writes=[T2])
                        k.op("act", lambda e: e.activation(out=T3[:], in_=T2[:], func=AF.Exp), reads=[T2], writes=[T3])
                        k.op("pool", lambda e: e.tensor_copy(out=EC[:, 0:32], in_=xv(T3)[:, :, 63]), reads=[T3], writes=[EC])
                        k.op("pool", lambda e: e.tensor_copy(out=EC[:, 32:33], in_=T3[:, 2063:2064]), reads=[T3], writes=[EC])
                        k.op("act", lambda e: e.activation(out=T2[:], in_=T2[:], func=AF.Exp, scale=-1.0),
                             reads=[T2], writes=[T2])
                        k.op("dve", lambda e: e.tensor_tensor(out=QEA[:], in0=qA[:], in1=T3[:], op=ALU.mult),
                             reads=[qA, T3], writes=[QEA])
                        k.op("dve", lambda e: e.tensor_tensor(out=QEB[:], in0=qB[:], in1=T3[:], op=ALU.mult),
                             reads=[qB, T3], writes=[QEB])
                        k.op("dve", lambda e: e.tensor_tensor(out=KE[:], in0=T1[:], in1=T2[:], op=ALU.mult),
                             reads=[T1, T2], writes=[KE])
                        k.op("dve", lambda e: e.tensor_tensor(
                            out=xv(KL), in0=xv(KE), in1=xv(T3)[:, :, 63:64].to_broadcast([128, 32, 64]), op=ALU.mult),
                            reads=[KE, T3], writes=[KL])
                        k.op("dve", lambda e: e.tensor_tensor(out=KL[:, 2048:2064], in0=KE[:, 2048:2064],
                                                              in1=T3[:, 2063:2064].to_broadcast([128, 16]), op=ALU.mult),
                             reads=[KE, T3], writes=[KL])
                        KLs = KL
                    else:
                        k.op("pool", lambda e: e.memset(T2[:, 0:1], 0.0), writes=[T2])
                        k.op("dve", lambda e: e.tensor_tensor_scan(out=T2[:, 1:L], data0=T3[:, 0:L - 1], data1=smk[:, 1:L],
                                                                   initial=0.0, op0=ALU.add, op1=ALU.mult),
                             reads=[smk, T3], writes=[T2])
                        k.op("dve", lambda e: e.tensor_tensor(out=EC[:, 0:32], in0=xv(T2)[:, :, 63], in1=xv(T3)[:, :, 63],
                                                              op=ALU.add), reads=[T2, T3], writes=[EC])
                        k.op("dve", lambda e: e.tensor_tensor(out=EC[:, 32:33], in0=T2[:, 2063:2064], in1=T3[:, 2063:2064],
                                                              op=ALU.add), reads=[T2, T3], writes=[EC])
                        k.op("act", lambda e: e.activation(out=EC[:, 0:33], in_=EC[:, 0:33], func=AF.Exp), reads=[EC], writes=[EC])
                        k.op("act", lambda e: e.activation(out=T3[:], in_=T2[:], func=AF.Exp, scale=-1.0),
                             reads=[T2], writes=[T3])
                        k.op("act", lambda e: e.activation(out=T2[:], in_=T2[:], func=AF.Exp), reads=[T2], writes=[T2])
                        k.op("dve", lambda e: e.tensor_tensor(out=QEA[:], in0=qA[:], in1=T3[:], op=ALU.mult),
                             reads=[qA, T3], writes=[QEA])
                        k.op("dve", lambda e: e.tensor_tensor(out=QEB[:], in0=qB[:], in1=T3[:], op=ALU.mult),
                             reads=[qB, T3], writes=[QEB])
                        k.op("dve", lambda e: e.tensor_tensor(out=KE[:], in0=T1[:], in1=T2[:], op=ALU.mult),
                             reads=[T1, T2], writes=[KE])
                        KLs = KE
                    if di == 0:
                        seq = [(16, 0)] + [(t_, ab_) for t_ in range(16) for ab_ in (0, 1)]
                    else:
                        seq = [(t_, ab_) for t_ in range(15, -1, -1) for ab_ in (1, 0)] + [(16, 0)]
                    cids = [32 if t_ == 16 else 2 * t_ + ab_ for (t_, ab_) in seq]
                    order = [16] + list(range(16)) if di == 0 else list(range(15, -1, -1)) + [16]
                    seqidx = {}
                    k.op("pool", lambda e: e.memset(S[0][:], 0.0), writes=[S[0]])
                    if di == 0:
                        groups = [[16]] + [[2 * g_, 2 * g_ + 1] for g_ in range(8)]
                    else:
                        groups = [[15 - 2 * g_, 14 - 2 * g_] for g_ in range(8)] + [[16]]
                    step = 0
                    tcount = 0
                    for gt in groups:
                        bank = ubanks[st["uq"] % 3]
                        st["uq"] += 1
                        slots = []
                        for ti in gt:
                            s, n = TT[ti]
                            par = tcount % 2
                            tcount += 1
                            pt = nxt("pst", pstq)
                            k.mms([lambda e: e.transpose(out=pt[0:n, 0:128], in_=KLs[:, s:s + n], identity=ident[:, :])],
                                  reads=[KLs, ident], writes=[pt])
                            nA = min(n, 64)
                            k.op("act", lambda e: e.copy(out=KLt[par][0][0:nA, :], in_=pt[0:nA, 0:128]), reads=[pt],
                                 writes=[KLt[par][0]])
                            if n == 128:
                                k.op("act", lambda e: e.copy(out=KLt[par][1][64:128, :], in_=pt[64:128, 0:128]), reads=[pt],
                                     writes=[KLt[par][1]])
                            chunks = [0] if n == 16 else ([0, 1] if di == 0 else [1, 0])
                            for ab in chunks:
                                cid = 32 if ti == 16 else 2 * ti + ab
                                q_ = len(slots)
                                seqidx[(ti, ab)] = step + q_
                                k.mms([lambda e: e.matmul(bank[:, q_ * 128:(q_ + 1) * 128], lhsT=KLt[par][ab][0:n, :],
                                                          rhs=V[0:n, ti, :], start=True, stop=True)],
                                      reads=[KLt[par][ab], V], writes=[bank])
                                slots.append((q_, cid))
                        for (q_, cid) in slots:
                            if step < 32:
                                so, sn = S[step % 4], S[(step + 1) % 4]
                                k.op("dve", lambda e: e.scalar_tensor_tensor(out=sn[:], in0=so[:], scalar=EC[:, cid:cid + 1],
                                                                             in1=bank[:, q_ * 128:(q_ + 1) * 128],
                                                                             op0=ALU.mult, op1=ALU.add),
                                     reads=[so, EC, bank], writes=[sn])
                                if di == 0:
                                    k.op("act", lambda e: e.copy(out=SBALL[:, step + 1, :], in_=sn[:]), reads=[sn], writes=[SBALL])
                                else:
                                    cn = cids[step + 1]
                                    k.op("act", lambda e: e.mul(out=SBALL[:, step + 1, :], in_=sn[:], mul=EC[:, cn:cn + 1]),
                                         reads=[sn, EC], writes=[SBALL])
                            step += 1
                    def at_stage(oi, ti):
                        s, n = TT[ti]
                        pa = nxt("ps", PSP[0])
                        k.mms([lambda e: e.matmul(pa[0:n, 0:n], lhsT=KE[:, s:s + n], rhs=QEA[:, s:s + n], start=True, stop=False),
                               lambda e: e.matmul(pa[0:n, 0:n], lhsT=KE[:, s:s + n], rhs=QEB[:, s:s + n], start=False, stop=True)],
                              reads=[KE, QEA, QEB], writes=[pa])
                        at = ATm[oi % 3]
                        k.op("dve", lambda e: e.tensor_tensor(out=at[0:n, 0:n], in0=pa[0:n, 0:n], in1=mtri[0:n, di, 0:n],
                                                              op=ALU.mult), reads=[pa, mtri], writes=[at])
                    at_stage(0, order[0])
                    for oi, ti in enumerate(order):
                        s, n = TT[ti]
                        if oi + 1 < len(order):
                            at_stage(oi + 1, order[oi + 1])
                        at = ATm[oi % 3]
                        chunks = [0] if n == 16 else ([0, 1] if di == 0 else [1, 0])
                        po = nxt("ps", PSP[0])
                        fns = [lambda e: e.matmul(po[0:n, 0:128], lhsT=at[0:n, 0:n], rhs=V[0:n, ti, :], start=True, stop=False)]
                        for ci, ab in enumerate(chunks):
                            qe = QEA if ab == 0 else QEB
                            sbi = seqidx[(ti, ab)]
                            fns.append(lambda e, qe=qe, sbi=sbi, last=(ci == len(chunks) - 1): e.matmul(
                                po[0:n, 0:128], lhsT=qe[:, s:s + n], rhs=SBALL[:, sbi, :], start=False, stop=last))
                        k.mms(fns, reads=[at, V, QEA, QEB, SBALL], writes=[po])
                        if di == 0:
                            k.op("act", lambda e: e.copy(out=OF[0:n, ti, :], in_=po[0:n, 0:128]), reads=[po], writes=[OF])
                        else:
                            s_ = sm[oi % 2]
                            x1, x2, yb = a1[oi % 2], a2[oi % 2], ybf[oi % 2]
                            k.op("dve", lambda e: e.tensor_tensor(out=x1[0:n, :], in0=po[0:n, 0:128], in1=OF[0:n, ti, :],
                                                                  op=ALU.add), reads=[po, OF], writes=[x1])
                            k.op("act", lambda e: e.activation(out=jk[0:n, :], in_=x1[0:n, :], func=AF.Square,
                                                               accum_out=s_[0:n, 0:1]), reads=[x1], writes=[jk, s_])
                            k.op("act", lambda e: e.activation(out=s_[0:n, 1:2], in_=s_[0:n, 0:1], func=AF.Ln,
                                                               scale=1.0 / 128, bias=epsc[0:n, 0:1]),
                                 reads=[s_, epsc], writes=[s_])
                            k.op("act", lambda e: e.activation(out=s_[0:n, 2:3], in_=s_[0:n, 1:2], func=AF.Exp, scale=-0.5),
                                 reads=[s_], writes=[s_])
                            k.op("dve", lambda e: e.scalar_tensor_tensor(out=x2[0:n, :], in0=x1[0:n, :], scalar=s_[0:n, 2:3],
                                                                         in1=gna[0:n, :], op0=ALU.mult, op1=ALU.mult),
                                 reads=[x1, s_, gna], writes=[x2])
                            k.op("dve", lambda e: e.tensor_tensor(out=yb[0:n, :], in0=x2[0:n, :], in1=G[0:n, ti, :],
                                                                   op=ALU.mult), reads=[x2, G], writes=[yb])
                            pt2 = nxt("pst", pstq)
                            k.mms([lambda e: e.transpose(out=pt2[:, 0:n], in_=yb[0:n, :], identity=ident[0:n, 0:n])],
                                  reads=[yb, ident], writes=[pt2])
                            k.op("act", lambda e: e.copy(out=yT[:, s:s + n], in_=pt2[:, 0:n]), reads=[pt2], writes=[yT])
                T1 = T1f
                k.dma("pool", yTd[h, :, :], yT[:], reads=[yT], writes=[yTd_b[h]])
            PSP[0] = ps

        def even_mixer_B(widx, xnT, ls):
            sl = slopes16()
            load_consts(ls)
            dabs = CT["dabs"]
            wb = [k.sb([128, 16, 256], BF16, ls, "wb") for _ in range(3)]
            QT = k.sb([128, 1, L], BF16, ls, "QT")
            KT = k.sb([128, 1, L], BF16, ls, "KT")
            Vx = k.sb([128, 17, 129], BF16, ls, "Vx")
            G = k.sb([128, 17, 128], BF16, ls, "G")
            yT = k.sb([128, L], BF16, ls, "yT")
            wabs = k.sb([128, 1152], F32, ls, "wabs")
            mclip = k.sb([128, 1024], F32, ls, "mclip")
            sink = k.sb([128, 16], F32, ls, "sink")
            esink = k.sb([128, 16], F32, ls, "esink")
            tmpf = [k.sb([128, 512], F32, ls, "tmpf") for _ in range(4)]
            PT = [k.sb([128, 512], BF16, ls, "PT") for _ in range(4)]
            O = [k.sb([128, 129], F32, ls, "O") for _ in range(4)]
            ga = k.sb([128, 128], F32, ls, "ga")
            gb = k.sb([128, 128], F32, ls, "gb")
            sm = [k.sb([128, 4], F32, ls, "sm") for _ in range(2)]
            a1 = [k.sb([128, 128], F32, ls, "a1") for _ in range(2)]
            ybf = [k.sb([128, 128], BF16, ls, "ybf") for _ in range(2)]
            k.dma("sp", wabs[:], wabs_d[:, :], writes=[wabs])
            k.dma("sp", mclip[:], mclip_d[:, :], writes=[mclip])
            k.dma("sp", sink[:], sink_d[widx:widx + 1, :].partition_broadcast(128), writes=[sink])
            k.op("act", lambda e: e.activation(out=esink[:], in_=sink[:], func=AF.Exp), reads=[sink], writes=[esink])
            k.op("pool", lambda e: e.memset(Vx[:, :, 128:129], 1.0), writes=[Vx])
            pp = 0
            def load_kv(kv_):
                load_w128((widx * 120 + 96 + kv_) * 128, wb[0], 0)
                load_w128((widx * 120 + 100 + kv_) * 128, wb[0], 128)

            def load_qg(hq_):
                load_w128((widx * 120 + 80 + hq_) * 128, wb[1 + hq_ % 2], 0)
                load_w128((widx * 120 + 104 + hq_) * 128, wb[1 + hq_ % 2], 128)
            load_kv(0)
            load_qg(0)
            for hq in range(16):
                kv = hq // 4
                if hq % 4 == 0:
                    wk, wv = (wb[0], 0), (wb[0], 128)
                    proj_fm(xnT, wk[0], wk[1], KT, 0, None)
                    for ti, (s, n) in enumerate(TT):
                        p = nxt("ps", PSP[0])
                        proj_tm(xnT, wv[0], ti, p, 128, wv[1])
                        k.op("act", lambda e: e.copy(out=Vx[0:n, ti, 0:128], in_=p[0:n, 0:128]), reads=[p], writes=[Vx])
                wq, wg = (wb[1 + hq % 2], 0), (wb[1 + hq % 2], 128)
                if hq + 1 < 16:
                    load_qg(hq + 1)
                    if (hq + 1) % 4 == 0:
                        load_kv((hq + 1) // 4)
                proj_fm(xnT, wq[0], wq[1], QT, 0, 128 ** -0.5)
                for ti, (s, n) in enumerate(TT):
                    p = nxt("ps", PSP[0])
                    proj_tm(xnT, wg[0], ti, p, 128, wg[1])
                    silu_from_psum(p, n, 128, G[0:n, ti, :], G, ga, gb)
                for (t0, N) in QB:
                    t0v = vidx(t0)
                    nqs = (N + 127) // 128
                    acc = ps[2:6]
                    if t0 < 2048:
                        xt = [s0 for s0 in range(t0 - 128, t0 + N + 1, 128) if 0 <= s0 <= 1920]
                    else:
                        xt = [0]
                    ktl = [(2048, 16)] + [(s0, 128) for s0 in xt]
                    pend = []
                    stb = [ps[0], ps[1], ps[6]] if ST3 else [ps[0], ps[1]]
                    for ki, (s0, kn) in enumerate(ktl):
                        s0v = vidx(s0)
                        kt = s0 // 128
                        stp = stb[ki % len(stb)]
                        k.mms([lambda e: e.matmul(stp[0:kn, 0:N], lhsT=KT[:, 0, s0:s0 + kn], rhs=QT[:, 0, t0:t0 + N],
                                                  start=True, stop=True)], reads=[KT, QT], writes=[stp])
                        if s0 == 2048 and t0 < 2048:
                            c0 = min(t0, 512)
                            dt_ap, dbuf = mclip[0:16, c0:c0 + N], mclip
                        elif s0 == 2048:
                            dt_ap, dbuf = dabs[0:16, 384:384 + N], dabs
                        else:
                            off = t0v - s0v
                            dt_ap, dbuf = wabs[0:kn, 512 + off:512 + off + N], wabs
                        tf = tmpf[pp % 4]
                        ptile = PT[pp % 4]
                        pp += 1
                        k.op("dve", lambda e: e.scalar_tensor_tensor(out=tf[0:kn, 0:N], in0=dt_ap, scalar=-sl[hq],
                                                                     in1=stp[0:kn, 0:N], op0=ALU.mult, op1=ALU.add),
                             reads=[dbuf, stp], writes=[tf])
                        k.op("act", lambda e: e.activation(out=ptile[0:kn, 0:N], in_=tf[0:kn, 0:N], func=AF.Exp),
                             reads=[tf], writes=[ptile])
                        if len(pend) >= LOOK:
                            pend.pop(0)()
                        def pv(s0=s0, kn=kn, kt=kt, ptile=ptile):
                            for qs in range(nqs):
                                qn = min(128, N - qs * 128)
                                tok0 = t0 + qs * 128
                                if t0 < 2048:
                                    rel = [s_ for s_ in (tok0 - 128, tok0, tok0 + 128) if 0 <= s_ <= 1920]
                                else:
                                    rel = [0]
                                if s0 != 2048 and s0 not in rel:
                                    continue
                                a = acc[qs]
                                k.mms([lambda e: e.matmul(a[0:qn, 0:129], lhsT=ptile[0:kn, qs * 128:qs * 128 + qn],
                                                          rhs=Vx[0:kn, kt, :], start=(s0 == 2048), stop=(s0 == rel[-1]))],
                                      reads=[ptile, Vx], writes=[a])
                        pend.append(pv)
                    while pend:
                        pend.pop(0)()
                    for qs in range(nqs):
                        qn = min(128, N - qs * 128)
                        tok0 = t0 + qs * 128
                        ti = tok0 // 128
                        o = O[qs]
                        s_ = sm[qs % 2]
                        x1, yb = a1[qs % 2], ybf[qs % 2]
                        k.op("act", lambda e: e.copy(out=o[0:qn, :], in_=acc[qs][0:qn, 0:129]), reads=[acc[qs]], writes=[o])
                        k.op("dve", lambda e: e.tensor_tensor(out=s_[0:qn, 0:1], in0=o[0:qn, 128:129],
                                                              in1=esink[0:qn, hq:hq + 1], op=ALU.add),
                             reads=[o, esink], writes=[s_])
                        k.op("dve", lambda e: e.reciprocal(out=s_[0:qn, 1:2], in_=s_[0:qn, 0:1]), reads=[s_], writes=[s_])
                        k.op("dve", lambda e: e.scalar_tensor_tensor(out=yb[0:qn, :], in0=o[0:qn, 0:128], scalar=s_[0:qn, 1:2],
                                                                     in1=G[0:qn, ti, :], op0=ALU.mult, op1=ALU.mult),
                             reads=[o, s_, G], writes=[yb])
                        pt = nxt("pst", pstq)
                        k.mms([lambda e: e.transpose(out=pt[:, 0:qn], in_=yb[0:qn, :], identity=ident[0:qn, 0:qn])],
                              reads=[yb, ident], writes=[pt])
                        k.op("act", lambda e: e.copy(out=yT[:, tok0:tok0 + qn], in_=pt[:, 0:qn]), reads=[pt], writes=[yT])
                k.dma("pool", yTd[16 + hq, :, :], yT[:], reads=[yT], writes=[yTd_b[hq]])

        def phase_out(first, wout_d, widx, ls):
            wob = [k.sb([128, 32, 512], BF16, ls, "wob") for _ in range(2)]
            yb = [k.sb([128, 32, 512], BF16, ls, "ytb") for _ in range(2)]
            hb = [k.sb([128, 512], F32, ls, "hb") for _ in range(4)]
            ho = [k.sb([128, 512], F32, ls, "ho") for _ in range(4)]
            cnt = 0
            for nb in range(4):
                wo = wob[nb % 2]
                for pc in range(8):
                    stg = nxt("wst", wst)
                    r0 = ((widx * 4 + nb) * 8 + pc) * 128
                    k.dma("sp", stg[:], wout_d[r0:r0 + 128, :], writes=[stg])
                    k.op("pool", lambda e, stg=stg, pc=pc: e.tensor_copy(
                        out=wo[:, 4 * pc:4 * pc + 4, :], in_=stg[:].rearrange("p (c n) -> p c n", c=4)),
                        reads=[stg], writes=[wo])
                for bi, (t0, N) in enumerate(QB):
                    y = yb[bi % 2]
                    k.dma("sp", y[:, :, 0:N], yTd[:, :, t0:t0 + N].rearrange("c p t -> p c t"),
                          reads=yTd_b, writes=[y])
                    for qs in range((N + 127) // 128):
                        qn = min(128, N - qs * 128)
                        tok0 = t0 + qs * 128
                        ti = tok0 // 128
                        p = nxt("ps", PSP[0])
                        k.mms([lambda e, c=c: e.matmul(p[0:qn, :], lhsT=y[:, c, qs * 128:qs * 128 + qn], rhs=wo[:, c, :],
                                                       start=(c == 0), stop=(c == 31)) for c in range(32)],
                              reads=[y, wo], writes=[p])
                        hi = hb[cnt % 4]
                        hn = ho[cnt % 4]
                        cnt += 1
                        src = h_src(first, ti)
                        k.dma("sp", hi[0:qn, :], src[:, nb * 512:(nb + 1) * 512], reads=[hd_b[ti]], writes=[hi])
                        k.op("dve", lambda e: e.tensor_tensor(out=hn[0:qn, :], in0=p[0:qn, :], in1=hi[0:qn, :], op=ALU.add),
                             reads=[p, hi], writes=[hn])
                        k.dma("pool", hd[tok0:tok0 + qn, nb * 512:(nb + 1) * 512], hn[0:qn, :], reads=[hn],
                              writes=[hd_b[ti]])

        def phase_final(first, ls):
            gt = k.sb([128, D], F32, ls, "gt")
            hb = [k.sb([128, D], F32, ls, "hb") for _ in range(2)]
            ob = [k.sb([128, D], F32, ls, "ob") for _ in range(2)]
            junk = k.sb([128, D], BF16, ls, "junk")
            ssb = [k.sb([128, 2], F32, ls, "ss") for _ in range(2)]
            k.dma("sp", gt[:], nrm_d[4:5, :].partition_broadcast(128), writes=[gt])
            toks = []
            for ti, (s, n) in enumerate(TT[:16]):
                h = hb[ti % 2]
                o = ob[ti % 2]
                ss = ssb[ti % 2]
                k.dma("sp", h[0:n, :], h_src(first, ti), reads=[hd_b[ti]], writes=[h])
                k.op("act", lambda e: e.activation(out=junk[0:n, :], in_=h[0:n, :], func=AF.Square,
                                                   accum_out=ss[0:n, 0:1]), reads=[h], writes=[junk, ss])
                k.op("act", lambda e: e.activation(out=ss[0:n, 1:2], in_=ss[0:n, 0:1], func=AF.Ln, scale=1.0 / D,
                                                   bias=epsc[0:n, 0:1]), reads=[ss, epsc], writes=[ss])
                k.op("act", lambda e: e.activation(out=ss[0:n, 0:1], in_=ss[0:n, 1:2], func=AF.Exp, scale=-0.5),
                     reads=[ss], writes=[ss])
                k.op("dve", lambda e: e.scalar_tensor_tensor(out=o[0:n, :], in0=h[0:n, :], scalar=ss[0:n, 0:1],
                                                             in1=gt[0:n, :], op0=ALU.mult, op1=ALU.mult),
                     reads=[h, ss, gt], writes=[o])
                toks.append(k.dma("pool", out_d[s:s + n, :], o[0:n, :], reads=[o]))
            return toks

        first = True
        for (kind, widx, layer_idx) in layers:
            with ExitStack() as ls:
                xnT = k.sb([128, 16, L], BF16, ls, "xnT")
                with ExitStack() as ls2:
                    phase_norm(first, (0 if kind == "E" else 2) + widx, xnT, ls2)
                    k.barrier()
                with ExitStack() as ls2:
                    if kind == "O":
                        odd_mixer(widx, layer_idx, xnT, ls2)
                    else:
                        even_mixer_A(widx, xnT, ls2)
                    k.barrier()
                if kind == "E":
                    with ExitStack() as ls2:
                        even_mixer_B(widx, xnT, ls2)
                        k.barrier()
            with ExitStack() as ls:
                phase_out(first, woutc_d if kind == "O" else wouta_d, widx, ls)
                k.barrier()
            first = False
        out_toks = []
        with ExitStack() as ls:
            if do_final:
                out_toks = phase_final(first, ls)
            else:
                hb = [k.sb([128, D], F32, ls, "hb") for _ in range(2)]
                for ti, (s, n) in enumerate(TT):
                    h = hb[ti % 2]
                    k.dma("sp", h[0:n, :], h_src(first, ti), reads=[hd_b[ti]], writes=[h])
                    out_toks.append(k.dma("pool", out_d[s:s + n, :], h[0:n, :], reads=[h]))
            k.barrier()
        k.check_deadlock()
    return nc


def const_tables():
    i = np.arange(128, dtype=np.float32)[:, None]
    ident = np.eye(128, dtype=np.float32)
    dlin = (np.arange(512, dtype=np.float32)[None, :] - i).astype(np.float32)
    dabs = np.abs(np.arange(896, dtype=np.float32)[None, :] - i - 384).astype(np.float32)
    sl = np.array(slopes16(), dtype=np.float64)
    cb = (-(sl[:, None] * np.array(DELTAS, dtype=np.float64)[None, :])).reshape(1, -1)
    cb = np.repeat(cb, 128, axis=0).astype(np.float32)
    return {"ident": ident, "dlin": dlin, "dabs": dabs, "cbtab": np.ascontiguousarray(cb)}


def layout_winc(w):
    a = w.reshape(2, 2, 8, 128, 4, 16, 256)
    a = a.transpose(0, 5, 4, 1, 3, 2, 6)
    return np.ascontiguousarray(a).reshape(2 * 16 * 4 * 2 * 128, 2048)


def layout_wina(w):
    a = w.reshape(2, 16, 128, 120, 128)
    a = a.transpose(0, 3, 2, 1, 4)
    return np.ascontiguousarray(a).reshape(2 * 120 * 128, 2048)


def even_tables():
    i = np.arange(128, dtype=np.float32)[:, None]
    smask = np.ones((128, L), np.float32)
    smask[:, 0:2048:64] = 0.0
    smask[:, 2048] = 0.0
    j = np.arange(512)
    mA = ((j % 128) < 64).astype(np.float32)
    mab = np.concatenate([np.tile(mA[None], (128, 1)), np.tile((1 - mA)[None], (128, 1))], axis=1)
    s_ = np.arange(128)[:, None]; t_ = np.arange(128)[None, :]
    same = (s_ // 64) == (t_ // 64)
    mf = (same & (s_ <= t_)).astype(np.float32)
    mb_ = (same & (s_ >= t_)).astype(np.float32)
    mtri = np.concatenate([mf, mb_], axis=1)
    dw = np.abs(np.arange(1152, dtype=np.float32)[None, :] - i - 512)
    wabs = np.where(dw <= 128, dw, 1e9).astype(np.float32)
    mclip = np.minimum(np.arange(1024, dtype=np.float32)[None, :] - i + 16, 128.0).astype(np.float32)
    return {"smask": smask, "mab": np.ascontiguousarray(mab), "mtri": np.ascontiguousarray(mtri),
            "wabs": wabs, "mclip": mclip}


def layout_wout(w):
    a = w.reshape(2, 8, 4, 128, 4, 512)
    a = a.transpose(0, 4, 1, 3, 2, 5)
    return np.ascontiguousarray(a).reshape(2 * 4 * 8 * 128, 2048)


def run_layers(layers, do_final, inputs, ncores=8):
    out_rows = NX if do_final else L
    nc = build(layers, do_final, out_rows)
    f = lambda a: np.ascontiguousarray(np.asarray(a, dtype=np.float32))
    shared = dict(const_tables())
    shared["meta"] = f(inputs["meta_tokens"])
    shared["norms"] = np.concatenate([f(inputs["norm_a"]), f(inputs["norm_c"]), f(inputs["final_norm"])[None]], axis=0)
    if any(l[0] == "E" for l in layers):
        shared.update(even_tables())
        shared["wina"] = layout_wina(f(inputs["w_in_a"]))
        shared["wouta"] = layout_wout(f(inputs["w_out_a"]))
        shared["lbl"] = np.ascontiguousarray(f(inputs["hgrn_lb"]).reshape(2, 2, 16, 128).transpose(3, 0, 1, 2)).reshape(128, 64)
        shared["hnorm"] = f(inputs["hgrn_norm"])
        shared["sink"] = f(inputs["sink_logits"])
    if any(l[0] == "O" for l in layers):
        shared["winc"] = layout_winc(f(inputs["w_in_c"]))
        shared["woutc"] = layout_wout(f(inputs["w_out_c"]))
        shared["dlam"] = f(inputs["diff_lambda"]).reshape(2, 512)
        shared["dnorm"] = f(inputs["diff_norm"])
    x = f(inputs["x"])
    in_maps = []
    for b in range(ncores):
        m = dict(shared)
        m["x"] = x[b]
        in_maps.append(m)
    res = run_bass_kernel_spmd(nc, in_maps, core_ids=list(range(ncores)))
    return np.stack([r["out"] for r in res.results], axis=0)


def kernel(**inputs):
    layers = [("E", 0, 0), ("O", 0, 1), ("E", 1, 2), ("O", 1, 3)]
    return run_layers(layers, True, inputs)
```

```python
import math
from contextlib import ExitStack
import numpy as np
import concourse.bass as bass
import concourse.mybir as mybir
from concourse.bass_utils import run_bass_kernel_spmd

F32 = mybir.dt.float32
BF16 = mybir.dt.bfloat16
AF = mybir.ActivationFunctionType
ALU = mybir.AluOpType
AX = mybir.AxisListType

L = 2064
NX = 2048
NMETA = 16
D = 2048
EPS = 1e-6
TT = [(i * 128, 128) for i in range(16)] + [(2048, 16)]
QB = [(i * 512, 512) for i in range(4)] + [(2048, 16)]
DELTAS = [128 * m for m in range(1, 16)] + [16 + 128 * m for m in range(16)]
NDEL = len(DELTAS)
SAME_ENGINE_SYNC = True
import os
LOOK = int(os.environ.get('KLOOK', '1'))
ST3 = int(os.environ.get('KST3', '1'))


def vidx(s):
    return s if s < 2048 else s - 2048 - 16


def slopes16():
    return [2.0 ** (-8.0 * (i + 1) / 16) for i in range(16)]


class Eng:
    def __init__(self, nc, es, name, e, ndma):
        self.name = name
        self.e = e
        self.sem = es.enter_context(nc.semaphore("s_" + name))
        self.cnt = 0
        self.seen = {}
        self.ring = [[es.enter_context(nc.semaphore("d_%s%d" % (name, i))), 0] for i in range(ndma)]
        self.ri = 0


class Buf:
    def __init__(self, t):
        self.t = t
        self.w = None
        self.rs = {}

    def __getitem__(self, k):
        return self.t[k]


class K:
    def __init__(self, nc, es):
        self.nc = nc
        self.es = es
        self.E = {
            "pe": Eng(nc, es, "pe", nc.tensor, 0),
            "dve": Eng(nc, es, "dve", nc.vector, 0),
            "act": Eng(nc, es, "act", nc.scalar, 0),
            "pool": Eng(nc, es, "pool", nc.gpsimd, 16),
            "sp": Eng(nc, es, "sp", nc.sync, 24),
        }
        self.nbuf = 0
        self.log = []

    def sb(self, shape, dt, es=None, name=None):
        self.nbuf += 1
        t = (es or self.es).enter_context(self.nc.sbuf_tensor("%s_%d" % (name or "sb", self.nbuf), list(shape), dt))
        return Buf(t)

    def wait(self, eng, tok):
        if tok is None:
            return
        sid, sem, val = tok
        if eng.seen.get(sid, 0) >= val:
            return
        if sid == id(eng.sem) and (eng.name == "pe" or not SAME_ENGINE_SYNC):
            return
        eng.e.wait_ge(sem, val)
        eng.seen[sid] = val
        self.log.append((eng.name, "w", sid, val))

    def _deps(self, eng, reads, writes):
        for b in reads:
            self.wait(eng, b.w)
        for b in writes:
            self.wait(eng, b.w)
            for tok in list(b.rs.values()):
                self.wait(eng, tok)

    def _mark(self, tok, reads, writes):
        for b in reads:
            old = b.rs.get(tok[0])
            if old is None or old[2] < tok[2]:
                b.rs[tok[0]] = tok
        for b in writes:
            b.w = tok
            b.rs = {}

    def op(self, en, fn, reads=(), writes=()):
        eng = self.E[en]
        self._deps(eng, reads, writes)
        ins = fn(eng.e)
        eng.cnt += 1
        ins.then_inc(eng.sem, 1)
        self.log.append((eng.name, "i", id(eng.sem), 1))
        tok = (id(eng.sem), eng.sem, eng.cnt)
        self._mark(tok, reads, writes)
        return tok

    def mms(self, fns, reads=(), writes=()):
        eng = self.E["pe"]
        self._deps(eng, reads, writes)
        ins = None
        for fn in fns:
            ins = fn(eng.e)
        eng.cnt += 1
        ins.then_inc(eng.sem, 1)
        self.log.append((eng.name, "i", id(eng.sem), 1))
        tok = (id(eng.sem), eng.sem, eng.cnt)
        self._mark(tok, reads, writes)
        return tok

    def dma(self, qn, out, in_, reads=(), writes=()):
        eng = self.E[qn]
        self._deps(eng, reads, writes)
        slot = eng.ring[eng.ri % len(eng.ring)]
        eng.ri += 1
        if slot[1] > 0:
            self.wait(eng, (id(slot[0]), slot[0], slot[1]))
        ins = eng.e.dma_start(out=out, in_=in_)
        slot[1] += 16
        ins.then_inc(slot[0], 16)
        self.log.append((eng.name, "i", id(slot[0]), 16))
        tok = (id(slot[0]), slot[0], slot[1])
        self._mark(tok, reads, writes)
        return tok

    def check_deadlock(self):
        qs = {}
        for ev in self.log:
            qs.setdefault(ev[0], []).append(ev)
        ptr = {n: 0 for n in qs}
        sem = {}
        prog = True
        while prog:
            prog = False
            for n, q in qs.items():
                while ptr[n] < len(q):
                    _, kind, sid, val = q[ptr[n]]
                    if kind == "i":
                        sem[sid] = sem.get(sid, 0) + val
                    elif sem.get(sid, 0) < val:
                        break
                    ptr[n] += 1
                    prog = True
        stuck = {n: (ptr[n], len(q), q[ptr[n]]) for n, q in qs.items() if ptr[n] < len(q)}
        if stuck:
            names = {id(e.sem): e.name for e in self.E.values()}
            for e in self.E.values():
                for i, sl in enumerate(e.ring):
                    names[id(sl[0])] = "%s_dma%d" % (e.name, i)
            msg = "; ".join("%s at %d/%d waits %s>=%d (have %d)" % (n, p, t, names.get(ev[2]), ev[3], sem.get(ev[2], 0))
                            for n, (p, t, ev) in stuck.items())
            raise RuntimeError("DEADLOCK in emitted program: " + msg)

    def barrier(self):
        toks = []
        for e in self.E.values():
            if e.cnt > 0:
                toks.append((id(e.sem), e.sem, e.cnt))
            for s in e.ring:
                if s[1] > 0:
                    toks.append((id(s[0]), s[0], s[1]))
        for e in self.E.values():
            for t in toks:
                if t[0] == id(e.sem):
                    continue
                self.wait(e, t)


def build(layers, do_final, out_rows):
    nc = bass.Bass("TRN2", target_bir_lowering=False)
    dr = {}

    def din(name, shape):
        dr[name] = nc.dram_tensor(name, list(shape), F32, kind="ExternalInput").ap()
        return dr[name]

    x_d = din("x", [NX, D])
    meta_d = din("meta", [NMETA, D])
    nrm_d = din("norms", [5, D])
    ident_d = din("ident", [128, 128])
    dlin_d = din("dlin", [128, 512])
    dabs_d = din("dabs", [128, 896])
    cb_d = din("cbtab", [128, 16 * NDEL])
    n_odd = sum(1 for l in layers if l[0] == "O")
    n_even = sum(1 for l in layers if l[0] == "E")
    if n_odd:
        winc_d = din("winc", [2 * 16 * 4 * 2 * 128, 2048])
        woutc_d = din("woutc", [2 * 4 * 8 * 128, 2048])
        dlam_d = din("dlam", [2, 512])
        dnorm_d = din("dnorm", [2, 256])
    if n_even:
        wina_d = din("wina", [2 * 120 * 128, 2048])
        wouta_d = din("wouta", [2 * 4 * 8 * 128, 2048])
        smask_d = din("smask", [128, L])
        mab_d = din("mab", [128, 1024])
        mtri_d = din("mtri", [128, 256])
        lb_d = din("lbl", [128, 64])
        hnorm_d = din("hnorm", [2, 128])
        wabs_d = din("wabs", [128, 1152])
        mclip_d = din("mclip", [128, 1024])
        sink_d = din("sink", [2, 16])
    out_d = nc.dram_tensor("out", [out_rows, D], F32, kind="ExternalOutput").ap()
    hd = nc.dram_tensor("hd", [L, D], F32, kind="Internal").ap()
    yTd = nc.dram_tensor("yTd", [32, 128, L], BF16, kind="Internal").ap()

    with ExitStack() as es:
        k = K(nc, es)
        ident_f = k.sb([128, 128], F32)
        ident = k.sb([128, 128], BF16)
        wst = [k.sb([128, 2048], F32, name="wst") for _ in range(2)]
        ps = [Buf(es.enter_context(nc.psum_tensor("ps%d" % i, [128, 512], F32))) for i in range(7)]
        pst1 = es.enter_context(nc.psum_tensor("pst", [128, 1024], BF16))
        pstq = [Buf(pst1)]
        PSP = [ps]
        hd_b = [Buf(None) for _ in TT]
        yTd_b = [Buf(None) for _ in range(16)]
        st = {"wst": 0, "wb": 0, "ps": 0, "pst": 0, "uq": 0}

        def nxt(key, lst):
            i = st[key] % len(lst)
            st[key] += 1
            return lst[i]

        epsc = k.sb([128, 4], F32)
        k.op("pool", lambda e: e.memset(epsc[:, 0:1], EPS), writes=[epsc])
        k.op("pool", lambda e: e.memset(epsc[:, 3:4], 1.0), writes=[epsc])
        for wi in range(2):
            k.op("pool", lambda e, wi=wi: e.memset(epsc[:, 1 + wi:2 + wi],
                                                   math.log(1.0 - (0.8 - 0.6 * math.exp(-0.3 * (2 * wi + 1))))),
                 writes=[epsc])
        k.dma("sp", ident_f[:], ident_d[:, :], writes=[ident_f])
        k.op("dve", lambda e: e.tensor_copy(out=ident[:], in_=ident_f[:]), reads=[ident_f], writes=[ident])

        if n_odd:
            lams = k.sb([128, 4], F32)
            lame = k.sb([128, 4], F32)
            neglam = k.sb([128, 2], F32)
            lam_es = ExitStack()
            lamt = k.sb([128, 2, 512], F32, lam_es)
            lamp = k.sb([128, 2, 2, 128], F32, lam_es)
            for i in range(2):
                k.dma("sp", lamt[:, i, :], dlam_d[i:i + 1, :].partition_broadcast(128), writes=[lamt])
            for i in range(2):
                for j in range(2):
                    k.op("dve", lambda e, i=i, j=j: e.tensor_tensor(
                        out=lamp[:, i, j, :], in0=lamt[:, i, 256 * j:256 * j + 128],
                        in1=lamt[:, i, 256 * j + 128:256 * j + 256], op=ALU.mult), reads=[lamt], writes=[lamp])
            for i in range(2):
                for j in range(2):
                    k.op("dve", lambda e, i=i, j=j: e.reduce_sum(
                        out=lams[:, 2 * i + j:2 * i + j + 1], in_=lamp[:, i, j, :], axis=AX.X),
                        reads=[lamp], writes=[lams])
            k.op("act", lambda e: e.activation(out=lame[:], in_=lams[:], func=AF.Exp), reads=[lams], writes=[lame])
            k.barrier()
            lam_es.close()

        def lam_init_of(layer_idx):
            return 0.8 - 0.6 * math.exp(-0.3 * layer_idx)

        def h_src(first, ti):
            s, n = TT[ti]
            if first:
                return (x_d[s:s + n, :] if s < 2048 else meta_d[0:n, :])
            return hd[s:s + n, :]

        def load_w_slice(row0, dst, ls):
            for half in range(2):
                stg = nxt("wst", wst)
                k.dma("sp", stg[:], ls[0][row0 + half * 128:row0 + half * 128 + 128, :], writes=[stg])
                k.op("pool", lambda e, stg=stg, half=half: e.tensor_copy(
                    out=dst[:, 8 * half:8 * half + 8, :],
                    in_=stg[:].rearrange("p (c n) -> p c n", c=8)), reads=[stg], writes=[dst])

        def phase_norm(first, nrow, xnT, ls):
            gt = k.sb([128, D], F32, ls, "gt")
            hb = [k.sb([128, D], F32, ls, "hb") for _ in range(2)]
            xb = [k.sb([128, D], BF16, ls, "xb") for _ in range(2)]
            junk = k.sb([128, D], BF16, ls, "junk")
            ssb = [k.sb([128, 2], F32, ls, "ss") for _ in range(2)]
            k.dma("sp", gt[:], nrm_d[nrow:nrow + 1, :].partition_broadcast(128), writes=[gt])
            for ti, (s, n) in enumerate(TT):
                h = hb[ti % 2]
                xn = xb[ti % 2]
                ss = ssb[ti % 2]
                k.dma("sp", h[0:n, :], h_src(first, ti), reads=[hd_b[ti]], writes=[h])
                k.op("act", lambda e: e.activation(out=junk[0:n, :], in_=h[0:n, :], func=AF.Square,
                                                   accum_out=ss[0:n, 0:1]), reads=[h], writes=[junk, ss])
                k.op("act", lambda e: e.activation(out=ss[0:n, 1:2], in_=ss[0:n, 0:1], func=AF.Ln, scale=1.0 / D,
                                                   bias=epsc[0:n, 0:1]), reads=[ss, epsc], writes=[ss])
                k.op("act", lambda e: e.activation(out=ss[0:n, 0:1], in_=ss[0:n, 1:2], func=AF.Exp, scale=-0.5),
                     reads=[ss], writes=[ss])
                k.op("dve", lambda e: e.scalar_tensor_tensor(out=xn[0:n, :], in0=h[0:n, :], scalar=ss[0:n, 0:1],
                                                             in1=gt[0:n, :], op0=ALU.mult, op1=ALU.mult),
                     reads=[h, ss, gt], writes=[xn])
                for g in range(2):
                    pt = nxt("pst", pstq)
                    k.mms([lambda e, c=c, g=g, pt=pt: e.transpose(
                        out=pt[:, c * 128:c * 128 + n], in_=xn[0:n, (8 * g + c) * 128:(8 * g + c + 1) * 128],
                        identity=ident[0:n, 0:n]) for c in range(8)], reads=[xn, ident], writes=[pt])
                    k.op("act" if g == 0 else "dve", lambda e, g=g, pt=pt: (e.copy if g == 0 else e.tensor_copy)(
                        out=xnT[:, 8 * g:8 * g + 8, s:s + n],
                        in_=pt[:, :].rearrange("p (c t) -> p c t", c=8)[:, :, 0:n]), reads=[pt], writes=[xnT])

        def proj_fm(xnT, w, c0, dst, dj, scale):
            for bi, (s, n) in enumerate(QB):
                p = nxt("ps", PSP[0])
                k.mms([lambda e, c=c, p=p: e.matmul(p[:, 0:n], lhsT=w[:, c, c0:c0 + 128], rhs=xnT[:, c, s:s + n],
                                                    start=(c == 0), stop=(c == 15)) for c in range(16)],
                      reads=[xnT, w], writes=[p])
                if scale is None:
                    k.op("act", lambda e, p=p: e.copy(out=dst[:, dj, s:s + n], in_=p[:, 0:n]), reads=[p], writes=[dst])
                else:
                    k.op("act", lambda e, p=p: e.mul(out=dst[:, dj, s:s + n], in_=p[:, 0:n], mul=scale),
                         reads=[p], writes=[dst])

        def proj_tm(xnT, w, ti, p, ncols=256, c0=0):
            s, n = TT[ti]
            k.mms([lambda e, c=c: e.matmul(p[0:n, 0:ncols], lhsT=xnT[:, c, s:s + n], rhs=w[:, c, c0:c0 + ncols],
                                           start=(c == 0), stop=(c == 15)) for c in range(16)],
                  reads=[xnT, w], writes=[p])

        def silu_from_psum(p, n, ncols, dst_ap, dstbuf, tmpa, tmpb, pc0=0):
            k.op("act", lambda e: e.activation(out=tmpa[0:n, 0:ncols], in_=p[0:n, pc0:pc0 + ncols], func=AF.Exp, scale=-1.0),
                 reads=[p], writes=[tmpa])
            k.op("dve", lambda e: e.tensor_scalar(out=tmpa[0:n, 0:ncols], in0=tmpa[0:n, 0:ncols], scalar1=1.0,
                                                  scalar2=None, op0=ALU.add), reads=[tmpa], writes=[tmpa])
            k.op("dve", lambda e: e.reciprocal(out=tmpb[0:n, 0:ncols], in_=tmpa[0:n, 0:ncols]), reads=[tmpa], writes=[tmpb])
            k.op("dve", lambda e: e.tensor_tensor(out=dst_ap, in0=p[0:n, pc0:pc0 + ncols], in1=tmpb[0:n, 0:ncols],
                                                  op=ALU.mult), reads=[p, tmpb], writes=[dstbuf])

        CT = {}

        def load_consts(ls):
            CT["dlin"] = k.sb([128, 512], F32, ls, "dlin")
            CT["dabs"] = k.sb([128, 896], F32, ls, "dabs")
            CT["cbt"] = k.sb([128, 16 * NDEL], F32, ls, "cbt")
            k.dma("sp", CT["dlin"][:], dlin_d[:, :], writes=[CT["dlin"]])
            k.dma("sp", CT["dabs"][:], dabs_d[:, :], writes=[CT["dabs"]])
            k.dma("sp", CT["cbt"][:], cb_d[:, :], writes=[CT["cbt"]])

        def bias_tile(t0v, N, s0v, kn, slope, h):
            dlin, dabs, cbt = CT["dlin"], CT["dabs"], CT["cbt"]
            if t0v < s0v + kn and s0v < t0v + N:
                off = t0v - s0v
                return dabs[0:kn, 384 + off:384 + off + N], dabs, -slope, None
            if t0v > s0v:
                dl = t0v - s0v
                return dlin[0:kn, 0:N], dlin, -slope, cbt[0:kn, h * NDEL + DELTAS.index(dl):h * NDEL + DELTAS.index(dl) + 1]
            dl = s0v - t0v
            return dlin[0:kn, 0:N], dlin, slope, cbt[0:kn, h * NDEL + DELTAS.index(dl):h * NDEL + DELTAS.index(dl) + 1]

        def odd_mixer(widx, layer_idx, xnT, ls):
            lam_init = lam_init_of(layer_idx)
            sl = slopes16()
            load_consts(ls)
            cbt = CT["cbt"]
            wb = [k.sb([128, 16, 256], BF16, ls, "wb") for _ in range(4)]
            QT = k.sb([128, 2, L], BF16, ls, "QT")
            KT = k.sb([128, 2, L], BF16, ls, "KT")
            Vx = k.sb([128, 17, 257], BF16, ls, "Vx")
            G = k.sb([128, 17, 256], BF16, ls, "G")
            yT = k.sb([128, 2, L], BF16, ls, "yT")
            tmpf = [k.sb([128, 512], F32, ls, "tmpf") for _ in range(4)]
            PT = [k.sb([128, 512], BF16, ls, "PT") for _ in range(4)]
            OT = [k.sb([128, 4, 257], F32, ls, "OT") for _ in range(2)]
            Oq = [[Buf(OT[j_].t[:, q_, :]) for q_ in range(4)] for j_ in range(2)]
            SM = k.sb([128, 4, 8], F32, ls, "SM")
            SS = [k.sb([128, 2], F32, ls, "SS") for _ in range(4)]
            A1 = k.sb([128, 4, 256], F32, ls, "A1")
            A2 = k.sb([128, 4, 256], F32, ls, "A2")
            YB = k.sb([128, 4, 256], BF16, ls, "YB")
            ga = k.sb([128, 256], F32, ls, "ga")
            gb = k.sb([128, 256], F32, ls, "gb")
            gn = k.sb([128, 256], F32, ls, "gn")
            k.dma("sp", gn[:], dnorm_d[widx:widx + 1, :].partition_broadcast(128), writes=[gn])
            k.op("pool", lambda e: e.memset(Vx[:, :, 256:257], 1.0), writes=[Vx])
            k.op("dve", lambda e: e.tensor_tensor(out=neglam[:, widx:widx + 1], in0=lame[:, 2 * widx + 1:2 * widx + 2],
                                                  in1=lame[:, 2 * widx:2 * widx + 1], op=ALU.subtract),
                 reads=[lame], writes=[neglam])
            k.op("dve", lambda e: e.tensor_scalar(out=neglam[:, widx:widx + 1], in0=neglam[:, widx:widx + 1],
                                                  scalar1=-lam_init, scalar2=None, op0=ALU.add),
                 reads=[neglam], writes=[neglam])
            nl = neglam
            pp = 0
            def load_head(h_):
                base_ = ((widx * 16 + h_) * 4) * 256
                for i_ in range(4):
                    load_w_slice(base_ + i_ * 256, wb[i_], [winc_d])
            load_head(0)
            for h in range(16):
                wq, wk, wv, wg = wb[0], wb[1], wb[2], wb[3]
                for j in range(2):
                    proj_fm(xnT, wq, j * 128, QT, j, 128 ** -0.5)
                for j in range(2):
                    proj_fm(xnT, wk, j * 128, KT, j, None)
                for ti, (s, n) in enumerate(TT):
                    p = nxt("ps", PSP[0])
                    proj_tm(xnT, wv, ti, p)
                    k.op("act", lambda e, p=p, ti=ti, n=n: e.copy(out=Vx[0:n, ti, 0:256], in_=p[0:n, 0:256]),
                         reads=[p], writes=[Vx])
                    p = nxt("ps", PSP[0])
                    proj_tm(xnT, wg, ti, p)
                    silu_from_psum(p, n, 256, G[0:n, ti, :], G, ga, gb)
                if h + 1 < 16:
                    load_head(h + 1)
                for (t0, N) in QB:
                    t0v = vidx(t0)
                    nqs = (N + 127) // 128
                    for j in range(2):
                        acc = ps[2:6]
                        pend = []
                        stb = [ps[0], ps[1], ps[6]] if ST3 else [ps[0], ps[1]]
                        for kt, (s0, kn) in enumerate(TT):
                            s0v = vidx(s0)
                            stp = stb[kt % len(stb)]
                            k.mms([lambda e: e.matmul(stp[0:kn, 0:N], lhsT=KT[:, j, s0:s0 + kn], rhs=QT[:, j, t0:t0 + N],
                                                      start=True, stop=True)], reads=[KT, QT], writes=[stp])
                            dt_ap, dbuf, coef, cb = bias_tile(t0v, N, s0v, kn, sl[h], h)
                            tf = tmpf[pp % 4]
                            ptile = PT[pp % 4]
                            pp += 1
                            k.op("dve", lambda e: e.scalar_tensor_tensor(out=tf[0:kn, 0:N], in0=dt_ap, scalar=coef,
                                                                         in1=stp[0:kn, 0:N], op0=ALU.mult, op1=ALU.add),
                                 reads=[dbuf, stp], writes=[tf])
                            if cb is None:
                                k.op("act", lambda e: e.activation(out=ptile[0:kn, 0:N], in_=tf[0:kn, 0:N], func=AF.Exp),
                                     reads=[tf], writes=[ptile])
                            else:
                                k.op("act", lambda e: e.activation(out=ptile[0:kn, 0:N], in_=tf[0:kn, 0:N], func=AF.Exp,
                                                                   bias=cb), reads=[tf, cbt], writes=[ptile])
                            if len(pend) >= LOOK:
                                pend.pop(0)()
                            def pv(kt=kt, kn=kn, ptile=ptile):
                                for qs in range(nqs):
                                    qn = min(128, N - qs * 128)
                                    a = acc[qs]
                                    k.mms([lambda e: e.matmul(a[0:qn, 0:257], lhsT=ptile[0:kn, qs * 128:qs * 128 + qn],
                                                              rhs=Vx[0:kn, kt, :], start=(kt == 0), stop=(kt == 16))],
                                          reads=[ptile, Vx], writes=[a])
                            pend.append(pv)
                        while pend:
                            pend.pop(0)()
                        for qs in range(nqs):
                            qn = min(128, N - qs * 128)
                            k.op("act" if qs % 2 == 0 else "dve",
                                 lambda e, qs=qs, qn=qn: (e.copy if qs % 2 == 0 else e.tensor_copy)(
                                     out=OT[j][0:qn, qs, :], in_=acc[qs][0:qn, 0:257]),
                                 reads=[acc[qs]], writes=[Oq[j][qs]])
                    nq = nqs
                    qn = min(128, N)
                    ti0 = t0 // 128
                    o1, o2 = OT[0].t, OT[1].t
                    rd1, rd2 = list(Oq[0][0:nq]), list(Oq[1][0:nq])
                    SMv = SM.t
                    bc = lambda col: SMv[0:qn, 0:nq, col:col + 1].to_broadcast([qn, nq, 256])
                    k.op("dve", lambda e: e.reciprocal(out=SMv[0:qn, 0:nq, 0:1], in_=o1[0:qn, 0:nq, 256:257]),
                         reads=rd1, writes=[SM])
                    k.op("dve", lambda e: e.reciprocal(out=SMv[0:qn, 0:nq, 1:2], in_=o2[0:qn, 0:nq, 256:257]),
                         reads=rd2, writes=[SM])
                    k.op("dve", lambda e: e.tensor_scalar(out=SMv[0:qn, 0:nq, 2:3], in0=SMv[0:qn, 0:nq, 1:2],
                                                          scalar1=nl[0:qn, widx:widx + 1], scalar2=None, op0=ALU.mult),
                         reads=[SM, nl], writes=[SM])
                    k.op("dve", lambda e: e.tensor_tensor(out=A1[0:qn, 0:nq, :], in0=o1[0:qn, 0:nq, 0:256], in1=bc(0), op=ALU.mult),
                         reads=rd1 + [SM], writes=[A1])
                    k.op("dve", lambda e: e.tensor_tensor(out=A2[0:qn, 0:nq, :], in0=o2[0:qn, 0:nq, 0:256], in1=bc(2), op=ALU.mult),
                         reads=rd2 + [SM], writes=[A2])
                    k.op("dve", lambda e: e.tensor_tensor(out=A2[0:qn, 0:nq, :], in0=A2[0:qn, 0:nq, :], in1=A1[0:qn, 0:nq, :],
                                                          op=ALU.add), reads=[A2, A1], writes=[A2])
                    for qs in range(nq):
                        k.op("act", lambda e, qs=qs: e.activation(out=A1[0:qn, qs, :], in_=A2[0:qn, qs, :], func=AF.Square,
                                                                  accum_out=SS[qs][0:qn, 0:1]), reads=[A2], writes=[SS[qs]])
                        k.op("act", lambda e, qs=qs: e.activation(out=SS[qs][0:qn, 1:2], in_=SS[qs][0:qn, 0:1], func=AF.Ln,
                                                                  scale=1.0 / 256, bias=epsc[0:qn, 0:1]),
                             reads=[SS[qs], epsc], writes=[SS[qs]])
                        k.op("act", lambda e, qs=qs: e.activation(out=SMv[0:qn, qs, 5:6], in_=SS[qs][0:qn, 1:2], func=AF.Exp,
                                                                  scale=-0.5, bias=epsc[0:qn, 1 + widx:2 + widx]),
                             reads=[SS[qs], epsc], writes=[SM])
                    k.op("dve", lambda e: e.tensor_tensor(out=A1[0:qn, 0:nq, :], in0=A2[0:qn, 0:nq, :], in1=bc(5), op=ALU.mult),
                         reads=[A2, SM], writes=[A1])
                    k.op("dve", lambda e: e.tensor_tensor(out=A1[0:qn, 0:nq, :], in0=A1[0:qn, 0:nq, :],
                                                          in1=gn[0:qn, :].unsqueeze(1).to_broadcast([qn, nq, 256]), op=ALU.mult),
                         reads=[A1, gn], writes=[A1])
                    k.op("dve", lambda e: e.tensor_tensor(out=YB[0:qn, 0:nq, :], in0=A1[0:qn, 0:nq, :],
                                                          in1=G[0:qn, ti0:ti0 + nq, :], op=ALU.mult), reads=[A1, G], writes=[YB])
                    pt = nxt("pst", pstq)
                    k.mms([lambda e, c=c, qs=qs: e.transpose(out=pt[:, (c * nq + qs) * 128:(c * nq + qs) * 128 + qn],
                                                             in_=YB[0:qn, qs, c * 128:(c + 1) * 128],
                                                             identity=ident[0:qn, 0:qn]) for c in range(2) for qs in range(nq)],
                          reads=[YB, ident], writes=[pt])
                    wd = (nq - 1) * 128 + qn
                    k.op("act", lambda e: e.copy(out=yT[:, :, t0:t0 + wd],
                                                 in_=pt[:, 0:2 * nq * 128].rearrange("p (c t) -> p c t", c=2)[:, :, 0:wd]),
                         reads=[pt], writes=[yT])
                k.dma("pool", yTd[2 * h:2 * h + 2, :, :].rearrange("c p t -> p c t"), yT[:], reads=[yT], writes=[yTd_b[h]])

        def proj_fm_cb(xnT, w, c0, cb):
            for bi, (s, n) in enumerate(QB):
                p = nxt("ps", PSP[0])
                k.mms([lambda e, c=c, p=p: e.matmul(p[:, 0:n], lhsT=w[:, c, c0:c0 + 128], rhs=xnT[:, c, s:s + n],
                                                    start=(c == 0), stop=(c == 15)) for c in range(16)],
                      reads=[xnT, w], writes=[p])
                cb(p, s, n)

        def load_w128(sidx_row0, dst, c0):
            stg = nxt("wst", wst)
            k.dma("sp", stg[:], wina_d[sidx_row0:sidx_row0 + 128, :], writes=[stg])
            k.op("pool", lambda e: e.tensor_copy(out=dst[:, :, c0:c0 + 128],
                                                 in_=stg[:].rearrange("p (c n) -> p c n", c=16)),
                 reads=[stg], writes=[dst])

        def even_mixer_A(widx, xnT, ls):
            wb = [k.sb([128, 16, 256], BF16, ls, "wb") for _ in range(2)] + [k.sb([128, 16, 128], BF16, ls, "wb")]
            qA = k.sb([128, L], BF16, ls, "qA")
            qB = k.sb([128, L], BF16, ls, "qB")
            T1 = k.sb([128, L], F32, ls, "T1")
            T1b = k.sb([128, L], F32, ls, "T1b")
            T2 = k.sb([128, L], F32, ls, "T2")
            T3 = k.sb([128, L], F32, ls, "T3")
            QEA = k.sb([128, L], BF16, ls, "QEA")
            QEB = k.sb([128, L], BF16, ls, "QEB")
            KE = k.sb([128, L], BF16, ls, "KE")
            KL = k.sb([128, L], BF16, ls, "KL")
            V = k.sb([128, 17, 128], BF16, ls, "V")
            G = k.sb([128, 17, 128], BF16, ls, "G")
            OF = k.sb([128, 17, 128], F32, ls, "OF")
            yT = k.sb([128, L], BF16, ls, "yT")
            smk = k.sb([128, L], BF16, ls, "smk")
            mAB = k.sb([128, 2, 512], BF16, ls, "mAB")
            mtri = k.sb([128, 2, 128], F32, ls, "mtri")
            lbt = k.sb([128, 64], F32, ls, "lbt")
            lbc = k.sb([128, 2, 16], F32, ls, "lbc")
            omc = k.sb([128, 2, 16], F32, ls, "omc")
            gna = k.sb([128, 128], F32, ls, "gna")
            ATm = [k.sb([128, 128], BF16, ls, "ATm") for _ in range(3)]
            KLt = [[k.sb([128, 128], BF16, ls, "KLt") for _ in range(2)] for _ in range(2)]
            S = [k.sb([128, 128], F32, ls, "S") for _ in range(4)]
            SBALL = k.sb([128, 33, 128], BF16, ls, "SBALL")
            ECs = [k.sb([128, 34], F32, ls, "EC") for _ in range(2)]
            k.op("pool", lambda e: e.memset(SBALL[:, 0, :], 0.0), writes=[SBALL])
            ubanks = [ps[4], ps[5], ps[6]]
            PSP[0] = ps[0:4]
            ga = k.sb([128, 128], F32, ls, "ga")
            gb = k.sb([128, 128], F32, ls, "gb")
            sm = [k.sb([128, 8], F32, ls, "sm") for _ in range(2)]
            a1 = [k.sb([128, 128], F32, ls, "a1") for _ in range(2)]
            a2 = [k.sb([128, 128], F32, ls, "a2") for _ in range(2)]
            jk = k.sb([128, 128], F32, ls, "jk")
            ybf = [k.sb([128, 128], BF16, ls, "ybf") for _ in range(2)]
            k.dma("pool", smk[:], smask_d[:, :], writes=[smk])
            k.dma("pool", mAB[:], mab_d[:, :].rearrange("p (a n) -> p a n", a=2), writes=[mAB])
            k.dma("sp", mtri[:], mtri_d[:, :].rearrange("p (a n) -> p a n", a=2), writes=[mtri])
            k.dma("sp", lbt[:], lb_d[:, :], writes=[lbt])
            k.dma("sp", gna[:], hnorm_d[widx:widx + 1, :].partition_broadcast(128), writes=[gna])
            for par in range(2):
                for ab in range(2):
                    k.op("pool", lambda e, par=par, ab=ab: e.memset(KLt[par][ab][:], 0.0), writes=[KLt[par][ab]])
            lb4 = lbt[:].rearrange("p (r l h) -> p r l h", r=2, l=2)
            if widx == 0:
                k.op("pool", lambda e: e.memset(lbc[:], 0.0), writes=[lbc])
                k.op("pool", lambda e: e.memset(omc[:], 1.0), writes=[omc])
            else:
                k.op("dve", lambda e: e.tensor_tensor(out=omc[:], in0=lb4[:, :, 0, :], in1=lb4[:, :, 1, :], op=ALU.subtract),
                     reads=[lbt], writes=[omc])
                k.op("act", lambda e: e.activation(out=omc[:], in_=omc[:], func=AF.Exp), reads=[omc], writes=[omc])
                k.op("dve", lambda e: e.tensor_scalar(out=omc[:], in0=omc[:], scalar1=1.0, scalar2=None, op0=ALU.add),
                     reads=[omc], writes=[omc])
                k.op("dve", lambda e: e.reciprocal(out=lbc[:], in_=omc[:]), reads=[omc], writes=[lbc])
                k.op("dve", lambda e: e.tensor_scalar(out=omc[:], in0=lbc[:], scalar1=-1.0, scalar2=1.0, op0=ALU.mult,
                                                      op1=ALU.add), reads=[lbc], writes=[omc])
            for h in range(16):
                wq, wzf, wzb, wv, wg = (wb[0], 0), (wb[0], 128), (wb[2], 0), (wb[1], 0), (wb[1], 128)

                def load_head(h_):
                    for (wt, c0), si in ((wzf, 16 + h_), (wzb, 32 + h_), (wq, h_), (wv, 48 + h_), (wg, 64 + h_)):
                        load_w128((widx * 120 + si) * 128, wt, c0)
                if h == 0:
                    load_head(0)
                for di_ in range(2):
                    wz_ = wzf if di_ == 0 else wzb
                    Te = T1 if di_ == 0 else T1b

                    def cbz(p, s, n, Te=Te):
                        k.op("act", lambda e: e.activation(out=Te[:, s:s + n], in_=p[:, 0:n], func=AF.Exp, scale=-1.0),
                             reads=[p], writes=[Te])
                    proj_fm_cb(xnT, wz_[0], wz_[1], cbz)
                def cbq(p, s, n):
                    k.op("dve", lambda e: e.tensor_tensor(out=qA[:, s:s + n], in0=p[:, 0:n], in1=mAB[:, 0, 0:n], op=ALU.mult),
                         reads=[p, mAB], writes=[qA])
                    k.op("dve", lambda e: e.tensor_tensor(out=qB[:, s:s + n], in0=p[:, 0:n], in1=mAB[:, 1, 0:n], op=ALU.mult),
                         reads=[p, mAB], writes=[qB])
                proj_fm_cb(xnT, wq[0], wq[1], cbq)
                for ti, (s, n) in enumerate(TT):
                    p = nxt("ps", PSP[0])
                    proj_tm(xnT, wb[1], ti, p, 256, 0)
                    k.op("act", lambda e: e.copy(out=V[0:n, ti, :], in_=p[0:n, 0:128]), reads=[p], writes=[V])
                    silu_from_psum(p, n, 128, G[0:n, ti, :], G, ga, gb, 128)
                if h + 1 < 16:
                    load_head(h + 1)
                T1f = T1
                for di in range(2):
                    T1 = T1f if di == 0 else T1b
                    k.op("act", lambda e: e.activation(out=T2[:], in_=T1[:], func=AF.Ln, bias=epsc[:, 3:4]),
                         reads=[T1, epsc], writes=[T2])
                    if widx == 0:
                        k.op("dve", lambda e: e.tensor_scalar(out=T3[:], in0=T2[:], scalar1=-1.0, scalar2=None, op0=ALU.mult),
                             reads=[T2], writes=[T3])
                    else:
                        k.op("act", lambda e: e.activation(out=T3[:], in_=T1[:], func=AF.Ln, scale=lbc[:, di, h:h + 1],
                                                           bias=epsc[:, 3:4]), reads=[T1, lbc, epsc], writes=[T3])
                        k.op("dve", lambda e: e.tensor_tensor(out=T3[:], in0=T3[:], in1=T2[:], op=ALU.subtract),
                             reads=[T3, T2], writes=[T3])
                    k.op("act", lambda e: e.activation(out=T2[:], in_=T2[:], func=AF.Exp, scale=-1.0), reads=[T2], writes=[T2])
                    k.op("dve", lambda e: e.scalar_tensor_tensor(out=T1[:], in0=T1[:], scalar=omc[:, di, h:h + 1], in1=T2[:],
                                                                 op0=ALU.mult, op1=ALU.mult), reads=[T1, omc, T2], writes=[T1])
                    EC = ECs[di]
                    xv = lambda t_: t_[:, 0:2048].rearrange("p (c t) -> p c t", t=64)
                    if di == 0:
                        k.op("dve", lambda e: e.tensor_tensor_scan(out=T2[:], data0=smk[:], data1=T3[:], initial=0.0,
                                                                   op0=ALU.mult, op1=ALU.add), reads=[smk, T3], writes=[T2])
                        k.op("act", lambda e: e.activation(out=T3[:], in_=T2[:], func=AF.Exp), reads=[T2], writes=[T3])
                        k.op("pool", lambda e: e.tensor_copy(out=EC[:, 0:32], in_=xv(T3)[:, :, 63]), reads=[T3], writes=[EC])
                        k.op("pool", lambda e: e.tensor_copy(out=EC[:, 32:33], in_=T3[:, 2063:2064]), reads=[T3], writes=[EC])
                        k.op("act", lambda e: e.activation(out=T2[:], in_=T2[:], func=AF.Exp, scale=-1.0),
                             reads=[T2], writes=[T2])
                        k.op("dve", lambda e: e.tensor_tensor(out=QEA[:], in0=qA[:], in1=T3[:], op=ALU.mult),
                             reads=[qA, T3], writes=[QEA])
                        k.op("dve", lambda e: e.tensor_tensor(out=QEB[:], in0=qB[:], in1=T3[:], op=ALU.mult),
                             reads=[qB, T3], writes=[QEB])
                        k.op("dve", lambda e: e.tensor_tensor(out=KE[:], in0=T1[:], in1=T2[:], op=ALU.mult),
                             reads=[T1, T2], writes=[KE])
                        k.op("dve", lambda e: e.tensor_tensor(
                            out=xv(KL), in0=xv(KE), in1=xv(T3)[:, :, 63:64].to_broadcast([128, 32, 64]), op=ALU.mult),
                            reads=[KE, T3], writes=[KL])
                        k.op("dve", lambda e: e.tensor_tensor(out=KL[:, 2048:2064], in0=KE[:, 2048:2064],
                                                              in1=T3[:, 2063:2064].to_broadcast([128, 16]), op=ALU.mult),
                             reads=[KE, T3], writes=[KL])
                        KLs = KL
                    else:
                        k.op("pool", lambda e: e.memset(T2[:, 0:1], 0.0), writes=[T2])
                        k.op("dve", lambda e: e.tensor_tensor_scan(out=T2[:, 1:L], data0=T3[:, 0:L - 1], data1=smk[:, 1:L],
                                                                   initial=0.0, op0=ALU.add, op1=ALU.mult),
                             reads=[smk, T3], writes=[T2])
                        k.op("dve", lambda e: e.tensor_tensor(out=EC[:, 0:32], in0=xv(T2)[:, :, 63], in1=xv(T3)[:, :, 63],
                                                              op=ALU.add), reads=[T2, T3], writes=[EC])
                        k.op("dve", lambda e: e.tensor_tensor(out=EC[:, 32:33], in0=T2[:, 2063:2064], in1=T3[:, 2063:2064],
                                                              op=ALU.add), reads=[T2, T3], writes=[EC])
                        k.op("act", lambda e: e.activation(out=EC[:, 0:33], in_=EC[:, 0:33], func=AF.Exp), reads=[EC], writes=[EC])
                        k.op("act", lambda e: e.activation(out=T3[:], in_=T2[:], func=AF.Exp, scale=-1.0),
                             reads=[T2], writes=[T3])
                        k.op("act", lambda e: e.activation(out=T2[:], in_=T2[:], func=AF.Exp), reads=[T2], writes=[T2])
                        k.op("dve", lambda e: e.tensor_tensor(out=QEA[:], in0=qA[:], in1=T3[:], op=ALU.mult),
                             reads=[qA, T3], writes=[QEA])
                        k.op("dve", lambda e: e.tensor_tensor(out=QEB[:], in0=qB[:], in1=T3[:], op=ALU.mult),
                             reads=[qB, T3], writes=[QEB])
                        k.op("dve", lambda e: e.tensor_tensor(out=KE[:], in0=T1[:], in1=T2[:], op=ALU.mult),
                             reads=[T1, T2], writes=[KE])
                        KLs = KE
                    if di == 0:
                        seq = [(16, 0)] + [(t_, ab_) for t_ in range(16) for ab_ in (0, 1)]
                    else:
                        seq = [(t_, ab_) for t_ in range(15, -1, -1) for ab_ in (1, 0)] + [(16, 0)]
                    cids = [32 if t_ == 16 else 2 * t_ + ab_ for (t_, ab_) in seq]
                    order = [16] + list(range(16)) if di == 0 else list(range(15, -1, -1)) + [16]
                    seqidx = {}
                    k.op("pool", lambda e: e.memset(S[0][:], 0.0), writes=[S[0]])
                    if di == 0:
                        groups = [[16]] + [[2 * g_, 2 * g_ + 1] for g_ in range(8)]
                    else:
                        groups = [[15 - 2 * g_, 14 - 2 * g_] for g_ in range(8)] + [[16]]
                    step = 0
                    tcount = 0
                    for gt in groups:
                        bank = ubanks[st["uq"] % 3]
                        st["uq"] += 1
                        slots = []
                        for ti in gt:
                            s, n = TT[ti]
                            par = tcount % 2
                            tcount += 1
                            pt = nxt("pst", pstq)
                            k.mms([lambda e: e.transpose(out=pt[0:n, 0:128], in_=KLs[:, s:s + n], identity=ident[:, :])],
                                  reads=[KLs, ident], writes=[pt])
                            nA = min(n, 64)
                            k.op("act", lambda e: e.copy(out=KLt[par][0][0:nA, :], in_=pt[0:nA, 0:128]), reads=[pt],
                                 writes=[KLt[par][0]])
                            if n == 128:
                                k.op("act", lambda e: e.copy(out=KLt[par][1][64:128, :], in_=pt[64:128, 0:128]), reads=[pt],
                                     writes=[KLt[par][1]])
                            chunks = [0] if n == 16 else ([0, 1] if di == 0 else [1, 0])
                            for ab in chunks:
                                cid = 32 if ti == 16 else 2 * ti + ab
                                q_ = len(slots)
                                seqidx[(ti, ab)] = step + q_
                                k.mms([lambda e: e.matmul(bank[:, q_ * 128:(q_ + 1) * 128], lhsT=KLt[par][ab][0:n, :],
                                                          rhs=V[0:n, ti, :], start=True, stop=True)],
                                      reads=[KLt[par][ab], V], writes=[bank])
                                slots.append((q_, cid))
                        for (q_, cid) in slots:
                            if step < 32:
                                so, sn = S[step % 4], S[(step + 1) % 4]
                                k.op("dve", lambda e: e.scalar_tensor_tensor(out=sn[:], in0=so[:], scalar=EC[:, cid:cid + 1],
                                                                             in1=bank[:, q_ * 128:(q_ + 1) * 128],
                                                                             op0=ALU.mult, op1=ALU.add),
                                     reads=[so, EC, bank], writes=[sn])
                                if di == 0:
                                    k.op("act", lambda e: e.copy(out=SBALL[:, step + 1, :], in_=sn[:]), reads=[sn], writes=[SBALL])
                                else:
                                    cn = cids[step + 1]
                                    k.op("act", lambda e: e.mul(out=SBALL[:, step + 1, :], in_=sn[:], mul=EC[:, cn:cn + 1]),
                                         reads=[sn, EC], writes=[SBALL])
                            step += 1
                    def at_stage(oi, ti):
                        s, n = TT[ti]
                        pa = nxt("ps", PSP[0])
                        k.mms([lambda e: e.matmul(pa[0:n, 0:n], lhsT=KE[:, s:s + n], rhs=QEA[:, s:s + n], start=True, stop=False),
                               lambda e: e.matmul(pa[0:n, 0:n], lhsT=KE[:, s:s + n], rhs=QEB[:, s:s + n], start=False, stop=True)],
                              reads=[KE, QEA, QEB], writes=[pa])
                        at = ATm[oi % 3]
                        k.op("dve", lambda e: e.tensor_tensor(out=at[0:n, 0:n], in0=pa[0:n, 0:n], in1=mtri[0:n, di, 0:n],
                                                              op=ALU.mult), reads=[pa, mtri], writes=[at])
                    at_stage(0, order[0])
                    for oi, ti in enumerate(order):
                        s, n = TT[ti]
                        if oi + 1 < len(order):
                            at_stage(oi + 1, order[oi + 1])
                        at = ATm[oi % 3]
                        chunks = [0] if n == 16 else ([0, 1] if di == 0 else [1, 0])
                        po = nxt("ps", PSP[0])
                        fns = [lambda e: e.matmul(po[0:n, 0:128], lhsT=at[0:n, 0:n], rhs=V[0:n, ti, :], start=True, stop=False)]
                        for ci, ab in enumerate(chunks):
                            qe = QEA if ab == 0 else QEB
                            sbi = seqidx[(ti, ab)]
                            fns.append(lambda e, qe=qe, sbi=sbi, last=(ci == len(chunks) - 1): e.matmul(
                                po[0:n, 0:128], lhsT=qe[:, s:s + n], rhs=SBALL[:, sbi, :], start=False, stop=last))
                        k.mms(fns, reads=[at, V, QEA, QEB, SBALL], writes=[po])
                        if di == 0:
                            k.op("act", lambda e: e.copy(out=OF[0:n, ti, :], in_=po[0:n, 0:128]), reads=[po], writes=[OF])
                        else:
                            s_ = sm[oi % 2]
                            x1, x2, yb = a1[oi % 2], a2[oi % 2], ybf[oi % 2]
                            k.op("dve", lambda e: e.tensor_tensor(out=x1[0:n, :], in0=po[0:n, 0:128], in1=OF[0:n, ti, :],
                                                                  op=ALU.add), reads=[po, OF], writes=[x1])
                            k.op("act", lambda e: e.activation(out=jk[0:n, :], in_=x1[0:n, :], func=AF.Square,
                                                               accum_out=s_[0:n, 0:1]), reads=[x1], writes=[jk, s_])
                            k.op("act", lambda e: e.activation(out=s_[0:n, 1:2], in_=s_[0:n, 0:1], func=AF.Ln,
                                                               scale=1.0 / 128, bias=epsc[0:n, 0:1]),
                                 reads=[s_, epsc], writes=[s_])
                            k.op("act", lambda e: e.activation(out=s_[0:n, 2:3], in_=s_[0:n, 1:2], func=AF.Exp, scale=-0.5),
                                 reads=[s_], writes=[s_])
                            k.op("dve", lambda e: e.scalar_tensor_tensor(out=x2[0:n, :], in0=x1[0:n, :], scalar=s_[0:n, 2:3],
                                                                         in1=gna[0:n, :], op0=ALU.mult, op1=ALU.mult),
                                 reads=[x1, s_, gna], writes=[x2])
                            k.op("dve", lambda e: e.tensor_tensor(out=yb[0:n, :], in0=x2[0:n, :], in1=G[0:n, ti, :],
                                                                   op=ALU.mult), reads=[x2, G], writes=[yb])
                            pt2 = nxt("pst", pstq)
                            k.mms([lambda e: e.transpose(out=pt2[:, 0:n], in_=yb[0:n, :], identity=ident[0:n, 0:n])],
                                  reads=[yb, ident], writes=[pt2])
                            k.op("act", lambda e: e.copy(out=yT[:, s:s + n], in_=pt2[:, 0:n]), reads=[pt2], writes=[yT])
                T1 = T1f
                k.dma("pool", yTd[h, :, :], yT[:], reads=[yT], writes=[yTd_b[h]])
            PSP[0] = ps

        def even_mixer_B(widx, xnT, ls):
            sl = slopes16()
            load_consts(ls)
            dabs = CT["dabs"]
            wb = [k.sb([128, 16, 256], BF16, ls, "wb") for _ in range(3)]
            QT = k.sb([128, 1, L], BF16, ls, "QT")
            KT = k.sb([128, 1, L], BF16, ls, "KT")
            Vx = k.sb([128, 17, 129], BF16, ls, "Vx")
            G = k.sb([128, 17, 128], BF16, ls, "G")
            yT = k.sb([128, L], BF16, ls, "yT")
            wabs = k.sb([128, 1152], F32, ls, "wabs")
            mclip = k.sb([128, 1024], F32, ls, "mclip")
            sink = k.sb([128, 16], F32, ls, "sink")
            esink = k.sb([128, 16], F32, ls, "esink")
            tmpf = [k.sb([128, 512], F32, ls, "tmpf") for _ in range(4)]
            PT = [k.sb([128, 512], BF16, ls, "PT") for _ in range(4)]
            OT = k.sb([128, 4, 129], F32, ls, "OT")
            Oq = [Buf(OT.t[:, q_, :]) for q_ in range(4)]
            SM = k.sb([128, 4, 4], F32, ls, "SM")
            A1 = k.sb([128, 4, 128], F32, ls, "A1")
            YB = k.sb([128, 4, 128], BF16, ls, "YB")
            ga = k.sb([128, 128], F32, ls, "ga")
            gb = k.sb([128, 128], F32, ls, "gb")
            sm = [k.sb([128, 4], F32, ls, "sm") for _ in range(2)]
            a1 = [k.sb([128, 128], F32, ls, "a1") for _ in range(2)]
            ybf = [k.sb([128, 128], BF16, ls, "ybf") for _ in range(2)]
            k.dma("sp", wabs[:], wabs_d[:, :], writes=[wabs])
            k.dma("sp", mclip[:], mclip_d[:, :], writes=[mclip])
            k.dma("sp", sink[:], sink_d[widx:widx + 1, :].partition_broadcast(128), writes=[sink])
            k.op("act", lambda e: e.activation(out=esink[:], in_=sink[:], func=AF.Exp), reads=[sink], writes=[esink])
            k.op("pool", lambda e: e.memset(Vx[:, :, 128:129], 1.0), writes=[Vx])
            pp = 0
            def load_kv(kv_):
                load_w128((widx * 120 + 96 + kv_) * 128, wb[0], 0)
                load_w128((widx * 120 + 100 + kv_) * 128, wb[0], 128)

            def load_qg(hq_):
                load_w128((widx * 120 + 80 + hq_) * 128, wb[1 + hq_ % 2], 0)
                load_w128((widx * 120 + 104 + hq_) * 128, wb[1 + hq_ % 2], 128)
            load_kv(0)
            load_qg(0)
            for hq in range(16):
                kv = hq // 4
                if hq % 4 == 0:
                    wk, wv = (wb[0], 0), (wb[0], 128)
                    proj_fm(xnT, wk[0], wk[1], KT, 0, None)
                    for ti, (s, n) in enumerate(TT):
                        p = nxt("ps", PSP[0])
                        proj_tm(xnT, wv[0], ti, p, 128, wv[1])
                        k.op("act", lambda e: e.copy(out=Vx[0:n, ti, 0:128], in_=p[0:n, 0:128]), reads=[p], writes=[Vx])
                wq, wg = (wb[1 + hq % 2], 0), (wb[1 + hq % 2], 128)
                if hq + 1 < 16:
                    load_qg(hq + 1)
                    if (hq + 1) % 4 == 0:
                        load_kv((hq + 1) // 4)
                proj_fm(xnT, wq[0], wq[1], QT, 0, 128 ** -0.5)
                for ti, (s, n) in enumerate(TT):
                    p = nxt("ps", PSP[0])
                    proj_tm(xnT, wg[0], ti, p, 128, wg[1])
                    silu_from_psum(p, n, 128, G[0:n, ti, :], G, ga, gb)
                for (t0, N) in QB:
                    t0v = vidx(t0)
                    nqs = (N + 127) // 128
                    acc = ps[2:6]
                    if t0 < 2048:
                        xt = [s0 for s0 in range(t0 - 128, t0 + N + 1, 128) if 0 <= s0 <= 1920]
                    else:
                        xt = [0]
                    ktl = [(2048, 16)] + [(s0, 128) for s0 in xt]
                    pend = []
                    stb = [ps[0], ps[1], ps[6]] if ST3 else [ps[0], ps[1]]
                    for ki, (s0, kn) in enumerate(ktl):
                        s0v = vidx(s0)
                        kt = s0 // 128
                        stp = stb[ki % len(stb)]
                        k.mms([lambda e: e.matmul(stp[0:kn, 0:N], lhsT=KT[:, 0, s0:s0 + kn], rhs=QT[:, 0, t0:t0 + N],
                                                  start=True, stop=True)], reads=[KT, QT], writes=[stp])
                        if s0 == 2048 and t0 < 2048:
                            c0 = min(t0, 512)
                            dt_ap, dbuf = mclip[0:16, c0:c0 + N], mclip
                        elif s0 == 2048:
                            dt_ap, dbuf = dabs[0:16, 384:384 + N], dabs
                        else:
                            off = t0v - s0v
                            dt_ap, dbuf = wabs[0:kn, 512 + off:512 + off + N], wabs
                        tf = tmpf[pp % 4]
                        ptile = PT[pp % 4]
                        pp += 1
                        k.op("dve", lambda e: e.scalar_tensor_tensor(out=tf[0:kn, 0:N], in0=dt_ap, scalar=-sl[hq],
                                                                     in1=stp[0:kn, 0:N], op0=ALU.mult, op1=ALU.add),
                             reads=[dbuf, stp], writes=[tf])
                        k.op("act", lambda e: e.activation(out=ptile[0:kn, 0:N], in_=tf[0:kn, 0:N], func=AF.Exp),
                             reads=[tf], writes=[ptile])
                        if len(pend) >= LOOK:
                            pend.pop(0)()
                        def pv(s0=s0, kn=kn, kt=kt, ptile=ptile):
                            for qs in range(nqs):
                                qn = min(128, N - qs * 128)
                                tok0 = t0 + qs * 128
                                if t0 < 2048:
                                    rel = [s_ for s_ in (tok0 - 128, tok0, tok0 + 128) if 0 <= s_ <= 1920]
                                else:
                                    rel = [0]
                                if s0 != 2048 and s0 not in rel:
                                    continue
                                a = acc[qs]
                                k.mms([lambda e: e.matmul(a[0:qn, 0:129], lhsT=ptile[0:kn, qs * 128:qs * 128 + qn],
                                                          rhs=Vx[0:kn, kt, :], start=(s0 == 2048), stop=(s0 == rel[-1]))],
                                      reads=[ptile, Vx], writes=[a])
                        pend.append(pv)
                    while pend:
                        pend.pop(0)()
                    nq = nqs
                    qn = min(128, N)
                    ti0 = t0 // 128
                    for qs in range(nq):
                        k.op("act" if qs % 2 == 0 else "dve",
                             lambda e, qs=qs: (e.copy if qs % 2 == 0 else e.tensor_copy)(
                                 out=OT[0:qn, qs, :], in_=acc[qs][0:qn, 0:129]), reads=[acc[qs]], writes=[Oq[qs]])
                    rdo = list(Oq[0:nq])
                    ot = OT.t
                    SMv = SM.t
                    k.op("dve", lambda e: e.tensor_scalar(out=SMv[0:qn, 0:nq, 0:1], in0=ot[0:qn, 0:nq, 128:129],
                                                          scalar1=esink[0:qn, hq:hq + 1], scalar2=None, op0=ALU.add),
                         reads=rdo + [esink], writes=[SM])
                    k.op("dve", lambda e: e.reciprocal(out=SMv[0:qn, 0:nq, 1:2], in_=SMv[0:qn, 0:nq, 0:1]), reads=[SM], writes=[SM])
                    k.op("dve", lambda e: e.tensor_tensor(out=A1[0:qn, 0:nq, :], in0=ot[0:qn, 0:nq, 0:128],
                                                          in1=SMv[0:qn, 0:nq, 1:2].to_broadcast([qn, nq, 128]), op=ALU.mult),
                         reads=rdo + [SM], writes=[A1])
                    k.op("dve", lambda e: e.tensor_tensor(out=YB[0:qn, 0:nq, :], in0=A1[0:qn, 0:nq, :],
                                                          in1=G[0:qn, ti0:ti0 + nq, :], op=ALU.mult), reads=[A1, G], writes=[YB])
                    pt = nxt("pst", pstq)
                    k.mms([lambda e, qs=qs: e.transpose(out=pt[:, qs * 128:qs * 128 + qn], in_=YB[0:qn, qs, :],
                                                        identity=ident[0:qn, 0:qn]) for qs in range(nq)],
                          reads=[YB, ident], writes=[pt])
                    wd = (nq - 1) * 128 + qn
                    k.op("act", lambda e: e.copy(out=yT[:, t0:t0 + wd], in_=pt[:, 0:wd]), reads=[pt], writes=[yT])
                k.dma("pool", yTd[16 + hq, :, :], yT[:], reads=[yT], writes=[yTd_b[hq]])

        def phase_out(first, wout_d, widx, ls):
            wob = [k.sb([128, 32, 512], BF16, ls, "wob") for _ in range(2)]
            yb = [k.sb([128, 32, 512], BF16, ls, "ytb") for _ in range(2)]
            hb = [k.sb([128, 512], F32, ls, "hb") for _ in range(4)]
            ho = [k.sb([128, 512], F32, ls, "ho") for _ in range(4)]
            cnt = 0
            for nb in range(4):
                wo = wob[nb % 2]
                for pc in range(8):
                    stg = nxt("wst", wst)
                    r0 = ((widx * 4 + nb) * 8 + pc) * 128
                    k.dma("sp", stg[:], wout_d[r0:r0 + 128, :], writes=[stg])
                    k.op("pool", lambda e, stg=stg, pc=pc: e.tensor_copy(
                        out=wo[:, 4 * pc:4 * pc + 4, :], in_=stg[:].rearrange("p (c n) -> p c n", c=4)),
                        reads=[stg], writes=[wo])
                for bi, (t0, N) in enumerate(QB):
                    y = yb[bi % 2]
                    k.dma("sp", y[:, :, 0:N], yTd[:, :, t0:t0 + N].rearrange("c p t -> p c t"),
                          reads=yTd_b, writes=[y])
                    for qs in range((N + 127) // 128):
                        qn = min(128, N - qs * 128)
                        tok0 = t0 + qs * 128
                        ti = tok0 // 128
                        p = nxt("ps", PSP[0])
                        k.mms([lambda e, c=c: e.matmul(p[0:qn, :], lhsT=y[:, c, qs * 128:qs * 128 + qn], rhs=wo[:, c, :],
                                                       start=(c == 0), stop=(c == 31)) for c in range(32)],
                              reads=[y, wo], writes=[p])
                        hi = hb[cnt % 4]
                        hn = ho[cnt % 4]
                        cnt += 1
                        src = h_src(first, ti)
                        k.dma("sp", hi[0:qn, :], src[:, nb * 512:(nb + 1) * 512], reads=[hd_b[ti]], writes=[hi])
                        k.op("dve", lambda e: e.tensor_tensor(out=hn[0:qn, :], in0=p[0:qn, :], in1=hi[0:qn, :], op=ALU.add),
                             reads=[p, hi], writes=[hn])
                        k.dma("pool", hd[tok0:tok0 + qn, nb * 512:(nb + 1) * 512], hn[0:qn, :], reads=[hn],
                              writes=[hd_b[ti]])

        def phase_final(first, ls):
            gt = k.sb([128, D], F32, ls, "gt")
            hb = [k.sb([128, D], F32, ls, "hb") for _ in range(2)]
            ob = [k.sb([128, D], F32, ls, "ob") for _ in range(2)]
            junk = k.sb([128, D], BF16, ls, "junk")
            ssb = [k.sb([128, 2], F32, ls, "ss") for _ in range(2)]
            k.dma("sp", gt[:], nrm_d[4:5, :].partition_broadcast(128), writes=[gt])
            toks = []
            for ti, (s, n) in enumerate(TT[:16]):
                h = hb[ti % 2]
                o = ob[ti % 2]
                ss = ssb[ti % 2]
                k.dma("sp", h[0:n, :], h_src(first, ti), reads=[hd_b[ti]], writes=[h])
                k.op("act", lambda e: e.activation(out=junk[0:n, :], in_=h[0:n, :], func=AF.Square,
                                                   accum_out=ss[0:n, 0:1]), reads=[h], writes=[junk, ss])
                k.op("act", lambda e: e.activation(out=ss[0:n, 1:2], in_=ss[0:n, 0:1], func=AF.Ln, scale=1.0 / D,
                                                   bias=epsc[0:n, 0:1]), reads=[ss, epsc], writes=[ss])
                k.op("act", lambda e: e.activation(out=ss[0:n, 0:1], in_=ss[0:n, 1:2], func=AF.Exp, scale=-0.5),
                     reads=[ss], writes=[ss])
                k.op("dve", lambda e: e.scalar_tensor_tensor(out=o[0:n, :], in0=h[0:n, :], scalar=ss[0:n, 0:1],
                                                             in1=gt[0:n, :], op0=ALU.mult, op1=ALU.mult),
                     reads=[h, ss, gt], writes=[o])
                toks.append(k.dma("pool", out_d[s:s + n, :], o[0:n, :], reads=[o]))
            return toks

        first = True
        for (kind, widx, layer_idx) in layers:
            with ExitStack() as ls:
                xnT = k.sb([128, 16, L], BF16, ls, "xnT")
                with ExitStack() as ls2:
                    phase_norm(first, (0 if kind == "E" else 2) + widx, xnT, ls2)
                    k.barrier()
                with ExitStack() as ls2:
                    if kind == "O":
                        odd_mixer(widx, layer_idx, xnT, ls2)
                    else:
                        even_mixer_A(widx, xnT, ls2)
                    k.barrier()
                if kind == "E":
                    with ExitStack() as ls2:
                        even_mixer_B(widx, xnT, ls2)
                        k.barrier()
            with ExitStack() as ls:
                phase_out(first, woutc_d if kind == "O" else wouta_d, widx, ls)
                k.barrier()
            first = False
        out_toks = []
        with ExitStack() as ls:
            if do_final:
                out_toks = phase_final(first, ls)
            else:
                hb = [k.sb([128, D], F32, ls, "hb") for _ in range(2)]
                for ti, (s, n) in enumerate(TT):
                    h = hb[ti % 2]
                    k.dma("sp", h[0:n, :], h_src(first, ti), reads=[hd_b[ti]], writes=[h])
                    out_toks.append(k.dma("pool", out_d[s:s + n, :], h[0:n, :], reads=[h]))
            k.barrier()
        k.check_deadlock()
    return nc


def const_tables():
    i = np.arange(128, dtype=np.float32)[:, None]
    ident = np.eye(128, dtype=np.float32)
    dlin = (np.arange(512, dtype=np.float32)[None, :] - i).astype(np.float32)
    dabs = np.abs(np.arange(896, dtype=np.float32)[None, :] - i - 384).astype(np.float32)
    sl = np.array(slopes16(), dtype=np.float64)
    cb = (-(sl[:, None] * np.array(DELTAS, dtype=np.float64)[None, :])).reshape(1, -1)
    cb = np.repeat(cb, 128, axis=0).astype(np.float32)
    return {"ident": ident, "dlin": dlin, "dabs": dabs, "cbtab": np.ascontiguousarray(cb)}


def layout_winc(w):
    a = w.reshape(2, 2, 8, 128, 4, 16, 256)
    a = a.transpose(0, 5, 4, 1, 3, 2, 6)
    return np.ascontiguousarray(a).reshape(2 * 16 * 4 * 2 * 128, 2048)


def layout_wina(w):
    a = w.reshape(2, 16, 128, 120, 128)
    a = a.transpose(0, 3, 2, 1, 4)
    return np.ascontiguousarray(a).reshape(2 * 120 * 128, 2048)


def even_tables():
    i = np.arange(128, dtype=np.float32)[:, None]
    smask = np.ones((128, L), np.float32)
    smask[:, 0:2048:64] = 0.0
    smask[:, 2048] = 0.0
    j = np.arange(512)
    mA = ((j % 128) < 64).astype(np.float32)
    mab = np.concatenate([np.tile(mA[None], (128, 1)), np.tile((1 - mA)[None], (128, 1))], axis=1)
    s_ = np.arange(128)[:, None]; t_ = np.arange(128)[None, :]
    same = (s_ // 64) == (t_ // 64)
    mf = (same & (s_ <= t_)).astype(np.float32)
    mb_ = (same & (s_ >= t_)).astype(np.float32)
    mtri = np.concatenate([mf, mb_], axis=1)
    dw = np.abs(np.arange(1152, dtype=np.float32)[None, :] - i - 512)
    wabs = np.where(dw <= 128, dw, 1e9).astype(np.float32)
    mclip = np.minimum(np.arange(1024, dtype=np.float32)[None, :] - i + 16, 128.0).astype(np.float32)
    return {"smask": smask, "mab": np.ascontiguousarray(mab), "mtri": np.ascontiguousarray(mtri),
            "wabs": wabs, "mclip": mclip}


def layout_wout(w):
    a = w.reshape(2, 8, 4, 128, 4, 512)
    a = a.transpose(0, 4, 1, 3, 2, 5)
    return np.ascontiguousarray(a).reshape(2 * 4 * 8 * 128, 2048)


def run_layers(layers, do_final, inputs, ncores=8):
    out_rows = NX if do_final else L
    nc = build(layers, do_final, out_rows)
    f = lambda a: np.ascontiguousarray(np.asarray(a, dtype=np.float32))
    shared = dict(const_tables())
    shared["meta"] = f(inputs["meta_tokens"])
    shared["norms"] = np.concatenate([f(inputs["norm_a"]), f(inputs["norm_c"]), f(inputs["final_norm"])[None]], axis=0)
    if any(l[0] == "E" for l in layers):
        shared.update(even_tables())
        shared["wina"] = layout_wina(f(inputs["w_in_a"]))
        shared["wouta"] = layout_wout(f(inputs["w_out_a"]))
        shared["lbl"] = np.ascontiguousarray(f(inputs["hgrn_lb"]).reshape(2, 2, 16, 128).transpose(3, 0, 1, 2)).reshape(128, 64)
        shared["hnorm"] = f(inputs["hgrn_norm"])
        shared["sink"] = f(inputs["sink_logits"])
    if any(l[0] == "O" for l in layers):
        shared["winc"] = layout_winc(f(inputs["w_in_c"]))
        shared["woutc"] = layout_wout(f(inputs["w_out_c"]))
        shared["dlam"] = f(inputs["diff_lambda"]).reshape(2, 512)
        shared["dnorm"] = f(inputs["diff_norm"])
    x = f(inputs["x"])
    in_maps = []
    for b in range(ncores):
        m = dict(shared)
        m["x"] = x[b]
        in_maps.append(m)
    res = run_bass_kernel_spmd(nc, in_maps, core_ids=list(range(ncores)))
    return np.stack([r["out"] for r in res.results], axis=0)


def kernel(**inputs):
    layers = [("E", 0, 0), ("O", 0, 1), ("E", 1, 2), ("O", 1, 3)]
    return run_layers(layers, True, inputs)
```

```python
import math
from contextlib import ExitStack
import numpy as np
import concourse.bass as bass
import concourse.mybir as mybir
from concourse.bass_utils import run_bass_kernel_spmd

F32 = mybir.dt.float32
BF16 = mybir.dt.bfloat16
AF = mybir.ActivationFunctionType
ALU = mybir.AluOpType
AX = mybir.AxisListType

L = 2064
NX = 2048
NMETA = 16
D = 2048
EPS = 1e-6
TT = [(i * 128, 128) for i in range(16)] + [(2048, 16)]
QB = [(i * 512, 512) for i in range(4)] + [(2048, 16)]
DELTAS = [128 * m for m in range(1, 16)] + [16 + 128 * m for m in range(16)]
NDEL = len(DELTAS)
SAME_ENGINE_SYNC = True
import os
LOOK = int(os.environ.get('KLOOK', '3'))
ST3 = int(os.environ.get('KST3', '1'))


def vidx(s):
    return s if s < 2048 else s - 2048 - 16


def slopes16():
    return [2.0 ** (-8.0 * (i + 1) / 16) for i in range(16)]


class Eng:
    def __init__(self, nc, es, name, e, ndma):
        self.name = name
        self.e = e
        self.sem = es.enter_context(nc.semaphore("s_" + name))
        self.cnt = 0
        self.seen = {}
        self.ring = [[es.enter_context(nc.semaphore("d_%s%d" % (name, i))), 0] for i in range(ndma)]
        self.ri = 0


class Buf:
    def __init__(self, t):
        self.t = t
        self.w = None
        self.rs = {}

    def __getitem__(self, k):
        return self.t[k]


class K:
    def __init__(self, nc, es):
        self.nc = nc
        self.es = es
        self.E = {
            "pe": Eng(nc, es, "pe", nc.tensor, 0),
            "dve": Eng(nc, es, "dve", nc.vector, 0),
            "act": Eng(nc, es, "act", nc.scalar, 0),
            "pool": Eng(nc, es, "pool", nc.gpsimd, 16),
            "sp": Eng(nc, es, "sp", nc.sync, 24),
        }
        self.nbuf = 0
        self.log = []

    def sb(self, shape, dt, es=None, name=None):
        self.nbuf += 1
        t = (es or self.es).enter_context(self.nc.sbuf_tensor("%s_%d" % (name or "sb", self.nbuf), list(shape), dt))
        return Buf(t)

    def wait(self, eng, tok):
        if tok is None:
            return
        sid, sem, val = tok
        if eng.seen.get(sid, 0) >= val:
            return
        if sid == id(eng.sem) and (eng.name == "pe" or not SAME_ENGINE_SYNC):
            return
        eng.e.wait_ge(sem, val)
        eng.seen[sid] = val
        self.log.append((eng.name, "w", sid, val))

    def _deps(self, eng, reads, writes):
        for b in reads:
            self.wait(eng, b.w)
        for b in writes:
            self.wait(eng, b.w)
            for tok in list(b.rs.values()):
                self.wait(eng, tok)

    def _mark(self, tok, reads, writes):
        for b in reads:
            old = b.rs.get(tok[0])
            if old is None or old[2] < tok[2]:
                b.rs[tok[0]] = tok
        for b in writes:
            b.w = tok
            b.rs = {}

    def op(self, en, fn, reads=(), writes=()):
        eng = self.E[en]
        self._deps(eng, reads, writes)
        ins = fn(eng.e)
        eng.cnt += 1
        ins.then_inc(eng.sem, 1)
        self.log.append((eng.name, "i", id(eng.sem), 1))
        tok = (id(eng.sem), eng.sem, eng.cnt)
        self._mark(tok, reads, writes)
        return tok

    def mms(self, fns, reads=(), writes=()):
        eng = self.E["pe"]
        self._deps(eng, reads, writes)
        ins = None
        for fn in fns:
            ins = fn(eng.e)
        eng.cnt += 1
        ins.then_inc(eng.sem, 1)
        self.log.append((eng.name, "i", id(eng.sem), 1))
        tok = (id(eng.sem), eng.sem, eng.cnt)
        self._mark(tok, reads, writes)
        return tok

    def dma(self, qn, out, in_, reads=(), writes=()):
        eng = self.E[qn]
        self._deps(eng, reads, writes)
        slot = eng.ring[eng.ri % len(eng.ring)]
        eng.ri += 1
        if slot[1] > 0:
            self.wait(eng, (id(slot[0]), slot[0], slot[1]))
        ins = eng.e.dma_start(out=out, in_=in_)
        slot[1] += 16
        ins.then_inc(slot[0], 16)
        self.log.append((eng.name, "i", id(slot[0]), 16))
        tok = (id(slot[0]), slot[0], slot[1])
        self._mark(tok, reads, writes)
        return tok

    def check_deadlock(self):
        qs = {}
        for ev in self.log:
            qs.setdefault(ev[0], []).append(ev)
        ptr = {n: 0 for n in qs}
        sem = {}
        prog = True
        while prog:
            prog = False
            for n, q in qs.items():
                while ptr[n] < len(q):
                    _, kind, sid, val = q[ptr[n]]
                    if kind == "i":
                        sem[sid] = sem.get(sid, 0) + val
                    elif sem.get(sid, 0) < val:
                        break
                    ptr[n] += 1
                    prog = True
        stuck = {n: (ptr[n], len(q), q[ptr[n]]) for n, q in qs.items() if ptr[n] < len(q)}
        if stuck:
            names = {id(e.sem): e.name for e in self.E.values()}
            for e in self.E.values():
                for i, sl in enumerate(e.ring):
                    names[id(sl[0])] = "%s_dma%d" % (e.name, i)
            msg = "; ".join("%s at %d/%d waits %s>=%d (have %d)" % (n, p, t, names.get(ev[2]), ev[3], sem.get(ev[2], 0))
                            for n, (p, t, ev) in stuck.items())
            raise RuntimeError("DEADLOCK in emitted program: " + msg)

    def barrier(self):
        toks = []
        for e in self.E.values():
            if e.cnt > 0:
                toks.append((id(e.sem), e.sem, e.cnt))
            for s in e.ring:
                if s[1] > 0:
                    toks.append((id(s[0]), s[0], s[1]))
        for e in self.E.values():
            for t in toks:
                if t[0] == id(e.sem):
                    continue
                self.wait(e, t)


def build(layers, do_final, out_rows):
    nc = bass.Bass("TRN2", target_bir_lowering=False)
    dr = {}

    def din(name, shape):
        dr[name] = nc.dram_tensor(name, list(shape), F32, kind="ExternalInput").ap()
        return dr[name]

    x_d = din("x", [NX, D])
    meta_d = din("meta", [NMETA, D])
    nrm_d = din("norms", [5, D])
    ident_d = din("ident", [128, 128])
    dlin_d = din("dlin", [128, 512])
    dabs_d = din("dabs", [128, 896])
    cb_d = din("cbtab", [128, 16 * NDEL])
    n_odd = sum(1 for l in layers if l[0] == "O")
    n_even = sum(1 for l in layers if l[0] == "E")
    if n_odd:
        winc_d = din("winc", [2 * 16 * 4 * 2 * 128, 2048])
        woutc_d = din("woutc", [2 * 4 * 8 * 128, 2048])
        dlam_d = din("dlam", [2, 512])
        dnorm_d = din("dnorm", [2, 256])
    if n_even:
        wina_d = din("wina", [2 * 120 * 128, 2048])
        wouta_d = din("wouta", [2 * 4 * 8 * 128, 2048])
        smask_d = din("smask", [128, L])
        mab_d = din("mab", [128, 1024])
        mtri_d = din("mtri", [128, 256])
        lb_d = din("lbl", [128, 64])
        hnorm_d = din("hnorm", [2, 128])
        wabs_d = din("wabs", [128, 1152])
        mclip_d = din("mclip", [128, 1024])
        sink_d = din("sink", [2, 16])
    out_d = nc.dram_tensor("out", [out_rows, D], F32, kind="ExternalOutput").ap()
    hd = nc.dram_tensor("hd", [L, D], F32, kind="Internal").ap()
    yTd = nc.dram_tensor("yTd", [32, 128, L], BF16, kind="Internal").ap()

    with ExitStack() as es:
        k = K(nc, es)
        ident_f = k.sb([128, 128], F32)
        ident = k.sb([128, 128], BF16)
        wst = [k.sb([128, 2048], F32, name="wst") for _ in range(2)]
        ps = [Buf(es.enter_context(nc.psum_tensor("ps%d" % i, [128, 512], F32))) for i in range(7)]
        pst1 = es.enter_context(nc.psum_tensor("pst", [128, 1024], BF16))
        pstq = [Buf(pst1)]
        PSP = [ps]
        hd_b = [Buf(None) for _ in TT]
        yTd_b = [Buf(None) for _ in range(16)]
        st = {"wst": 0, "wb": 0, "ps": 0, "pst": 0, "uq": 0}

        def nxt(key, lst):
            i = st[key] % len(lst)
            st[key] += 1
            return lst[i]

        epsc = k.sb([128, 4], F32)
        k.op("pool", lambda e: e.memset(epsc[:, 0:1], EPS), writes=[epsc])
        k.op("pool", lambda e: e.memset(epsc[:, 3:4], 1.0), writes=[epsc])
        for wi in range(2):
            k.op("pool", lambda e, wi=wi: e.memset(epsc[:, 1 + wi:2 + wi],
                                                   math.log(1.0 - (0.8 - 0.6 * math.exp(-0.3 * (2 * wi + 1))))),
                 writes=[epsc])
        k.dma("sp", ident_f[:], ident_d[:, :], writes=[ident_f])
        k.op("dve", lambda e: e.tensor_copy(out=ident[:], in_=ident_f[:]), reads=[ident_f], writes=[ident])

        if n_odd:
            lams = k.sb([128, 4], F32)
            lame = k.sb([128, 4], F32)
            neglam = k.sb([128, 2], F32)
            lam_es = ExitStack()
            lamt = k.sb([128, 2, 512], F32, lam_es)
            lamp = k.sb([128, 2, 2, 128], F32, lam_es)
            for i in range(2):
                k.dma("sp", lamt[:, i, :], dlam_d[i:i + 1, :].partition_broadcast(128), writes=[lamt])
            for i in range(2):
                for j in range(2):
                    k.op("dve", lambda e, i=i, j=j: e.tensor_tensor(
                        out=lamp[:, i, j, :], in0=lamt[:, i, 256 * j:256 * j + 128],
                        in1=lamt[:, i, 256 * j + 128:256 * j + 256], op=ALU.mult), reads=[lamt], writes=[lamp])
            for i in range(2):
                for j in range(2):
                    k.op("dve", lambda e, i=i, j=j: e.reduce_sum(
                        out=lams[:, 2 * i + j:2 * i + j + 1], in_=lamp[:, i, j, :], axis=AX.X),
                        reads=[lamp], writes=[lams])
            k.op("act", lambda e: e.activation(out=lame[:], in_=lams[:], func=AF.Exp), reads=[lams], writes=[lame])
            k.barrier()
            lam_es.close()

        def lam_init_of(layer_idx):
            return 0.8 - 0.6 * math.exp(-0.3 * layer_idx)

        def h_src(first, ti):
            s, n = TT[ti]
            if first:
                return (x_d[s:s + n, :] if s < 2048 else meta_d[0:n, :])
            return hd[s:s + n, :]

        def load_w_slice(row0, dst, ls):
            for half in range(2):
                stg = nxt("wst", wst)
                k.dma("sp", stg[:], ls[0][row0 + half * 128:row0 + half * 128 + 128, :], writes=[stg])
                k.op("pool", lambda e, stg=stg, half=half: e.tensor_copy(
                    out=dst[:, 8 * half:8 * half + 8, :],
                    in_=stg[:].rearrange("p (c n) -> p c n", c=8)), reads=[stg], writes=[dst])

        def phase_norm(first, nrow, xnT, ls):
            gt = k.sb([128, D], F32, ls, "gt")
            hb = [k.sb([128, D], F32, ls, "hb") for _ in range(2)]
            xb = [k.sb([128, D], BF16, ls, "xb") for _ in range(2)]
            junk = k.sb([128, D], BF16, ls, "junk")
            ssb = [k.sb([128, 2], F32, ls, "ss") for _ in range(2)]
            k.dma("sp", gt[:], nrm_d[nrow:nrow + 1, :].partition_broadcast(128), writes=[gt])
            for ti, (s, n) in enumerate(TT):
                h = hb[ti % 2]
                xn = xb[ti % 2]
                ss = ssb[ti % 2]
                k.dma("sp", h[0:n, :], h_src(first, ti), reads=[hd_b[ti]], writes=[h])
                k.op("act", lambda e: e.activation(out=junk[0:n, :], in_=h[0:n, :], func=AF.Square,
                                                   accum_out=ss[0:n, 0:1]), reads=[h], writes=[junk, ss])
                k.op("act", lambda e: e.activation(out=ss[0:n, 1:2], in_=ss[0:n, 0:1], func=AF.Ln, scale=1.0 / D,
                                                   bias=epsc[0:n, 0:1]), reads=[ss, epsc], writes=[ss])
                k.op("act", lambda e: e.activation(out=ss[0:n, 0:1], in_=ss[0:n, 1:2], func=AF.Exp, scale=-0.5),
                     reads=[ss], writes=[ss])
                k.op("dve", lambda e: e.scalar_tensor_tensor(out=xn[0:n, :], in0=h[0:n, :], scalar=ss[0:n, 0:1],
                                                             in1=gt[0:n, :], op0=ALU.mult, op1=ALU.mult),
                     reads=[h, ss, gt], writes=[xn])
                for g in range(2):
                    pt = nxt("pst", pstq)
                    k.mms([lambda e, c=c, g=g, pt=pt: e.transpose(
                        out=pt[:, c * 128:c * 128 + n], in_=xn[0:n, (8 * g + c) * 128:(8 * g + c + 1) * 128],
                        identity=ident[0:n, 0:n]) for c in range(8)], reads=[xn, ident], writes=[pt])
                    k.op("act" if g == 0 else "dve", lambda e, g=g, pt=pt: (e.copy if g == 0 else e.tensor_copy)(
                        out=xnT[:, 8 * g:8 * g + 8, s:s + n],
                        in_=pt[:, :].rearrange("p (c t) -> p c t", c=8)[:, :, 0:n]), reads=[pt], writes=[xnT])

        def proj_fm(xnT, w, c0, dst, dj, scale):
            for bi, (s, n) in enumerate(QB):
                p = nxt("ps", PSP[0])
                k.mms([lambda e, c=c, p=p: e.matmul(p[:, 0:n], lhsT=w[:, c, c0:c0 + 128], rhs=xnT[:, c, s:s + n],
                                                    start=(c == 0), stop=(c == 15)) for c in range(16)],
                      reads=[xnT, w], writes=[p])
                if scale is None:
                    k.op("act", lambda e, p=p: e.copy(out=dst[:, dj, s:s + n], in_=p[:, 0:n]), reads=[p], writes=[dst])
                else:
                    k.op("act", lambda e, p=p: e.mul(out=dst[:, dj, s:s + n], in_=p[:, 0:n], mul=scale),
                         reads=[p], writes=[dst])

        def proj_tm(xnT, w, ti, p, ncols=256, c0=0):
            s, n = TT[ti]
            k.mms([lambda e, c=c: e.matmul(p[0:n, 0:ncols], lhsT=xnT[:, c, s:s + n], rhs=w[:, c, c0:c0 + ncols],
                                           start=(c == 0), stop=(c == 15)) for c in range(16)],
                  reads=[xnT, w], writes=[p])

        def silu_from_psum(p, n, ncols, dst_ap, dstbuf, tmpa, tmpb, pc0=0):
            k.op("act", lambda e: e.activation(out=tmpa[0:n, 0:ncols], in_=p[0:n, pc0:pc0 + ncols], func=AF.Exp, scale=-1.0),
                 reads=[p], writes=[tmpa])
            k.op("dve", lambda e: e.tensor_scalar(out=tmpa[0:n, 0:ncols], in0=tmpa[0:n, 0:ncols], scalar1=1.0,
                                                  scalar2=None, op0=ALU.add), reads=[tmpa], writes=[tmpa])
            k.op("dve", lambda e: e.reciprocal(out=tmpb[0:n, 0:ncols], in_=tmpa[0:n, 0:ncols]), reads=[tmpa], writes=[tmpb])
            k.op("dve", lambda e: e.tensor_tensor(out=dst_ap, in0=p[0:n, pc0:pc0 + ncols], in1=tmpb[0:n, 0:ncols],
                                                  op=ALU.mult), reads=[p, tmpb], writes=[dstbuf])

        CT = {}

        def load_consts(ls):
            CT["dlin"] = k.sb([128, 512], F32, ls, "dlin")
            CT["dabs"] = k.sb([128, 896], F32, ls, "dabs")
            CT["cbt"] = k.sb([128, 16 * NDEL], F32, ls, "cbt")
            k.dma("sp", CT["dlin"][:], dlin_d[:, :], writes=[CT["dlin"]])
            k.dma("sp", CT["dabs"][:], dabs_d[:, :], writes=[CT["dabs"]])
            k.dma("sp", CT["cbt"][:], cb_d[:, :], writes=[CT["cbt"]])

        def bias_tile(t0v, N, s0v, kn, slope, h):
            dlin, dabs, cbt = CT["dlin"], CT["dabs"], CT["cbt"]
            if t0v < s0v + kn and s0v < t0v + N:
                off = t0v - s0v
                return dabs[0:kn, 384 + off:384 + off + N], dabs, -slope, None
            if t0v > s0v:
                dl = t0v - s0v
                return dlin[0:kn, 0:N], dlin, -slope, cbt[0:kn, h * NDEL + DELTAS.index(dl):h * NDEL + DELTAS.index(dl) + 1]
            dl = s0v - t0v
            return dlin[0:kn, 0:N], dlin, slope, cbt[0:kn, h * NDEL + DELTAS.index(dl):h * NDEL + DELTAS.index(dl) + 1]

        def odd_mixer(widx, layer_idx, xnT, ls):
            lam_init = lam_init_of(layer_idx)
            sl = slopes16()
            load_consts(ls)
            cbt = CT["cbt"]
            wb = [k.sb([128, 16, 256], BF16, ls, "wb") for _ in range(4)]
            QT = k.sb([128, 2, L], BF16, ls, "QT")
            KT = k.sb([128, 2, L], BF16, ls, "KT")
            Vx = k.sb([128, 17, 257], BF16, ls, "Vx")
            G = k.sb([128, 17, 256], BF16, ls, "G")
            yT = k.sb([128, 2, L], BF16, ls, "yT")
            tmpf = [k.sb([128, 512], F32, ls, "tmpf") for _ in range(4)]
            PT = [k.sb([128, 512], BF16, ls, "PT") for _ in range(4)]
            OT = [k.sb([128, 4, 257], F32, ls, "OT") for _ in range(2)]
            Oq = [[Buf(OT[j_].t[:, q_, :]) for q_ in range(4)] for j_ in range(2)]
            SM = k.sb([128, 4, 8], F32, ls, "SM")
            SS = [k.sb([128, 2], F32, ls, "SS") for _ in range(4)]
            A1 = k.sb([128, 4, 256], F32, ls, "A1")
            A2 = k.sb([128, 4, 256], F32, ls, "A2")
            YB = k.sb([128, 4, 256], BF16, ls, "YB")
            ga = k.sb([128, 256], F32, ls, "ga")
            gb = k.sb([128, 256], F32, ls, "gb")
            gn = k.sb([128, 256], F32, ls, "gn")
            k.dma("sp", gn[:], dnorm_d[widx:widx + 1, :].partition_broadcast(128), writes=[gn])
            k.op("pool", lambda e: e.memset(Vx[:, :, 256:257], 1.0), writes=[Vx])
            k.op("dve", lambda e: e.tensor_tensor(out=neglam[:, widx:widx + 1], in0=lame[:, 2 * widx + 1:2 * widx + 2],
                                                  in1=lame[:, 2 * widx:2 * widx + 1], op=ALU.subtract),
                 reads=[lame], writes=[neglam])
            k.op("dve", lambda e: e.tensor_scalar(out=neglam[:, widx:widx + 1], in0=neglam[:, widx:widx + 1],
                                                  scalar1=-lam_init, scalar2=None, op0=ALU.add),
                 reads=[neglam], writes=[neglam])
            nl = neglam
            pp = 0
            def load_head(h_):
                base_ = ((widx * 16 + h_) * 4) * 256
                for i_ in range(4):
                    load_w_slice(base_ + i_ * 256, wb[i_], [winc_d])
            load_head(0)
            for h in range(16):
                wq, wk, wv, wg = wb[0], wb[1], wb[2], wb[3]
                for j in range(2):
                    proj_fm(xnT, wq, j * 128, QT, j, 128 ** -0.5)
                for j in range(2):
                    proj_fm(xnT, wk, j * 128, KT, j, None)
                for ti, (s, n) in enumerate(TT):
                    p = nxt("ps", PSP[0])
                    proj_tm(xnT, wv, ti, p)
                    k.op("act", lambda e, p=p, ti=ti, n=n: e.copy(out=Vx[0:n, ti, 0:256], in_=p[0:n, 0:256]),
                         reads=[p], writes=[Vx])
                    p = nxt("ps", PSP[0])
                    proj_tm(xnT, wg, ti, p)
                    silu_from_psum(p, n, 256, G[0:n, ti, :], G, ga, gb)
                if h + 1 < 16:
                    load_head(h + 1)
                for (t0, N) in QB:
                    t0v = vidx(t0)
                    nqs = (N + 127) // 128
                    for j in range(2):
                        acc = ps[2:6]
                        pend = []
                        stb = [ps[0], ps[1], ps[6]] if ST3 else [ps[0], ps[1]]
                        for kt, (s0, kn) in enumerate(TT):
                            s0v = vidx(s0)
                            stp = stb[kt % len(stb)]
                            k.mms([lambda e: e.matmul(stp[0:kn, 0:N], lhsT=KT[:, j, s0:s0 + kn], rhs=QT[:, j, t0:t0 + N],
                                                      start=True, stop=True)], reads=[KT, QT], writes=[stp])
                            dt_ap, dbuf, coef, cb = bias_tile(t0v, N, s0v, kn, sl[h], h)
                            tf = tmpf[pp % 4]
                            ptile = PT[pp % 4]
                            pp += 1
                            k.op("dve", lambda e: e.scalar_tensor_tensor(out=tf[0:kn, 0:N], in0=dt_ap, scalar=coef,
                                                                         in1=stp[0:kn, 0:N], op0=ALU.mult, op1=ALU.add),
                                 reads=[dbuf, stp], writes=[tf])
                            if cb is None:
                                k.op("act", lambda e: e.activation(out=ptile[0:kn, 0:N], in_=tf[0:kn, 0:N], func=AF.Exp),
                                     reads=[tf], writes=[ptile])
                            else:
                                k.op("act", lambda e: e.activation(out=ptile[0:kn, 0:N], in_=tf[0:kn, 0:N], func=AF.Exp,
                                                                   bias=cb), reads=[tf, cbt], writes=[ptile])
                            if len(pend) >= LOOK:
                                pend.pop(0)()
                            def pv(kt=kt, kn=kn, ptile=ptile):
                                for qs in range(nqs):
                                    qn = min(128, N - qs * 128)
                                    a = acc[qs]
                                    k.mms([lambda e: e.matmul(a[0:qn, 0:257], lhsT=ptile[0:kn, qs * 128:qs * 128 + qn],
                                                              rhs=Vx[0:kn, kt, :], start=(kt == 0), stop=(kt == 16))],
                                          reads=[ptile, Vx], writes=[a])
                            pend.append(pv)
                        while pend:
                            pend.pop(0)()
                        for qs in range(nqs):
                            qn = min(128, N - qs * 128)
                            k.op("act" if qs % 2 == 0 else "dve",
                                 lambda e, qs=qs, qn=qn: (e.copy if qs % 2 == 0 else e.tensor_copy)(
                                     out=OT[j][0:qn, qs, :], in_=acc[qs][0:qn, 0:257]),
                                 reads=[acc[qs]], writes=[Oq[j][qs]])
                    nq = nqs
                    qn = min(128, N)
                    ti0 = t0 // 128
                    o1, o2 = OT[0].t, OT[1].t
                    rd1, rd2 = list(Oq[0][0:nq]), list(Oq[1][0:nq])
                    SMv = SM.t
                    bc = lambda col: SMv[0:qn, 0:nq, col:col + 1].to_broadcast([qn, nq, 256])
                    k.op("dve", lambda e: e.reciprocal(out=SMv[0:qn, 0:nq, 0:1], in_=o1[0:qn, 0:nq, 256:257]),
                         reads=rd1, writes=[SM])
                    k.op("dve", lambda e: e.reciprocal(out=SMv[0:qn, 0:nq, 1:2], in_=o2[0:qn, 0:nq, 256:257]),
                         reads=rd2, writes=[SM])
                    k.op("dve", lambda e: e.tensor_scalar(out=SMv[0:qn, 0:nq, 2:3], in0=SMv[0:qn, 0:nq, 1:2],
                                                          scalar1=nl[0:qn, widx:widx + 1], scalar2=None, op0=ALU.mult),
                         reads=[SM, nl], writes=[SM])
                    k.op("dve", lambda e: e.tensor_tensor(out=A1[0:qn, 0:nq, :], in0=o1[0:qn, 0:nq, 0:256], in1=bc(0), op=ALU.mult),
                         reads=rd1 + [SM], writes=[A1])
                    k.op("dve", lambda e: e.tensor_tensor(out=A2[0:qn, 0:nq, :], in0=o2[0:qn, 0:nq, 0:256], in1=bc(2), op=ALU.mult),
                         reads=rd2 + [SM], writes=[A2])
                    k.op("dve", lambda e: e.tensor_tensor(out=A2[0:qn, 0:nq, :], in0=A2[0:qn, 0:nq, :], in1=A1[0:qn, 0:nq, :],
                                                          op=ALU.add), reads=[A2, A1], writes=[A2])
                    for qs in range(nq):
                        k.op("act", lambda e, qs=qs: e.activation(out=A1[0:qn, qs, :], in_=A2[0:qn, qs, :], func=AF.Square,
                                                                  accum_out=SS[qs][0:qn, 0:1]), reads=[A2], writes=[SS[qs]])
                        k.op("act", lambda e, qs=qs: e.activation(out=SS[qs][0:qn, 1:2], in_=SS[qs][0:qn, 0:1], func=AF.Ln,
                                                                  scale=1.0 / 256, bias=epsc[0:qn, 0:1]),
                             reads=[SS[qs], epsc], writes=[SS[qs]])
                        k.op("act", lambda e, qs=qs: e.activation(out=SMv[0:qn, qs, 5:6], in_=SS[qs][0:qn, 1:2], func=AF.Exp,
                                                                  scale=-0.5, bias=epsc[0:qn, 1 + widx:2 + widx]),
                             reads=[SS[qs], epsc], writes=[SM])
                    k.op("dve", lambda e: e.tensor_tensor(out=A1[0:qn, 0:nq, :], in0=A2[0:qn, 0:nq, :], in1=bc(5), op=ALU.mult),
                         reads=[A2, SM], writes=[A1])
                    k.op("dve", lambda e: e.tensor_tensor(out=A1[0:qn, 0:nq, :], in0=A1[0:qn, 0:nq, :],
                                                          in1=gn[0:qn, :].unsqueeze(1).to_broadcast([qn, nq, 256]), op=ALU.mult),
                         reads=[A1, gn], writes=[A1])
                    k.op("dve", lambda e: e.tensor_tensor(out=YB[0:qn, 0:nq, :], in0=A1[0:qn, 0:nq, :],
                                                          in1=G[0:qn, ti0:ti0 + nq, :], op=ALU.mult), reads=[A1, G], writes=[YB])
                    pt = nxt("pst", pstq)
                    k.mms([lambda e, c=c, qs=qs: e.transpose(out=pt[:, (c * nq + qs) * 128:(c * nq + qs) * 128 + qn],
                                                             in_=YB[0:qn, qs, c * 128:(c + 1) * 128],
                                                             identity=ident[0:qn, 0:qn]) for c in range(2) for qs in range(nq)],
                          reads=[YB, ident], writes=[pt])
                    wd = (nq - 1) * 128 + qn
                    k.op("act", lambda e: e.copy(out=yT[:, :, t0:t0 + wd],
                                                 in_=pt[:, 0:2 * nq * 128].rearrange("p (c t) -> p c t", c=2)[:, :, 0:wd]),
                         reads=[pt], writes=[yT])
                k.dma("pool", yTd[2 * h:2 * h + 2, :, :].rearrange("c p t -> p c t"), yT[:], reads=[yT], writes=[yTd_b[h]])

        def proj_fm_cb(xnT, w, c0, cb):
            for bi, (s, n) in enumerate(QB):
                p = nxt("ps", PSP[0])
                k.mms([lambda e, c=c, p=p: e.matmul(p[:, 0:n], lhsT=w[:, c, c0:c0 + 128], rhs=xnT[:, c, s:s + n],
                                                    start=(c == 0), stop=(c == 15)) for c in range(16)],
                      reads=[xnT, w], writes=[p])
                cb(p, s, n)

        def load_w128(sidx_row0, dst, c0):
            stg = nxt("wst", wst)
            k.dma("sp", stg[:], wina_d[sidx_row0:sidx_row0 + 128, :], writes=[stg])
            k.op("pool", lambda e: e.tensor_copy(out=dst[:, :, c0:c0 + 128],
                                                 in_=stg[:].rearrange("p (c n) -> p c n", c=16)),
                 reads=[stg], writes=[dst])

        def even_mixer_A(widx, xnT, ls):
            wb = [k.sb([128, 16, 256], BF16, ls, "wb") for _ in range(2)] + [k.sb([128, 16, 128], BF16, ls, "wb")]
            qA = k.sb([128, L], BF16, ls, "qA")
            qB = k.sb([128, L], BF16, ls, "qB")
            T1 = k.sb([128, L], F32, ls, "T1")
            T1b = k.sb([128, L], F32, ls, "T1b")
            T2 = k.sb([128, L], F32, ls, "T2")
            T3 = k.sb([128, L], F32, ls, "T3")
            QEA = k.sb([128, L], BF16, ls, "QEA")
            QEB = k.sb([128, L], BF16, ls, "QEB")
            KE = k.sb([128, L], BF16, ls, "KE")
            KL = k.sb([128, L], BF16, ls, "KL")
            V = k.sb([128, 17, 128], BF16, ls, "V")
            G = k.sb([128, 17, 128], BF16, ls, "G")
            OF = k.sb([128, 17, 128], F32, ls, "OF")
            yT = k.sb([128, L], BF16, ls, "yT")
            smk = k.sb([128, L], BF16, ls, "smk")
            mAB = k.sb([128, 2, 512], BF16, ls, "mAB")
            mtri = k.sb([128, 2, 128], F32, ls, "mtri")
            lbt = k.sb([128, 64], F32, ls, "lbt")
            lbc = k.sb([128, 2, 16], F32, ls, "lbc")
            omc = k.sb([128, 2, 16], F32, ls, "omc")
            gna = k.sb([128, 128], F32, ls, "gna")
            ATm = [k.sb([128, 128], BF16, ls, "ATm") for _ in range(3)]
            KLt = [[k.sb([128, 128], BF16, ls, "KLt") for _ in range(2)] for _ in range(2)]
            S = [k.sb([128, 128], F32, ls, "S") for _ in range(4)]
            SBALL = k.sb([128, 33, 128], BF16, ls, "SBALL")
            ECs = [k.sb([128, 34], F32, ls, "EC") for _ in range(2)]
            k.op("pool", lambda e: e.memset(SBALL[:, 0, :], 0.0), writes=[SBALL])
            ubanks = [ps[4], ps[5], ps[6]]
            PSP[0] = ps[0:4]
            ga = k.sb([128, 128], F32, ls, "ga")
            gb = k.sb([128, 128], F32, ls, "gb")
            sm = [k.sb([128, 8], F32, ls, "sm") for _ in range(2)]
            a1 = [k.sb([128, 128], F32, ls, "a1") for _ in range(2)]
            a2 = [k.sb([128, 128], F32, ls, "a2") for _ in range(2)]
            jk = k.sb([128, 128], F32, ls, "jk")
            ybf = [k.sb([128, 128], BF16, ls, "ybf") for _ in range(2)]
            k.dma("pool", smk[:], smask_d[:, :], writes=[smk])
            k.dma("pool", mAB[:], mab_d[:, :].rearrange("p (a n) -> p a n", a=2), writes=[mAB])
            k.dma("sp", mtri[:], mtri_d[:, :].rearrange("p (a n) -> p a n", a=2), writes=[mtri])
            k.dma("sp", lbt[:], lb_d[:, :], writes=[lbt])
            k.dma("sp", gna[:], hnorm_d[widx:widx + 1, :].partition_broadcast(128), writes=[gna])
            for par in range(2):
                for ab in range(2):
                    k.op("pool", lambda e, par=par, ab=ab: e.memset(KLt[par][ab][:], 0.0), writes=[KLt[par][ab]])
            lb4 = lbt[:].rearrange("p (r l h) -> p r l h", r=2, l=2)
            if widx == 0:
                k.op("pool", lambda e: e.memset(lbc[:], 0.0), writes=[lbc])
                k.op("pool", lambda e: e.memset(omc[:], 1.0), writes=[omc])
            else:
                k.op("dve", lambda e: e.tensor_tensor(out=omc[:], in0=lb4[:, :, 0, :], in1=lb4[:, :, 1, :], op=ALU.subtract),
                     reads=[lbt], writes=[omc])
                k.op("act", lambda e: e.activation(out=omc[:], in_=omc[:], func=AF.Exp), reads=[omc], writes=[omc])
                k.op("dve", lambda e: e.tensor_scalar(out=omc[:], in0=omc[:], scalar1=1.0, scalar2=None, op0=ALU.add),
                     reads=[omc], writes=[omc])
                k.op("dve", lambda e: e.reciprocal(out=lbc[:], in_=omc[:]), reads=[omc], writes=[lbc])
                k.op("dve", lambda e: e.tensor_scalar(out=omc[:], in0=lbc[:], scalar1=-1.0, scalar2=1.0, op0=ALU.mult,
                                                      op1=ALU.add), reads=[lbc], writes=[omc])
            for h in range(16):
                wq, wzf, wzb, wv, wg = (wb[0], 0), (wb[0], 128), (wb[2], 0), (wb[1], 0), (wb[1], 128)

                def load_head(h_):
                    for (wt, c0), si in ((wzf, 16 + h_), (wzb, 32 + h_), (wq, h_), (wv, 48 + h_), (wg, 64 + h_)):
                        load_w128((widx * 120 + si) * 128, wt, c0)
                if h == 0:
                    load_head(0)
                for di_ in range(2):
                    wz_ = wzf if di_ == 0 else wzb
                    Te = T1 if di_ == 0 else T1b

                    def cbz(p, s, n, Te=Te):
                        k.op("act", lambda e: e.activation(out=Te[:, s:s + n], in_=p[:, 0:n], func=AF.Exp, scale=-1.0),
                             reads=[p], writes=[Te])
                    proj_fm_cb(xnT, wz_[0], wz_[1], cbz)
                def cbq(p, s, n):
                    k.op("dve", lambda e: e.tensor_tensor(out=qA[:, s:s + n], in0=p[:, 0:n], in1=mAB[:, 0, 0:n], op=ALU.mult),
                         reads=[p, mAB], writes=[qA])
                    k.op("dve", lambda e: e.tensor_tensor(out=qB[:, s:s + n], in0=p[:, 0:n], in1=mAB[:, 1, 0:n], op=ALU.mult),
                         reads=[p, mAB], writes=[qB])
                proj_fm_cb(xnT, wq[0], wq[1], cbq)
                for ti, (s, n) in enumerate(TT):
                    p = nxt("ps", PSP[0])
                    proj_tm(xnT, wb[1], ti, p, 256, 0)
                    k.op("act", lambda e: e.copy(out=V[0:n, ti, :], in_=p[0:n, 0:128]), reads=[p], writes=[V])
                    silu_from_psum(p, n, 128, G[0:n, ti, :], G, ga, gb, 128)
                if h + 1 < 16:
                    load_head(h + 1)
                T1f = T1
                for di in range(2):
                    T1 = T1f if di == 0 else T1b
                    k.op("act", lambda e: e.activation(out=T2[:], in_=T1[:], func=AF.Ln, bias=epsc[:, 3:4]),
                         reads=[T1, epsc], writes=[T2])
                    if widx == 0:
                        k.op("dve", lambda e: e.tensor_scalar(out=T3[:], in0=T2[:], scalar1=-1.0, scalar2=None, op0=ALU.mult),
                             reads=[T2], writes=[T3])
                    else:
                        k.op("act", lambda e: e.activation(out=T3[:], in_=T1[:], func=AF.Ln, scale=lbc[:, di, h:h + 1],
                                                           bias=epsc[:, 3:4]), reads=[T1, lbc, epsc], writes=[T3])
                        k.op("dve", lambda e: e.tensor_tensor(out=T3[:], in0=T3[:], in1=T2[:], op=ALU.subtract),
                             reads=[T3, T2], writes=[T3])
                    k.op("act", lambda e: e.activation(out=T2[:], in_=T2[:], func=AF.Exp, scale=-1.0), reads=[T2], writes=[T2])
                    k.op("dve", lambda e: e.scalar_tensor_tensor(out=T1[:], in0=T1[:], scalar=omc[:, di, h:h + 1], in1=T2[:],
                                                                 op0=ALU.mult, op1=ALU.mult), reads=[T1, omc, T2], writes=[T1])
                    EC = ECs[di]
                    xv = lambda t_: t_[:, 0:2048].rearrange("p (c t) -> p c t", t=64)
                    if di == 0:
                        k.op("dve", lambda e: e.tensor_tensor_scan(out=T2[:], data0=smk[:], data1=T3[:], initial=0.0,
                                                                   op0=ALU.mult, op1=ALU.add), reads=[smk, T3], writes=[T2])
                        k.op("act", lambda e: e.activation(out=T3[:], in_=T2[:], func=AF.Exp), reads=[T2], writes=[T3])
                        k.op("pool", lambda e: e.tensor_copy(out=EC[:, 0:32], in_=xv(T3)[:, :, 63]), reads=[T3], writes=[EC])
                        k.op("pool", lambda e: e.tensor_copy(out=EC[:, 32:33], in_=T3[:, 2063:2064]), reads=[T3], writes=[EC])
                        k.op("act", lambda e: e.activation(out=T2[:], in_=T2[:], func=AF.Exp, scale=-1.0),
                             reads=[T2], writes=[T2])
                        k.op("dve", lambda e: e.tensor_tensor(out=QEA[:], in0=qA[:], in1=T3[:], op=ALU.mult),
                             reads=[qA, T3], writes=[QEA])
                        k.op("dve", lambda e: e.tensor_tensor(out=QEB[:], in0=qB[:], in1=T3[:], op=ALU.mult),
                             reads=[qB, T3], writes=[QEB])
                        k.op("dve", lambda e: e.tensor_tensor(out=KE[:], in0=T1[:], in1=T2[:], op=ALU.mult),
                             reads=[T1, T2], writes=[KE])
                        k.op("dve", lambda e: e.tensor_tensor(
                            out=xv(KL), in0=xv(KE), in1=xv(T3)[:, :, 63:64].to_broadcast([128, 32, 64]), op=ALU.mult),
                            reads=[KE, T3], writes=[KL])
                        k.op("dve", lambda e: e.tensor_tensor(out=KL[:, 2048:2064], in0=KE[:, 2048:2064],
                                                              in1=T3[:, 2063:2064].to_broadcast([128, 16]), op=ALU.mult),
                             reads=[KE, T3], writes=[KL])
                        KLs = KL
                    else:
                        k.op("pool", lambda e: e.memset(T2[:, 0:1], 0.0), writes=[T2])
                        k.op("dve", lambda e: e.tensor_tensor_scan(out=T2[:, 1:L], data0=T3[:, 0:L - 1], data1=smk[:, 1:L],
                                                                   initial=0.0, op0=ALU.add, op1=ALU.mult),
                             reads=[smk, T3], writes=[T2])
                        k.op("dve", lambda e: e.tensor_tensor(out=EC[:, 0:32], in0=xv(T2)[:, :, 63], in1=xv(T3)[:, :, 63],
                                                              op=ALU.add), reads=[T2, T3], writes=[EC])
                        k.op("dve", lambda e: e.tensor_tensor(out=EC[:, 32:33], in0=T2[:, 2063:2064], in1=T3[:, 2063:2064],
                                                              op=ALU.add), reads=[T2, T3], writes=[EC])
                        k.op("act", lambda e: e.activation(out=EC[:, 0:33], in_=EC[:, 0:33], func=AF.Exp), reads=[EC], writes=[EC])
                        k.op("act", lambda e: e.activation(out=T3[:], in_=T2[:], func=AF.Exp, scale=-1.0),
                             reads=[T2], writes=[T3])
                        k.op("act", lambda e: e.activation(out=T2[:], in_=T2[:], func=AF.Exp), reads=[T2], writes=[T2])
                        k.op("dve", lambda e: e.tensor_tensor(out=QEA[:], in0=qA[:], in1=T3[:], op=ALU.mult),
                             reads=[qA, T3], writes=[QEA])
                        k.op("dve", lambda e: e.tensor_tensor(out=QEB[:], in0=qB[:], in1=T3[:], op=ALU.mult),
                             reads=[qB, T3], writes=[QEB])
                        k.op("dve", lambda e: e.tensor_tensor(out=KE[:], in0=T1[:], in1=T2[:], op=ALU.mult),
                             reads=[T1, T2], writes=[KE])
                        KLs = KE
                    if di == 0:
                        seq = [(16, 0)] + [(t_, ab_) for t_ in range(16) for ab_ in (0, 1)]
                    else:
                        seq = [(t_, ab_) for t_ in range(15, -1, -1) for ab_ in (1, 0)] + [(16, 0)]
                    cids = [32 if t_ == 16 else 2 * t_ + ab_ for (t_, ab_) in seq]
                    order = [16] + list(range(16)) if di == 0 else list(range(15, -1, -1)) + [16]
                    seqidx = {}
                    k.op("pool", lambda e: e.memset(S[0][:], 0.0), writes=[S[0]])
                    if di == 0:
                        groups = [[16]] + [[2 * g_, 2 * g_ + 1] for g_ in range(8)]
                    else:
                        groups = [[15 - 2 * g_, 14 - 2 * g_] for g_ in range(8)] + [[16]]
                    step = 0
                    tcount = 0
                    for gt in groups:
                        bank = ubanks[st["uq"] % 3]
                        st["uq"] += 1
                        slots = []
                        for ti in gt:
                            s, n = TT[ti]
                            par = tcount % 2
                            tcount += 1
                            pt = nxt("pst", pstq)
                            k.mms([lambda e: e.transpose(out=pt[0:n, 0:128], in_=KLs[:, s:s + n], identity=ident[:, :])],
                                  reads=[KLs, ident], writes=[pt])
                            nA = min(n, 64)
                            k.op("act", lambda e: e.copy(out=KLt[par][0][0:nA, :], in_=pt[0:nA, 0:128]), reads=[pt],
                                 writes=[KLt[par][0]])
                            if n == 128:
                                k.op("act", lambda e: e.copy(out=KLt[par][1][64:128, :], in_=pt[64:128, 0:128]), reads=[pt],
                                     writes=[KLt[par][1]])
                            chunks = [0] if n == 16 else ([0, 1] if di == 0 else [1, 0])
                            for ab in chunks:
                                cid = 32 if ti == 16 else 2 * ti + ab
                                q_ = len(slots)
                                seqidx[(ti, ab)] = step + q_
                                k.mms([lambda e: e.matmul(bank[:, q_ * 128:(q_ + 1) * 128], lhsT=KLt[par][ab][0:n, :],
                                                          rhs=V[0:n, ti, :], start=True, stop=True)],
                                      reads=[KLt[par][ab], V], writes=[bank])
                                slots.append((q_, cid))
                        for (q_, cid) in slots:
                            if step < 32:
                                so, sn = S[step % 4], S[(step + 1) % 4]
                                k.op("dve", lambda e: e.scalar_tensor_tensor(out=sn[:], in0=so[:], scalar=EC[:, cid:cid + 1],
                                                                             in1=bank[:, q_ * 128:(q_ + 1) * 128],
                                                                             op0=ALU.mult, op1=ALU.add),
                                     reads=[so, EC, bank], writes=[sn])
                                if di == 0:
                                    k.op("pool", lambda e: e.tensor_copy(out=SBALL[:, step + 1, :], in_=sn[:]), reads=[sn],
                                         writes=[SBALL])
                                else:
                                    cn = cids[step + 1]
                                    k.op("pool", lambda e: e.tensor_tensor(out=SBALL[:, step + 1, :], in0=sn[:],
                                                                           in1=EC[:, cn:cn + 1].to_broadcast([128, 128]),
                                                                           op=ALU.mult), reads=[sn, EC], writes=[SBALL])
                            step += 1
                    def at_stage(oi, ti):
                        s, n = TT[ti]
                        pa = nxt("ps", PSP[0])
                        k.mms([lambda e: e.matmul(pa[0:n, 0:n], lhsT=KE[:, s:s + n], rhs=QEA[:, s:s + n], start=True, stop=False),
                               lambda e: e.matmul(pa[0:n, 0:n], lhsT=KE[:, s:s + n], rhs=QEB[:, s:s + n], start=False, stop=True)],
                              reads=[KE, QEA, QEB], writes=[pa])
                        at = ATm[oi % 3]
                        k.op("dve", lambda e: e.tensor_tensor(out=at[0:n, 0:n], in0=pa[0:n, 0:n], in1=mtri[0:n, di, 0:n],
                                                              op=ALU.mult), reads=[pa, mtri], writes=[at])
                    at_stage(0, order[0])
                    pendB, pendC = [], []
                    for oi, ti in enumerate(order):
                        s, n = TT[ti]
                        if oi + 1 < len(order):
                            at_stage(oi + 1, order[oi + 1])
                        at = ATm[oi % 3]
                        chunks = [0] if n == 16 else ([0, 1] if di == 0 else [1, 0])
                        po = nxt("ps", PSP[0])
                        fns = [lambda e: e.matmul(po[0:n, 0:128], lhsT=at[0:n, 0:n], rhs=V[0:n, ti, :], start=True, stop=False)]
                        for ci, ab in enumerate(chunks):
                            qe = QEA if ab == 0 else QEB
                            sbi = seqidx[(ti, ab)]
                            fns.append(lambda e, qe=qe, sbi=sbi, last=(ci == len(chunks) - 1): e.matmul(
                                po[0:n, 0:128], lhsT=qe[:, s:s + n], rhs=SBALL[:, sbi, :], start=False, stop=last))
                        k.mms(fns, reads=[at, V, QEA, QEB, SBALL], writes=[po])
                        if di == 0:
                            k.op("act", lambda e: e.copy(out=OF[0:n, ti, :], in_=po[0:n, 0:128]), reads=[po], writes=[OF])
                        else:
                            s_ = sm[oi % 2]
                            x1, x2, yb = a1[oi % 2], a2[oi % 2], ybf[oi % 2]
                            k.op("dve", lambda e: e.tensor_tensor(out=x1[0:n, :], in0=po[0:n, 0:128], in1=OF[0:n, ti, :],
                                                                  op=ALU.add), reads=[po, OF], writes=[x1])
                            k.op("act", lambda e: e.activation(out=jk[0:n, :], in_=x1[0:n, :], func=AF.Square,
                                                               accum_out=s_[0:n, 0:1]), reads=[x1], writes=[jk, s_])
                            k.op("act", lambda e: e.activation(out=s_[0:n, 1:2], in_=s_[0:n, 0:1], func=AF.Ln,
                                                               scale=1.0 / 128, bias=epsc[0:n, 0:1]),
                                 reads=[s_, epsc], writes=[s_])
                            k.op("act", lambda e: e.activation(out=s_[0:n, 2:3], in_=s_[0:n, 1:2], func=AF.Exp, scale=-0.5),
                                 reads=[s_], writes=[s_])
                            def stB(s=s, n=n, ti=ti, s_=s_, x1=x1, x2=x2, yb=yb):
                                k.op("dve", lambda e: e.scalar_tensor_tensor(out=x2[0:n, :], in0=x1[0:n, :], scalar=s_[0:n, 2:3],
                                                                             in1=gna[0:n, :], op0=ALU.mult, op1=ALU.mult),
                                     reads=[x1, s_, gna], writes=[x2])
                                k.op("dve", lambda e: e.tensor_tensor(out=yb[0:n, :], in0=x2[0:n, :], in1=G[0:n, ti, :],
                                                                       op=ALU.mult), reads=[x2, G], writes=[yb])

                                def stC():
                                    pt2 = nxt("pst", pstq)
                                    k.mms([lambda e: e.transpose(out=pt2[:, 0:n], in_=yb[0:n, :], identity=ident[0:n, 0:n])],
                                          reads=[yb, ident], writes=[pt2])
                                    k.op("act", lambda e: e.copy(out=yT[:, s:s + n], in_=pt2[:, 0:n]), reads=[pt2], writes=[yT])
                                pendC.append(stC)
                            runB, runC = pendB[:], pendC[:]
                            del pendB[:]
                            del pendC[:]
                            for f_ in runB:
                                f_()
                            for f_ in runC:
                                f_()
                            pendB.append(stB)
                    for f_ in pendB[:]:
                        f_()
                    for f_ in pendC[:]:
                        f_()
                T1 = T1f
                k.dma("pool", yTd[h, :, :], yT[:], reads=[yT], writes=[yTd_b[h]])
            PSP[0] = ps

        def even_mixer_B(widx, xnT, ls):
            sl = slopes16()
            load_consts(ls)
            dabs = CT["dabs"]
            wb = [k.sb([128, 16, 256], BF16, ls, "wb") for _ in range(3)]
            QT = k.sb([128, 1, L], BF16, ls, "QT")
            KT = k.sb([128, 1, L], BF16, ls, "KT")
            Vx = k.sb([128, 17, 129], BF16, ls, "Vx")
            G = k.sb([128, 17, 128], BF16, ls, "G")
            yT = k.sb([128, L], BF16, ls, "yT")
            wabs = k.sb([128, 1152], F32, ls, "wabs")
            mclip = k.sb([128, 1024], F32, ls, "mclip")
            sink = k.sb([128, 16], F32, ls, "sink")
            esink = k.sb([128, 16], F32, ls, "esink")
            tmpf = [k.sb([128, 512], F32, ls, "tmpf") for _ in range(4)]
            PT = [k.sb([128, 512], BF16, ls, "PT") for _ in range(4)]
            OT = k.sb([128, 4, 129], F32, ls, "OT")
            Oq = [Buf(OT.t[:, q_, :]) for q_ in range(4)]
            SM = k.sb([128, 4, 4], F32, ls, "SM")
            A1 = k.sb([128, 4, 128], F32, ls, "A1")
            YB = k.sb([128, 4, 128], BF16, ls, "YB")
            ga = k.sb([128, 128], F32, ls, "ga")
            gb = k.sb([128, 128], F32, ls, "gb")
            sm = [k.sb([128, 4], F32, ls, "sm") for _ in range(2)]
            a1 = [k.sb([128, 128], F32, ls, "a1") for _ in range(2)]
            ybf = [k.sb([128, 128], BF16, ls, "ybf") for _ in range(2)]
            k.dma("sp", wabs[:], wabs_d[:, :], writes=[wabs])
            k.dma("sp", mclip[:], mclip_d[:, :], writes=[mclip])
            k.dma("sp", sink[:], sink_d[widx:widx + 1, :].partition_broadcast(128), writes=[sink])
            k.op("act", lambda e: e.activation(out=esink[:], in_=sink[:], func=AF.Exp), reads=[sink], writes=[esink])
            k.op("pool", lambda e: e.memset(Vx[:, :, 128:129], 1.0), writes=[Vx])
            pp = 0
            def load_kv(kv_):
                load_w128((widx * 120 + 96 + kv_) * 128, wb[0], 0)
                load_w128((widx * 120 + 100 + kv_) * 128, wb[0], 128)

            def load_qg(hq_):
                load_w128((widx * 120 + 80 + hq_) * 128, wb[1 + hq_ % 2], 0)
                load_w128((widx * 120 + 104 + hq_) * 128, wb[1 + hq_ % 2], 128)
            load_kv(0)
            load_qg(0)
            for hq in range(16):
                kv = hq // 4
                if hq % 4 == 0:
                    wk, wv = (wb[0], 0), (wb[0], 128)
                    proj_fm(xnT, wk[0], wk[1], KT, 0, None)
                    for ti, (s, n) in enumerate(TT):
                        p = nxt("ps", PSP[0])
                        proj_tm(xnT, wv[0], ti, p, 128, wv[1])
                        k.op("act", lambda e: e.copy(out=Vx[0:n, ti, 0:128], in_=p[0:n, 0:128]), reads=[p], writes=[Vx])
                wq, wg = (wb[1 + hq % 2], 0), (wb[1 + hq % 2], 128)
                if hq + 1 < 16:
                    load_qg(hq + 1)
                    if (hq + 1) % 4 == 0:
                        load_kv((hq + 1) // 4)
                proj_fm(xnT, wq[0], wq[1], QT, 0, 128 ** -0.5)
                for ti, (s, n) in enumerate(TT):
                    p = nxt("ps", PSP[0])
                    proj_tm(xnT, wg[0], ti, p, 128, wg[1])
                    silu_from_psum(p, n, 128, G[0:n, ti, :], G, ga, gb)
                for (t0, N) in QB:
                    t0v = vidx(t0)
                    nqs = (N + 127) // 128
                    acc = ps[2:6]
                    if t0 < 2048:
                        xt = [s0 for s0 in range(t0 - 128, t0 + N + 1, 128) if 0 <= s0 <= 1920]
                    else:
                        xt = [0]
                    ktl = [(2048, 16)] + [(s0, 128) for s0 in xt]
                    pend = []
                    stb = [ps[0], ps[1], ps[6]] if ST3 else [ps[0], ps[1]]
                    for ki, (s0, kn) in enumerate(ktl):
                        s0v = vidx(s0)
                        kt = s0 // 128
                        stp = stb[ki % len(stb)]
                        k.mms([lambda e: e.matmul(stp[0:kn, 0:N], lhsT=KT[:, 0, s0:s0 + kn], rhs=QT[:, 0, t0:t0 + N],
                                                  start=True, stop=True)], reads=[KT, QT], writes=[stp])
                        if s0 == 2048 and t0 < 2048:
                            c0 = min(t0, 512)
                            dt_ap, dbuf = mclip[0:16, c0:c0 + N], mclip
                        elif s0 == 2048:
                            dt_ap, dbuf = dabs[0:16, 384:384 + N], dabs
                        else:
                            off = t0v - s0v
                            dt_ap, dbuf = wabs[0:kn, 512 + off:512 + off + N], wabs
                        tf = tmpf[pp % 4]
                        ptile = PT[pp % 4]
                        pp += 1
                        k.op("dve", lambda e: e.scalar_tensor_tensor(out=tf[0:kn, 0:N], in0=dt_ap, scalar=-sl[hq],
                                                                     in1=stp[0:kn, 0:N], op0=ALU.mult, op1=ALU.add),
                             reads=[dbuf, stp], writes=[tf])
                        k.op("act", lambda e: e.activation(out=ptile[0:kn, 0:N], in_=tf[0:kn, 0:N], func=AF.Exp),
                             reads=[tf], writes=[ptile])
                        if len(pend) >= LOOK:
                            pend.pop(0)()
                        def pv(s0=s0, kn=kn, kt=kt, ptile=ptile):
                            for qs in range(nqs):
                                qn = min(128, N - qs * 128)
                                tok0 = t0 + qs * 128
                                if t0 < 2048:
                                    rel = [s_ for s_ in (tok0 - 128, tok0, tok0 + 128) if 0 <= s_ <= 1920]
                                else:
                                    rel = [0]
                                if s0 != 2048 and s0 not in rel:
                                    continue
                                a = acc[qs]
                                k.mms([lambda e: e.matmul(a[0:qn, 0:129], lhsT=ptile[0:kn, qs * 128:qs * 128 + qn],
                                                          rhs=Vx[0:kn, kt, :], start=(s0 == 2048), stop=(s0 == rel[-1]))],
                                      reads=[ptile, Vx], writes=[a])
                        pend.append(pv)
                    while pend:
                        pend.pop(0)()
                    nq = nqs
                    qn = min(128, N)
                    ti0 = t0 // 128
                    for qs in range(nq):
                        k.op("act" if qs % 2 == 0 else "dve",
                             lambda e, qs=qs: (e.copy if qs % 2 == 0 else e.tensor_copy)(
                                 out=OT[0:qn, qs, :], in_=acc[qs][0:qn, 0:129]), reads=[acc[qs]], writes=[Oq[qs]])
                    rdo = list(Oq[0:nq])
                    ot = OT.t
                    SMv = SM.t
                    k.op("dve", lambda e: e.tensor_scalar(out=SMv[0:qn, 0:nq, 0:1], in0=ot[0:qn, 0:nq, 128:129],
                                                          scalar1=esink[0:qn, hq:hq + 1], scalar2=None, op0=ALU.add),
                         reads=rdo + [esink], writes=[SM])
                    k.op("dve", lambda e: e.reciprocal(out=SMv[0:qn, 0:nq, 1:2], in_=SMv[0:qn, 0:nq, 0:1]), reads=[SM], writes=[SM])
                    k.op("dve", lambda e: e.tensor_tensor(out=A1[0:qn, 0:nq, :], in0=ot[0:qn, 0:nq, 0:128],
                                                          in1=SMv[0:qn, 0:nq, 1:2].to_broadcast([qn, nq, 128]), op=ALU.mult),
                         reads=rdo + [SM], writes=[A1])
                    k.op("dve", lambda e: e.tensor_tensor(out=YB[0:qn, 0:nq, :], in0=A1[0:qn, 0:nq, :],
                                                          in1=G[0:qn, ti0:ti0 + nq, :], op=ALU.mult), reads=[A1, G], writes=[YB])
                    pt = nxt("pst", pstq)
                    k.mms([lambda e, qs=qs: e.transpose(out=pt[:, qs * 128:qs * 128 + qn], in_=YB[0:qn, qs, :],
                                                        identity=ident[0:qn, 0:qn]) for qs in range(nq)],
                          reads=[YB, ident], writes=[pt])
                    wd = (nq - 1) * 128 + qn
                    k.op("act", lambda e: e.copy(out=yT[:, t0:t0 + wd], in_=pt[:, 0:wd]), reads=[pt], writes=[yT])
                k.dma("pool", yTd[16 + hq, :, :], yT[:], reads=[yT], writes=[yTd_b[hq]])

        def phase_out(first, wout_d, widx, ls):
            wob = [k.sb([128, 32, 512], BF16, ls, "wob") for _ in range(2)]
            yb = [k.sb([128, 32, 512], BF16, ls, "ytb") for _ in range(2)]
            hb = [k.sb([128, 512], F32, ls, "hb") for _ in range(4)]
            ho = [k.sb([128, 512], F32, ls, "ho") for _ in range(4)]
            cnt = 0
            for nb in range(4):
                wo = wob[nb % 2]
                for pc in range(8):
                    stg = nxt("wst", wst)
                    r0 = ((widx * 4 + nb) * 8 + pc) * 128
                    k.dma("sp", stg[:], wout_d[r0:r0 + 128, :], writes=[stg])
                    k.op("pool", lambda e, stg=stg, pc=pc: e.tensor_copy(
                        out=wo[:, 4 * pc:4 * pc + 4, :], in_=stg[:].rearrange("p (c n) -> p c n", c=4)),
                        reads=[stg], writes=[wo])
                for bi, (t0, N) in enumerate(QB):
                    y = yb[bi % 2]
                    k.dma("sp", y[:, :, 0:N], yTd[:, :, t0:t0 + N].rearrange("c p t -> p c t"),
                          reads=yTd_b, writes=[y])
                    for qs in range((N + 127) // 128):
                        qn = min(128, N - qs * 128)
                        tok0 = t0 + qs * 128
                        ti = tok0 // 128
                        p = nxt("ps", PSP[0])
                        k.mms([lambda e, c=c: e.matmul(p[0:qn, :], lhsT=y[:, c, qs * 128:qs * 128 + qn], rhs=wo[:, c, :],
                                                       start=(c == 0), stop=(c == 31)) for c in range(32)],
                              reads=[y, wo], writes=[p])
                        hi = hb[cnt % 4]
                        hn = ho[cnt % 4]
                        cnt += 1
                        src = h_src(first, ti)
                        k.dma("sp", hi[0:qn, :], src[:, nb * 512:(nb + 1) * 512], reads=[hd_b[ti]], writes=[hi])
                        k.op("dve", lambda e: e.tensor_tensor(out=hn[0:qn, :], in0=p[0:qn, :], in1=hi[0:qn, :], op=ALU.add),
                             reads=[p, hi], writes=[hn])
                        k.dma("pool", hd[tok0:tok0 + qn, nb * 512:(nb + 1) * 512], hn[0:qn, :], reads=[hn],
                              writes=[hd_b[ti]])

        def phase_final(first, ls):
            gt = k.sb([128, D], F32, ls, "gt")
            hb = [k.sb([128, D], F32, ls, "hb") for _ in range(2)]
            ob = [k.sb([128, D], F32, ls, "ob") for _ in range(2)]
            junk = k.sb([128, D], BF16, ls, "junk")
            ssb = [k.sb([128, 2], F32, ls, "ss") for _ in range(2)]
            k.dma("sp", gt[:], nrm_d[4:5, :].partition_broadcast(128), writes=[gt])
            toks = []
            for ti, (s, n) in enumerate(TT[:16]):
                h = hb[ti % 2]
                o = ob[ti % 2]
                ss = ssb[ti % 2]
                k.dma("sp", h[0:n, :], h_src(first, ti), reads=[hd_b[ti]], writes=[h])
                k.op("act", lambda e: e.activation(out=junk[0:n, :], in_=h[0:n, :], func=AF.Square,
                                                   accum_out=ss[0:n, 0:1]), reads=[h], writes=[junk, ss])
                k.op("act", lambda e: e.activation(out=ss[0:n, 1:2], in_=ss[0:n, 0:1], func=AF.Ln, scale=1.0 / D,
                                                   bias=epsc[0:n, 0:1]), reads=[ss, epsc], writes=[ss])
                k.op("act", lambda e: e.activation(out=ss[0:n, 0:1], in_=ss[0:n, 1:2], func=AF.Exp, scale=-0.5),
                     reads=[ss], writes=[ss])
                k.op("dve", lambda e: e.scalar_tensor_tensor(out=o[0:n, :], in0=h[0:n, :], scalar=ss[0:n, 0:1],
                                                             in1=gt[0:n, :], op0=ALU.mult, op1=ALU.mult),
                     reads=[h, ss, gt], writes=[o])
                toks.append(k.dma("pool", out_d[s:s + n, :], o[0:n, :], reads=[o]))
            return toks

        first = True
        for (kind, widx, layer_idx) in layers:
            with ExitStack() as ls:
                xnT = k.sb([128, 16, L], BF16, ls, "xnT")
                with ExitStack() as ls2:
                    phase_norm(first, (0 if kind == "E" else 2) + widx, xnT, ls2)
                    k.barrier()
                with ExitStack() as ls2:
                    if kind == "O":
                        odd_mixer(widx, layer_idx, xnT, ls2)
                    else:
                        even_mixer_A(widx, xnT, ls2)
                    k.barrier()
                if kind == "E":
                    with ExitStack() as ls2:
                        even_mixer_B(widx, xnT, ls2)
                        k.barrier()
            with ExitStack() as ls:
                phase_out(first, woutc_d if kind == "O" else wouta_d, widx, ls)
                k.barrier()
            first = False
        out_toks = []
        with ExitStack() as ls:
            if do_final:
                out_toks = phase_final(first, ls)
            else:
                hb = [k.sb([128, D], F32, ls, "hb") for _ in range(2)]
                for ti, (s, n) in enumerate(TT):
                    h = hb[ti % 2]
                    k.dma("sp", h[0:n, :], h_src(first, ti), reads=[hd_b[ti]], writes=[h])
                    out_toks.append(k.dma("pool", out_d[s:s + n, :], h[0:n, :], reads=[h]))
            k.barrier()
        k.check_deadlock()
    return nc


def const_tables():
    i = np.arange(128, dtype=np.float32)[:, None]
    ident = np.eye(128, dtype=np.float32)
    dlin = (np.arange(512, dtype=np.float32)[None, :] - i).astype(np.float32)
    dabs = np.abs(np.arange(896, dtype=np.float32)[None, :] - i - 384).astype(np.float32)
    sl = np.array(slopes16(), dtype=np.float64)
    cb = (-(sl[:, None] * np.array(DELTAS, dtype=np.float64)[None, :])).reshape(1, -1)
    cb = np.repeat(cb, 128, axis=0).astype(np.float32)
    return {"ident": ident, "dlin": dlin, "dabs": dabs, "cbtab": np.ascontiguousarray(cb)}


def layout_winc(w):
    a = w.reshape(2, 2, 8, 128, 4, 16, 256)
    a = a.transpose(0, 5, 4, 1, 3, 2, 6)
    return np.ascontiguousarray(a).reshape(2 * 16 * 4 * 2 * 128, 2048)


def layout_wina(w):
    a = w.reshape(2, 16, 128, 120, 128)
    a = a.transpose(0, 3, 2, 1, 4)
    return np.ascontiguousarray(a).reshape(2 * 120 * 128, 2048)


def even_tables():
    i = np.arange(128, dtype=np.float32)[:, None]
    smask = np.ones((128, L), np.float32)
    smask[:, 0:2048:64] = 0.0
    smask[:, 2048] = 0.0
    j = np.arange(512)
    mA = ((j % 128) < 64).astype(np.float32)
    mab = np.concatenate([np.tile(mA[None], (128, 1)), np.tile((1 - mA)[None], (128, 1))], axis=1)
    s_ = np.arange(128)[:, None]; t_ = np.arange(128)[None, :]
    same = (s_ // 64) == (t_ // 64)
    mf = (same & (s_ <= t_)).astype(np.float32)
    mb_ = (same & (s_ >= t_)).astype(np.float32)
    mtri = np.concatenate([mf, mb_], axis=1)
    dw = np.abs(np.arange(1152, dtype=np.float32)[None, :] - i - 512)
    wabs = np.where(dw <= 128, dw, 1e9).astype(np.float32)
    mclip = np.minimum(np.arange(1024, dtype=np.float32)[None, :] - i + 16, 128.0).astype(np.float32)
    return {"smask": smask, "mab": np.ascontiguousarray(mab), "mtri": np.ascontiguousarray(mtri),
            "wabs": wabs, "mclip": mclip}


def layout_wout(w):
    a = w.reshape(2, 8, 4, 128, 4, 512)
    a = a.transpose(0, 4, 1, 3, 2, 5)
    return np.ascontiguousarray(a).reshape(2 * 4 * 8 * 128, 2048)


def run_layers(layers, do_final, inputs, ncores=8):
    out_rows = NX if do_final else L
    nc = build(layers, do_final, out_rows)
    f = lambda a: np.ascontiguousarray(np.asarray(a, dtype=np.float32))
    shared = dict(const_tables())
    shared["meta"] = f(inputs["meta_tokens"])
    shared["norms"] = np.concatenate([f(inputs["norm_a"]), f(inputs["norm_c"]), f(inputs["final_norm"])[None]], axis=0)
    if any(l[0] == "E" for l in layers):
        shared.update(even_tables())
        shared["wina"] = layout_wina(f(inputs["w_in_a"]))
        shared["wouta"] = layout_wout(f(inputs["w_out_a"]))
        shared["lbl"] = np.ascontiguousarray(f(inputs["hgrn_lb"]).reshape(2, 2, 16, 128).transpose(3, 0, 1, 2)).reshape(128, 64)
        shared["hnorm"] = f(inputs["hgrn_norm"])
        shared["sink"] = f(inputs["sink_logits"])
    if any(l[0] == "O" for l in layers):
        shared["winc"] = layout_winc(f(inputs["w_in_c"]))
        shared["woutc"] = layout_wout(f(inputs["w_out_c"]))
        shared["dlam"] = f(inputs["diff_lambda"]).reshape(2, 512)
        shared["dnorm"] = f(inputs["diff_norm"])
    x = f(inputs["x"])
    in_maps = []
    for b in range(ncores):
        m = dict(shared)
        m["x"] = x[b]
        in_maps.append(m)
    res = run_bass_kernel_spmd(nc, in_maps, core_ids=list(range(ncores)))
    return np.stack([r["out"] for r in res.results], axis=0)


def kernel(**inputs):
    layers = [("E", 0, 0), ("O", 0, 1), ("E", 1, 2), ("O", 1, 3)]
    return run_layers(layers, True, inputs)
```

```python
import math
from contextlib import ExitStack
import numpy as np
import concourse.bass as bass
import concourse.mybir as mybir
from concourse.bass_utils import run_bass_kernel_spmd

F32 = mybir.dt.float32
BF16 = mybir.dt.bfloat16
AF = mybir.ActivationFunctionType
ALU = mybir.AluOpType
AX = mybir.AxisListType

L = 2064
NX = 2048
NMETA = 16
D = 2048
EPS = 1e-6
TT = [(i * 128, 128) for i in range(16)] + [(2048, 16)]
QB = [(i * 512, 512) for i in range(4)] + [(2048, 16)]
DELTAS = [128 * m for m in range(1, 16)] + [16 + 128 * m for m in range(16)]
NDEL = len(DELTAS)
SAME_ENGINE_SYNC = True
import os
LOOK = int(os.environ.get('KLOOK', '3'))
ST3 = int(os.environ.get('KST3', '1'))


def vidx(s):
    return s if s < 2048 else s - 2048 - 16


def slopes16():
    return [2.0 ** (-8.0 * (i + 1) / 16) for i in range(16)]


class Eng:
    def __init__(self, nc, es, name, e, ndma):
        self.name = name
        self.e = e
        self.sem = es.enter_context(nc.semaphore("s_" + name))
        self.cnt = 0
        self.seen = {}
        self.ring = [[es.enter_context(nc.semaphore("d_%s%d" % (name, i))), 0] for i in range(ndma)]
        self.ri = 0


class Buf:
    def __init__(self, t):
        self.t = t
        self.w = None
        self.rs = {}

    def __getitem__(self, k):
        return self.t[k]


class K:
    def __init__(self, nc, es):
        self.nc = nc
        self.es = es
        self.E = {
            "pe": Eng(nc, es, "pe", nc.tensor, 0),
            "dve": Eng(nc, es, "dve", nc.vector, 0),
            "act": Eng(nc, es, "act", nc.scalar, 0),
            "pool": Eng(nc, es, "pool", nc.gpsimd, 16),
            "sp": Eng(nc, es, "sp", nc.sync, 24),
        }
        self.nbuf = 0
        self.log = []

    def sb(self, shape, dt, es=None, name=None):
        self.nbuf += 1
        t = (es or self.es).enter_context(self.nc.sbuf_tensor("%s_%d" % (name or "sb", self.nbuf), list(shape), dt))
        return Buf(t)

    def wait(self, eng, tok):
        if tok is None:
            return
        sid, sem, val = tok
        if eng.seen.get(sid, 0) >= val:
            return
        if sid == id(eng.sem) and (eng.name == "pe" or not SAME_ENGINE_SYNC):
            return
        eng.e.wait_ge(sem, val)
        eng.seen[sid] = val
        self.log.append((eng.name, "w", sid, val))

    def _deps(self, eng, reads, writes):
        for b in reads:
            self.wait(eng, b.w)
        for b in writes:
            self.wait(eng, b.w)
            for tok in list(b.rs.values()):
                self.wait(eng, tok)

    def _mark(self, tok, reads, writes):
        for b in reads:
            old = b.rs.get(tok[0])
            if old is None or old[2] < tok[2]:
                b.rs[tok[0]] = tok
        for b in writes:
            b.w = tok
            b.rs = {}

    def op(self, en, fn, reads=(), writes=()):
        eng = self.E[en]
        self._deps(eng, reads, writes)
        ins = fn(eng.e)
        eng.cnt += 1
        ins.then_inc(eng.sem, 1)
        self.log.append((eng.name, "i", id(eng.sem), 1))
        tok = (id(eng.sem), eng.sem, eng.cnt)
        self._mark(tok, reads, writes)
        return tok

    def mms(self, fns, reads=(), writes=()):
        eng = self.E["pe"]
        self._deps(eng, reads, writes)
        ins = None
        for fn in fns:
            ins = fn(eng.e)
        eng.cnt += 1
        ins.then_inc(eng.sem, 1)
        self.log.append((eng.name, "i", id(eng.sem), 1))
        tok = (id(eng.sem), eng.sem, eng.cnt)
        self._mark(tok, reads, writes)
        return tok

    def dma(self, qn, out, in_, reads=(), writes=()):
        eng = self.E[qn]
        self._deps(eng, reads, writes)
        slot = eng.ring[eng.ri % len(eng.ring)]
        eng.ri += 1
        if slot[1] > 0:
            self.wait(eng, (id(slot[0]), slot[0], slot[1]))
        ins = eng.e.dma_start(out=out, in_=in_)
        slot[1] += 16
        ins.then_inc(slot[0], 16)
        self.log.append((eng.name, "i", id(slot[0]), 16))
        tok = (id(slot[0]), slot[0], slot[1])
        self._mark(tok, reads, writes)
        return tok

    def check_deadlock(self):
        qs = {}
        for ev in self.log:
            qs.setdefault(ev[0], []).append(ev)
        ptr = {n: 0 for n in qs}
        sem = {}
        prog = True
        while prog:
            prog = False
            for n, q in qs.items():
                while ptr[n] < len(q):
                    _, kind, sid, val = q[ptr[n]]
                    if kind == "i":
                        sem[sid] = sem.get(sid, 0) + val
                    elif sem.get(sid, 0) < val:
                        break
                    ptr[n] += 1
                    prog = True
        stuck = {n: (ptr[n], len(q), q[ptr[n]]) for n, q in qs.items() if ptr[n] < len(q)}
        if stuck:
            names = {id(e.sem): e.name for e in self.E.values()}
            for e in self.E.values():
                for i, sl in enumerate(e.ring):
                    names[id(sl[0])] = "%s_dma%d" % (e.name, i)
            msg = "; ".join("%s at %d/%d waits %s>=%d (have %d)" % (n, p, t, names.get(ev[2]), ev[3], sem.get(ev[2], 0))
                            for n, (p, t, ev) in stuck.items())
            raise RuntimeError("DEADLOCK in emitted program: " + msg)

    def barrier(self):
        toks = []
        for e in self.E.values():
            if e.cnt > 0:
                toks.append((id(e.sem), e.sem, e.cnt))
            for s in e.ring:
                if s[1] > 0:
                    toks.append((id(s[0]), s[0], s[1]))
        for e in self.E.values():
            for t in toks:
                if t[0] == id(e.sem):
                    continue
                self.wait(e, t)


def build(layers, do_final, out_rows):
    nc = bass.Bass("TRN2", target_bir_lowering=False)
    dr = {}

    def din(name, shape):
        dr[name] = nc.dram_tensor(name, list(shape), F32, kind="ExternalInput").ap()
        return dr[name]

    x_d = din("x", [NX, D])
    meta_d = din("meta", [NMETA, D])
    nrm_d = din("norms", [5, D])
    ident_d = din("ident", [128, 128])
    dlin_d = din("dlin", [128, 512])
    dabs_d = din("dabs", [128, 896])
    cb_d = din("cbtab", [128, 16 * NDEL])
    n_odd = sum(1 for l in layers if l[0] == "O")
    n_even = sum(1 for l in layers if l[0] == "E")
    if n_odd:
        winc_d = din("winc", [2 * 16 * 4 * 2 * 128, 2048])
        woutc_d = din("woutc", [2 * 4 * 8 * 128, 2048])
        dlam_d = din("dlam", [2, 512])
        dnorm_d = din("dnorm", [2, 256])
    if n_even:
        wina_d = din("wina", [2 * 120 * 128, 2048])
        wouta_d = din("wouta", [2 * 4 * 8 * 128, 2048])
        smask_d = din("smask", [128, L])
        mab_d = din("mab", [128, 1024])
        mtri_d = din("mtri", [128, 256])
        lb_d = din("lbl", [128, 64])
        hnorm_d = din("hnorm", [2, 128])
        wabs_d = din("wabs", [128, 1152])
        mclip_d = din("mclip", [128, 1024])
        sink_d = din("sink", [2, 16])
    out_d = nc.dram_tensor("out", [out_rows, D], F32, kind="ExternalOutput").ap()
    hd = nc.dram_tensor("hd", [L, D], F32, kind="Internal").ap()
    yTd = nc.dram_tensor("yTd", [32, 128, L], BF16, kind="Internal").ap()

    with ExitStack() as es:
        k = K(nc, es)
        ident_f = k.sb([128, 128], F32)
        ident = k.sb([128, 128], BF16)
        wst = [k.sb([128, 2048], F32, name="wst") for _ in range(2)]
        ps = [Buf(es.enter_context(nc.psum_tensor("ps%d" % i, [128, 512], F32))) for i in range(7)]
        pst1 = es.enter_context(nc.psum_tensor("pst", [128, 1024], BF16))
        pstq = [Buf(pst1)]
        PSP = [ps]
        hd_b = [Buf(None) for _ in TT]
        yTd_b = [Buf(None) for _ in range(16)]
        st = {"wst": 0, "wb": 0, "ps": 0, "pst": 0, "uq": 0}

        def nxt(key, lst):
            i = st[key] % len(lst)
            st[key] += 1
            return lst[i]

        epsc = k.sb([128, 4], F32)
        k.op("pool", lambda e: e.memset(epsc[:, 0:1], EPS), writes=[epsc])
        k.op("pool", lambda e: e.memset(epsc[:, 3:4], 1.0), writes=[epsc])
        for wi in range(2):
            k.op("pool", lambda e, wi=wi: e.memset(epsc[:, 1 + wi:2 + wi],
                                                   math.log(1.0 - (0.8 - 0.6 * math.exp(-0.3 * (2 * wi + 1))))),
                 writes=[epsc])
        k.dma("sp", ident_f[:], ident_d[:, :], writes=[ident_f])
        k.op("dve", lambda e: e.tensor_copy(out=ident[:], in_=ident_f[:]), reads=[ident_f], writes=[ident])

        if n_odd:
            lams = k.sb([128, 4], F32)
            lame = k.sb([128, 4], F32)
            neglam = k.sb([128, 2], F32)
            lam_es = ExitStack()
            lamt = k.sb([128, 2, 512], F32, lam_es)
            lamp = k.sb([128, 2, 2, 128], F32, lam_es)
            for i in range(2):
                k.dma("sp", lamt[:, i, :], dlam_d[i:i + 1, :].partition_broadcast(128), writes=[lamt])
            for i in range(2):
                for j in range(2):
                    k.op("dve", lambda e, i=i, j=j: e.tensor_tensor(
                        out=lamp[:, i, j, :], in0=lamt[:, i, 256 * j:256 * j + 128],
                        in1=lamt[:, i, 256 * j + 128:256 * j + 256], op=ALU.mult), reads=[lamt], writes=[lamp])
            for i in range(2):
                for j in range(2):
                    k.op("dve", lambda e, i=i, j=j: e.reduce_sum(
                        out=lams[:, 2 * i + j:2 * i + j + 1], in_=lamp[:, i, j, :], axis=AX.X),
                        reads=[lamp], writes=[lams])
            k.op("act", lambda e: e.activation(out=lame[:], in_=lams[:], func=AF.Exp), reads=[lams], writes=[lame])
            k.barrier()
            lam_es.close()

        def lam_init_of(layer_idx):
            return 0.8 - 0.6 * math.exp(-0.3 * layer_idx)

        def h_src(first, ti):
            s, n = TT[ti]
            if first:
                return (x_d[s:s + n, :] if s < 2048 else meta_d[0:n, :])
            return hd[s:s + n, :]

        def load_w_slice(row0, dst, ls):
            for half in range(2):
                stg = nxt("wst", wst)
                k.dma("sp", stg[:], ls[0][row0 + half * 128:row0 + half * 128 + 128, :], writes=[stg])
                k.op("pool", lambda e, stg=stg, half=half: e.tensor_copy(
                    out=dst[:, 8 * half:8 * half + 8, :],
                    in_=stg[:].rearrange("p (c n) -> p c n", c=8)), reads=[stg], writes=[dst])

        def phase_norm(first, nrow, xnT, ls):
            gt = k.sb([128, D], F32, ls, "gt")
            hb = [k.sb([128, D], F32, ls, "hb") for _ in range(2)]
            xb = [k.sb([128, D], BF16, ls, "xb") for _ in range(2)]
            junk = k.sb([128, D], BF16, ls, "junk")
            ssb = [k.sb([128, 2], F32, ls, "ss") for _ in range(2)]
            k.dma("sp", gt[:], nrm_d[nrow:nrow + 1, :].partition_broadcast(128), writes=[gt])
            for ti, (s, n) in enumerate(TT):
                h = hb[ti % 2]
                xn = xb[ti % 2]
                ss = ssb[ti % 2]
                k.dma("sp", h[0:n, :], h_src(first, ti), reads=[hd_b[ti]], writes=[h])
                k.op("act", lambda e: e.activation(out=junk[0:n, :], in_=h[0:n, :], func=AF.Square,
                                                   accum_out=ss[0:n, 0:1]), reads=[h], writes=[junk, ss])
                k.op("act", lambda e: e.activation(out=ss[0:n, 1:2], in_=ss[0:n, 0:1], func=AF.Ln, scale=1.0 / D,
                                                   bias=epsc[0:n, 0:1]), reads=[ss, epsc], writes=[ss])
                k.op("act", lambda e: e.activation(out=ss[0:n, 0:1], in_=ss[0:n, 1:2], func=AF.Exp, scale=-0.5),
                     reads=[ss], writes=[ss])
                k.op("dve", lambda e: e.scalar_tensor_tensor(out=xn[0:n, :], in0=h[0:n, :], scalar=ss[0:n, 0:1],
                                                             in1=gt[0:n, :], op0=ALU.mult, op1=ALU.mult),
                     reads=[h, ss, gt], writes=[xn])
                for g in range(2):
                    pt = nxt("pst", pstq)
                    k.mms([lambda e, c=c, g=g, pt=pt: e.transpose(
                        out=pt[:, c * 128:c * 128 + n], in_=xn[0:n, (8 * g + c) * 128:(8 * g + c + 1) * 128],
                        identity=ident[0:n, 0:n]) for c in range(8)], reads=[xn, ident], writes=[pt])
                    k.op("act" if g == 0 else "dve", lambda e, g=g, pt=pt: (e.copy if g == 0 else e.tensor_copy)(
                        out=xnT[:, 8 * g:8 * g + 8, s:s + n],
                        in_=pt[:, :].rearrange("p (c t) -> p c t", c=8)[:, :, 0:n]), reads=[pt], writes=[xnT])

        def proj_fm(xnT, w, c0, dst, dj, scale):
            for bi, (s, n) in enumerate(QB):
                p = nxt("ps", PSP[0])
                k.mms([lambda e, c=c, p=p: e.matmul(p[:, 0:n], lhsT=w[:, c, c0:c0 + 128], rhs=xnT[:, c, s:s + n],
                                                    start=(c == 0), stop=(c == 15)) for c in range(16)],
                      reads=[xnT, w], writes=[p])
                if scale is None:
                    k.op("act", lambda e, p=p: e.copy(out=dst[:, dj, s:s + n], in_=p[:, 0:n]), reads=[p], writes=[dst])
                else:
                    k.op("act", lambda e, p=p: e.mul(out=dst[:, dj, s:s + n], in_=p[:, 0:n], mul=scale),
                         reads=[p], writes=[dst])

        def proj_tm(xnT, w, ti, p, ncols=256, c0=0):
            s, n = TT[ti]
            k.mms([lambda e, c=c: e.matmul(p[0:n, 0:ncols], lhsT=xnT[:, c, s:s + n], rhs=w[:, c, c0:c0 + ncols],
                                           start=(c == 0), stop=(c == 15)) for c in range(16)],
                  reads=[xnT, w], writes=[p])

        def silu_from_psum(p, n, ncols, dst_ap, dstbuf, tmpa, tmpb, pc0=0):
            k.op("act", lambda e: e.activation(out=tmpa[0:n, 0:ncols], in_=p[0:n, pc0:pc0 + ncols], func=AF.Exp, scale=-1.0),
                 reads=[p], writes=[tmpa])
            k.op("dve", lambda e: e.tensor_scalar(out=tmpa[0:n, 0:ncols], in0=tmpa[0:n, 0:ncols], scalar1=1.0,
                                                  scalar2=None, op0=ALU.add), reads=[tmpa], writes=[tmpa])
            k.op("dve", lambda e: e.reciprocal(out=tmpb[0:n, 0:ncols], in_=tmpa[0:n, 0:ncols]), reads=[tmpa], writes=[tmpb])
            k.op("dve", lambda e: e.tensor_tensor(out=dst_ap, in0=p[0:n, pc0:pc0 + ncols], in1=tmpb[0:n, 0:ncols],
                                                  op=ALU.mult), reads=[p, tmpb], writes=[dstbuf])

        CT = {}

        def load_consts(ls):
            CT["dlin"] = k.sb([128, 512], F32, ls, "dlin")
            CT["dabs"] = k.sb([128, 896], F32, ls, "dabs")
            CT["cbt"] = k.sb([128, 16 * NDEL], F32, ls, "cbt")
            k.dma("sp", CT["dlin"][:], dlin_d[:, :], writes=[CT["dlin"]])
            k.dma("sp", CT["dabs"][:], dabs_d[:, :], writes=[CT["dabs"]])
            k.dma("sp", CT["cbt"][:], cb_d[:, :], writes=[CT["cbt"]])

        def bias_tile(t0v, N, s0v, kn, slope, h):
            dlin, dabs, cbt = CT["dlin"], CT["dabs"], CT["cbt"]
            if t0v < s0v + kn and s0v < t0v + N:
                off = t0v - s0v
                return dabs[0:kn, 384 + off:384 + off + N], dabs, -slope, None
            if t0v > s0v:
                dl = t0v - s0v
                return dlin[0:kn, 0:N], dlin, -slope, cbt[0:kn, h * NDEL + DELTAS.index(dl):h * NDEL + DELTAS.index(dl) + 1]
            dl = s0v - t0v
            return dlin[0:kn, 0:N], dlin, slope, cbt[0:kn, h * NDEL + DELTAS.index(dl):h * NDEL + DELTAS.index(dl) + 1]

        def odd_mixer(widx, layer_idx, xnT, ls):
            lam_init = lam_init_of(layer_idx)
            sl = slopes16()
            load_consts(ls)
            cbt = CT["cbt"]
            wb = [k.sb([128, 16, 256], BF16, ls, "wb") for _ in range(4)]
            QT = k.sb([128, 2, L], BF16, ls, "QT")
            KT = k.sb([128, 2, L], BF16, ls, "KT")
            Vx = k.sb([128, 17, 257], BF16, ls, "Vx")
            G = k.sb([128, 17, 256], BF16, ls, "G")
            yT = k.sb([128, 2, L], BF16, ls, "yT")
            tmpf = [k.sb([128, 512], F32, ls, "tmpf") for _ in range(4)]
            PT = [k.sb([128, 512], BF16, ls, "PT") for _ in range(4)]
            OT = [k.sb([128, 4, 257], F32, ls, "OT") for _ in range(2)]
            Oq = [[Buf(OT[j_].t[:, q_, :]) for q_ in range(4)] for j_ in range(2)]
            SM = k.sb([128, 4, 8], F32, ls, "SM")
            SS = [k.sb([128, 2], F32, ls, "SS") for _ in range(4)]
            A1 = k.sb([128, 4, 256], F32, ls, "A1")
            A2 = k.sb([128, 4, 256], F32, ls, "A2")
            YB = k.sb([128, 4, 256], BF16, ls, "YB")
            ga = k.sb([128, 256], F32, ls, "ga")
            gb = k.sb([128, 256], F32, ls, "gb")
            gn = k.sb([128, 256], F32, ls, "gn")
            k.dma("sp", gn[:], dnorm_d[widx:widx + 1, :].partition_broadcast(128), writes=[gn])
            k.op("pool", lambda e: e.memset(Vx[:, :, 256:257], 1.0), writes=[Vx])
            k.op("dve", lambda e: e.tensor_tensor(out=neglam[:, widx:widx + 1], in0=lame[:, 2 * widx + 1:2 * widx + 2],
                                                  in1=lame[:, 2 * widx:2 * widx + 1], op=ALU.subtract),
                 reads=[lame], writes=[neglam])
            k.op("dve", lambda e: e.tensor_scalar(out=neglam[:, widx:widx + 1], in0=neglam[:, widx:widx + 1],
                                                  scalar1=-lam_init, scalar2=None, op0=ALU.add),
                 reads=[neglam], writes=[neglam])
            nl = neglam
            pp = 0
            def load_head(h_):
                base_ = ((widx * 16 + h_) * 4) * 256
                for i_ in range(4):
                    load_w_slice(base_ + i_ * 256, wb[i_], [winc_d])
            load_head(0)
            for h in range(16):
                wq, wk, wv, wg = wb[0], wb[1], wb[2], wb[3]
                for j in range(2):
                    proj_fm(xnT, wq, j * 128, QT, j, 128 ** -0.5)
                for j in range(2):
                    proj_fm(xnT, wk, j * 128, KT, j, None)
                for ti, (s, n) in enumerate(TT):
                    p = nxt("ps", PSP[0])
                    proj_tm(xnT, wv, ti, p)
                    k.op("act", lambda e, p=p, ti=ti, n=n: e.copy(out=Vx[0:n, ti, 0:256], in_=p[0:n, 0:256]),
                         reads=[p], writes=[Vx])
                    p = nxt("ps", PSP[0])
                    proj_tm(xnT, wg, ti, p)
                    silu_from_psum(p, n, 256, G[0:n, ti, :], G, ga, gb)
                if h + 1 < 16:
                    load_head(h + 1)
                for (t0, N) in QB:
                    t0v = vidx(t0)
                    nqs = (N + 127) // 128
                    for j in range(2):
                        acc = ps[2:6]
                        pend = []
                        stb = [ps[0], ps[1], ps[6]] if ST3 else [ps[0], ps[1]]
                        for kt, (s0, kn) in enumerate(TT):
                            s0v = vidx(s0)
                            stp = stb[kt % len(stb)]
                            k.mms([lambda e: e.matmul(stp[0:kn, 0:N], lhsT=KT[:, j, s0:s0 + kn], rhs=QT[:, j, t0:t0 + N],
                                                      start=True, stop=True)], reads=[KT, QT], writes=[stp])
                            dt_ap, dbuf, coef, cb = bias_tile(t0v, N, s0v, kn, sl[h], h)
                            tf = tmpf[pp % 4]
                            ptile = PT[pp % 4]
                            pp += 1
                            k.op("dve", lambda e: e.scalar_tensor_tensor(out=tf[0:kn, 0:N], in0=dt_ap, scalar=coef,
                                                                         in1=stp[0:kn, 0:N], op0=ALU.mult, op1=ALU.add),
                                 reads=[dbuf, stp], writes=[tf])
                            if cb is None:
                                k.op("act", lambda e: e.activation(out=ptile[0:kn, 0:N], in_=tf[0:kn, 0:N], func=AF.Exp),
                                     reads=[tf], writes=[ptile])
                            else:
                                k.op("act", lambda e: e.activation(out=ptile[0:kn, 0:N], in_=tf[0:kn, 0:N], func=AF.Exp,
                                                                   bias=cb), reads=[tf, cbt], writes=[ptile])
                            if len(pend) >= LOOK:
                                pend.pop(0)()
                            def pv(kt=kt, kn=kn, ptile=ptile):
                                for qs in range(nqs):
                                    qn = min(128, N - qs * 128)
                                    a = acc[qs]
                                    k.mms([lambda e: e.matmul(a[0:qn, 0:257], lhsT=ptile[0:kn, qs * 128:qs * 128 + qn],
                                                              rhs=Vx[0:kn, kt, :], start=(kt == 0), stop=(kt == 16))],
                                          reads=[ptile, Vx], writes=[a])
                            pend.append(pv)
                        while pend:
                            pend.pop(0)()
                        for qs in range(nqs):
                            qn = min(128, N - qs * 128)
                            k.op("act" if qs % 2 == 0 else "dve",
                                 lambda e, qs=qs, qn=qn: (e.copy if qs % 2 == 0 else e.tensor_copy)(
                                     out=OT[j][0:qn, qs, :], in_=acc[qs][0:qn, 0:257]),
                                 reads=[acc[qs]], writes=[Oq[j][qs]])
                    nq = nqs
                    qn = min(128, N)
                    ti0 = t0 // 128
                    o1, o2 = OT[0].t, OT[1].t
                    rd1, rd2 = list(Oq[0][0:nq]), list(Oq[1][0:nq])
                    SMv = SM.t
                    bc = lambda col: SMv[0:qn, 0:nq, col:col + 1].to_broadcast([qn, nq, 256])
                    k.op("dve", lambda e: e.reciprocal(out=SMv[0:qn, 0:nq, 0:1], in_=o1[0:qn, 0:nq, 256:257]),
                         reads=rd1, writes=[SM])
                    k.op("dve", lambda e: e.reciprocal(out=SMv[0:qn, 0:nq, 1:2], in_=o2[0:qn, 0:nq, 256:257]),
                         reads=rd2, writes=[SM])
                    k.op("dve", lambda e: e.tensor_scalar(out=SMv[0:qn, 0:nq, 2:3], in0=SMv[0:qn, 0:nq, 1:2],
                                                          scalar1=nl[0:qn, widx:widx + 1], scalar2=None, op0=ALU.mult),
                         reads=[SM, nl], writes=[SM])
                    k.op("dve", lambda e: e.tensor_tensor(out=A1[0:qn, 0:nq, :], in0=o1[0:qn, 0:nq, 0:256], in1=bc(0), op=ALU.mult),
                         reads=rd1 + [SM], writes=[A1])
                    k.op("dve", lambda e: e.tensor_tensor(out=A2[0:qn, 0:nq, :], in0=o2[0:qn, 0:nq, 0:256], in1=bc(2), op=ALU.mult),
                         reads=rd2 + [SM], writes=[A2])
                    k.op("dve", lambda e: e.tensor_tensor(out=A2[0:qn, 0:nq, :], in0=A2[0:qn, 0:nq, :], in1=A1[0:qn, 0:nq, :],
                                                          op=ALU.add), reads=[A2, A1], writes=[A2])
                    for qs in range(nq):
                        k.op("act", lambda e, qs=qs: e.activation(out=A1[0:qn, qs, :], in_=A2[0:qn, qs, :], func=AF.Square,
                                                                  accum_out=SS[qs][0:qn, 0:1]), reads=[A2], writes=[SS[qs]])
                        k.op("act", lambda e, qs=qs: e.activation(out=SS[qs][0:qn, 1:2], in_=SS[qs][0:qn, 0:1], func=AF.Ln,
                                                                  scale=1.0 / 256, bias=epsc[0:qn, 0:1]),
                             reads=[SS[qs], epsc], writes=[SS[qs]])
                        k.op("act", lambda e, qs=qs: e.activation(out=SMv[0:qn, qs, 5:6], in_=SS[qs][0:qn, 1:2], func=AF.Exp,
                                                                  scale=-0.5, bias=epsc[0:qn, 1 + widx:2 + widx]),
                             reads=[SS[qs], epsc], writes=[SM])
                    k.op("dve", lambda e: e.tensor_tensor(out=A1[0:qn, 0:nq, :], in0=A2[0:qn, 0:nq, :], in1=bc(5), op=ALU.mult),
                         reads=[A2, SM], writes=[A1])
                    k.op("dve", lambda e: e.tensor_tensor(out=A1[0:qn, 0:nq, :], in0=A1[0:qn, 0:nq, :],
                                                          in1=gn[0:qn, :].unsqueeze(1).to_broadcast([qn, nq, 256]), op=ALU.mult),
                         reads=[A1, gn], writes=[A1])
                    k.op("dve", lambda e: e.tensor_tensor(out=YB[0:qn, 0:nq, :], in0=A1[0:qn, 0:nq, :],
                                                          in1=G[0:qn, ti0:ti0 + nq, :], op=ALU.mult), reads=[A1, G], writes=[YB])
                    pt = nxt("pst", pstq)
                    k.mms([lambda e, c=c, qs=qs: e.transpose(out=pt[:, (c * nq + qs) * 128:(c * nq + qs) * 128 + qn],
                                                             in_=YB[0:qn, qs, c * 128:(c + 1) * 128],
                                                             identity=ident[0:qn, 0:qn]) for c in range(2) for qs in range(nq)],
                          reads=[YB, ident], writes=[pt])
                    wd = (nq - 1) * 128 + qn
                    k.op("act", lambda e: e.copy(out=yT[:, :, t0:t0 + wd],
                                                 in_=pt[:, 0:2 * nq * 128].rearrange("p (c t) -> p c t", c=2)[:, :, 0:wd]),
                         reads=[pt], writes=[yT])
                k.dma("pool", yTd[2 * h:2 * h + 2, :, :].rearrange("c p t -> p c t"), yT[:], reads=[yT], writes=[yTd_b[h]])

        def proj_fm_cb(xnT, w, c0, cb):
            for bi, (s, n) in enumerate(QB):
                p = nxt("ps", PSP[0])
                k.mms([lambda e, c=c, p=p: e.matmul(p[:, 0:n], lhsT=w[:, c, c0:c0 + 128], rhs=xnT[:, c, s:s + n],
                                                    start=(c == 0), stop=(c == 15)) for c in range(16)],
                      reads=[xnT, w], writes=[p])
                cb(p, s, n)

        def load_w128(sidx_row0, dst, c0):
            stg = nxt("wst", wst)
            k.dma("sp", stg[:], wina_d[sidx_row0:sidx_row0 + 128, :], writes=[stg])
            k.op("pool", lambda e: e.tensor_copy(out=dst[:, :, c0:c0 + 128],
                                                 in_=stg[:].rearrange("p (c n) -> p c n", c=16)),
                 reads=[stg], writes=[dst])

        def even_mixer_A(widx, xnT, ls):
            wb = [k.sb([128, 16, 256], BF16, ls, "wb") for _ in range(2)] + [k.sb([128, 16, 128], BF16, ls, "wb")]
            qA = k.sb([128, L], BF16, ls, "qA")
            qB = k.sb([128, L], BF16, ls, "qB")
            T1 = k.sb([128, L], F32, ls, "T1")
            T1b = k.sb([128, L], F32, ls, "T1b")
            T2 = k.sb([128, L], F32, ls, "T2")
            T3 = k.sb([128, L], F32, ls, "T3")
            QEA = k.sb([128, L], BF16, ls, "QEA")
            QEB = k.sb([128, L], BF16, ls, "QEB")
            KE = k.sb([128, L], BF16, ls, "KE")
            KL = k.sb([128, L], BF16, ls, "KL")
            V = k.sb([128, 17, 128], BF16, ls, "V")
            G = k.sb([128, 17, 128], BF16, ls, "G")
            OF = k.sb([128, 17, 128], F32, ls, "OF")
            yT = k.sb([128, L], BF16, ls, "yT")
            smk = k.sb([128, L], BF16, ls, "smk")
            mAB = k.sb([128, 2, 512], BF16, ls, "mAB")
            mtri = k.sb([128, 2, 128], F32, ls, "mtri")
            lbt = k.sb([128, 64], F32, ls, "lbt")
            lbc = k.sb([128, 2, 16], F32, ls, "lbc")
            omc = k.sb([128, 2, 16], F32, ls, "omc")
            gna = k.sb([128, 128], F32, ls, "gna")
            ATm = [k.sb([128, 128], BF16, ls, "ATm") for _ in range(3)]
            KLt = [[k.sb([128, 128], BF16, ls, "KLt") for _ in range(2)] for _ in range(2)]
            S = [k.sb([128, 128], F32, ls, "S") for _ in range(4)]
            SBALL = k.sb([128, 33, 128], BF16, ls, "SBALL")
            ECs = [k.sb([128, 34], F32, ls, "EC") for _ in range(2)]
            k.op("pool", lambda e: e.memset(SBALL[:, 0, :], 0.0), writes=[SBALL])
            ubanks = [ps[4], ps[5], ps[6]]
            PSP[0] = ps[0:4]
            ga = k.sb([128, 128], F32, ls, "ga")
            gb = k.sb([128, 128], F32, ls, "gb")
            sm = [k.sb([128, 8], F32, ls, "sm") for _ in range(2)]
            a1 = [k.sb([128, 128], F32, ls, "a1") for _ in range(2)]
            a2 = [k.sb([128, 128], F32, ls, "a2") for _ in range(2)]
            jk = k.sb([128, 128], F32, ls, "jk")
            ybf = [k.sb([128, 128], BF16, ls, "ybf") for _ in range(2)]
            k.dma("pool", smk[:], smask_d[:, :], writes=[smk])
            k.dma("pool", mAB[:], mab_d[:, :].rearrange("p (a n) -> p a n", a=2), writes=[mAB])
            k.dma("sp", mtri[:], mtri_d[:, :].rearrange("p (a n) -> p a n", a=2), writes=[mtri])
            k.dma("sp", lbt[:], lb_d[:, :], writes=[lbt])
            k.dma("sp", gna[:], hnorm_d[widx:widx + 1, :].partition_broadcast(128), writes=[gna])
            for par in range(2):
                for ab in range(2):
                    k.op("pool", lambda e, par=par, ab=ab: e.memset(KLt[par][ab][:], 0.0), writes=[KLt[par][ab]])
            lb4 = lbt[:].rearrange("p (r l h) -> p r l h", r=2, l=2)
            if widx == 0:
                k.op("pool", lambda e: e.memset(lbc[:], 0.0), writes=[lbc])
                k.op("pool", lambda e: e.memset(omc[:], 1.0), writes=[omc])
            else:
                k.op("dve", lambda e: e.tensor_tensor(out=omc[:], in0=lb4[:, :, 0, :], in1=lb4[:, :, 1, :], op=ALU.subtract),
                     reads=[lbt], writes=[omc])
                k.op("act", lambda e: e.activation(out=omc[:], in_=omc[:], func=AF.Exp), reads=[omc], writes=[omc])
                k.op("dve", lambda e: e.tensor_scalar(out=omc[:], in0=omc[:], scalar1=1.0, scalar2=None, op0=ALU.add),
                     reads=[omc], writes=[omc])
                k.op("dve", lambda e: e.reciprocal(out=lbc[:], in_=omc[:]), reads=[omc], writes=[lbc])
                k.op("dve", lambda e: e.tensor_scalar(out=omc[:], in0=lbc[:], scalar1=-1.0, scalar2=1.0, op0=ALU.mult,
                                                      op1=ALU.add), reads=[lbc], writes=[omc])
            for h in range(16):
                wq, wzf, wzb, wv, wg = (wb[0], 0), (wb[0], 128), (wb[2], 0), (wb[1], 0), (wb[1], 128)

                def load_head(h_):
                    for (wt, c0), si in ((wzf, 16 + h_), (wzb, 32 + h_), (wq, h_), (wv, 48 + h_), (wg, 64 + h_)):
                        load_w128((widx * 120 + si) * 128, wt, c0)
                if h == 0:
                    load_head(0)
                for di_ in range(2):
                    wz_ = wzf if di_ == 0 else wzb
                    Te = T1 if di_ == 0 else T1b

                    def cbz(p, s, n, Te=Te):
                        k.op("act", lambda e: e.activation(out=Te[:, s:s + n], in_=p[:, 0:n], func=AF.Exp, scale=-1.0),
                             reads=[p], writes=[Te])
                    proj_fm_cb(xnT, wz_[0], wz_[1], cbz)
                def cbq(p, s, n):
                    k.op("dve", lambda e: e.tensor_tensor(out=qA[:, s:s + n], in0=p[:, 0:n], in1=mAB[:, 0, 0:n], op=ALU.mult),
                         reads=[p, mAB], writes=[qA])
                    k.op("dve", lambda e: e.tensor_tensor(out=qB[:, s:s + n], in0=p[:, 0:n], in1=mAB[:, 1, 0:n], op=ALU.mult),
                         reads=[p, mAB], writes=[qB])
                proj_fm_cb(xnT, wq[0], wq[1], cbq)
                def ew(di, T1):
                    ops = []
                    EC = ECs[di]
                    xv = lambda t_: t_[:, 0:2048].rearrange("p (c t) -> p c t", t=64)
                    ops.append(lambda: k.op("act", lambda e: e.activation(out=T2[:], in_=T1[:], func=AF.Ln, bias=epsc[:, 3:4]),
                         reads=[T1, epsc], writes=[T2]))
                    if widx == 0:
                        ops.append(lambda: k.op("dve", lambda e: e.tensor_scalar(out=T3[:], in0=T2[:], scalar1=-1.0, scalar2=None, op0=ALU.mult),
                             reads=[T2], writes=[T3]))
                    else:
                        ops.append(lambda: k.op("act", lambda e: e.activation(out=T3[:], in_=T1[:], func=AF.Ln, scale=lbc[:, di, h:h + 1],
                                                           bias=epsc[:, 3:4]), reads=[T1, lbc, epsc], writes=[T3]))
                        ops.append(lambda: k.op("dve", lambda e: e.tensor_tensor(out=T3[:], in0=T3[:], in1=T2[:], op=ALU.subtract),
                             reads=[T3, T2], writes=[T3]))
                    ops.append(lambda: k.op("act", lambda e: e.activation(out=T2[:], in_=T2[:], func=AF.Exp, scale=-1.0), reads=[T2], writes=[T2]))
                    ops.append(lambda: k.op("dve", lambda e: e.scalar_tensor_tensor(out=T1[:], in0=T1[:], scalar=omc[:, di, h:h + 1], in1=T2[:],
                                                                 op0=ALU.mult, op1=ALU.mult), reads=[T1, omc, T2], writes=[T1]))
                    if di == 0:
                        ops.append(lambda: k.op("dve", lambda e: e.tensor_tensor_scan(out=T2[:], data0=smk[:], data1=T3[:], initial=0.0,
                                                                   op0=ALU.mult, op1=ALU.add), reads=[smk, T3], writes=[T2]))
                        ops.append(lambda: k.op("act", lambda e: e.activation(out=T3[:], in_=T2[:], func=AF.Exp), reads=[T2], writes=[T3]))
                        ops.append(lambda: k.op("pool", lambda e: e.tensor_copy(out=EC[:, 0:32], in_=xv(T3)[:, :, 63]), reads=[T3], writes=[EC]))
                        ops.append(lambda: k.op("pool", lambda e: e.tensor_copy(out=EC[:, 32:33], in_=T3[:, 2063:2064]), reads=[T3], writes=[EC]))
                        ops.append(lambda: k.op("act", lambda e: e.activation(out=T2[:], in_=T2[:], func=AF.Exp, scale=-1.0),
                             reads=[T2], writes=[T2]))
                        ops.append(lambda: k.op("dve", lambda e: e.tensor_tensor(out=QEA[:], in0=qA[:], in1=T3[:], op=ALU.mult),
                             reads=[qA, T3], writes=[QEA]))
                        ops.append(lambda: k.op("dve", lambda e: e.tensor_tensor(out=QEB[:], in0=qB[:], in1=T3[:], op=ALU.mult),
                             reads=[qB, T3], writes=[QEB]))
                        ops.append(lambda: k.op("dve", lambda e: e.tensor_tensor(out=KE[:], in0=T1[:], in1=T2[:], op=ALU.mult),
                             reads=[T1, T2], writes=[KE]))
                        ops.append(lambda: k.op("dve", lambda e: e.tensor_tensor(
                            out=xv(KL), in0=xv(KE), in1=xv(T3)[:, :, 63:64].to_broadcast([128, 32, 64]), op=ALU.mult),
                            reads=[KE, T3], writes=[KL]))
                        ops.append(lambda: k.op("dve", lambda e: e.tensor_tensor(out=KL[:, 2048:2064], in0=KE[:, 2048:2064],
                                                              in1=T3[:, 2063:2064].to_broadcast([128, 16]), op=ALU.mult),
                             reads=[KE, T3], writes=[KL]))
                        KLs = KL
                    else:
                        ops.append(lambda: k.op("pool", lambda e: e.memset(T2[:, 0:1], 0.0), writes=[T2]))
                        ops.append(lambda: k.op("dve", lambda e: e.tensor_tensor_scan(out=T2[:, 1:L], data0=T3[:, 0:L - 1], data1=smk[:, 1:L],
                                                                   initial=0.0, op0=ALU.add, op1=ALU.mult),
                             reads=[smk, T3], writes=[T2]))
                        ops.append(lambda: k.op("dve", lambda e: e.tensor_tensor(out=EC[:, 0:32], in0=xv(T2)[:, :, 63], in1=xv(T3)[:, :, 63],
                                                              op=ALU.add), reads=[T2, T3], writes=[EC]))
                        ops.append(lambda: k.op("dve", lambda e: e.tensor_tensor(out=EC[:, 32:33], in0=T2[:, 2063:2064], in1=T3[:, 2063:2064],
                                                              op=ALU.add), reads=[T2, T3], writes=[EC]))
                        ops.append(lambda: k.op("act", lambda e: e.activation(out=EC[:, 0:33], in_=EC[:, 0:33], func=AF.Exp), reads=[EC], writes=[EC]))
                        ops.append(lambda: k.op("act", lambda e: e.activation(out=T3[:], in_=T2[:], func=AF.Exp, scale=-1.0),
                             reads=[T2], writes=[T3]))
                        ops.append(lambda: k.op("act", lambda e: e.activation(out=T2[:], in_=T2[:], func=AF.Exp), reads=[T2], writes=[T2]))
                        ops.append(lambda: k.op("dve", lambda e: e.tensor_tensor(out=QEA[:], in0=qA[:], in1=T3[:], op=ALU.mult),
                             reads=[qA, T3], writes=[QEA]))
                        ops.append(lambda: k.op("dve", lambda e: e.tensor_tensor(out=QEB[:], in0=qB[:], in1=T3[:], op=ALU.mult),
                             reads=[qB, T3], writes=[QEB]))
                        ops.append(lambda: k.op("dve", lambda e: e.tensor_tensor(out=KE[:], in0=T1[:], in1=T2[:], op=ALU.mult),
                             reads=[T1, T2], writes=[KE]))
                        KLs = KE
                    return ops, KLs

                ops0, KLs0 = ew(0, T1)
                for ti, (s, n) in enumerate(TT):
                    p = nxt("ps", PSP[0])
                    proj_tm(xnT, wb[1], ti, p, 256, 0)
                    k.op("act", lambda e: e.copy(out=V[0:n, ti, :], in_=p[0:n, 0:128]), reads=[p], writes=[V])
                    silu_from_psum(p, n, 128, G[0:n, ti, :], G, ga, gb, 128)
                    if ops0:
                        ops0.pop(0)()
                while ops0:
                    ops0.pop(0)()
                if h + 1 < 16:
                    load_head(h + 1)
                for di in range(2):
                    EC = ECs[di]
                    if di == 0:
                        KLs = KLs0
                    else:
                        ops1, KLs = ew(1, T1b)
                        for f_ in ops1:
                            f_()
                    if di == 0:
                        seq = [(16, 0)] + [(t_, ab_) for t_ in range(16) for ab_ in (0, 1)]
                    else:
                        seq = [(t_, ab_) for t_ in range(15, -1, -1) for ab_ in (1, 0)] + [(16, 0)]
                    cids = [32 if t_ == 16 else 2 * t_ + ab_ for (t_, ab_) in seq]
                    order = [16] + list(range(16)) if di == 0 else list(range(15, -1, -1)) + [16]
                    seqidx = {}
                    k.op("pool", lambda e: e.memset(S[0][:], 0.0), writes=[S[0]])
                    if di == 0:
                        groups = [[16]] + [[2 * g_, 2 * g_ + 1] for g_ in range(8)]
                    else:
                        groups = [[15 - 2 * g_, 14 - 2 * g_] for g_ in range(8)] + [[16]]
                    step = 0
                    tcount = 0
                    for gt in groups:
                        bank = ubanks[st["uq"] % 3]
                        st["uq"] += 1
                        slots = []
                        for ti in gt:
                            s, n = TT[ti]
                            par = tcount % 2
                            tcount += 1
                            pt = nxt("pst", pstq)
                            k.mms([lambda e: e.transpose(out=pt[0:n, 0:128], in_=KLs[:, s:s + n], identity=ident[:, :])],
                                  reads=[KLs, ident], writes=[pt])
                            nA = min(n, 64)
                            k.op("act", lambda e: e.copy(out=KLt[par][0][0:nA, :], in_=pt[0:nA, 0:128]), reads=[pt],
                                 writes=[KLt[par][0]])
                            if n == 128:
                                k.op("act", lambda e: e.copy(out=KLt[par][1][64:128, :], in_=pt[64:128, 0:128]), reads=[pt],
                                     writes=[KLt[par][1]])
                            chunks = [0] if n == 16 else ([0, 1] if di == 0 else [1, 0])
                            for ab in chunks:
                                cid = 32 if ti == 16 else 2 * ti + ab
                                q_ = len(slots)
                                seqidx[(ti, ab)] = step + q_
                                k.mms([lambda e: e.matmul(bank[:, q_ * 128:(q_ + 1) * 128], lhsT=KLt[par][ab][0:n, :],
                                                          rhs=V[0:n, ti, :], start=True, stop=True)],
                                      reads=[KLt[par][ab], V], writes=[bank])
                                slots.append((q_, cid))
                        for (q_, cid) in slots:
                            if step < 32:
                                so, sn = S[step % 4], S[(step + 1) % 4]
                                k.op("dve", lambda e: e.scalar_tensor_tensor(out=sn[:], in0=so[:], scalar=EC[:, cid:cid + 1],
                                                                             in1=bank[:, q_ * 128:(q_ + 1) * 128],
                                                                             op0=ALU.mult, op1=ALU.add),
                                     reads=[so, EC, bank], writes=[sn])
                                if di == 0:
                                    k.op("pool", lambda e: e.tensor_copy(out=SBALL[:, step + 1, :], in_=sn[:]), reads=[sn],
                                         writes=[SBALL])
                                else:
                                    cn = cids[step + 1]
                                    k.op("pool", lambda e: e.tensor_tensor(out=SBALL[:, step + 1, :], in0=sn[:],
                                                                           in1=EC[:, cn:cn + 1].to_broadcast([128, 128]),
                                                                           op=ALU.mult), reads=[sn, EC], writes=[SBALL])
                            step += 1
                    def at_stage(oi, ti):
                        s, n = TT[ti]
                        pa = nxt("ps", PSP[0])
                        k.mms([lambda e: e.matmul(pa[0:n, 0:n], lhsT=KE[:, s:s + n], rhs=QEA[:, s:s + n], start=True, stop=False),
                               lambda e: e.matmul(pa[0:n, 0:n], lhsT=KE[:, s:s + n], rhs=QEB[:, s:s + n], start=False, stop=True)],
                              reads=[KE, QEA, QEB], writes=[pa])
                        at = ATm[oi % 3]
                        k.op("dve", lambda e: e.tensor_tensor(out=at[0:n, 0:n], in0=pa[0:n, 0:n], in1=mtri[0:n, di, 0:n],
                                                              op=ALU.mult), reads=[pa, mtri], writes=[at])
                    at_stage(0, order[0])
                    pendB, pendC = [], []
                    for oi, ti in enumerate(order):
                        s, n = TT[ti]
                        if oi + 1 < len(order):
                            at_stage(oi + 1, order[oi + 1])
                        at = ATm[oi % 3]
                        chunks = [0] if n == 16 else ([0, 1] if di == 0 else [1, 0])
                        po = nxt("ps", PSP[0])
                        fns = [lambda e: e.matmul(po[0:n, 0:128], lhsT=at[0:n, 0:n], rhs=V[0:n, ti, :], start=True, stop=False)]
                        for ci, ab in enumerate(chunks):
                            qe = QEA if ab == 0 else QEB
                            sbi = seqidx[(ti, ab)]
                            fns.append(lambda e, qe=qe, sbi=sbi, last=(ci == len(chunks) - 1): e.matmul(
                                po[0:n, 0:128], lhsT=qe[:, s:s + n], rhs=SBALL[:, sbi, :], start=False, stop=last))
                        k.mms(fns, reads=[at, V, QEA, QEB, SBALL], writes=[po])
                        if di == 0:
                            k.op("act", lambda e: e.copy(out=OF[0:n, ti, :], in_=po[0:n, 0:128]), reads=[po], writes=[OF])
                        else:
                            s_ = sm[oi % 2]
                            x1, x2, yb = a1[oi % 2], a2[oi % 2], ybf[oi % 2]
                            k.op("dve", lambda e: e.tensor_tensor(out=x1[0:n, :], in0=po[0:n, 0:128], in1=OF[0:n, ti, :],
                                                                  op=ALU.add), reads=[po, OF], writes=[x1])
                            k.op("act", lambda e: e.activation(out=jk[0:n, :], in_=x1[0:n, :], func=AF.Square,
                                                               accum_out=s_[0:n, 0:1]), reads=[x1], writes=[jk, s_])
                            k.op("act", lambda e: e.activation(out=s_[0:n, 1:2], in_=s_[0:n, 0:1], func=AF.Ln,
                                                               scale=1.0 / 128, bias=epsc[0:n, 0:1]),
                                 reads=[s_, epsc], writes=[s_])
                            k.op("act", lambda e: e.activation(out=s_[0:n, 2:3], in_=s_[0:n, 1:2], func=AF.Exp, scale=-0.5),
                                 reads=[s_], writes=[s_])
                            def stB(s=s, n=n, ti=ti, s_=s_, x1=x1, x2=x2, yb=yb):
                                k.op("dve", lambda e: e.scalar_tensor_tensor(out=x2[0:n, :], in0=x1[0:n, :], scalar=s_[0:n, 2:3],
                                                                             in1=gna[0:n, :], op0=ALU.mult, op1=ALU.mult),
                                     reads=[x1, s_, gna], writes=[x2])
                                k.op("dve", lambda e: e.tensor_tensor(out=yb[0:n, :], in0=x2[0:n, :], in1=G[0:n, ti, :],
                                                                       op=ALU.mult), reads=[x2, G], writes=[yb])

                                def stC():
                                    pt2 = nxt("pst", pstq)
                                    k.mms([lambda e: e.transpose(out=pt2[:, 0:n], in_=yb[0:n, :], identity=ident[0:n, 0:n])],
                                          reads=[yb, ident], writes=[pt2])
                                    k.op("act", lambda e: e.copy(out=yT[:, s:s + n], in_=pt2[:, 0:n]), reads=[pt2], writes=[yT])
                                pendC.append(stC)
                            runB, runC = pendB[:], pendC[:]
                            del pendB[:]
                            del pendC[:]
                            for f_ in runB:
                                f_()
                            for f_ in runC:
                                f_()
                            pendB.append(stB)
                    for f_ in pendB[:]:
                        f_()
                    for f_ in pendC[:]:
                        f_()
                k.dma("pool", yTd[h, :, :], yT[:], reads=[yT], writes=[yTd_b[h]])
            PSP[0] = ps

        def even_mixer_B(widx, xnT, ls):
            sl = slopes16()
            load_consts(ls)
            dabs = CT["dabs"]
            wb = [k.sb([128, 16, 256], BF16, ls, "wb") for _ in range(3)]
            QT = k.sb([128, 1, L], BF16, ls, "QT")
            KT = k.sb([128, 1, L], BF16, ls, "KT")
            Vx = k.sb([128, 17, 129], BF16, ls, "Vx")
            G = k.sb([128, 17, 128], BF16, ls, "G")
            yT = k.sb([128, L], BF16, ls, "yT")
            wabs = k.sb([128, 1152], F32, ls, "wabs")
            mclip = k.sb([128, 1024], F32, ls, "mclip")
            sink = k.sb([128, 16], F32, ls, "sink")
            esink = k.sb([128, 16], F32, ls, "esink")
            tmpf = [k.sb([128, 512], F32, ls, "tmpf") for _ in range(4)]
            PT = [k.sb([128, 512], BF16, ls, "PT") for _ in range(4)]
            OT = k.sb([128, 4, 129], F32, ls, "OT")
            Oq = [Buf(OT.t[:, q_, :]) for q_ in range(4)]
            SM = k.sb([128, 4, 4], F32, ls, "SM")
            A1 = k.sb([128, 4, 128], F32, ls, "A1")
            YB = k.sb([128, 4, 128], BF16, ls, "YB")
            ga = k.sb([128, 128], F32, ls, "ga")
            gb = k.sb([128, 128], F32, ls, "gb")
            sm = [k.sb([128, 4], F32, ls, "sm") for _ in range(2)]
            a1 = [k.sb([128, 128], F32, ls, "a1") for _ in range(2)]
            ybf = [k.sb([128, 128], BF16, ls, "ybf") for _ in range(2)]
            k.dma("sp", wabs[:], wabs_d[:, :], writes=[wabs])
            k.dma("sp", mclip[:], mclip_d[:, :], writes=[mclip])
            k.dma("sp", sink[:], sink_d[widx:widx + 1, :].partition_broadcast(128), writes=[sink])
            k.op("act", lambda e: e.activation(out=esink[:], in_=sink[:], func=AF.Exp), reads=[sink], writes=[esink])
            k.op("pool", lambda e: e.memset(Vx[:, :, 128:129], 1.0), writes=[Vx])
            pp = 0
            def load_kv(kv_):
                load_w128((widx * 120 + 96 + kv_) * 128, wb[0], 0)
                load_w128((widx * 120 + 100 + kv_) * 128, wb[0], 128)

            def load_qg(hq_):
                load_w128((widx * 120 + 80 + hq_) * 128, wb[1 + hq_ % 2], 0)
                load_w128((widx * 120 + 104 + hq_) * 128, wb[1 + hq_ % 2], 128)
            load_kv(0)
            load_qg(0)
            for hq in range(16):
                kv = hq // 4
                if hq % 4 == 0:
                    wk, wv = (wb[0], 0), (wb[0], 128)
                    proj_fm(xnT, wk[0], wk[1], KT, 0, None)
                    for ti, (s, n) in enumerate(TT):
                        p = nxt("ps", PSP[0])
                        proj_tm(xnT, wv[0], ti, p, 128, wv[1])
                        k.op("act", lambda e: e.copy(out=Vx[0:n, ti, 0:128], in_=p[0:n, 0:128]), reads=[p], writes=[Vx])
                wq, wg = (wb[1 + hq % 2], 0), (wb[1 + hq % 2], 128)
                if hq + 1 < 16:
                    load_qg(hq + 1)
                    if (hq + 1) % 4 == 0:
                        load_kv((hq + 1) // 4)
                proj_fm(xnT, wq[0], wq[1], QT, 0, 128 ** -0.5)
                for ti, (s, n) in enumerate(TT):
                    p = nxt("ps", PSP[0])
                    proj_tm(xnT, wg[0], ti, p, 128, wg[1])
                    silu_from_psum(p, n, 128, G[0:n, ti, :], G, ga, gb)
                for (t0, N) in QB:
                    t0v = vidx(t0)
                    nqs = (N + 127) // 128
                    acc = ps[2:6]
                    if t0 < 2048:
                        xt = [s0 for s0 in range(t0 - 128, t0 + N + 1, 128) if 0 <= s0 <= 1920]
                    else:
                        xt = [0]
                    ktl = [(2048, 16)] + [(s0, 128) for s0 in xt]
                    pend = []
                    stb = [ps[0], ps[1], ps[6]] if ST3 else [ps[0], ps[1]]
                    for ki, (s0, kn) in enumerate(ktl):
                        s0v = vidx(s0)
                        kt = s0 // 128
                        stp = stb[ki % len(stb)]
                        k.mms([lambda e: e.matmul(stp[0:kn, 0:N], lhsT=KT[:, 0, s0:s0 + kn], rhs=QT[:, 0, t0:t0 + N],
                                                  start=True, stop=True)], reads=[KT, QT], writes=[stp])
                        if s0 == 2048 and t0 < 2048:
                            c0 = min(t0, 512)
                            dt_ap, dbuf = mclip[0:16, c0:c0 + N], mclip
                        elif s0 == 2048:
                            dt_ap, dbuf = dabs[0:16, 384:384 + N], dabs
                        else:
                            off = t0v - s0v
                            dt_ap, dbuf = wabs[0:kn, 512 + off:512 + off + N], wabs
                        tf = tmpf[pp % 4]
                        ptile = PT[pp % 4]
                        pp += 1
                        k.op("dve", lambda e: e.scalar_tensor_tensor(out=tf[0:kn, 0:N], in0=dt_ap, scalar=-sl[hq],
                                                                     in1=stp[0:kn, 0:N], op0=ALU.mult, op1=ALU.add),
                             reads=[dbuf, stp], writes=[tf])
                        k.op("act", lambda e: e.activation(out=ptile[0:kn, 0:N], in_=tf[0:kn, 0:N], func=AF.Exp),
                             reads=[tf], writes=[ptile])
                        if len(pend) >= LOOK:
                            pend.pop(0)()
                        def pv(s0=s0, kn=kn, kt=kt, ptile=ptile):
                            for qs in range(nqs):
                                qn = min(128, N - qs * 128)
                                tok0 = t0 + qs * 128
                                if t0 < 2048:
                                    rel = [s_ for s_ in (tok0 - 128, tok0, tok0 + 128) if 0 <= s_ <= 1920]
                                else:
                                    rel = [0]
                                if s0 != 2048 and s0 not in rel:
                                    continue
                                a = acc[qs]
                                k.mms([lambda e: e.matmul(a[0:qn, 0:129], lhsT=ptile[0:kn, qs * 128:qs * 128 + qn],
                                                          rhs=Vx[0:kn, kt, :], start=(s0 == 2048), stop=(s0 == rel[-1]))],
                                      reads=[ptile, Vx], writes=[a])
                        pend.append(pv)
                    while pend:
                        pend.pop(0)()
                    nq = nqs
                    qn = min(128, N)
                    ti0 = t0 // 128
                    for qs in range(nq):
                        k.op("act" if qs % 2 == 0 else "dve",
                             lambda e, qs=qs: (e.copy if qs % 2 == 0 else e.tensor_copy)(
                                 out=OT[0:qn, qs, :], in_=acc[qs][0:qn, 0:129]), reads=[acc[qs]], writes=[Oq[qs]])
                    rdo = list(Oq[0:nq])
                    ot = OT.t
                    SMv = SM.t
                    k.op("dve", lambda e: e.tensor_scalar(out=SMv[0:qn, 0:nq, 0:1], in0=ot[0:qn, 0:nq, 128:129],
                                                          scalar1=esink[0:qn, hq:hq + 1], scalar2=None, op0=ALU.add),
                         reads=rdo + [esink], writes=[SM])
                    k.op("dve", lambda e: e.reciprocal(out=SMv[0:qn, 0:nq, 1:2], in_=SMv[0:qn, 0:nq, 0:1]), reads=[SM], writes=[SM])
                    k.op("dve", lambda e: e.tensor_tensor(out=A1[0:qn, 0:nq, :], in0=ot[0:qn, 0:nq, 0:128],
                                                          in1=SMv[0:qn, 0:nq, 1:2].to_broadcast([qn, nq, 128]), op=ALU.mult),
                         reads=rdo + [SM], writes=[A1])
                    k.op("dve", lambda e: e.tensor_tensor(out=YB[0:qn, 0:nq, :], in0=A1[0:qn, 0:nq, :],
                                                          in1=G[0:qn, ti0:ti0 + nq, :], op=ALU.mult), reads=[A1, G], writes=[YB])
                    pt = nxt("pst", pstq)
                    k.mms([lambda e, qs=qs: e.transpose(out=pt[:, qs * 128:qs * 128 + qn], in_=YB[0:qn, qs, :],
                                                        identity=ident[0:qn, 0:qn]) for qs in range(nq)],
                          reads=[YB, ident], writes=[pt])
                    wd = (nq - 1) * 128 + qn
                    k.op("act", lambda e: e.copy(out=yT[:, t0:t0 + wd], in_=pt[:, 0:wd]), reads=[pt], writes=[yT])
                k.dma("pool", yTd[16 + hq, :, :], yT[:], reads=[yT], writes=[yTd_b[hq]])

        def phase_out(first, wout_d, widx, ls):
            wob = [k.sb([128, 32, 512], BF16, ls, "wob") for _ in range(2)]
            yb = [k.sb([128, 32, 512], BF16, ls, "ytb") for _ in range(2)]
            hb = [k.sb([128, 512], F32, ls, "hb") for _ in range(4)]
            ho = [k.sb([128, 512], F32, ls, "ho") for _ in range(4)]
            cnt = 0

            def load_w(nb_):
                wo_ = wob[nb_ % 2]
                stgs = []
                for pc in range(8):
                    stg = nxt("wst", wst)
                    r0 = ((widx * 4 + nb_) * 8 + pc) * 128
                    k.dma("pool", stg[:], wout_d[r0:r0 + 128, :], writes=[stg])
                    stgs.append(stg)
                    if pc >= 1:
                        sp_, pp_ = stgs[pc - 1], pc - 1
                        k.op("pool", lambda e, sp_=sp_, pp_=pp_: e.tensor_copy(
                            out=wo_[:, 4 * pp_:4 * pp_ + 4, :], in_=sp_[:].rearrange("p (c n) -> p c n", c=4)),
                            reads=[sp_], writes=[wo_])
                sp_, pp_ = stgs[7], 7
                k.op("pool", lambda e: e.tensor_copy(
                    out=wo_[:, 4 * pp_:4 * pp_ + 4, :], in_=sp_[:].rearrange("p (c n) -> p c n", c=4)),
                    reads=[sp_], writes=[wo_])

            def load_y(nb_, bi_):
                t0_, N_ = QB[bi_]
                y_ = yb[(nb_ * len(QB) + bi_) % 2]
                k.dma("sp", y_[:, :, 0:N_], yTd[:, :, t0_:t0_ + N_].rearrange("c p t -> p c t"),
                      reads=yTd_b, writes=[y_])

            seq = [(nb_, bi_) for nb_ in range(4) for bi_ in range(len(QB))]
            load_w(0)
            load_y(0, 0)
            for idx, (nb, bi) in enumerate(seq):
                wo = wob[nb % 2]
                t0, N = QB[bi]
                y = yb[idx % 2]
                if bi == 0 and nb + 1 < 4:
                    load_w(nb + 1)
                if idx + 1 < len(seq):
                    load_y(*seq[idx + 1])
                tiles = []
                for qs in range((N + 127) // 128):
                    qn = min(128, N - qs * 128)
                    tok0 = t0 + qs * 128
                    ti = tok0 // 128
                    hi = hb[cnt % 4]
                    hn = ho[cnt % 4]
                    cnt += 1
                    src_ = h_src(first, ti)
                    k.dma("sp", hi[0:qn, :], src_[:, nb * 512:(nb + 1) * 512], reads=[hd_b[ti]], writes=[hi])
                    tiles.append((qs, qn, tok0, ti, hi, hn))
                for (qs, qn, tok0, ti, hi, hn) in tiles:
                    p = nxt("ps", PSP[0])
                    k.mms([lambda e, c=c: e.matmul(p[0:qn, :], lhsT=y[:, c, qs * 128:qs * 128 + qn], rhs=wo[:, c, :],
                                                   start=(c == 0), stop=(c == 31)) for c in range(32)],
                          reads=[y, wo], writes=[p])
                    k.op("dve", lambda e: e.tensor_tensor(out=hn[0:qn, :], in0=p[0:qn, :], in1=hi[0:qn, :], op=ALU.add),
                         reads=[p, hi], writes=[hn])
                    k.dma("sp", hd[tok0:tok0 + qn, nb * 512:(nb + 1) * 512], hn[0:qn, :], reads=[hn],
                          writes=[hd_b[ti]])

        def phase_final(first, ls):
            gt = k.sb([128, D], F32, ls, "gt")
            hb = [k.sb([128, D], F32, ls, "hb") for _ in range(2)]
            ob = [k.sb([128, D], F32, ls, "ob") for _ in range(2)]
            junk = k.sb([128, D], BF16, ls, "junk")
            ssb = [k.sb([128, 2], F32, ls, "ss") for _ in range(2)]
            k.dma("sp", gt[:], nrm_d[4:5, :].partition_broadcast(128), writes=[gt])
            toks = []
            for ti, (s, n) in enumerate(TT[:16]):
                h = hb[ti % 2]
                o = ob[ti % 2]
                ss = ssb[ti % 2]
                k.dma("sp", h[0:n, :], h_src(first, ti), reads=[hd_b[ti]], writes=[h])
                k.op("act", lambda e: e.activation(out=junk[0:n, :], in_=h[0:n, :], func=AF.Square,
                                                   accum_out=ss[0:n, 0:1]), reads=[h], writes=[junk, ss])
                k.op("act", lambda e: e.activation(out=ss[0:n, 1:2], in_=ss[0:n, 0:1], func=AF.Ln, scale=1.0 / D,
                                                   bias=epsc[0:n, 0:1]), reads=[ss, epsc], writes=[ss])
                k.op("act", lambda e: e.activation(out=ss[0:n, 0:1], in_=ss[0:n, 1:2], func=AF.Exp, scale=-0.5),
                     reads=[ss], writes=[ss])
                k.op("dve", lambda e: e.scalar_tensor_tensor(out=o[0:n, :], in0=h[0:n, :], scalar=ss[0:n, 0:1],
                                                             in1=gt[0:n, :], op0=ALU.mult, op1=ALU.mult),
                     reads=[h, ss, gt], writes=[o])
                toks.append(k.dma("pool", out_d[s:s + n, :], o[0:n, :], reads=[o]))
            return toks

        first = True
        for (kind, widx, layer_idx) in layers:
            with ExitStack() as ls:
                xnT = k.sb([128, 16, L], BF16, ls, "xnT")
                with ExitStack() as ls2:
                    phase_norm(first, (0 if kind == "E" else 2) + widx, xnT, ls2)
                    k.barrier()
                with ExitStack() as ls2:
                    if kind == "O":
                        odd_mixer(widx, layer_idx, xnT, ls2)
                    else:
                        even_mixer_A(widx, xnT, ls2)
                    k.barrier()
                if kind == "E":
                    with ExitStack() as ls2:
                        even_mixer_B(widx, xnT, ls2)
                        k.barrier()
            with ExitStack() as ls:
                phase_out(first, woutc_d if kind == "O" else wouta_d, widx, ls)
                k.barrier()
            first = False
        out_toks = []
        with ExitStack() as ls:
            if do_final:
                out_toks = phase_final(first, ls)
            else:
                hb = [k.sb([128, D], F32, ls, "hb") for _ in range(2)]
                for ti, (s, n) in enumerate(TT):
                    h = hb[ti % 2]
                    k.dma("sp", h[0:n, :], h_src(first, ti), reads=[hd_b[ti]], writes=[h])
                    out_toks.append(k.dma("pool", out_d[s:s + n, :], h[0:n, :], reads=[h]))
            k.barrier()
        k.check_deadlock()
    return nc


def const_tables():
    i = np.arange(128, dtype=np.float32)[:, None]
    ident = np.eye(128, dtype=np.float32)
    dlin = (np.arange(512, dtype=np.float32)[None, :] - i).astype(np.float32)
    dabs = np.abs(np.arange(896, dtype=np.float32)[None, :] - i - 384).astype(np.float32)
    sl = np.array(slopes16(), dtype=np.float64)
    cb = (-(sl[:, None] * np.array(DELTAS, dtype=np.float64)[None, :])).reshape(1, -1)
    cb = np.repeat(cb, 128, axis=0).astype(np.float32)
    return {"ident": ident, "dlin": dlin, "dabs": dabs, "cbtab": np.ascontiguousarray(cb)}


def layout_winc(w):
    a = w.reshape(2, 2, 8, 128, 4, 16, 256)
    a = a.transpose(0, 5, 4, 1, 3, 2, 6)
    return np.ascontiguousarray(a).reshape(2 * 16 * 4 * 2 * 128, 2048)


def layout_wina(w):
    a = w.reshape(2, 16, 128, 120, 128)
    a = a.transpose(0, 3, 2, 1, 4)
    return np.ascontiguousarray(a).reshape(2 * 120 * 128, 2048)


def even_tables():
    i = np.arange(128, dtype=np.float32)[:, None]
    smask = np.ones((128, L), np.float32)
    smask[:, 0:2048:64] = 0.0
    smask[:, 2048] = 0.0
    j = np.arange(512)
    mA = ((j % 128) < 64).astype(np.float32)
    mab = np.concatenate([np.tile(mA[None], (128, 1)), np.tile((1 - mA)[None], (128, 1))], axis=1)
    s_ = np.arange(128)[:, None]; t_ = np.arange(128)[None, :]
    same = (s_ // 64) == (t_ // 64)
    mf = (same & (s_ <= t_)).astype(np.float32)
    mb_ = (same & (s_ >= t_)).astype(np.float32)
    mtri = np.concatenate([mf, mb_], axis=1)
    dw = np.abs(np.arange(1152, dtype=np.float32)[None, :] - i - 512)
    wabs = np.where(dw <= 128, dw, 1e9).astype(np.float32)
    mclip = np.minimum(np.arange(1024, dtype=np.float32)[None, :] - i + 16, 128.0).astype(np.float32)
    return {"smask": smask, "mab": np.ascontiguousarray(mab), "mtri": np.ascontiguousarray(mtri),
            "wabs": wabs, "mclip": mclip}


def layout_wout(w):
    a = w.reshape(2, 8, 4, 128, 4, 512)
    a = a.transpose(0, 4, 1, 3, 2, 5)
    return np.ascontiguousarray(a).reshape(2 * 4 * 8 * 128, 2048)


def run_layers(layers, do_final, inputs, ncores=8):
    out_rows = NX if do_final else L
    nc = build(layers, do_final, out_rows)
    f = lambda a: np.ascontiguousarray(np.asarray(a, dtype=np.float32))
    shared = dict(const_tables())
    shared["meta"] = f(inputs["meta_tokens"])
    shared["norms"] = np.concatenate([f(inputs["norm_a"]), f(inputs["norm_c"]), f(inputs["final_norm"])[None]], axis=0)
    if any(l[0] == "E" for l in layers):
        shared.update(even_tables())
        shared["wina"] = layout_wina(f(inputs["w_in_a"]))
        shared["wouta"] = layout_wout(f(inputs["w_out_a"]))
        shared["lbl"] = np.ascontiguousarray(f(inputs["hgrn_lb"]).reshape(2, 2, 16, 128).transpose(3, 0, 1, 2)).reshape(128, 64)
        shared["hnorm"] = f(inputs["hgrn_norm"])
        shared["sink"] = f(inputs["sink_logits"])
    if any(l[0] == "O" for l in layers):
        shared["winc"] = layout_winc(f(inputs["w_in_c"]))
        shared["woutc"] = layout_wout(f(inputs["w_out_c"]))
        shared["dlam"] = f(inputs["diff_lambda"]).reshape(2, 512)
        shared["dnorm"] = f(inputs["diff_norm"])
    x = f(inputs["x"])
    in_maps = []
    for b in range(ncores):
        m = dict(shared)
        m["x"] = x[b]
        in_maps.append(m)
    res = run_bass_kernel_spmd(nc, in_maps, core_ids=list(range(ncores)))
    return np.stack([r["out"] for r in res.results], axis=0)


def kernel(**inputs):
    layers = [("E", 0, 0), ("O", 0, 1), ("E", 1, 2), ("O", 1, 3)]
    return run_layers(layers, True, inputs)
```

```python
import math
from contextlib import ExitStack
import numpy as np
import concourse.bass as bass
import concourse.mybir as mybir
from concourse.bass_utils import run_bass_kernel_spmd

F32 = mybir.dt.float32
BF16 = mybir.dt.bfloat16
AF = mybir.ActivationFunctionType
ALU = mybir.AluOpType
AX = mybir.AxisListType

L = 2064
NX = 2048
NMETA = 16
D = 2048
EPS = 1e-6
TT = [(i * 128, 128) for i in range(16)] + [(2048, 16)]
QB = [(i * 512, 512) for i in range(4)] + [(2048, 16)]
DELTAS = [128 * m for m in range(1, 16)] + [16 + 128 * m for m in range(16)]
NDEL = len(DELTAS)
SAME_ENGINE_SYNC = True
import os
LOOK = int(os.environ.get('KLOOK', '3'))
ST3 = int(os.environ.get('KST3', '1'))


def vidx(s):
    return s if s < 2048 else s - 2048 - 16


def slopes16():
    return [2.0 ** (-8.0 * (i + 1) / 16) for i in range(16)]


class Eng:
    def __init__(self, nc, es, name, e, ndma):
        self.name = name
        self.e = e
        self.sem = es.enter_context(nc.semaphore("s_" + name))
        self.cnt = 0
        self.seen = {}
        self.ring = [[es.enter_context(nc.semaphore("d_%s%d" % (name, i))), 0] for i in range(ndma)]
        self.ri = 0


class Buf:
    def __init__(self, t):
        self.t = t
        self.w = None
        self.rs = {}

    def __getitem__(self, k):
        return self.t[k]


class K:
    def __init__(self, nc, es):
        self.nc = nc
        self.es = es
        self.E = {
            "pe": Eng(nc, es, "pe", nc.tensor, 0),
            "dve": Eng(nc, es, "dve", nc.vector, 0),
            "act": Eng(nc, es, "act", nc.scalar, 0),
            "pool": Eng(nc, es, "pool", nc.gpsimd, 16),
            "sp": Eng(nc, es, "sp", nc.sync, 24),
        }
        self.nbuf = 0
        self.log = []

    def sb(self, shape, dt, es=None, name=None):
        self.nbuf += 1
        t = (es or self.es).enter_context(self.nc.sbuf_tensor("%s_%d" % (name or "sb", self.nbuf), list(shape), dt))
        return Buf(t)

    def wait(self, eng, tok):
        if tok is None:
            return
        sid, sem, val = tok
        if eng.seen.get(sid, 0) >= val:
            return
        if sid == id(eng.sem) and (eng.name == "pe" or not SAME_ENGINE_SYNC):
            return
        eng.e.wait_ge(sem, val)
        eng.seen[sid] = val
        self.log.append((eng.name, "w", sid, val))

    def _deps(self, eng, reads, writes):
        for b in reads:
            self.wait(eng, b.w)
        for b in writes:
            self.wait(eng, b.w)
            for tok in list(b.rs.values()):
                self.wait(eng, tok)

    def _mark(self, tok, reads, writes):
        for b in reads:
            old = b.rs.get(tok[0])
            if old is None or old[2] < tok[2]:
                b.rs[tok[0]] = tok
        for b in writes:
            b.w = tok
            b.rs = {}

    def op(self, en, fn, reads=(), writes=()):
        eng = self.E[en]
        self._deps(eng, reads, writes)
        ins = fn(eng.e)
        eng.cnt += 1
        ins.then_inc(eng.sem, 1)
        self.log.append((eng.name, "i", id(eng.sem), 1))
        tok = (id(eng.sem), eng.sem, eng.cnt)
        self._mark(tok, reads, writes)
        return tok

    def mms(self, fns, reads=(), writes=()):
        eng = self.E["pe"]
        self._deps(eng, reads, writes)
        ins = None
        for fn in fns:
            ins = fn(eng.e)
        eng.cnt += 1
        ins.then_inc(eng.sem, 1)
        self.log.append((eng.name, "i", id(eng.sem), 1))
        tok = (id(eng.sem), eng.sem, eng.cnt)
        self._mark(tok, reads, writes)
        return tok

    def dma(self, qn, out, in_, reads=(), writes=()):
        eng = self.E[qn]
        self._deps(eng, reads, writes)
        slot = eng.ring[eng.ri % len(eng.ring)]
        eng.ri += 1
        if slot[1] > 0:
            self.wait(eng, (id(slot[0]), slot[0], slot[1]))
        ins = eng.e.dma_start(out=out, in_=in_)
        slot[1] += 16
        ins.then_inc(slot[0], 16)
        self.log.append((eng.name, "i", id(slot[0]), 16))
        tok = (id(slot[0]), slot[0], slot[1])
        self._mark(tok, reads, writes)
        return tok

    def check_deadlock(self):
        qs = {}
        for ev in self.log:
            qs.setdefault(ev[0], []).append(ev)
        ptr = {n: 0 for n in qs}
        sem = {}
        prog = True
        while prog:
            prog = False
            for n, q in qs.items():
                while ptr[n] < len(q):
                    _, kind, sid, val = q[ptr[n]]
                    if kind == "i":
                        sem[sid] = sem.get(sid, 0) + val
                    elif sem.get(sid, 0) < val:
                        break
                    ptr[n] += 1
                    prog = True
        stuck = {n: (ptr[n], len(q), q[ptr[n]]) for n, q in qs.items() if ptr[n] < len(q)}
        if stuck:
            names = {id(e.sem): e.name for e in self.E.values()}
            for e in self.E.values():
                for i, sl in enumerate(e.ring):
                    names[id(sl[0])] = "%s_dma%d" % (e.name, i)
            msg = "; ".join("%s at %d/%d waits %s>=%d (have %d)" % (n, p, t, names.get(ev[2]), ev[3], sem.get(ev[2], 0))
                            for n, (p, t, ev) in stuck.items())
            raise RuntimeError("DEADLOCK in emitted program: " + msg)

    def barrier(self):
        toks = []
        for e in self.E.values():
            if e.cnt > 0:
                toks.append((id(e.sem), e.sem, e.cnt))
            for s in e.ring:
                if s[1] > 0:
                    toks.append((id(s[0]), s[0], s[1]))
        for e in self.E.values():
            for t in toks:
                if t[0] == id(e.sem):
                    continue
                self.wait(e, t)


def build(layers, do_final, out_rows):
    nc = bass.Bass("TRN2", target_bir_lowering=False)
    dr = {}

    def din(name, shape):
        dr[name] = nc.dram_tensor(name, list(shape), F32, kind="ExternalInput").ap()
        return dr[name]

    x_d = din("x", [NX, D])
    meta_d = din("meta", [NMETA, D])
    nrm_d = din("norms", [5, D])
    ident_d = din("ident", [128, 128])
    dlin_d = din("dlin", [128, 512])
    dabs_d = din("dabs", [128, 896])
    cb_d = din("cbtab", [128, 16 * NDEL])
    n_odd = sum(1 for l in layers if l[0] == "O")
    n_even = sum(1 for l in layers if l[0] == "E")
    if n_odd:
        winc_d = din("winc", [2 * 16 * 4 * 2 * 128, 2048])
        woutc_d = din("woutc", [2 * 4 * 8 * 128, 2048])
        dlam_d = din("dlam", [2, 512])
        dnorm_d = din("dnorm", [2, 256])
    if n_even:
        wina_d = din("wina", [2 * 120 * 128, 2048])
        wouta_d = din("wouta", [2 * 4 * 8 * 128, 2048])
        smask_d = din("smask", [128, L])
        mab_d = din("mab", [128, 1024])
        mtri_d = din("mtri", [128, 256])
        lb_d = din("lbl", [128, 64])
        hnorm_d = din("hnorm", [2, 128])
        wabs_d = din("wabs", [128, 1152])
        mclip_d = din("mclip", [128, 1024])
        sink_d = din("sink", [2, 16])
    out_d = nc.dram_tensor("out", [out_rows, D], F32, kind="ExternalOutput").ap()
    hd = nc.dram_tensor("hd", [L, D], F32, kind="Internal").ap()
    yTd = nc.dram_tensor("yTd", [32, 128, L], BF16, kind="Internal").ap()

    with ExitStack() as es:
        k = K(nc, es)
        ident_f = k.sb([128, 128], F32)
        ident = k.sb([128, 128], BF16)
        wst = [k.sb([128, 2048], F32, name="wst") for _ in range(2)]
        ps = [Buf(es.enter_context(nc.psum_tensor("ps%d" % i, [128, 512], F32))) for i in range(7)]
        pst1 = es.enter_context(nc.psum_tensor("pst", [128, 1024], BF16))
        pstq = [Buf(pst1)]
        PSP = [ps]
        hd_b = [Buf(None) for _ in TT]
        yTd_b = [Buf(None) for _ in range(16)]
        st = {"wst": 0, "wb": 0, "ps": 0, "pst": 0, "uq": 0}

        def nxt(key, lst):
            i = st[key] % len(lst)
            st[key] += 1
            return lst[i]

        epsc = k.sb([128, 4], F32)
        k.op("pool", lambda e: e.memset(epsc[:, 0:1], EPS), writes=[epsc])
        k.op("pool", lambda e: e.memset(epsc[:, 3:4], 1.0), writes=[epsc])
        for wi in range(2):
            k.op("pool", lambda e, wi=wi: e.memset(epsc[:, 1 + wi:2 + wi],
                                                   math.log(1.0 - (0.8 - 0.6 * math.exp(-0.3 * (2 * wi + 1))))),
                 writes=[epsc])
        k.dma("sp", ident_f[:], ident_d[:, :], writes=[ident_f])
        k.op("dve", lambda e: e.tensor_copy(out=ident[:], in_=ident_f[:]), reads=[ident_f], writes=[ident])

        if n_odd:
            lams = k.sb([128, 4], F32)
            lame = k.sb([128, 4], F32)
            neglam = k.sb([128, 2], F32)
            lam_es = ExitStack()
            lamt = k.sb([128, 2, 512], F32, lam_es)
            lamp = k.sb([128, 2, 2, 128], F32, lam_es)
            for i in range(2):
                k.dma("sp", lamt[:, i, :], dlam_d[i:i + 1, :].partition_broadcast(128), writes=[lamt])
            for i in range(2):
                for j in range(2):
                    k.op("dve", lambda e, i=i, j=j: e.tensor_tensor(
                        out=lamp[:, i, j, :], in0=lamt[:, i, 256 * j:256 * j + 128],
                        in1=lamt[:, i, 256 * j + 128:256 * j + 256], op=ALU.mult), reads=[lamt], writes=[lamp])
            for i in range(2):
                for j in range(2):
                    k.op("dve", lambda e, i=i, j=j: e.reduce_sum(
                        out=lams[:, 2 * i + j:2 * i + j + 1], in_=lamp[:, i, j, :], axis=AX.X),
                        reads=[lamp], writes=[lams])
            k.op("act", lambda e: e.activation(out=lame[:], in_=lams[:], func=AF.Exp), reads=[lams], writes=[lame])
            k.barrier()
            lam_es.close()

        def lam_init_of(layer_idx):
            return 0.8 - 0.6 * math.exp(-0.3 * layer_idx)

        def h_src(first, ti):
            s, n = TT[ti]
            if first:
                return (x_d[s:s + n, :] if s < 2048 else meta_d[0:n, :])
            return hd[s:s + n, :]

        def load_w_slice(row0, dst, ls, col0=0):
            for half in range(2):
                stg = nxt("wst", wst)
                k.dma("sp", stg[:], ls[0][row0 + half * 128:row0 + half * 128 + 128, :], writes=[stg])
                k.op("pool", lambda e, stg=stg, half=half: e.tensor_copy(
                    out=dst[:, 8 * half:8 * half + 8, col0:col0 + 256],
                    in_=stg[:].rearrange("p (c n) -> p c n", c=8)), reads=[stg], writes=[dst])

        def phase_norm(first, nrow, xnT, ls):
            gt = k.sb([128, D], F32, ls, "gt")
            hb = [k.sb([128, D], F32, ls, "hb") for _ in range(2)]
            xb = [k.sb([128, D], BF16, ls, "xb") for _ in range(2)]
            junk = k.sb([128, D], BF16, ls, "junk")
            ssb = [k.sb([128, 2], F32, ls, "ss") for _ in range(2)]
            k.dma("sp", gt[:], nrm_d[nrow:nrow + 1, :].partition_broadcast(128), writes=[gt])
            for ti, (s, n) in enumerate(TT):
                h = hb[ti % 2]
                xn = xb[ti % 2]
                ss = ssb[ti % 2]
                k.dma("sp", h[0:n, :], h_src(first, ti), reads=[hd_b[ti]], writes=[h])
                k.op("act", lambda e: e.activation(out=junk[0:n, :], in_=h[0:n, :], func=AF.Square,
                                                   accum_out=ss[0:n, 0:1]), reads=[h], writes=[junk, ss])
                k.op("act", lambda e: e.activation(out=ss[0:n, 1:2], in_=ss[0:n, 0:1], func=AF.Ln, scale=1.0 / D,
                                                   bias=epsc[0:n, 0:1]), reads=[ss, epsc], writes=[ss])
                k.op("act", lambda e: e.activation(out=ss[0:n, 0:1], in_=ss[0:n, 1:2], func=AF.Exp, scale=-0.5),
                     reads=[ss], writes=[ss])
                k.op("dve", lambda e: e.scalar_tensor_tensor(out=xn[0:n, :], in0=h[0:n, :], scalar=ss[0:n, 0:1],
                                                             in1=gt[0:n, :], op0=ALU.mult, op1=ALU.mult),
                     reads=[h, ss, gt], writes=[xn])
                for g in range(2):
                    pt = nxt("pst", pstq)
                    k.mms([lambda e, c=c, g=g, pt=pt: e.transpose(
                        out=pt[:, c * 128:c * 128 + n], in_=xn[0:n, (8 * g + c) * 128:(8 * g + c + 1) * 128],
                        identity=ident[0:n, 0:n]) for c in range(8)], reads=[xn, ident], writes=[pt])
                    k.op("act" if g == 0 else "dve", lambda e, g=g, pt=pt: (e.copy if g == 0 else e.tensor_copy)(
                        out=xnT[:, 8 * g:8 * g + 8, s:s + n],
                        in_=pt[:, :].rearrange("p (c t) -> p c t", c=8)[:, :, 0:n]), reads=[pt], writes=[xnT])

        def proj_fm(xnT, w, c0, dst, dj, scale):
            for bi, (s, n) in enumerate(QB):
                p = nxt("ps", PSP[0])
                k.mms([lambda e, c=c, p=p: e.matmul(p[:, 0:n], lhsT=w[:, c, c0:c0 + 128], rhs=xnT[:, c, s:s + n],
                                                    start=(c == 0), stop=(c == 15)) for c in range(16)],
                      reads=[xnT, w], writes=[p])
                if scale is None:
                    k.op("act", lambda e, p=p: e.copy(out=dst[:, dj, s:s + n], in_=p[:, 0:n]), reads=[p], writes=[dst])
                else:
                    k.op("act", lambda e, p=p: e.mul(out=dst[:, dj, s:s + n], in_=p[:, 0:n], mul=scale),
                         reads=[p], writes=[dst])

        def proj_tm(xnT, w, ti, p, ncols=256, c0=0):
            s, n = TT[ti]
            k.mms([lambda e, c=c: e.matmul(p[0:n, 0:ncols], lhsT=xnT[:, c, s:s + n], rhs=w[:, c, c0:c0 + ncols],
                                           start=(c == 0), stop=(c == 15)) for c in range(16)],
                  reads=[xnT, w], writes=[p])

        def silu_from_psum(p, n, ncols, dst_ap, dstbuf, tmpa, tmpb, pc0=0):
            k.op("act", lambda e: e.activation(out=tmpa[0:n, 0:ncols], in_=p[0:n, pc0:pc0 + ncols], func=AF.Exp, scale=-1.0),
                 reads=[p], writes=[tmpa])
            k.op("dve", lambda e: e.tensor_scalar(out=tmpa[0:n, 0:ncols], in0=tmpa[0:n, 0:ncols], scalar1=1.0,
                                                  scalar2=None, op0=ALU.add), reads=[tmpa], writes=[tmpa])
            k.op("dve", lambda e: e.reciprocal(out=tmpb[0:n, 0:ncols], in_=tmpa[0:n, 0:ncols]), reads=[tmpa], writes=[tmpb])
            k.op("dve", lambda e: e.tensor_tensor(out=dst_ap, in0=p[0:n, pc0:pc0 + ncols], in1=tmpb[0:n, 0:ncols],
                                                  op=ALU.mult), reads=[p, tmpb], writes=[dstbuf])

        CT = {}

        def load_consts(ls):
            CT["dlin"] = k.sb([128, 512], F32, ls, "dlin")
            CT["dabs"] = k.sb([128, 896], F32, ls, "dabs")
            CT["cbt"] = k.sb([128, 16 * NDEL], F32, ls, "cbt")
            k.dma("sp", CT["dlin"][:], dlin_d[:, :], writes=[CT["dlin"]])
            k.dma("sp", CT["dabs"][:], dabs_d[:, :], writes=[CT["dabs"]])
            k.dma("sp", CT["cbt"][:], cb_d[:, :], writes=[CT["cbt"]])

        def bias_tile(t0v, N, s0v, kn, slope, h):
            dlin, dabs, cbt = CT["dlin"], CT["dabs"], CT["cbt"]
            if t0v < s0v + kn and s0v < t0v + N:
                off = t0v - s0v
                return dabs[0:kn, 384 + off:384 + off + N], dabs, -slope, None
            if t0v > s0v:
                dl = t0v - s0v
                return dlin[0:kn, 0:N], dlin, -slope, cbt[0:kn, h * NDEL + DELTAS.index(dl):h * NDEL + DELTAS.index(dl) + 1]
            dl = s0v - t0v
            return dlin[0:kn, 0:N], dlin, slope, cbt[0:kn, h * NDEL + DELTAS.index(dl):h * NDEL + DELTAS.index(dl) + 1]

        def odd_mixer(widx, layer_idx, xnT, ls):
            lam_init = lam_init_of(layer_idx)
            sl = slopes16()
            load_consts(ls)
            cbt = CT["cbt"]
            wb = [k.sb([128, 16, 256], BF16, ls, "wb") for _ in range(2)] + [k.sb([128, 16, 512], BF16, ls, "wvg")]
            QT = k.sb([128, 2, L], BF16, ls, "QT")
            KT = k.sb([128, 2, L], BF16, ls, "KT")
            Vx = k.sb([128, 17, 257], BF16, ls, "Vx")
            G = k.sb([128, 17, 256], BF16, ls, "G")
            yT = k.sb([128, 2, L], BF16, ls, "yT")
            tmpf = [k.sb([128, 512], F32, ls, "tmpf") for _ in range(4)]
            PT = [k.sb([128, 512], BF16, ls, "PT") for _ in range(4)]
            OT = [k.sb([128, 4, 257], F32, ls, "OT") for _ in range(2)]
            Oq = [[Buf(OT[j_].t[:, q_, :]) for q_ in range(4)] for j_ in range(2)]
            SM = k.sb([128, 4, 8], F32, ls, "SM")
            SS = [k.sb([128, 2], F32, ls, "SS") for _ in range(4)]
            A1 = k.sb([128, 4, 256], F32, ls, "A1")
            A2 = k.sb([128, 4, 256], F32, ls, "A2")
            YB = k.sb([128, 4, 256], BF16, ls, "YB")
            ga = k.sb([128, 256], F32, ls, "ga")
            gb = k.sb([128, 256], F32, ls, "gb")
            gn = k.sb([128, 256], F32, ls, "gn")
            k.dma("sp", gn[:], dnorm_d[widx:widx + 1, :].partition_broadcast(128), writes=[gn])
            k.op("pool", lambda e: e.memset(Vx[:, :, 256:257], 1.0), writes=[Vx])
            k.op("dve", lambda e: e.tensor_tensor(out=neglam[:, widx:widx + 1], in0=lame[:, 2 * widx + 1:2 * widx + 2],
                                                  in1=lame[:, 2 * widx:2 * widx + 1], op=ALU.subtract),
                 reads=[lame], writes=[neglam])
            k.op("dve", lambda e: e.tensor_scalar(out=neglam[:, widx:widx + 1], in0=neglam[:, widx:widx + 1],
                                                  scalar1=-lam_init, scalar2=None, op0=ALU.add),
                 reads=[neglam], writes=[neglam])
            nl = neglam
            pp = 0
            def load_head(h_):
                base_ = ((widx * 16 + h_) * 4) * 256
                for i_ in range(4):
                    load_w_slice(base_ + i_ * 256, wb[min(i_, 2)], [winc_d], 256 if i_ == 3 else 0)
            load_head(0)
            for h in range(16):
                wq, wk, wvg = wb[0], wb[1], wb[2]
                for j in range(2):
                    proj_fm(xnT, wq, j * 128, QT, j, 128 ** -0.5)
                for j in range(2):
                    proj_fm(xnT, wk, j * 128, KT, j, None)
                for ti, (s, n) in enumerate(TT):
                    p = nxt("ps", PSP[0])
                    proj_tm(xnT, wvg, ti, p, 512, 0)
                    k.op("act", lambda e, p=p, ti=ti, n=n: e.copy(out=Vx[0:n, ti, 0:256], in_=p[0:n, 0:256]),
                         reads=[p], writes=[Vx])
                    silu_from_psum(p, n, 256, G[0:n, ti, :], G, ga, gb, 256)
                if h + 1 < 16:
                    load_head(h + 1)
                for (t0, N) in QB:
                    t0v = vidx(t0)
                    nqs = (N + 127) // 128
                    for j in range(2):
                        acc = ps[2:6]
                        pend = []
                        stb = [ps[0], ps[1], ps[6]] if ST3 else [ps[0], ps[1]]
                        for kt, (s0, kn) in enumerate(TT):
                            s0v = vidx(s0)
                            stp = stb[kt % len(stb)]
                            k.mms([lambda e: e.matmul(stp[0:kn, 0:N], lhsT=KT[:, j, s0:s0 + kn], rhs=QT[:, j, t0:t0 + N],
                                                      start=True, stop=True)], reads=[KT, QT], writes=[stp])
                            dt_ap, dbuf, coef, cb = bias_tile(t0v, N, s0v, kn, sl[h], h)
                            tf = tmpf[pp % 4]
                            ptile = PT[pp % 4]
                            pp += 1
                            k.op("dve", lambda e: e.scalar_tensor_tensor(out=tf[0:kn, 0:N], in0=dt_ap, scalar=coef,
                                                                         in1=stp[0:kn, 0:N], op0=ALU.mult, op1=ALU.add),
                                 reads=[dbuf, stp], writes=[tf])
                            if cb is None:
                                k.op("act", lambda e: e.activation(out=ptile[0:kn, 0:N], in_=tf[0:kn, 0:N], func=AF.Exp),
                                     reads=[tf], writes=[ptile])
                            else:
                                k.op("act", lambda e: e.activation(out=ptile[0:kn, 0:N], in_=tf[0:kn, 0:N], func=AF.Exp,
                                                                   bias=cb), reads=[tf, cbt], writes=[ptile])
                            if len(pend) >= LOOK:
                                pend.pop(0)()
                            def pv(kt=kt, kn=kn, ptile=ptile):
                                for qs in range(nqs):
                                    qn = min(128, N - qs * 128)
                                    a = acc[qs]
                                    k.mms([lambda e: e.matmul(a[0:qn, 0:257], lhsT=ptile[0:kn, qs * 128:qs * 128 + qn],
                                                              rhs=Vx[0:kn, kt, :], start=(kt == 0), stop=(kt == 16))],
                                          reads=[ptile, Vx], writes=[a])
                            pend.append(pv)
                        while pend:
                            pend.pop(0)()
                        for qs in range(nqs):
                            qn = min(128, N - qs * 128)
                            k.op("act" if qs % 2 == 0 else "dve",
                                 lambda e, qs=qs, qn=qn: (e.copy if qs % 2 == 0 else e.tensor_copy)(
                                     out=OT[j][0:qn, qs, :], in_=acc[qs][0:qn, 0:257]),
                                 reads=[acc[qs]], writes=[Oq[j][qs]])
                    nq = nqs
                    qn = min(128, N)
                    ti0 = t0 // 128
                    o1, o2 = OT[0].t, OT[1].t
                    rd1, rd2 = list(Oq[0][0:nq]), list(Oq[1][0:nq])
                    SMv = SM.t
                    bc = lambda col: SMv[0:qn, 0:nq, col:col + 1].to_broadcast([qn, nq, 256])
                    k.op("dve", lambda e: e.reciprocal(out=SMv[0:qn, 0:nq, 0:1], in_=o1[0:qn, 0:nq, 256:257]),
                         reads=rd1, writes=[SM])
                    k.op("dve", lambda e: e.reciprocal(out=SMv[0:qn, 0:nq, 1:2], in_=o2[0:qn, 0:nq, 256:257]),
                         reads=rd2, writes=[SM])
                    k.op("dve", lambda e: e.tensor_scalar(out=SMv[0:qn, 0:nq, 2:3], in0=SMv[0:qn, 0:nq, 1:2],
                                                          scalar1=nl[0:qn, widx:widx + 1], scalar2=None, op0=ALU.mult),
                         reads=[SM, nl], writes=[SM])
                    k.op("dve", lambda e: e.tensor_tensor(out=A1[0:qn, 0:nq, :], in0=o1[0:qn, 0:nq, 0:256], in1=bc(0), op=ALU.mult),
                         reads=rd1 + [SM], writes=[A1])
                    k.op("dve", lambda e: e.tensor_tensor(out=A2[0:qn, 0:nq, :], in0=o2[0:qn, 0:nq, 0:256], in1=bc(2), op=ALU.mult),
                         reads=rd2 + [SM], writes=[A2])
                    k.op("dve", lambda e: e.tensor_tensor(out=A2[0:qn, 0:nq, :], in0=A2[0:qn, 0:nq, :], in1=A1[0:qn, 0:nq, :],
                                                          op=ALU.add), reads=[A2, A1], writes=[A2])
                    for qs in range(nq):
                        k.op("act", lambda e, qs=qs: e.activation(out=A1[0:qn, qs, :], in_=A2[0:qn, qs, :], func=AF.Square,
                                                                  accum_out=SS[qs][0:qn, 0:1]), reads=[A2], writes=[SS[qs]])
                        k.op("act", lambda e, qs=qs: e.activation(out=SS[qs][0:qn, 1:2], in_=SS[qs][0:qn, 0:1], func=AF.Ln,
                                                                  scale=1.0 / 256, bias=epsc[0:qn, 0:1]),
                             reads=[SS[qs], epsc], writes=[SS[qs]])
                        k.op("act", lambda e, qs=qs: e.activation(out=SMv[0:qn, qs, 5:6], in_=SS[qs][0:qn, 1:2], func=AF.Exp,
                                                                  scale=-0.5, bias=epsc[0:qn, 1 + widx:2 + widx]),
                             reads=[SS[qs], epsc], writes=[SM])
                    k.op("dve", lambda e: e.tensor_tensor(out=A1[0:qn, 0:nq, :], in0=A2[0:qn, 0:nq, :], in1=bc(5), op=ALU.mult),
                         reads=[A2, SM], writes=[A1])
                    k.op("dve", lambda e: e.tensor_tensor(out=A1[0:qn, 0:nq, :], in0=A1[0:qn, 0:nq, :],
                                                          in1=gn[0:qn, :].unsqueeze(1).to_broadcast([qn, nq, 256]), op=ALU.mult),
                         reads=[A1, gn], writes=[A1])
                    k.op("dve", lambda e: e.tensor_tensor(out=YB[0:qn, 0:nq, :], in0=A1[0:qn, 0:nq, :],
                                                          in1=G[0:qn, ti0:ti0 + nq, :], op=ALU.mult), reads=[A1, G], writes=[YB])
                    pt = nxt("pst", pstq)
                    k.mms([lambda e, c=c, qs=qs: e.transpose(out=pt[:, (c * nq + qs) * 128:(c * nq + qs) * 128 + qn],
                                                             in_=YB[0:qn, qs, c * 128:(c + 1) * 128],
                                                             identity=ident[0:qn, 0:qn]) for c in range(2) for qs in range(nq)],
                          reads=[YB, ident], writes=[pt])
                    wd = (nq - 1) * 128 + qn
                    k.op("act", lambda e: e.copy(out=yT[:, :, t0:t0 + wd],
                                                 in_=pt[:, 0:2 * nq * 128].rearrange("p (c t) -> p c t", c=2)[:, :, 0:wd]),
                         reads=[pt], writes=[yT])
                k.dma("pool", yTd[2 * h:2 * h + 2, :, :].rearrange("c p t -> p c t"), yT[:], reads=[yT], writes=[yTd_b[h]])

        def proj_fm_cb(xnT, w, c0, cb):
            for bi, (s, n) in enumerate(QB):
                p = nxt("ps", PSP[0])
                k.mms([lambda e, c=c, p=p: e.matmul(p[:, 0:n], lhsT=w[:, c, c0:c0 + 128], rhs=xnT[:, c, s:s + n],
                                                    start=(c == 0), stop=(c == 15)) for c in range(16)],
                      reads=[xnT, w], writes=[p])
                cb(p, s, n)

        def load_w128(sidx_row0, dst, c0):
            stg = nxt("wst", wst)
            k.dma("sp", stg[:], wina_d[sidx_row0:sidx_row0 + 128, :], writes=[stg])
            k.op("pool", lambda e: e.tensor_copy(out=dst[:, :, c0:c0 + 128],
                                                 in_=stg[:].rearrange("p (c n) -> p c n", c=16)),
                 reads=[stg], writes=[dst])

        def even_mixer_A(widx, xnT, ls):
            wb = [k.sb([128, 16, 256], BF16, ls, "wb") for _ in range(2)] + [k.sb([128, 16, 128], BF16, ls, "wb")]
            qA = k.sb([128, L], BF16, ls, "qA")
            qB = k.sb([128, L], BF16, ls, "qB")
            T1 = k.sb([128, L], F32, ls, "T1")
            T1b = k.sb([128, L], F32, ls, "T1b")
            T2 = k.sb([128, L], F32, ls, "T2")
            T3 = k.sb([128, L], F32, ls, "T3")
            QEA = k.sb([128, L], BF16, ls, "QEA")
            QEB = k.sb([128, L], BF16, ls, "QEB")
            KE = k.sb([128, L], BF16, ls, "KE")
            KL = k.sb([128, L], BF16, ls, "KL")
            V = k.sb([128, 17, 128], BF16, ls, "V")
            G = k.sb([128, 17, 128], BF16, ls, "G")
            OF = k.sb([128, 17, 128], F32, ls, "OF")
            yT = k.sb([128, L], BF16, ls, "yT")
            smk = k.sb([128, L], BF16, ls, "smk")
            mAB = k.sb([128, 2, 512], BF16, ls, "mAB")
            mtri = k.sb([128, 2, 128], F32, ls, "mtri")
            lbt = k.sb([128, 64], F32, ls, "lbt")
            lbc = k.sb([128, 2, 16], F32, ls, "lbc")
            omc = k.sb([128, 2, 16], F32, ls, "omc")
            gna = k.sb([128, 128], F32, ls, "gna")
            ATm = [k.sb([128, 128], BF16, ls, "ATm") for _ in range(3)]
            KLt = [[k.sb([128, 128], BF16, ls, "KLt") for _ in range(2)] for _ in range(2)]
            S = [k.sb([128, 128], F32, ls, "S") for _ in range(4)]
            SBALL = k.sb([128, 33, 128], BF16, ls, "SBALL")
            ECs = [k.sb([128, 34], F32, ls, "EC") for _ in range(2)]
            k.op("pool", lambda e: e.memset(SBALL[:, 0, :], 0.0), writes=[SBALL])
            ubanks = [ps[4], ps[5], ps[6]]
            PSP[0] = ps[0:4]
            ga = k.sb([128, 128], F32, ls, "ga")
            gb = k.sb([128, 128], F32, ls, "gb")
            sm = [k.sb([128, 8], F32, ls, "sm") for _ in range(2)]
            a1 = [k.sb([128, 128], F32, ls, "a1") for _ in range(2)]
            a2 = [k.sb([128, 128], F32, ls, "a2") for _ in range(2)]
            jk = k.sb([128, 128], F32, ls, "jk")
            ybf = [k.sb([128, 128], BF16, ls, "ybf") for _ in range(2)]
            k.dma("pool", smk[:], smask_d[:, :], writes=[smk])
            k.dma("pool", mAB[:], mab_d[:, :].rearrange("p (a n) -> p a n", a=2), writes=[mAB])
            k.dma("sp", mtri[:], mtri_d[:, :].rearrange("p (a n) -> p a n", a=2), writes=[mtri])
            k.dma("sp", lbt[:], lb_d[:, :], writes=[lbt])
            k.dma("sp", gna[:], hnorm_d[widx:widx + 1, :].partition_broadcast(128), writes=[gna])
            for par in range(2):
                for ab in range(2):
                    k.op("pool", lambda e, par=par, ab=ab: e.memset(KLt[par][ab][:], 0.0), writes=[KLt[par][ab]])
            lb4 = lbt[:].rearrange("p (r l h) -> p r l h", r=2, l=2)
            if widx == 0:
                k.op("pool", lambda e: e.memset(lbc[:], 0.0), writes=[lbc])
                k.op("pool", lambda e: e.memset(omc[:], 1.0), writes=[omc])
            else:
                k.op("dve", lambda e: e.tensor_tensor(out=omc[:], in0=lb4[:, :, 0, :], in1=lb4[:, :, 1, :], op=ALU.subtract),
                     reads=[lbt], writes=[omc])
                k.op("act", lambda e: e.activation(out=omc[:], in_=omc[:], func=AF.Exp), reads=[omc], writes=[omc])
                k.op("dve", lambda e: e.tensor_scalar(out=omc[:], in0=omc[:], scalar1=1.0, scalar2=None, op0=ALU.add),
                     reads=[omc], writes=[omc])
                k.op("dve", lambda e: e.reciprocal(out=lbc[:], in_=omc[:]), reads=[omc], writes=[lbc])
                k.op("dve", lambda e: e.tensor_scalar(out=omc[:], in0=lbc[:], scalar1=-1.0, scalar2=1.0, op0=ALU.mult,
                                                      op1=ALU.add), reads=[lbc], writes=[omc])
            for h in range(16):
                wq, wzf, wzb, wv, wg = (wb[0], 0), (wb[0], 128), (wb[2], 0), (wb[1], 0), (wb[1], 128)

                def load_head(h_):
                    for (wt, c0), si in ((wzf, 16 + h_), (wzb, 32 + h_), (wq, h_), (wv, 48 + h_), (wg, 64 + h_)):
                        load_w128((widx * 120 + si) * 128, wt, c0)
                if h == 0:
                    load_head(0)
                for di_ in range(2):
                    wz_ = wzf if di_ == 0 else wzb
                    Te = T1 if di_ == 0 else T1b

                    def cbz(p, s, n, Te=Te):
                        k.op("act", lambda e: e.activation(out=Te[:, s:s + n], in_=p[:, 0:n], func=AF.Exp, scale=-1.0),
                             reads=[p], writes=[Te])
                    proj_fm_cb(xnT, wz_[0], wz_[1], cbz)
                def cbq(p, s, n):
                    k.op("dve", lambda e: e.tensor_tensor(out=qA[:, s:s + n], in0=p[:, 0:n], in1=mAB[:, 0, 0:n], op=ALU.mult),
                         reads=[p, mAB], writes=[qA])
                    k.op("dve", lambda e: e.tensor_tensor(out=qB[:, s:s + n], in0=p[:, 0:n], in1=mAB[:, 1, 0:n], op=ALU.mult),
                         reads=[p, mAB], writes=[qB])
                proj_fm_cb(xnT, wq[0], wq[1], cbq)
                def ew(di, T1):
                    ops = []
                    EC = ECs[di]
                    xv = lambda t_: t_[:, 0:2048].rearrange("p (c t) -> p c t", t=64)
                    ops.append(lambda: k.op("act", lambda e: e.activation(out=T2[:], in_=T1[:], func=AF.Ln, bias=epsc[:, 3:4]),
                         reads=[T1, epsc], writes=[T2]))
                    if widx == 0:
                        ops.append(lambda: k.op("dve", lambda e: e.tensor_scalar(out=T3[:], in0=T2[:], scalar1=-1.0, scalar2=None, op0=ALU.mult),
                             reads=[T2], writes=[T3]))
                    else:
                        ops.append(lambda: k.op("act", lambda e: e.activation(out=T3[:], in_=T1[:], func=AF.Ln, scale=lbc[:, di, h:h + 1],
                                                           bias=epsc[:, 3:4]), reads=[T1, lbc, epsc], writes=[T3]))
                        ops.append(lambda: k.op("dve", lambda e: e.tensor_tensor(out=T3[:], in0=T3[:], in1=T2[:], op=ALU.subtract),
                             reads=[T3, T2], writes=[T3]))
                    ops.append(lambda: k.op("act", lambda e: e.activation(out=T2[:], in_=T2[:], func=AF.Exp, scale=-1.0), reads=[T2], writes=[T2]))
                    ops.append(lambda: k.op("dve", lambda e: e.scalar_tensor_tensor(out=T1[:], in0=T1[:], scalar=omc[:, di, h:h + 1], in1=T2[:],
                                                                 op0=ALU.mult, op1=ALU.mult), reads=[T1, omc, T2], writes=[T1]))
                    if di == 0:
                        ops.append(lambda: k.op("dve", lambda e: e.tensor_tensor_scan(out=T2[:], data0=smk[:], data1=T3[:], initial=0.0,
                                                                   op0=ALU.mult, op1=ALU.add), reads=[smk, T3], writes=[T2]))
                        ops.append(lambda: k.op("act", lambda e: e.activation(out=T3[:], in_=T2[:], func=AF.Exp), reads=[T2], writes=[T3]))
                        ops.append(lambda: k.op("pool", lambda e: e.tensor_copy(out=EC[:, 0:32], in_=xv(T3)[:, :, 63]), reads=[T3], writes=[EC]))
                        ops.append(lambda: k.op("pool", lambda e: e.tensor_copy(out=EC[:, 32:33], in_=T3[:, 2063:2064]), reads=[T3], writes=[EC]))
                        ops.append(lambda: k.op("act", lambda e: e.activation(out=T2[:], in_=T2[:], func=AF.Exp, scale=-1.0),
                             reads=[T2], writes=[T2]))
                        ops.append(lambda: k.op("dve", lambda e: e.tensor_tensor(out=QEA[:], in0=qA[:], in1=T3[:], op=ALU.mult),
                             reads=[qA, T3], writes=[QEA]))
                        ops.append(lambda: k.op("dve", lambda e: e.tensor_tensor(out=QEB[:], in0=qB[:], in1=T3[:], op=ALU.mult),
                             reads=[qB, T3], writes=[QEB]))
                        ops.append(lambda: k.op("dve", lambda e: e.tensor_tensor(out=KE[:], in0=T1[:], in1=T2[:], op=ALU.mult),
                             reads=[T1, T2], writes=[KE]))
                        ops.append(lambda: k.op("dve", lambda e: e.tensor_tensor(
                            out=xv(KL), in0=xv(KE), in1=xv(T3)[:, :, 63:64].to_broadcast([128, 32, 64]), op=ALU.mult),
                            reads=[KE, T3], writes=[KL]))
                        ops.append(lambda: k.op("dve", lambda e: e.tensor_tensor(out=KL[:, 2048:2064], in0=KE[:, 2048:2064],
                                                              in1=T3[:, 2063:2064].to_broadcast([128, 16]), op=ALU.mult),
                             reads=[KE, T3], writes=[KL]))
                        KLs = KL
                    else:
                        ops.append(lambda: k.op("pool", lambda e: e.memset(T2[:, 0:1], 0.0), writes=[T2]))
                        ops.append(lambda: k.op("dve", lambda e: e.tensor_tensor_scan(out=T2[:, 1:L], data0=T3[:, 0:L - 1], data1=smk[:, 1:L],
                                                                   initial=0.0, op0=ALU.add, op1=ALU.mult),
                             reads=[smk, T3], writes=[T2]))
                        ops.append(lambda: k.op("dve", lambda e: e.tensor_tensor(out=EC[:, 0:32], in0=xv(T2)[:, :, 63], in1=xv(T3)[:, :, 63],
                                                              op=ALU.add), reads=[T2, T3], writes=[EC]))
                        ops.append(lambda: k.op("dve", lambda e: e.tensor_tensor(out=EC[:, 32:33], in0=T2[:, 2063:2064], in1=T3[:, 2063:2064],
                                                              op=ALU.add), reads=[T2, T3], writes=[EC]))
                        ops.append(lambda: k.op("act", lambda e: e.activation(out=EC[:, 0:33], in_=EC[:, 0:33], func=AF.Exp), reads=[EC], writes=[EC]))
                        ops.append(lambda: k.op("act", lambda e: e.activation(out=T3[:], in_=T2[:], func=AF.Exp, scale=-1.0),
                             reads=[T2], writes=[T3]))
                        ops.append(lambda: k.op("act", lambda e: e.activation(out=T2[:], in_=T2[:], func=AF.Exp), reads=[T2], writes=[T2]))
                        ops.append(lambda: k.op("dve", lambda e: e.tensor_tensor(out=QEA[:], in0=qA[:], in1=T3[:], op=ALU.mult),
                             reads=[qA, T3], writes=[QEA]))
                        ops.append(lambda: k.op("dve", lambda e: e.tensor_tensor(out=QEB[:], in0=qB[:], in1=T3[:], op=ALU.mult),
                             reads=[qB, T3], writes=[QEB]))
                        ops.append(lambda: k.op("dve", lambda e: e.tensor_tensor(out=KE[:], in0=T1[:], in1=T2[:], op=ALU.mult),
                             reads=[T1, T2], writes=[KE]))
                        KLs = KE
                    return ops, KLs

                ops0, KLs0 = ew(0, T1)
                for ti, (s, n) in enumerate(TT):
                    p = nxt("ps", PSP[0])
                    proj_tm(xnT, wb[1], ti, p, 256, 0)
                    k.op("act", lambda e: e.copy(out=V[0:n, ti, :], in_=p[0:n, 0:128]), reads=[p], writes=[V])
                    silu_from_psum(p, n, 128, G[0:n, ti, :], G, ga, gb, 128)
                    if ops0:
                        ops0.pop(0)()
                while ops0:
                    ops0.pop(0)()
                if h + 1 < 16:
                    load_head(h + 1)
                for di in range(2):
                    EC = ECs[di]
                    if di == 0:
                        KLs = KLs0
                    else:
                        ops1, KLs = ew(1, T1b)
                        for f_ in ops1:
                            f_()
                    if di == 0:
                        seq = [(16, 0)] + [(t_, ab_) for t_ in range(16) for ab_ in (0, 1)]
                    else:
                        seq = [(t_, ab_) for t_ in range(15, -1, -1) for ab_ in (1, 0)] + [(16, 0)]
                    cids = [32 if t_ == 16 else 2 * t_ + ab_ for (t_, ab_) in seq]
                    order = [16] + list(range(16)) if di == 0 else list(range(15, -1, -1)) + [16]
                    seqidx = {}
                    k.op("pool", lambda e: e.memset(S[0][:], 0.0), writes=[S[0]])
                    if di == 0:
                        groups = [[16]] + [[2 * g_, 2 * g_ + 1] for g_ in range(8)]
                    else:
                        groups = [[15 - 2 * g_, 14 - 2 * g_] for g_ in range(8)] + [[16]]
                    step = 0
                    tcount = 0
                    for gt in groups:
                        bank = ubanks[st["uq"] % 3]
                        st["uq"] += 1
                        slots = []
                        for ti in gt:
                            s, n = TT[ti]
                            par = tcount % 2
                            tcount += 1
                            pt = nxt("pst", pstq)
                            k.mms([lambda e: e.transpose(out=pt[0:n, 0:128], in_=KLs[:, s:s + n], identity=ident[:, :])],
                                  reads=[KLs, ident], writes=[pt])
                            nA = min(n, 64)
                            k.op("act", lambda e: e.copy(out=KLt[par][0][0:nA, :], in_=pt[0:nA, 0:128]), reads=[pt],
                                 writes=[KLt[par][0]])
                            if n == 128:
                                k.op("act", lambda e: e.copy(out=KLt[par][1][64:128, :], in_=pt[64:128, 0:128]), reads=[pt],
                                     writes=[KLt[par][1]])
                            chunks = [0] if n == 16 else ([0, 1] if di == 0 else [1, 0])
                            for ab in chunks:
                                cid = 32 if ti == 16 else 2 * ti + ab
                                q_ = len(slots)
                                seqidx[(ti, ab)] = step + q_
                                k.mms([lambda e: e.matmul(bank[:, q_ * 128:(q_ + 1) * 128], lhsT=KLt[par][ab][0:n, :],
                                                          rhs=V[0:n, ti, :], start=True, stop=True)],
                                      reads=[KLt[par][ab], V], writes=[bank])
                                slots.append((q_, cid))
                        for (q_, cid) in slots:
                            if step < 32:
                                so, sn = S[step % 4], S[(step + 1) % 4]
                                k.op("dve", lambda e: e.scalar_tensor_tensor(out=sn[:], in0=so[:], scalar=EC[:, cid:cid + 1],
                                                                             in1=bank[:, q_ * 128:(q_ + 1) * 128],
                                                                             op0=ALU.mult, op1=ALU.add),
                                     reads=[so, EC, bank], writes=[sn])
                                if di == 0:
                                    k.op("pool", lambda e: e.tensor_copy(out=SBALL[:, step + 1, :], in_=sn[:]), reads=[sn],
                                         writes=[SBALL])
                                else:
                                    cn = cids[step + 1]
                                    k.op("pool", lambda e: e.tensor_tensor(out=SBALL[:, step + 1, :], in0=sn[:],
                                                                           in1=EC[:, cn:cn + 1].to_broadcast([128, 128]),
                                                                           op=ALU.mult), reads=[sn, EC], writes=[SBALL])
                            step += 1
                    def at_stage(oi, ti):
                        s, n = TT[ti]
                        pa = nxt("ps", PSP[0])
                        k.mms([lambda e: e.matmul(pa[0:n, 0:n], lhsT=KE[:, s:s + n], rhs=QEA[:, s:s + n], start=True, stop=False),
                               lambda e: e.matmul(pa[0:n, 0:n], lhsT=KE[:, s:s + n], rhs=QEB[:, s:s + n], start=False, stop=True)],
                              reads=[KE, QEA, QEB], writes=[pa])
                        at = ATm[oi % 3]
                        k.op("dve", lambda e: e.tensor_tensor(out=at[0:n, 0:n], in0=pa[0:n, 0:n], in1=mtri[0:n, di, 0:n],
                                                              op=ALU.mult), reads=[pa, mtri], writes=[at])
                    at_stage(0, order[0])
                    pendB, pendC = [], []
                    for oi, ti in enumerate(order):
                        s, n = TT[ti]
                        if oi + 1 < len(order):
                            at_stage(oi + 1, order[oi + 1])
                        at = ATm[oi % 3]
                        chunks = [0] if n == 16 else ([0, 1] if di == 0 else [1, 0])
                        po = nxt("ps", PSP[0])
                        fns = [lambda e: e.matmul(po[0:n, 0:128], lhsT=at[0:n, 0:n], rhs=V[0:n, ti, :], start=True, stop=False)]
                        for ci, ab in enumerate(chunks):
                            qe = QEA if ab == 0 else QEB
                            sbi = seqidx[(ti, ab)]
                            fns.append(lambda e, qe=qe, sbi=sbi, last=(ci == len(chunks) - 1): e.matmul(
                                po[0:n, 0:128], lhsT=qe[:, s:s + n], rhs=SBALL[:, sbi, :], start=False, stop=last))
                        k.mms(fns, reads=[at, V, QEA, QEB, SBALL], writes=[po])
                        if di == 0:
                            k.op("act", lambda e: e.copy(out=OF[0:n, ti, :], in_=po[0:n, 0:128]), reads=[po], writes=[OF])
                        else:
                            s_ = sm[oi % 2]
                            x1, x2, yb = a1[oi % 2], a2[oi % 2], ybf[oi % 2]
                            k.op("dve", lambda e: e.tensor_tensor(out=x1[0:n, :], in0=po[0:n, 0:128], in1=OF[0:n, ti, :],
                                                                  op=ALU.add), reads=[po, OF], writes=[x1])
                            k.op("act", lambda e: e.activation(out=jk[0:n, :], in_=x1[0:n, :], func=AF.Square,
                                                               accum_out=s_[0:n, 0:1]), reads=[x1], writes=[jk, s_])
                            k.op("act", lambda e: e.activation(out=s_[0:n, 1:2], in_=s_[0:n, 0:1], func=AF.Ln,
                                                               scale=1.0 / 128, bias=epsc[0:n, 0:1]),
                                 reads=[s_, epsc], writes=[s_])
                            k.op("act", lambda e: e.activation(out=s_[0:n, 2:3], in_=s_[0:n, 1:2], func=AF.Exp, scale=-0.5),
                                 reads=[s_], writes=[s_])
                            def stB(s=s, n=n, ti=ti, s_=s_, x1=x1, x2=x2, yb=yb):
                                k.op("dve", lambda e: e.scalar_tensor_tensor(out=x2[0:n, :], in0=x1[0:n, :], scalar=s_[0:n, 2:3],
                                                                             in1=gna[0:n, :], op0=ALU.mult, op1=ALU.mult),
                                     reads=[x1, s_, gna], writes=[x2])
                                k.op("dve", lambda e: e.tensor_tensor(out=yb[0:n, :], in0=x2[0:n, :], in1=G[0:n, ti, :],
                                                                       op=ALU.mult), reads=[x2, G], writes=[yb])

                                def stC():
                                    pt2 = nxt("pst", pstq)
                                    k.mms([lambda e: e.transpose(out=pt2[:, 0:n], in_=yb[0:n, :], identity=ident[0:n, 0:n])],
                                          reads=[yb, ident], writes=[pt2])
                                    k.op("act", lambda e: e.copy(out=yT[:, s:s + n], in_=pt2[:, 0:n]), reads=[pt2], writes=[yT])
                                pendC.append(stC)
                            runB, runC = pendB[:], pendC[:]
                            del pendB[:]
                            del pendC[:]
                            for f_ in runB:
                                f_()
                            for f_ in runC:
                                f_()
                            pendB.append(stB)
                    for f_ in pendB[:]:
                        f_()
                    for f_ in pendC[:]:
                        f_()
                k.dma("pool", yTd[h, :, :], yT[:], reads=[yT], writes=[yTd_b[h]])
            PSP[0] = ps

        def even_mixer_B(widx, xnT, ls):
            sl = slopes16()
            load_consts(ls)
            dabs = CT["dabs"]
            wb = [k.sb([128, 16, 256], BF16, ls, "wb") for _ in range(3)]
            QT = k.sb([128, 1, L], BF16, ls, "QT")
            KT = k.sb([128, 1, L], BF16, ls, "KT")
            Vx = k.sb([128, 17, 129], BF16, ls, "Vx")
            G = k.sb([128, 17, 128], BF16, ls, "G")
            yT = k.sb([128, L], BF16, ls, "yT")
            wabs = k.sb([128, 1152], F32, ls, "wabs")
            mclip = k.sb([128, 1024], F32, ls, "mclip")
            sink = k.sb([128, 16], F32, ls, "sink")
            esink = k.sb([128, 16], F32, ls, "esink")
            tmpf = [k.sb([128, 512], F32, ls, "tmpf") for _ in range(4)]
            PT = [k.sb([128, 512], BF16, ls, "PT") for _ in range(4)]
            OT = k.sb([128, 4, 129], F32, ls, "OT")
            Oq = [Buf(OT.t[:, q_, :]) for q_ in range(4)]
            SM = k.sb([128, 4, 4], F32, ls, "SM")
            A1 = k.sb([128, 4, 128], F32, ls, "A1")
            YB = k.sb([128, 4, 128], BF16, ls, "YB")
            ga = k.sb([128, 128], F32, ls, "ga")
            gb = k.sb([128, 128], F32, ls, "gb")
            sm = [k.sb([128, 4], F32, ls, "sm") for _ in range(2)]
            a1 = [k.sb([128, 128], F32, ls, "a1") for _ in range(2)]
            ybf = [k.sb([128, 128], BF16, ls, "ybf") for _ in range(2)]
            k.dma("sp", wabs[:], wabs_d[:, :], writes=[wabs])
            k.dma("sp", mclip[:], mclip_d[:, :], writes=[mclip])
            k.dma("sp", sink[:], sink_d[widx:widx + 1, :].partition_broadcast(128), writes=[sink])
            k.op("act", lambda e: e.activation(out=esink[:], in_=sink[:], func=AF.Exp), reads=[sink], writes=[esink])
            k.op("pool", lambda e: e.memset(Vx[:, :, 128:129], 1.0), writes=[Vx])
            pp = 0
            def load_kv(kv_):
                load_w128((widx * 120 + 96 + kv_) * 128, wb[0], 0)
                load_w128((widx * 120 + 100 + kv_) * 128, wb[0], 128)

            def load_qg(hq_):
                load_w128((widx * 120 + 80 + hq_) * 128, wb[1 + hq_ % 2], 0)
                load_w128((widx * 120 + 104 + hq_) * 128, wb[1 + hq_ % 2], 128)
            load_kv(0)
            load_qg(0)
            for hq in range(16):
                kv = hq // 4
                if hq % 4 == 0:
                    wk, wv = (wb[0], 0), (wb[0], 128)
                    proj_fm(xnT, wk[0], wk[1], KT, 0, None)
                    for ti, (s, n) in enumerate(TT):
                        p = nxt("ps", PSP[0])
                        proj_tm(xnT, wv[0], ti, p, 128, wv[1])
                        k.op("act", lambda e: e.copy(out=Vx[0:n, ti, 0:128], in_=p[0:n, 0:128]), reads=[p], writes=[Vx])
                wq, wg = (wb[1 + hq % 2], 0), (wb[1 + hq % 2], 128)
                if hq + 1 < 16:
                    load_qg(hq + 1)
                    if (hq + 1) % 4 == 0:
                        load_kv((hq + 1) // 4)
                proj_fm(xnT, wq[0], wq[1], QT, 0, 128 ** -0.5)
                for ti, (s, n) in enumerate(TT):
                    p = nxt("ps", PSP[0])
                    proj_tm(xnT, wg[0], ti, p, 128, wg[1])
                    silu_from_psum(p, n, 128, G[0:n, ti, :], G, ga, gb)
                for (t0, N) in QB:
                    t0v = vidx(t0)
                    nqs = (N + 127) // 128
                    acc = ps[2:6]
                    if t0 < 2048:
                        xt = [s0 for s0 in range(t0 - 128, t0 + N + 1, 128) if 0 <= s0 <= 1920]
                    else:
                        xt = [0]
                    ktl = [(2048, 16)] + [(s0, 128) for s0 in xt]
                    pend = []
                    stb = [ps[0], ps[1], ps[6]] if ST3 else [ps[0], ps[1]]
                    for ki, (s0, kn) in enumerate(ktl):
                        s0v = vidx(s0)
                        kt = s0 // 128
                        stp = stb[ki % len(stb)]
                        k.mms([lambda e: e.matmul(stp[0:kn, 0:N], lhsT=KT[:, 0, s0:s0 + kn], rhs=QT[:, 0, t0:t0 + N],
                                                  start=True, stop=True)], reads=[KT, QT], writes=[stp])
                        if s0 == 2048 and t0 < 2048:
                            c0 = min(t0, 512)
                            dt_ap, dbuf = mclip[0:16, c0:c0 + N], mclip
                        elif s0 == 2048:
                            dt_ap, dbuf = dabs[0:16, 384:384 + N], dabs
                        else:
                            off = t0v - s0v
                            dt_ap, dbuf = wabs[0:kn, 512 + off:512 + off + N], wabs
                        tf = tmpf[pp % 4]
                        ptile = PT[pp % 4]
                        pp += 1
                        k.op("dve", lambda e: e.scalar_tensor_tensor(out=tf[0:kn, 0:N], in0=dt_ap, scalar=-sl[hq],
                                                                     in1=stp[0:kn, 0:N], op0=ALU.mult, op1=ALU.add),
                             reads=[dbuf, stp], writes=[tf])
                        k.op("act", lambda e: e.activation(out=ptile[0:kn, 0:N], in_=tf[0:kn, 0:N], func=AF.Exp),
                             reads=[tf], writes=[ptile])
                        if len(pend) >= LOOK:
                            pend.pop(0)()
                        def pv(s0=s0, kn=kn, kt=kt, ptile=ptile):
                            for qs in range(nqs):
                                qn = min(128, N - qs * 128)
                                tok0 = t0 + qs * 128
                                if t0 < 2048:
                                    rel = [s_ for s_ in (tok0 - 128, tok0, tok0 + 128) if 0 <= s_ <= 1920]
                                else:
                                    rel = [0]
                                if s0 != 2048 and s0 not in rel:
                                    continue
                                a = acc[qs]
                                k.mms([lambda e: e.matmul(a[0:qn, 0:129], lhsT=ptile[0:kn, qs * 128:qs * 128 + qn],
                                                          rhs=Vx[0:kn, kt, :], start=(s0 == 2048), stop=(s0 == rel[-1]))],
                                      reads=[ptile, Vx], writes=[a])
                        pend.append(pv)
                    while pend:
                        pend.pop(0)()
                    nq = nqs
                    qn = min(128, N)
                    ti0 = t0 // 128
                    for qs in range(nq):
                        k.op("act" if qs % 2 == 0 else "dve",
                             lambda e, qs=qs: (e.copy if qs % 2 == 0 else e.tensor_copy)(
                                 out=OT[0:qn, qs, :], in_=acc[qs][0:qn, 0:129]), reads=[acc[qs]], writes=[Oq[qs]])
                    rdo = list(Oq[0:nq])
                    ot = OT.t
                    SMv = SM.t
                    k.op("dve", lambda e: e.tensor_scalar(out=SMv[0:qn, 0:nq, 0:1], in0=ot[0:qn, 0:nq, 128:129],
                                                          scalar1=esink[0:qn, hq:hq + 1], scalar2=None, op0=ALU.add),
                         reads=rdo + [esink], writes=[SM])
                    k.op("dve", lambda e: e.reciprocal(out=SMv[0:qn, 0:nq, 1:2], in_=SMv[0:qn, 0:nq, 0:1]), reads=[SM], writes=[SM])
                    k.op("dve", lambda e: e.tensor_tensor(out=A1[0:qn, 0:nq, :], in0=ot[0:qn, 0:nq, 0:128],
                                                          in1=SMv[0:qn, 0:nq, 1:2].to_broadcast([qn, nq, 128]), op=ALU.mult),
                         reads=rdo + [SM], writes=[A1])
                    k.op("dve", lambda e: e.tensor_tensor(out=YB[0:qn, 0:nq, :], in0=A1[0:qn, 0:nq, :],
                                                          in1=G[0:qn, ti0:ti0 + nq, :], op=ALU.mult), reads=[A1, G], writes=[YB])
                    pt = nxt("pst", pstq)
                    k.mms([lambda e, qs=qs: e.transpose(out=pt[:, qs * 128:qs * 128 + qn], in_=YB[0:qn, qs, :],
                                                        identity=ident[0:qn, 0:qn]) for qs in range(nq)],
                          reads=[YB, ident], writes=[pt])
                    wd = (nq - 1) * 128 + qn
                    k.op("act", lambda e: e.copy(out=yT[:, t0:t0 + wd], in_=pt[:, 0:wd]), reads=[pt], writes=[yT])
                k.dma("pool", yTd[16 + hq, :, :], yT[:], reads=[yT], writes=[yTd_b[hq]])

        def phase_out(first, wout_d, widx, ls):
            wob = [k.sb([128, 32, 512], BF16, ls, "wob") for _ in range(2)]
            yb = [k.sb([128, 32, 512], BF16, ls, "ytb") for _ in range(2)]
            hb = [k.sb([128, 512], F32, ls, "hb") for _ in range(4)]
            ho = [k.sb([128, 512], F32, ls, "ho") for _ in range(4)]
            cnt = 0

            def load_w(nb_):
                wo_ = wob[nb_ % 2]
                stgs = []
                for pc in range(8):
                    stg = nxt("wst", wst)
                    r0 = ((widx * 4 + nb_) * 8 + pc) * 128
                    k.dma("pool", stg[:], wout_d[r0:r0 + 128, :], writes=[stg])
                    stgs.append(stg)
                    if pc >= 1:
                        sp_, pp_ = stgs[pc - 1], pc - 1
                        k.op("pool", lambda e, sp_=sp_, pp_=pp_: e.tensor_copy(
                            out=wo_[:, 4 * pp_:4 * pp_ + 4, :], in_=sp_[:].rearrange("p (c n) -> p c n", c=4)),
                            reads=[sp_], writes=[wo_])
                sp_, pp_ = stgs[7], 7
                k.op("pool", lambda e: e.tensor_copy(
                    out=wo_[:, 4 * pp_:4 * pp_ + 4, :], in_=sp_[:].rearrange("p (c n) -> p c n", c=4)),
                    reads=[sp_], writes=[wo_])

            def load_y(nb_, bi_):
                t0_, N_ = QB[bi_]
                y_ = yb[(nb_ * len(QB) + bi_) % 2]
                k.dma("sp", y_[:, :, 0:N_], yTd[:, :, t0_:t0_ + N_].rearrange("c p t -> p c t"),
                      reads=yTd_b, writes=[y_])

            seq = [(nb_, bi_) for nb_ in range(4) for bi_ in range(len(QB))]
            load_w(0)
            load_y(0, 0)
            for idx, (nb, bi) in enumerate(seq):
                wo = wob[nb % 2]
                t0, N = QB[bi]
                y = yb[idx % 2]
                if bi == 0 and nb + 1 < 4:
                    load_w(nb + 1)
                if idx + 1 < len(seq):
                    load_y(*seq[idx + 1])
                tiles = []
                for qs in range((N + 127) // 128):
                    qn = min(128, N - qs * 128)
                    tok0 = t0 + qs * 128
                    ti = tok0 // 128
                    hi = hb[cnt % 4]
                    hn = ho[cnt % 4]
                    cnt += 1
                    src_ = h_src(first, ti)
                    k.dma("sp", hi[0:qn, :], src_[:, nb * 512:(nb + 1) * 512], reads=[hd_b[ti]], writes=[hi])
                    tiles.append((qs, qn, tok0, ti, hi, hn))
                for (qs, qn, tok0, ti, hi, hn) in tiles:
                    p = nxt("ps", PSP[0])
                    k.mms([lambda e, c=c: e.matmul(p[0:qn, :], lhsT=y[:, c, qs * 128:qs * 128 + qn], rhs=wo[:, c, :],
                                                   start=(c == 0), stop=(c == 31)) for c in range(32)],
                          reads=[y, wo], writes=[p])
                    k.op("dve", lambda e: e.tensor_tensor(out=hn[0:qn, :], in0=p[0:qn, :], in1=hi[0:qn, :], op=ALU.add),
                         reads=[p, hi], writes=[hn])
                    k.dma("sp", hd[tok0:tok0 + qn, nb * 512:(nb + 1) * 512], hn[0:qn, :], reads=[hn],
                          writes=[hd_b[ti]])

        def phase_final(first, ls):
            gt = k.sb([128, D], F32, ls, "gt")
            hb = [k.sb([128, D], F32, ls, "hb") for _ in range(2)]
            ob = [k.sb([128, D], F32, ls, "ob") for _ in range(2)]
            junk = k.sb([128, D], BF16, ls, "junk")
            ssb = [k.sb([128, 2], F32, ls, "ss") for _ in range(2)]
            k.dma("sp", gt[:], nrm_d[4:5, :].partition_broadcast(128), writes=[gt])
            toks = []
            for ti, (s, n) in enumerate(TT[:16]):
                h = hb[ti % 2]
                o = ob[ti % 2]
                ss = ssb[ti % 2]
                k.dma("sp", h[0:n, :], h_src(first, ti), reads=[hd_b[ti]], writes=[h])
                k.op("act", lambda e: e.activation(out=junk[0:n, :], in_=h[0:n, :], func=AF.Square,
                                                   accum_out=ss[0:n, 0:1]), reads=[h], writes=[junk, ss])
                k.op("act", lambda e: e.activation(out=ss[0:n, 1:2], in_=ss[0:n, 0:1], func=AF.Ln, scale=1.0 / D,
                                                   bias=epsc[0:n, 0:1]), reads=[ss, epsc], writes=[ss])
                k.op("act", lambda e: e.activation(out=ss[0:n, 0:1], in_=ss[0:n, 1:2], func=AF.Exp, scale=-0.5),
                     reads=[ss], writes=[ss])
                k.op("dve", lambda e: e.scalar_tensor_tensor(out=o[0:n, :], in0=h[0:n, :], scalar=ss[0:n, 0:1],
                                                             in1=gt[0:n, :], op0=ALU.mult, op1=ALU.mult),
                     reads=[h, ss, gt], writes=[o])
                toks.append(k.dma("pool", out_d[s:s + n, :], o[0:n, :], reads=[o]))
            return toks

        first = True
        for (kind, widx, layer_idx) in layers:
            with ExitStack() as ls:
                xnT = k.sb([128, 16, L], BF16, ls, "xnT")
                with ExitStack() as ls2:
                    phase_norm(first, (0 if kind == "E" else 2) + widx, xnT, ls2)
                    k.barrier()
                with ExitStack() as ls2:
                    if kind == "O":
                        odd_mixer(widx, layer_idx, xnT, ls2)
                    else:
                        even_mixer_A(widx, xnT, ls2)
                    k.barrier()
                if kind == "E":
                    with ExitStack() as ls2:
                        even_mixer_B(widx, xnT, ls2)
                        k.barrier()
            with ExitStack() as ls:
                phase_out(first, woutc_d if kind == "O" else wouta_d, widx, ls)
                k.barrier()
            first = False
        out_toks = []
        with ExitStack() as ls:
            if do_final:
                out_toks = phase_final(first, ls)
            else:
                hb = [k.sb([128, D], F32, ls, "hb") for _ in range(2)]
                for ti, (s, n) in enumerate(TT):
                    h = hb[ti % 2]
                    k.dma("sp", h[0:n, :], h_src(first, ti), reads=[hd_b[ti]], writes=[h])
                    out_toks.append(k.dma("pool", out_d[s:s + n, :], h[0:n, :], reads=[h]))
            k.barrier()
        k.check_deadlock()
    return nc


def const_tables():
    i = np.arange(128, dtype=np.float32)[:, None]
    ident = np.eye(128, dtype=np.float32)
    dlin = (np.arange(512, dtype=np.float32)[None, :] - i).astype(np.float32)
    dabs = np.abs(np.arange(896, dtype=np.float32)[None, :] - i - 384).astype(np.float32)
    sl = np.array(slopes16(), dtype=np.float64)
    cb = (-(sl[:, None] * np.array(DELTAS, dtype=np.float64)[None, :])).reshape(1, -1)
    cb = np.repeat(cb, 128, axis=0).astype(np.float32)
    return {"ident": ident, "dlin": dlin, "dabs": dabs, "cbtab": np.ascontiguousarray(cb)}


def layout_winc(w):
    a = w.reshape(2, 2, 8, 128, 4, 16, 256)
    a = a.transpose(0, 5, 4, 1, 3, 2, 6)
    return np.ascontiguousarray(a).reshape(2 * 16 * 4 * 2 * 128, 2048)


def layout_wina(w):
    a = w.reshape(2, 16, 128, 120, 128)
    a = a.transpose(0, 3, 2, 1, 4)
    return np.ascontiguousarray(a).reshape(2 * 120 * 128, 2048)


def even_tables():
    i = np.arange(128, dtype=np.float32)[:, None]
    smask = np.ones((128, L), np.float32)
    smask[:, 0:2048:64] = 0.0
    smask[:, 2048] = 0.0
    j = np.arange(512)
    mA = ((j % 128) < 64).astype(np.float32)
    mab = np.concatenate([np.tile(mA[None], (128, 1)), np.tile((1 - mA)[None], (128, 1))], axis=1)
    s_ = np.arange(128)[:, None]; t_ = np.arange(128)[None, :]
    same = (s_ // 64) == (t_ // 64)
    mf = (same & (s_ <= t_)).astype(np.float32)
    mb_ = (same & (s_ >= t_)).astype(np.float32)
    mtri = np.concatenate([mf, mb_], axis=1)
    dw = np.abs(np.arange(1152, dtype=np.float32)[None, :] - i - 512)
    wabs = np.where(dw <= 128, dw, 1e9).astype(np.float32)
    mclip = np.minimum(np.arange(1024, dtype=np.float32)[None, :] - i + 16, 128.0).astype(np.float32)
    return {"smask": smask, "mab": np.ascontiguousarray(mab), "mtri": np.ascontiguousarray(mtri),
            "wabs": wabs, "mclip": mclip}


def layout_wout(w):
    a = w.reshape(2, 8, 4, 128, 4, 512)
    a = a.transpose(0, 4, 1, 3, 2, 5)
    return np.ascontiguousarray(a).reshape(2 * 4 * 8 * 128, 2048)


def run_layers(layers, do_final, inputs, ncores=8):
    out_rows = NX if do_final else L
    nc = build(layers, do_final, out_rows)
    f = lambda a: np.ascontiguousarray(np.asarray(a, dtype=np.float32))
    shared = dict(const_tables())
    shared["meta"] = f(inputs["meta_tokens"])
    shared["norms"] = np.concatenate([f(inputs["norm_a"]), f(inputs["norm_c"]), f(inputs["final_norm"])[None]], axis=0)
    if any(l[0] == "E" for l in layers):
        shared.update(even_tables())
        shared["wina"] = layout_wina(f(inputs["w_in_a"]))
        shared["wouta"] = layout_wout(f(inputs["w_out_a"]))
        shared["lbl"] = np.ascontiguousarray(f(inputs["hgrn_lb"]).reshape(2, 2, 16, 128).transpose(3, 0, 1, 2)).reshape(128, 64)
        shared["hnorm"] = f(inputs["hgrn_norm"])
        shared["sink"] = f(inputs["sink_logits"])
    if any(l[0] == "O" for l in layers):
        shared["winc"] = layout_winc(f(inputs["w_in_c"]))
        shared["woutc"] = layout_wout(f(inputs["w_out_c"]))
        shared["dlam"] = f(inputs["diff_lambda"]).reshape(2, 512)
        shared["dnorm"] = f(inputs["diff_norm"])
    x = f(inputs["x"])
    in_maps = []
    for b in range(ncores):
        m = dict(shared)
        m["x"] = x[b]
        in_maps.append(m)
    res = run_bass_kernel_spmd(nc, in_maps, core_ids=list(range(ncores)))
    return np.stack([r["out"] for r in res.results], axis=0)


def kernel(**inputs):
    layers = [("E", 0, 0), ("O", 0, 1), ("E", 1, 2), ("O", 1, 3)]
    return run_layers(layers, True, inputs)
```

```python
import math
from contextlib import ExitStack
import numpy as np
import concourse.bass as bass
import concourse.mybir as mybir
from concourse.bass_utils import run_bass_kernel_spmd

F32 = mybir.dt.float32
BF16 = mybir.dt.bfloat16
AF = mybir.ActivationFunctionType
ALU = mybir.AluOpType
AX = mybir.AxisListType

L = 2064
NX = 2048
NMETA = 16
D = 2048
EPS = 1e-6
TT = [(i * 128, 128) for i in range(16)] + [(2048, 16)]
QB = [(i * 512, 512) for i in range(4)] + [(2048, 16)]
DELTAS = [128 * m for m in range(1, 16)] + [16 + 128 * m for m in range(16)]
NDEL = len(DELTAS)
SAME_ENGINE_SYNC = True
import os
LOOK = int(os.environ.get('KLOOK', '3'))
ST3 = int(os.environ.get('KST3', '1'))


def vidx(s):
    return s if s < 2048 else s - 2048 - 16


def slopes16():
    return [2.0 ** (-8.0 * (i + 1) / 16) for i in range(16)]


class Eng:
    def __init__(self, nc, es, name, e, ndma):
        self.name = name
        self.e = e
        self.sem = es.enter_context(nc.semaphore("s_" + name))
        self.cnt = 0
        self.seen = {}
        self.ring = [[es.enter_context(nc.semaphore("d_%s%d" % (name, i))), 0] for i in range(ndma)]
        self.ri = 0


class Buf:
    def __init__(self, t):
        self.t = t
        self.w = None
        self.rs = {}

    def __getitem__(self, k):
        return self.t[k]


class K:
    def __init__(self, nc, es):
        self.nc = nc
        self.es = es
        self.E = {
            "pe": Eng(nc, es, "pe", nc.tensor, 0),
            "dve": Eng(nc, es, "dve", nc.vector, 0),
            "act": Eng(nc, es, "act", nc.scalar, 0),
            "pool": Eng(nc, es, "pool", nc.gpsimd, 16),
            "sp": Eng(nc, es, "sp", nc.sync, 24),
        }
        self.nbuf = 0
        self.log = []

    def sb(self, shape, dt, es=None, name=None):
        self.nbuf += 1
        t = (es or self.es).enter_context(self.nc.sbuf_tensor("%s_%d" % (name or "sb", self.nbuf), list(shape), dt))
        return Buf(t)

    def wait(self, eng, tok):
        if tok is None:
            return
        sid, sem, val = tok
        if eng.seen.get(sid, 0) >= val:
            return
        if sid == id(eng.sem) and (eng.name == "pe" or not SAME_ENGINE_SYNC):
            return
        eng.e.wait_ge(sem, val)
        eng.seen[sid] = val
        self.log.append((eng.name, "w", sid, val))

    def _deps(self, eng, reads, writes):
        for b in reads:
            self.wait(eng, b.w)
        for b in writes:
            self.wait(eng, b.w)
            for tok in list(b.rs.values()):
                self.wait(eng, tok)

    def _mark(self, tok, reads, writes):
        for b in reads:
            old = b.rs.get(tok[0])
            if old is None or old[2] < tok[2]:
                b.rs[tok[0]] = tok
        for b in writes:
            b.w = tok
            b.rs = {}

    def op(self, en, fn, reads=(), writes=()):
        eng = self.E[en]
        self._deps(eng, reads, writes)
        ins = fn(eng.e)
        eng.cnt += 1
        ins.then_inc(eng.sem, 1)
        self.log.append((eng.name, "i", id(eng.sem), 1))
        tok = (id(eng.sem), eng.sem, eng.cnt)
        self._mark(tok, reads, writes)
        return tok

    def mms(self, fns, reads=(), writes=()):
        eng = self.E["pe"]
        self._deps(eng, reads, writes)
        ins = None
        for fn in fns:
            ins = fn(eng.e)
        eng.cnt += 1
        ins.then_inc(eng.sem, 1)
        self.log.append((eng.name, "i", id(eng.sem), 1))
        tok = (id(eng.sem), eng.sem, eng.cnt)
        self._mark(tok, reads, writes)
        return tok

    def dma(self, qn, out, in_, reads=(), writes=()):
        eng = self.E[qn]
        self._deps(eng, reads, writes)
        slot = eng.ring[eng.ri % len(eng.ring)]
        eng.ri += 1
        if slot[1] > 0:
            self.wait(eng, (id(slot[0]), slot[0], slot[1]))
        ins = eng.e.dma_start(out=out, in_=in_)
        slot[1] += 16
        ins.then_inc(slot[0], 16)
        self.log.append((eng.name, "i", id(slot[0]), 16))
        tok = (id(slot[0]), slot[0], slot[1])
        self._mark(tok, reads, writes)
        return tok

    def check_deadlock(self):
        qs = {}
        for ev in self.log:
            qs.setdefault(ev[0], []).append(ev)
        ptr = {n: 0 for n in qs}
        sem = {}
        prog = True
        while prog:
            prog = False
            for n, q in qs.items():
                while ptr[n] < len(q):
                    _, kind, sid, val = q[ptr[n]]
                    if kind == "i":
                        sem[sid] = sem.get(sid, 0) + val
                    elif sem.get(sid, 0) < val:
                        break
                    ptr[n] += 1
                    prog = True
        stuck = {n: (ptr[n], len(q), q[ptr[n]]) for n, q in qs.items() if ptr[n] < len(q)}
        if stuck:
            names = {id(e.sem): e.name for e in self.E.values()}
            for e in self.E.values():
                for i, sl in enumerate(e.ring):
                    names[id(sl[0])] = "%s_dma%d" % (e.name, i)
            msg = "; ".join("%s at %d/%d waits %s>=%d (have %d)" % (n, p, t, names.get(ev[2]), ev[3], sem.get(ev[2], 0))
                            for n, (p, t, ev) in stuck.items())
            raise RuntimeError("DEADLOCK in emitted program: " + msg)

    def barrier(self):
        toks = []
        for e in self.E.values():
            if e.cnt > 0:
                toks.append((id(e.sem), e.sem, e.cnt))
            for s in e.ring:
                if s[1] > 0:
                    toks.append((id(s[0]), s[0], s[1]))
        for e in self.E.values():
            for t in toks:
                if t[0] == id(e.sem):
                    continue
                self.wait(e, t)


def build(layers, do_final, out_rows):
    nc = bass.Bass("TRN2", target_bir_lowering=False)
    dr = {}

    def din(name, shape):
        dr[name] = nc.dram_tensor(name, list(shape), F32, kind="ExternalInput").ap()
        return dr[name]

    x_d = din("x", [NX, D])
    meta_d = din("meta", [NMETA, D])
    nrm_d = din("norms", [5, D])
    ident_d = din("ident", [128, 128])
    dlin_d = din("dlin", [128, 512])
    dabs_d = din("dabs", [128, 896])
    cb_d = din("cbtab", [128, 16 * NDEL])
    n_odd = sum(1 for l in layers if l[0] == "O")
    n_even = sum(1 for l in layers if l[0] == "E")
    if n_odd:
        winc_d = din("winc", [2 * 16 * 4 * 2 * 128, 2048])
        woutc_d = din("woutc", [2 * 4 * 8 * 128, 2048])
        dlam_d = din("dlam", [2, 512])
        dnorm_d = din("dnorm", [2, 256])
    if n_even:
        wina_d = din("wina", [2 * 120 * 128, 2048])
        wouta_d = din("wouta", [2 * 4 * 8 * 128, 2048])
        smask_d = din("smask", [128, L])
        mab_d = din("mab", [128, 1024])
        mtri_d = din("mtri", [128, 256])
        lb_d = din("lbl", [128, 64])
        hnorm_d = din("hnorm", [2, 128])
        wabs_d = din("wabs", [128, 1152])
        mclip_d = din("mclip", [128, 1024])
        sink_d = din("sink", [2, 16])
    out_d = nc.dram_tensor("out", [out_rows, D], F32, kind="ExternalOutput").ap()
    hd = nc.dram_tensor("hd", [L, D], F32, kind="Internal").ap()
    yTd = nc.dram_tensor("yTd", [32, 128, L], BF16, kind="Internal").ap()

    with ExitStack() as es:
        k = K(nc, es)
        ident_f = k.sb([128, 128], F32)
        ident = k.sb([128, 128], BF16)
        wst = [k.sb([128, 2048], F32, name="wst") for _ in range(2)]
        ps = [Buf(es.enter_context(nc.psum_tensor("ps%d" % i, [128, 512], F32))) for i in range(7)]
        pst1 = es.enter_context(nc.psum_tensor("pst", [128, 1024], BF16))
        pstq = [Buf(pst1)]
        PSP = [ps]
        hd_b = [Buf(None) for _ in TT]
        yTd_b = [Buf(None) for _ in range(16)]
        st = {"wst": 0, "wb": 0, "ps": 0, "pst": 0, "uq": 0}

        def nxt(key, lst):
            i = st[key] % len(lst)
            st[key] += 1
            return lst[i]

        epsc = k.sb([128, 4], F32)
        k.op("pool", lambda e: e.memset(epsc[:, 0:1], EPS), writes=[epsc])
        k.op("pool", lambda e: e.memset(epsc[:, 3:4], 1.0), writes=[epsc])
        for wi in range(2):
            k.op("pool", lambda e, wi=wi: e.memset(epsc[:, 1 + wi:2 + wi],
                                                   math.log(1.0 - (0.8 - 0.6 * math.exp(-0.3 * (2 * wi + 1))))),
                 writes=[epsc])
        k.dma("sp", ident_f[:], ident_d[:, :], writes=[ident_f])
        k.op("dve", lambda e: e.tensor_copy(out=ident[:], in_=ident_f[:]), reads=[ident_f], writes=[ident])

        if n_odd:
            lams = k.sb([128, 4], F32)
            lame = k.sb([128, 4], F32)
            neglam = k.sb([128, 2], F32)
            lam_es = ExitStack()
            lamt = k.sb([128, 2, 512], F32, lam_es)
            lamp = k.sb([128, 2, 2, 128], F32, lam_es)
            for i in range(2):
                k.dma("sp", lamt[:, i, :], dlam_d[i:i + 1, :].partition_broadcast(128), writes=[lamt])
            for i in range(2):
                for j in range(2):
                    k.op("dve", lambda e, i=i, j=j: e.tensor_tensor(
                        out=lamp[:, i, j, :], in0=lamt[:, i, 256 * j:256 * j + 128],
                        in1=lamt[:, i, 256 * j + 128:256 * j + 256], op=ALU.mult), reads=[lamt], writes=[lamp])
            for i in range(2):
                for j in range(2):
                    k.op("dve", lambda e, i=i, j=j: e.reduce_sum(
                        out=lams[:, 2 * i + j:2 * i + j + 1], in_=lamp[:, i, j, :], axis=AX.X),
                        reads=[lamp], writes=[lams])
            k.op("act", lambda e: e.activation(out=lame[:], in_=lams[:], func=AF.Exp), reads=[lams], writes=[lame])
            k.barrier()
            lam_es.close()

        def lam_init_of(layer_idx):
            return 0.8 - 0.6 * math.exp(-0.3 * layer_idx)

        def h_src(first, ti):
            s, n = TT[ti]
            if first:
                return (x_d[s:s + n, :] if s < 2048 else meta_d[0:n, :])
            return hd[s:s + n, :]

        def load_w_slice(row0, dst, ls):
            for half in range(2):
                stg = nxt("wst", wst)
                k.dma("sp", stg[:], ls[0][row0 + half * 128:row0 + half * 128 + 128, :], writes=[stg])
                k.op("pool", lambda e, stg=stg, half=half: e.tensor_copy(
                    out=dst[:, 8 * half:8 * half + 8, :],
                    in_=stg[:].rearrange("p (c n) -> p c n", c=8)), reads=[stg], writes=[dst])

        def phase_norm(first, nrow, xnT, ls):
            gt = k.sb([128, D], F32, ls, "gt")
            hb = [k.sb([128, D], F32, ls, "hb") for _ in range(2)]
            xb = [k.sb([128, D], BF16, ls, "xb") for _ in range(2)]
            junk = k.sb([128, D], BF16, ls, "junk")
            ssb = [k.sb([128, 2], F32, ls, "ss") for _ in range(2)]
            k.dma("sp", gt[:], nrm_d[nrow:nrow + 1, :].partition_broadcast(128), writes=[gt])
            for ti, (s, n) in enumerate(TT):
                h = hb[ti % 2]
                xn = xb[ti % 2]
                ss = ssb[ti % 2]
                k.dma("sp", h[0:n, :], h_src(first, ti), reads=[hd_b[ti]], writes=[h])
                k.op("act", lambda e: e.activation(out=junk[0:n, :], in_=h[0:n, :], func=AF.Square,
                                                   accum_out=ss[0:n, 0:1]), reads=[h], writes=[junk, ss])
                k.op("act", lambda e: e.activation(out=ss[0:n, 1:2], in_=ss[0:n, 0:1], func=AF.Ln, scale=1.0 / D,
                                                   bias=epsc[0:n, 0:1]), reads=[ss, epsc], writes=[ss])
                k.op("act", lambda e: e.activation(out=ss[0:n, 0:1], in_=ss[0:n, 1:2], func=AF.Exp, scale=-0.5),
                     reads=[ss], writes=[ss])
                k.op("dve", lambda e: e.scalar_tensor_tensor(out=xn[0:n, :], in0=h[0:n, :], scalar=ss[0:n, 0:1],
                                                             in1=gt[0:n, :], op0=ALU.mult, op1=ALU.mult),
                     reads=[h, ss, gt], writes=[xn])
                for g in range(2):
                    pt = nxt("pst", pstq)
                    k.mms([lambda e, c=c, g=g, pt=pt: e.transpose(
                        out=pt[:, c * 128:c * 128 + n], in_=xn[0:n, (8 * g + c) * 128:(8 * g + c + 1) * 128],
                        identity=ident[0:n, 0:n]) for c in range(8)], reads=[xn, ident], writes=[pt])
                    k.op("act" if g == 0 else "dve", lambda e, g=g, pt=pt: (e.copy if g == 0 else e.tensor_copy)(
                        out=xnT[:, 8 * g:8 * g + 8, s:s + n],
                        in_=pt[:, :].rearrange("p (c t) -> p c t", c=8)[:, :, 0:n]), reads=[pt], writes=[xnT])

        def proj_fm(xnT, w, c0, dst, dj, scale):
            for bi, (s, n) in enumerate(QB):
                p = nxt("ps", PSP[0])
                k.mms([lambda e, c=c, p=p: e.matmul(p[:, 0:n], lhsT=w[:, c, c0:c0 + 128], rhs=xnT[:, c, s:s + n],
                                                    start=(c == 0), stop=(c == 15)) for c in range(16)],
                      reads=[xnT, w], writes=[p])
                if scale is None:
                    k.op("act", lambda e, p=p: e.copy(out=dst[:, dj, s:s + n], in_=p[:, 0:n]), reads=[p], writes=[dst])
                else:
                    k.op("act", lambda e, p=p: e.mul(out=dst[:, dj, s:s + n], in_=p[:, 0:n], mul=scale),
                         reads=[p], writes=[dst])

        def proj_tm(xnT, w, ti, p, ncols=256, c0=0):
            s, n = TT[ti]
            k.mms([lambda e, c=c: e.matmul(p[0:n, 0:ncols], lhsT=xnT[:, c, s:s + n], rhs=w[:, c, c0:c0 + ncols],
                                           start=(c == 0), stop=(c == 15)) for c in range(16)],
                  reads=[xnT, w], writes=[p])

        def silu_from_psum(p, n, ncols, dst_ap, dstbuf, tmpa, tmpb, pc0=0):
            k.op("act", lambda e: e.activation(out=tmpa[0:n, 0:ncols], in_=p[0:n, pc0:pc0 + ncols], func=AF.Exp, scale=-1.0),
                 reads=[p], writes=[tmpa])
            k.op("act", lambda e: e.activation(out=tmpb[0:n, 0:ncols], in_=tmpa[0:n, 0:ncols], func=AF.Ln,
                                               bias=epsc[0:n, 3:4]), reads=[tmpa, epsc], writes=[tmpb])
            k.op("act", lambda e: e.activation(out=tmpb[0:n, 0:ncols], in_=tmpb[0:n, 0:ncols], func=AF.Exp, scale=-1.0),
                 reads=[tmpb], writes=[tmpb])
            k.op("dve", lambda e: e.tensor_tensor(out=dst_ap, in0=p[0:n, pc0:pc0 + ncols], in1=tmpb[0:n, 0:ncols],
                                                  op=ALU.mult), reads=[p, tmpb], writes=[dstbuf])

        CT = {}

        def load_consts(ls):
            CT["dlin"] = k.sb([128, 512], F32, ls, "dlin")
            CT["dabs"] = k.sb([128, 896], F32, ls, "dabs")
            CT["cbt"] = k.sb([128, 16 * NDEL], F32, ls, "cbt")
            k.dma("sp", CT["dlin"][:], dlin_d[:, :], writes=[CT["dlin"]])
            k.dma("sp", CT["dabs"][:], dabs_d[:, :], writes=[CT["dabs"]])
            k.dma("sp", CT["cbt"][:], cb_d[:, :], writes=[CT["cbt"]])

        def bias_tile(t0v, N, s0v, kn, slope, h):
            dlin, dabs, cbt = CT["dlin"], CT["dabs"], CT["cbt"]
            if t0v < s0v + kn and s0v < t0v + N:
                off = t0v - s0v
                return dabs[0:kn, 384 + off:384 + off + N], dabs, -slope, None
            if t0v > s0v:
                dl = t0v - s0v
                return dlin[0:kn, 0:N], dlin, -slope, cbt[0:kn, h * NDEL + DELTAS.index(dl):h * NDEL + DELTAS.index(dl) + 1]
            dl = s0v - t0v
            return dlin[0:kn, 0:N], dlin, slope, cbt[0:kn, h * NDEL + DELTAS.index(dl):h * NDEL + DELTAS.index(dl) + 1]

        def odd_mixer(widx, layer_idx, xnT, ls):
            lam_init = lam_init_of(layer_idx)
            sl = slopes16()
            load_consts(ls)
            cbt = CT["cbt"]
            wb = [k.sb([128, 16, 256], BF16, ls, "wb") for _ in range(4)]
            QT = k.sb([128, 2, L], BF16, ls, "QT")
            KT = k.sb([128, 2, L], BF16, ls, "KT")
            Vx = k.sb([128, 17, 257], BF16, ls, "Vx")
            G = k.sb([128, 17, 256], BF16, ls, "G")
            yT = k.sb([128, 2, L], BF16, ls, "yT")
            tmpf = [k.sb([128, 512], F32, ls, "tmpf") for _ in range(4)]
            PT = [k.sb([128, 512], BF16, ls, "PT") for _ in range(4)]
            OT = [k.sb([128, 4, 257], F32, ls, "OT") for _ in range(2)]
            Oq = [[Buf(OT[j_].t[:, q_, :]) for q_ in range(4)] for j_ in range(2)]
            SM = k.sb([128, 4, 8], F32, ls, "SM")
            SS = [k.sb([128, 2], F32, ls, "SS") for _ in range(4)]
            A1 = k.sb([128, 4, 256], F32, ls, "A1")
            A2 = k.sb([128, 4, 256], F32, ls, "A2")
            YB = k.sb([128, 4, 256], BF16, ls, "YB")
            ga = k.sb([128, 256], F32, ls, "ga")
            gb = k.sb([128, 256], F32, ls, "gb")
            gn = k.sb([128, 256], F32, ls, "gn")
            k.dma("sp", gn[:], dnorm_d[widx:widx + 1, :].partition_broadcast(128), writes=[gn])
            k.op("pool", lambda e: e.memset(Vx[:, :, 256:257], 1.0), writes=[Vx])
            k.op("dve", lambda e: e.tensor_tensor(out=neglam[:, widx:widx + 1], in0=lame[:, 2 * widx + 1:2 * widx + 2],
                                                  in1=lame[:, 2 * widx:2 * widx + 1], op=ALU.subtract),
                 reads=[lame], writes=[neglam])
            k.op("dve", lambda e: e.tensor_scalar(out=neglam[:, widx:widx + 1], in0=neglam[:, widx:widx + 1],
                                                  scalar1=-lam_init, scalar2=None, op0=ALU.add),
                 reads=[neglam], writes=[neglam])
            nl = neglam
            pp = 0
            def load_head(h_):
                base_ = ((widx * 16 + h_) * 4) * 256
                for i_ in range(4):
                    load_w_slice(base_ + i_ * 256, wb[i_], [winc_d])
            load_head(0)
            for h in range(16):
                wq, wk, wv, wg = wb[0], wb[1], wb[2], wb[3]
                for j in range(2):
                    proj_fm(xnT, wq, j * 128, QT, j, 128 ** -0.5)
                for j in range(2):
                    proj_fm(xnT, wk, j * 128, KT, j, None)
                for ti, (s, n) in enumerate(TT):
                    p = nxt("ps", PSP[0])
                    proj_tm(xnT, wv, ti, p)
                    k.op("act", lambda e, p=p, ti=ti, n=n: e.copy(out=Vx[0:n, ti, 0:256], in_=p[0:n, 0:256]),
                         reads=[p], writes=[Vx])
                    p = nxt("ps", PSP[0])
                    proj_tm(xnT, wg, ti, p)
                    silu_from_psum(p, n, 256, G[0:n, ti, :], G, ga, gb)
                if h + 1 < 16:
                    load_head(h + 1)
                for (t0, N) in QB:
                    t0v = vidx(t0)
                    nqs = (N + 127) // 128
                    for j in range(2):
                        acc = ps[2:6]
                        pend = []
                        stb = [ps[0], ps[1], ps[6]] if ST3 else [ps[0], ps[1]]
                        for kt, (s0, kn) in enumerate(TT):
                            s0v = vidx(s0)
                            stp = stb[kt % len(stb)]
                            k.mms([lambda e: e.matmul(stp[0:kn, 0:N], lhsT=KT[:, j, s0:s0 + kn], rhs=QT[:, j, t0:t0 + N],
                                                      start=True, stop=True)], reads=[KT, QT], writes=[stp])
                            dt_ap, dbuf, coef, cb = bias_tile(t0v, N, s0v, kn, sl[h], h)
                            tf = tmpf[pp % 4]
                            ptile = PT[pp % 4]
                            pp += 1
                            k.op("dve", lambda e: e.scalar_tensor_tensor(out=tf[0:kn, 0:N], in0=dt_ap, scalar=coef,
                                                                         in1=stp[0:kn, 0:N], op0=ALU.mult, op1=ALU.add),
                                 reads=[dbuf, stp], writes=[tf])
                            if cb is None:
                                k.op("act", lambda e: e.activation(out=ptile[0:kn, 0:N], in_=tf[0:kn, 0:N], func=AF.Exp),
                                     reads=[tf], writes=[ptile])
                            else:
                                k.op("act", lambda e: e.activation(out=ptile[0:kn, 0:N], in_=tf[0:kn, 0:N], func=AF.Exp,
                                                                   bias=cb), reads=[tf, cbt], writes=[ptile])
                            if len(pend) >= LOOK:
                                pend.pop(0)()
                            def pv(kt=kt, kn=kn, ptile=ptile):
                                for qs in range(nqs):
                                    qn = min(128, N - qs * 128)
                                    a = acc[qs]
                                    k.mms([lambda e: e.matmul(a[0:qn, 0:257], lhsT=ptile[0:kn, qs * 128:qs * 128 + qn],
                                                              rhs=Vx[0:kn, kt, :], start=(kt == 0), stop=(kt == 16))],
                                          reads=[ptile, Vx], writes=[a])
                            pend.append(pv)
                        while pend:
                            pend.pop(0)()
                        for qs in range(nqs):
                            qn = min(128, N - qs * 128)
                            k.op("act" if qs % 2 == 0 else "dve",
                                 lambda e, qs=qs, qn=qn: (e.copy if qs % 2 == 0 else e.tensor_copy)(
                                     out=OT[j][0:qn, qs, :], in_=acc[qs][0:qn, 0:257]),
                                 reads=[acc[qs]], writes=[Oq[j][qs]])
                    nq = nqs
                    qn = min(128, N)
                    ti0 = t0 // 128
                    o1, o2 = OT[0].t, OT[1].t
                    rd1, rd2 = list(Oq[0][0:nq]), list(Oq[1][0:nq])
                    SMv = SM.t
                    bc = lambda col: SMv[0:qn, 0:nq, col:col + 1].to_broadcast([qn, nq, 256])
                    k.op("dve", lambda e: e.reciprocal(out=SMv[0:qn, 0:nq, 0:1], in_=o1[0:qn, 0:nq, 256:257]),
                         reads=rd1, writes=[SM])
                    k.op("dve", lambda e: e.reciprocal(out=SMv[0:qn, 0:nq, 1:2], in_=o2[0:qn, 0:nq, 256:257]),
                         reads=rd2, writes=[SM])
                    k.op("dve", lambda e: e.tensor_scalar(out=SMv[0:qn, 0:nq, 2:3], in0=SMv[0:qn, 0:nq, 1:2],
                                                          scalar1=nl[0:qn, widx:widx + 1], scalar2=None, op0=ALU.mult),
                         reads=[SM, nl], writes=[SM])
                    k.op("dve", lambda e: e.tensor_tensor(out=A1[0:qn, 0:nq, :], in0=o1[0:qn, 0:nq, 0:256], in1=bc(0), op=ALU.mult),
                         reads=rd1 + [SM], writes=[A1])
                    k.op("dve", lambda e: e.tensor_tensor(out=A2[0:qn, 0:nq, :], in0=o2[0:qn, 0:nq, 0:256], in1=bc(2), op=ALU.mult),
                         reads=rd2 + [SM], writes=[A2])
                    k.op("dve", lambda e: e.tensor_tensor(out=A2[0:qn, 0:nq, :], in0=A2[0:qn, 0:nq, :], in1=A1[0:qn, 0:nq, :],
                                                          op=ALU.add), reads=[A2, A1], writes=[A2])
                    for qs in range(nq):
                        k.op("act", lambda e, qs=qs: e.activation(out=A1[0:qn, qs, :], in_=A2[0:qn, qs, :], func=AF.Square,
                                                                  accum_out=SS[qs][0:qn, 0:1]), reads=[A2], writes=[SS[qs]])
                        k.op("act", lambda e, qs=qs: e.activation(out=SS[qs][0:qn, 1:2], in_=SS[qs][0:qn, 0:1], func=AF.Ln,
                                                                  scale=1.0 / 256, bias=epsc[0:qn, 0:1]),
                             reads=[SS[qs], epsc], writes=[SS[qs]])
                        k.op("act", lambda e, qs=qs: e.activation(out=SMv[0:qn, qs, 5:6], in_=SS[qs][0:qn, 1:2], func=AF.Exp,
                                                                  scale=-0.5, bias=epsc[0:qn, 1 + widx:2 + widx]),
                             reads=[SS[qs], epsc], writes=[SM])
                    k.op("dve", lambda e: e.tensor_tensor(out=A1[0:qn, 0:nq, :], in0=A2[0:qn, 0:nq, :], in1=bc(5), op=ALU.mult),
                         reads=[A2, SM], writes=[A1])
                    k.op("dve", lambda e: e.tensor_tensor(out=A1[0:qn, 0:nq, :], in0=A1[0:qn, 0:nq, :],
                                                          in1=gn[0:qn, :].unsqueeze(1).to_broadcast([qn, nq, 256]), op=ALU.mult),
                         reads=[A1, gn], writes=[A1])
                    k.op("dve", lambda e: e.tensor_tensor(out=YB[0:qn, 0:nq, :], in0=A1[0:qn, 0:nq, :],
                                                          in1=G[0:qn, ti0:ti0 + nq, :], op=ALU.mult), reads=[A1, G], writes=[YB])
                    pt = nxt("pst", pstq)
                    k.mms([lambda e, c=c, qs=qs: e.transpose(out=pt[:, (c * nq + qs) * 128:(c * nq + qs) * 128 + qn],
                                                             in_=YB[0:qn, qs, c * 128:(c + 1) * 128],
                                                             identity=ident[0:qn, 0:qn]) for c in range(2) for qs in range(nq)],
                          reads=[YB, ident], writes=[pt])
                    wd = (nq - 1) * 128 + qn
                    k.op("act", lambda e: e.copy(out=yT[:, :, t0:t0 + wd],
                                                 in_=pt[:, 0:2 * nq * 128].rearrange("p (c t) -> p c t", c=2)[:, :, 0:wd]),
                         reads=[pt], writes=[yT])
                k.dma("pool", yTd[2 * h:2 * h + 2, :, :].rearrange("c p t -> p c t"), yT[:], reads=[yT], writes=[yTd_b[h]])

        def proj_fm_cb(xnT, w, c0, cb):
            for bi, (s, n) in enumerate(QB):
                p = nxt("ps", PSP[0])
                k.mms([lambda e, c=c, p=p: e.matmul(p[:, 0:n], lhsT=w[:, c, c0:c0 + 128], rhs=xnT[:, c, s:s + n],
                                                    start=(c == 0), stop=(c == 15)) for c in range(16)],
                      reads=[xnT, w], writes=[p])
                cb(p, s, n)

        def load_w128(sidx_row0, dst, c0):
            stg = nxt("wst", wst)
            k.dma("sp", stg[:], wina_d[sidx_row0:sidx_row0 + 128, :], writes=[stg])
            k.op("pool", lambda e: e.tensor_copy(out=dst[:, :, c0:c0 + 128],
                                                 in_=stg[:].rearrange("p (c n) -> p c n", c=16)),
                 reads=[stg], writes=[dst])

        def even_mixer_A(widx, xnT, ls):
            wb = [k.sb([128, 16, 256], BF16, ls, "wb") for _ in range(2)] + [k.sb([128, 16, 128], BF16, ls, "wb")]
            qA = k.sb([128, L], BF16, ls, "qA")
            qB = k.sb([128, L], BF16, ls, "qB")
            T1 = k.sb([128, L], F32, ls, "T1")
            T1b = k.sb([128, L], F32, ls, "T1b")
            T2 = k.sb([128, L], F32, ls, "T2")
            T3 = k.sb([128, L], F32, ls, "T3")
            QEA = k.sb([128, L], BF16, ls, "QEA")
            QEB = k.sb([128, L], BF16, ls, "QEB")
            KE = k.sb([128, L], BF16, ls, "KE")
            KL = k.sb([128, L], BF16, ls, "KL")
            V = k.sb([128, 17, 128], BF16, ls, "V")
            G = k.sb([128, 17, 128], BF16, ls, "G")
            OF = k.sb([128, 17, 128], F32, ls, "OF")
            yT = k.sb([128, L], BF16, ls, "yT")
            smk = k.sb([128, L], BF16, ls, "smk")
            mAB = k.sb([128, 2, 512], BF16, ls, "mAB")
            mtri = k.sb([128, 2, 128], F32, ls, "mtri")
            lbt = k.sb([128, 64], F32, ls, "lbt")
            lbc = k.sb([128, 2, 16], F32, ls, "lbc")
            omc = k.sb([128, 2, 16], F32, ls, "omc")
            gna = k.sb([128, 128], F32, ls, "gna")
            ATm = [k.sb([128, 128], BF16, ls, "ATm") for _ in range(3)]
            KLt = [[k.sb([128, 128], BF16, ls, "KLt") for _ in range(2)] for _ in range(2)]
            S = [k.sb([128, 128], F32, ls, "S") for _ in range(4)]
            SBALL = k.sb([128, 33, 128], BF16, ls, "SBALL")
            ECs = [k.sb([128, 34], F32, ls, "EC") for _ in range(2)]
            k.op("pool", lambda e: e.memset(SBALL[:, 0, :], 0.0), writes=[SBALL])
            ubanks = [ps[4], ps[5], ps[6]]
            PSP[0] = ps[0:4]
            ga = k.sb([128, 128], F32, ls, "ga")
            gb = k.sb([128, 128], F32, ls, "gb")
            sm = [k.sb([128, 8], F32, ls, "sm") for _ in range(2)]
            a1 = [k.sb([128, 128], F32, ls, "a1") for _ in range(2)]
            a2 = [k.sb([128, 128], F32, ls, "a2") for _ in range(2)]
            jk = k.sb([128, 128], F32, ls, "jk")
            ybf = [k.sb([128, 128], BF16, ls, "ybf") for _ in range(2)]
            k.dma("pool", smk[:], smask_d[:, :], writes=[smk])
            k.dma("pool", mAB[:], mab_d[:, :].rearrange("p (a n) -> p a n", a=2), writes=[mAB])
            k.dma("sp", mtri[:], mtri_d[:, :].rearrange("p (a n) -> p a n", a=2), writes=[mtri])
            k.dma("sp", lbt[:], lb_d[:, :], writes=[lbt])
            k.dma("sp", gna[:], hnorm_d[widx:widx + 1, :].partition_broadcast(128), writes=[gna])
            for par in range(2):
                for ab in range(2):
                    k.op("pool", lambda e, par=par, ab=ab: e.memset(KLt[par][ab][:], 0.0), writes=[KLt[par][ab]])
            lb4 = lbt[:].rearrange("p (r l h) -> p r l h", r=2, l=2)
            if widx == 0:
                k.op("pool", lambda e: e.memset(lbc[:], 0.0), writes=[lbc])
                k.op("pool", lambda e: e.memset(omc[:], 1.0), writes=[omc])
            else:
                k.op("dve", lambda e: e.tensor_tensor(out=omc[:], in0=lb4[:, :, 0, :], in1=lb4[:, :, 1, :], op=ALU.subtract),
                     reads=[lbt], writes=[omc])
                k.op("act", lambda e: e.activation(out=omc[:], in_=omc[:], func=AF.Exp), reads=[omc], writes=[omc])
                k.op("dve", lambda e: e.tensor_scalar(out=omc[:], in0=omc[:], scalar1=1.0, scalar2=None, op0=ALU.add),
                     reads=[omc], writes=[omc])
                k.op("dve", lambda e: e.reciprocal(out=lbc[:], in_=omc[:]), reads=[omc], writes=[lbc])
                k.op("dve", lambda e: e.tensor_scalar(out=omc[:], in0=lbc[:], scalar1=-1.0, scalar2=1.0, op0=ALU.mult,
                                                      op1=ALU.add), reads=[lbc], writes=[omc])
            for h in range(16):
                wq, wzf, wzb, wv, wg = (wb[0], 0), (wb[0], 128), (wb[2], 0), (wb[1], 0), (wb[1], 128)

                def load_head(h_):
                    for (wt, c0), si in ((wzf, 16 + h_), (wzb, 32 + h_), (wq, h_), (wv, 48 + h_), (wg, 64 + h_)):
                        load_w128((widx * 120 + si) * 128, wt, c0)
                if h == 0:
                    load_head(0)
                for di_ in range(2):
                    wz_ = wzf if di_ == 0 else wzb
                    Te = T1 if di_ == 0 else T1b

                    def cbz(p, s, n, Te=Te):
                        k.op("act", lambda e: e.activation(out=Te[:, s:s + n], in_=p[:, 0:n], func=AF.Exp, scale=-1.0),
                             reads=[p], writes=[Te])
                    proj_fm_cb(xnT, wz_[0], wz_[1], cbz)
                def cbq(p, s, n):
                    k.op("dve", lambda e: e.tensor_tensor(out=qA[:, s:s + n], in0=p[:, 0:n], in1=mAB[:, 0, 0:n], op=ALU.mult),
                         reads=[p, mAB], writes=[qA])
                    k.op("dve", lambda e: e.tensor_tensor(out=qB[:, s:s + n], in0=p[:, 0:n], in1=mAB[:, 1, 0:n], op=ALU.mult),
                         reads=[p, mAB], writes=[qB])
                proj_fm_cb(xnT, wq[0], wq[1], cbq)
                def ew(di, T1):
                    ops = []
                    EC = ECs[di]
                    xv = lambda t_: t_[:, 0:2048].rearrange("p (c t) -> p c t", t=64)
                    ops.append(lambda: k.op("act", lambda e: e.activation(out=T2[:], in_=T1[:], func=AF.Ln, bias=epsc[:, 3:4]),
                         reads=[T1, epsc], writes=[T2]))
                    if widx == 0:
                        ops.append(lambda: k.op("dve", lambda e: e.tensor_scalar(out=T3[:], in0=T2[:], scalar1=-1.0, scalar2=None, op0=ALU.mult),
                             reads=[T2], writes=[T3]))
                    else:
                        ops.append(lambda: k.op("act", lambda e: e.activation(out=T3[:], in_=T1[:], func=AF.Ln, scale=lbc[:, di, h:h + 1],
                                                           bias=epsc[:, 3:4]), reads=[T1, lbc, epsc], writes=[T3]))
                        ops.append(lambda: k.op("dve", lambda e: e.tensor_tensor(out=T3[:], in0=T3[:], in1=T2[:], op=ALU.subtract),
                             reads=[T3, T2], writes=[T3]))
                    ops.append(lambda: k.op("act", lambda e: e.activation(out=T2[:], in_=T2[:], func=AF.Exp, scale=-1.0), reads=[T2], writes=[T2]))
                    ops.append(lambda: k.op("dve", lambda e: e.scalar_tensor_tensor(out=T1[:], in0=T1[:], scalar=omc[:, di, h:h + 1], in1=T2[:],
                                                                 op0=ALU.mult, op1=ALU.mult), reads=[T1, omc, T2], writes=[T1]))
                    if di == 0:
                        ops.append(lambda: k.op("dve", lambda e: e.tensor_tensor_scan(out=T2[:], data0=smk[:], data1=T3[:], initial=0.0,
                                                                   op0=ALU.mult, op1=ALU.add), reads=[smk, T3], writes=[T2]))
                        ops.append(lambda: k.op("act", lambda e: e.activation(out=T3[:], in_=T2[:], func=AF.Exp), reads=[T2], writes=[T3]))
                        ops.append(lambda: k.op("pool", lambda e: e.tensor_copy(out=EC[:, 0:32], in_=xv(T3)[:, :, 63]), reads=[T3], writes=[EC]))
                        ops.append(lambda: k.op("pool", lambda e: e.tensor_copy(out=EC[:, 32:33], in_=T3[:, 2063:2064]), reads=[T3], writes=[EC]))
                        ops.append(lambda: k.op("act", lambda e: e.activation(out=T2[:], in_=T2[:], func=AF.Exp, scale=-1.0),
                             reads=[T2], writes=[T2]))
                        ops.append(lambda: k.op("dve", lambda e: e.tensor_tensor(out=QEA[:], in0=qA[:], in1=T3[:], op=ALU.mult),
                             reads=[qA, T3], writes=[QEA]))
                        ops.append(lambda: k.op("dve", lambda e: e.tensor_tensor(out=QEB[:], in0=qB[:], in1=T3[:], op=ALU.mult),
                             reads=[qB, T3], writes=[QEB]))
                        ops.append(lambda: k.op("dve", lambda e: e.tensor_tensor(out=KE[:], in0=T1[:], in1=T2[:], op=ALU.mult),
                             reads=[T1, T2], writes=[KE]))
                        ops.append(lambda: k.op("dve", lambda e: e.tensor_tensor(
                            out=xv(KL), in0=xv(KE), in1=xv(T3)[:, :, 63:64].to_broadcast([128, 32, 64]), op=ALU.mult),
                            reads=[KE, T3], writes=[KL]))
                        ops.append(lambda: k.op("dve", lambda e: e.tensor_tensor(out=KL[:, 2048:2064], in0=KE[:, 2048:2064],
                                                              in1=T3[:, 2063:2064].to_broadcast([128, 16]), op=ALU.mult),
                             reads=[KE, T3], writes=[KL]))
                        KLs = KL
                    else:
                        ops.append(lambda: k.op("pool", lambda e: e.memset(T2[:, 0:1], 0.0), writes=[T2]))
                        ops.append(lambda: k.op("dve", lambda e: e.tensor_tensor_scan(out=T2[:, 1:L], data0=T3[:, 0:L - 1], data1=smk[:, 1:L],
                                                                   initial=0.0, op0=ALU.add, op1=ALU.mult),
                             reads=[smk, T3], writes=[T2]))
                        ops.append(lambda: k.op("dve", lambda e: e.tensor_tensor(out=EC[:, 0:32], in0=xv(T2)[:, :, 63], in1=xv(T3)[:, :, 63],
                                                              op=ALU.add), reads=[T2, T3], writes=[EC]))
                        ops.append(lambda: k.op("dve", lambda e: e.tensor_tensor(out=EC[:, 32:33], in0=T2[:, 2063:2064], in1=T3[:, 2063:2064],
                                                              op=ALU.add), reads=[T2, T3], writes=[EC]))
                        ops.append(lambda: k.op("act", lambda e: e.activation(out=EC[:, 0:33], in_=EC[:, 0:33], func=AF.Exp), reads=[EC], writes=[EC]))
                        ops.append(lambda: k.op("act", lambda e: e.activation(out=T3[:], in_=T2[:], func=AF.Exp, scale=-1.0),
                             reads=[T2], writes=[T3]))
                        ops.append(lambda: k.op("act", lambda e: e.activation(out=T2[:], in_=T2[:], func=AF.Exp), reads=[T2], writes=[T2]))
                        ops.append(lambda: k.op("dve", lambda e: e.tensor_tensor(out=QEA[:], in0=qA[:], in1=T3[:], op=ALU.mult),
                             reads=[qA, T3], writes=[QEA]))
                        ops.append(lambda: k.op("dve", lambda e: e.tensor_tensor(out=QEB[:], in0=qB[:], in1=T3[:], op=ALU.mult),
                             reads=[qB, T3], writes=[QEB]))
                        ops.append(lambda: k.op("dve", lambda e: e.tensor_tensor(out=KE[:], in0=T1[:], in1=T2[:], op=ALU.mult),
                             reads=[T1, T2], writes=[KE]))
                        KLs = KE
                    return ops, KLs

                ops0, KLs0 = ew(0, T1)
                for ti, (s, n) in enumerate(TT):
                    p = nxt("ps", PSP[0])
                    proj_tm(xnT, wb[1], ti, p, 256, 0)
                    k.op("act", lambda e: e.copy(out=V[0:n, ti, :], in_=p[0:n, 0:128]), reads=[p], writes=[V])
                    silu_from_psum(p, n, 128, G[0:n, ti, :], G, ga, gb, 128)
                    if ops0:
                        ops0.pop(0)()
                while ops0:
                    ops0.pop(0)()
                if h + 1 < 16:
                    load_head(h + 1)
                for di in range(2):
                    EC = ECs[di]
                    if di == 0:
                        KLs = KLs0
                    else:
                        ops1, KLs = ew(1, T1b)
                        for f_ in ops1:
                            f_()
                    if di == 0:
                        seq = [(16, 0)] + [(t_, ab_) for t_ in range(16) for ab_ in (0, 1)]
                    else:
                        seq = [(t_, ab_) for t_ in range(15, -1, -1) for ab_ in (1, 0)] + [(16, 0)]
                    cids = [32 if t_ == 16 else 2 * t_ + ab_ for (t_, ab_) in seq]
                    order = [16] + list(range(16)) if di == 0 else list(range(15, -1, -1)) + [16]
                    seqidx = {}
                    k.op("pool", lambda e: e.memset(S[0][:], 0.0), writes=[S[0]])
                    if di == 0:
                        groups = [[16]] + [[2 * g_, 2 * g_ + 1] for g_ in range(8)]
                    else:
                        groups = [[15 - 2 * g_, 14 - 2 * g_] for g_ in range(8)] + [[16]]
                    step = 0
                    tcount = 0
                    for gt in groups:
                        bank = ubanks[st["uq"] % 3]
                        st["uq"] += 1
                        slots = []
                        for ti in gt:
                            s, n = TT[ti]
                            par = tcount % 2
                            tcount += 1
                            pt = nxt("pst", pstq)
                            k.mms([lambda e: e.transpose(out=pt[0:n, 0:128], in_=KLs[:, s:s + n], identity=ident[:, :])],
                                  reads=[KLs, ident], writes=[pt])
                            nA = min(n, 64)
                            k.op("act", lambda e: e.copy(out=KLt[par][0][0:nA, :], in_=pt[0:nA, 0:128]), reads=[pt],
                                 writes=[KLt[par][0]])
                            if n == 128:
                                k.op("act", lambda e: e.copy(out=KLt[par][1][64:128, :], in_=pt[64:128, 0:128]), reads=[pt],
                                     writes=[KLt[par][1]])
                            chunks = [0] if n == 16 else ([0, 1] if di == 0 else [1, 0])
                            for ab in chunks:
                                cid = 32 if ti == 16 else 2 * ti + ab
                                q_ = len(slots)
                                seqidx[(ti, ab)] = step + q_
                                k.mms([lambda e: e.matmul(bank[:, q_ * 128:(q_ + 1) * 128], lhsT=KLt[par][ab][0:n, :],
                                                          rhs=V[0:n, ti, :], start=True, stop=True)],
                                      reads=[KLt[par][ab], V], writes=[bank])
                                slots.append((q_, cid))
                        for (q_, cid) in slots:
                            if step < 32:
                                so, sn = S[step % 4], S[(step + 1) % 4]
                                k.op("dve", lambda e: e.scalar_tensor_tensor(out=sn[:], in0=so[:], scalar=EC[:, cid:cid + 1],
                                                                             in1=bank[:, q_ * 128:(q_ + 1) * 128],
                                                                             op0=ALU.mult, op1=ALU.add),
                                     reads=[so, EC, bank], writes=[sn])
                                if di == 0:
                                    k.op("pool", lambda e: e.tensor_copy(out=SBALL[:, step + 1, :], in_=sn[:]), reads=[sn],
                                         writes=[SBALL])
                                else:
                                    cn = cids[step + 1]
                                    k.op("pool", lambda e: e.tensor_tensor(out=SBALL[:, step + 1, :], in0=sn[:],
                                                                           in1=EC[:, cn:cn + 1].to_broadcast([128, 128]),
                                                                           op=ALU.mult), reads=[sn, EC], writes=[SBALL])
                            step += 1
                    def at_stage(oi, ti):
                        s, n = TT[ti]
                        pa = nxt("ps", PSP[0])
                        k.mms([lambda e: e.matmul(pa[0:n, 0:n], lhsT=KE[:, s:s + n], rhs=QEA[:, s:s + n], start=True, stop=False),
                               lambda e: e.matmul(pa[0:n, 0:n], lhsT=KE[:, s:s + n], rhs=QEB[:, s:s + n], start=False, stop=True)],
                              reads=[KE, QEA, QEB], writes=[pa])
                        at = ATm[oi % 3]
                        k.op("dve", lambda e: e.tensor_tensor(out=at[0:n, 0:n], in0=pa[0:n, 0:n], in1=mtri[0:n, di, 0:n],
                                                              op=ALU.mult), reads=[pa, mtri], writes=[at])
                    at_stage(0, order[0])
                    pendB, pendC = [], []
                    for oi, ti in enumerate(order):
                        s, n = TT[ti]
                        if oi + 1 < len(order):
                            at_stage(oi + 1, order[oi + 1])
                        at = ATm[oi % 3]
                        chunks = [0] if n == 16 else ([0, 1] if di == 0 else [1, 0])
                        po = nxt("ps", PSP[0])
                        fns = [lambda e: e.matmul(po[0:n, 0:128], lhsT=at[0:n, 0:n], rhs=V[0:n, ti, :], start=True, stop=False)]
                        for ci, ab in enumerate(chunks):
                            qe = QEA if ab == 0 else QEB
                            sbi = seqidx[(ti, ab)]
                            fns.append(lambda e, qe=qe, sbi=sbi, last=(ci == len(chunks) - 1): e.matmul(
                                po[0:n, 0:128], lhsT=qe[:, s:s + n], rhs=SBALL[:, sbi, :], start=False, stop=last))
                        k.mms(fns, reads=[at, V, QEA, QEB, SBALL], writes=[po])
                        if di == 0:
                            k.op("act", lambda e: e.copy(out=OF[0:n, ti, :], in_=po[0:n, 0:128]), reads=[po], writes=[OF])
                        else:
                            s_ = sm[oi % 2]
                            x1, x2, yb = a1[oi % 2], a2[oi % 2], ybf[oi % 2]
                            k.op("dve", lambda e: e.tensor_tensor(out=x1[0:n, :], in0=po[0:n, 0:128], in1=OF[0:n, ti, :],
                                                                  op=ALU.add), reads=[po, OF], writes=[x1])
                            k.op("act", lambda e: e.activation(out=jk[0:n, :], in_=x1[0:n, :], func=AF.Square,
                                                               accum_out=s_[0:n, 0:1]), reads=[x1], writes=[jk, s_])
                            k.op("act", lambda e: e.activation(out=s_[0:n, 1:2], in_=s_[0:n, 0:1], func=AF.Ln,
                                                               scale=1.0 / 128, bias=epsc[0:n, 0:1]),
                                 reads=[s_, epsc], writes=[s_])
                            k.op("act", lambda e: e.activation(out=s_[0:n, 2:3], in_=s_[0:n, 1:2], func=AF.Exp, scale=-0.5),
                                 reads=[s_], writes=[s_])
                            def stB(s=s, n=n, ti=ti, s_=s_, x1=x1, x2=x2, yb=yb):
                                k.op("dve", lambda e: e.scalar_tensor_tensor(out=x2[0:n, :], in0=x1[0:n, :], scalar=s_[0:n, 2:3],
                                                                             in1=gna[0:n, :], op0=ALU.mult, op1=ALU.mult),
                                     reads=[x1, s_, gna], writes=[x2])
                                k.op("dve", lambda e: e.tensor_tensor(out=yb[0:n, :], in0=x2[0:n, :], in1=G[0:n, ti, :],
                                                                       op=ALU.mult), reads=[x2, G], writes=[yb])

                                def stC():
                                    pt2 = nxt("pst", pstq)
                                    k.mms([lambda e: e.transpose(out=pt2[:, 0:n], in_=yb[0:n, :], identity=ident[0:n, 0:n])],
                                          reads=[yb, ident], writes=[pt2])
                                    k.op("act", lambda e: e.copy(out=yT[:, s:s + n], in_=pt2[:, 0:n]), reads=[pt2], writes=[yT])
                                pendC.append(stC)
                            runB, runC = pendB[:], pendC[:]
                            del pendB[:]
                            del pendC[:]
                            for f_ in runB:
                                f_()
                            for f_ in runC:
                                f_()
                            pendB.append(stB)
                    for f_ in pendB[:]:
                        f_()
                    for f_ in pendC[:]:
                        f_()
                k.dma("pool", yTd[h, :, :], yT[:], reads=[yT], writes=[yTd_b[h]])
            PSP[0] = ps

        def even_mixer_B(widx, xnT, ls):
            sl = slopes16()
            load_consts(ls)
            dabs = CT["dabs"]
            wb = [k.sb([128, 16, 256], BF16, ls, "wb") for _ in range(3)]
            QT = k.sb([128, 1, L], BF16, ls, "QT")
            KT = k.sb([128, 1, L], BF16, ls, "KT")
            Vx = k.sb([128, 17, 129], BF16, ls, "Vx")
            G = k.sb([128, 17, 128], BF16, ls, "G")
            yT = k.sb([128, L], BF16, ls, "yT")
            wabs = k.sb([128, 1152], F32, ls, "wabs")
            mclip = k.sb([128, 1024], F32, ls, "mclip")
            sink = k.sb([128, 16], F32, ls, "sink")
            esink = k.sb([128, 16], F32, ls, "esink")
            tmpf = [k.sb([128, 512], F32, ls, "tmpf") for _ in range(4)]
            PT = [k.sb([128, 512], BF16, ls, "PT") for _ in range(4)]
            OT = k.sb([128, 4, 129], F32, ls, "OT")
            Oq = [Buf(OT.t[:, q_, :]) for q_ in range(4)]
            SM = k.sb([128, 4, 4], F32, ls, "SM")
            A1 = k.sb([128, 4, 128], F32, ls, "A1")
            YB = k.sb([128, 4, 128], BF16, ls, "YB")
            ga = k.sb([128, 128], F32, ls, "ga")
            gb = k.sb([128, 128], F32, ls, "gb")
            sm = [k.sb([128, 4], F32, ls, "sm") for _ in range(2)]
            a1 = [k.sb([128, 128], F32, ls, "a1") for _ in range(2)]
            ybf = [k.sb([128, 128], BF16, ls, "ybf") for _ in range(2)]
            k.dma("sp", wabs[:], wabs_d[:, :], writes=[wabs])
            k.dma("sp", mclip[:], mclip_d[:, :], writes=[mclip])
            k.dma("sp", sink[:], sink_d[widx:widx + 1, :].partition_broadcast(128), writes=[sink])
            k.op("act", lambda e: e.activation(out=esink[:], in_=sink[:], func=AF.Exp), reads=[sink], writes=[esink])
            k.op("pool", lambda e: e.memset(Vx[:, :, 128:129], 1.0), writes=[Vx])
            pp = 0
            def load_kv(kv_):
                load_w128((widx * 120 + 96 + kv_) * 128, wb[0], 0)
                load_w128((widx * 120 + 100 + kv_) * 128, wb[0], 128)

            def load_qg(hq_):
                load_w128((widx * 120 + 80 + hq_) * 128, wb[1 + hq_ % 2], 0)
                load_w128((widx * 120 + 104 + hq_) * 128, wb[1 + hq_ % 2], 128)
            load_kv(0)
            load_qg(0)
            for hq in range(16):
                kv = hq // 4
                if hq % 4 == 0:
                    wk, wv = (wb[0], 0), (wb[0], 128)
                    proj_fm(xnT, wk[0], wk[1], KT, 0, None)
                    for ti, (s, n) in enumerate(TT):
                        p = nxt("ps", PSP[0])
                        proj_tm(xnT, wv[0], ti, p, 128, wv[1])
                        k.op("act", lambda e: e.copy(out=Vx[0:n, ti, 0:128], in_=p[0:n, 0:128]), reads=[p], writes=[Vx])
                wq, wg = (wb[1 + hq % 2], 0), (wb[1 + hq % 2], 128)
                if hq + 1 < 16:
                    load_qg(hq + 1)
                    if (hq + 1) % 4 == 0:
                        load_kv((hq + 1) // 4)
                proj_fm(xnT, wq[0], wq[1], QT, 0, 128 ** -0.5)
                for ti, (s, n) in enumerate(TT):
                    p = nxt("ps", PSP[0])
                    proj_tm(xnT, wg[0], ti, p, 128, wg[1])
                    silu_from_psum(p, n, 128, G[0:n, ti, :], G, ga, gb)
                for (t0, N) in QB:
                    t0v = vidx(t0)
                    nqs = (N + 127) // 128
                    acc = ps[2:6]
                    if t0 < 2048:
                        xt = [s0 for s0 in range(t0 - 128, t0 + N + 1, 128) if 0 <= s0 <= 1920]
                    else:
                        xt = [0]
                    ktl = [(2048, 16)] + [(s0, 128) for s0 in xt]
                    pend = []
                    stb = [ps[0], ps[1], ps[6]] if ST3 else [ps[0], ps[1]]
                    for ki, (s0, kn) in enumerate(ktl):
                        s0v = vidx(s0)
                        kt = s0 // 128
                        stp = stb[ki % len(stb)]
                        k.mms([lambda e: e.matmul(stp[0:kn, 0:N], lhsT=KT[:, 0, s0:s0 + kn], rhs=QT[:, 0, t0:t0 + N],
                                                  start=True, stop=True)], reads=[KT, QT], writes=[stp])
                        if s0 == 2048 and t0 < 2048:
                            c0 = min(t0, 512)
                            dt_ap, dbuf = mclip[0:16, c0:c0 + N], mclip
                        elif s0 == 2048:
                            dt_ap, dbuf = dabs[0:16, 384:384 + N], dabs
                        else:
                            off = t0v - s0v
                            dt_ap, dbuf = wabs[0:kn, 512 + off:512 + off + N], wabs
                        tf = tmpf[pp % 4]
                        ptile = PT[pp % 4]
                        pp += 1
                        k.op("dve", lambda e: e.scalar_tensor_tensor(out=tf[0:kn, 0:N], in0=dt_ap, scalar=-sl[hq],
                                                                     in1=stp[0:kn, 0:N], op0=ALU.mult, op1=ALU.add),
                             reads=[dbuf, stp], writes=[tf])
                        k.op("act", lambda e: e.activation(out=ptile[0:kn, 0:N], in_=tf[0:kn, 0:N], func=AF.Exp),
                             reads=[tf], writes=[ptile])
                        if len(pend) >= LOOK:
                            pend.pop(0)()
                        def pv(s0=s0, kn=kn, kt=kt, ptile=ptile):
                            for qs in range(nqs):
                                qn = min(128, N - qs * 128)
                                tok0 = t0 + qs * 128
                                if t0 < 2048:
                                    rel = [s_ for s_ in (tok0 - 128, tok0, tok0 + 128) if 0 <= s_ <= 1920]
                                else:
                                    rel = [0]
                                if s0 != 2048 and s0 not in rel:
                                    continue
                                a = acc[qs]
                                k.mms([lambda e: e.matmul(a[0:qn, 0:129], lhsT=ptile[0:kn, qs * 128:qs * 128 + qn],
                                                          rhs=Vx[0:kn, kt, :], start=(s0 == 2048), stop=(s0 == rel[-1]))],
                                      reads=[ptile, Vx], writes=[a])
                        pend.append(pv)
                    while pend:
                        pend.pop(0)()
                    nq = nqs
                    qn = min(128, N)
                    ti0 = t0 // 128
                    for qs in range(nq):
                        k.op("act" if qs % 2 == 0 else "dve",
                             lambda e, qs=qs: (e.copy if qs % 2 == 0 else e.tensor_copy)(
                                 out=OT[0:qn, qs, :], in_=acc[qs][0:qn, 0:129]), reads=[acc[qs]], writes=[Oq[qs]])
                    rdo = list(Oq[0:nq])
                    ot = OT.t
                    SMv = SM.t
                    k.op("dve", lambda e: e.tensor_scalar(out=SMv[0:qn, 0:nq, 0:1], in0=ot[0:qn, 0:nq, 128:129],
                                                          scalar1=esink[0:qn, hq:hq + 1], scalar2=None, op0=ALU.add),
                         reads=rdo + [esink], writes=[SM])
                    k.op("dve", lambda e: e.reciprocal(out=SMv[0:qn, 0:nq, 1:2], in_=SMv[0:qn, 0:nq, 0:1]), reads=[SM], writes=[SM])
                    k.op("dve", lambda e: e.tensor_tensor(out=A1[0:qn, 0:nq, :], in0=ot[0:qn, 0:nq, 0:128],
                                                          in1=SMv[0:qn, 0:nq, 1:2].to_broadcast([qn, nq, 128]), op=ALU.mult),
                         reads=rdo + [SM], writes=[A1])
                    k.op("dve", lambda e: e.tensor_tensor(out=YB[0:qn, 0:nq, :], in0=A1[0:qn, 0:nq, :],
                                                          in1=G[0:qn, ti0:ti0 + nq, :], op=ALU.mult), reads=[A1, G], writes=[YB])
                    pt = nxt("pst", pstq)
                    k.mms([lambda e, qs=qs: e.transpose(out=pt[:, qs * 128:qs * 128 + qn], in_=YB[0:qn, qs, :],
                                                        identity=ident[0:qn, 0:qn]) for qs in range(nq)],
                          reads=[YB, ident], writes=[pt])
                    wd = (nq - 1) * 128 + qn
                    k.op("act", lambda e: e.copy(out=yT[:, t0:t0 + wd], in_=pt[:, 0:wd]), reads=[pt], writes=[yT])
                k.dma("pool", yTd[16 + hq, :, :], yT[:], reads=[yT], writes=[yTd_b[hq]])

        def phase_out(first, wout_d, widx, ls):
            wob = [k.sb([128, 32, 512], BF16, ls, "wob") for _ in range(2)]
            yb = [k.sb([128, 32, 512], BF16, ls, "ytb") for _ in range(2)]
            hb = [k.sb([128, 512], F32, ls, "hb") for _ in range(4)]
            ho = [k.sb([128, 512], F32, ls, "ho") for _ in range(4)]
            cnt = 0

            def load_w(nb_):
                wo_ = wob[nb_ % 2]
                stgs = []
                for pc in range(8):
                    stg = nxt("wst", wst)
                    r0 = ((widx * 4 + nb_) * 8 + pc) * 128
                    k.dma("pool", stg[:], wout_d[r0:r0 + 128, :], writes=[stg])
                    stgs.append(stg)
                    if pc >= 1:
                        sp_, pp_ = stgs[pc - 1], pc - 1
                        k.op("pool", lambda e, sp_=sp_, pp_=pp_: e.tensor_copy(
                            out=wo_[:, 4 * pp_:4 * pp_ + 4, :], in_=sp_[:].rearrange("p (c n) -> p c n", c=4)),
                            reads=[sp_], writes=[wo_])
                sp_, pp_ = stgs[7], 7
                k.op("pool", lambda e: e.tensor_copy(
                    out=wo_[:, 4 * pp_:4 * pp_ + 4, :], in_=sp_[:].rearrange("p (c n) -> p c n", c=4)),
                    reads=[sp_], writes=[wo_])

            def load_y(nb_, bi_):
                t0_, N_ = QB[bi_]
                y_ = yb[(nb_ * len(QB) + bi_) % 2]
                k.dma("sp", y_[:, :, 0:N_], yTd[:, :, t0_:t0_ + N_].rearrange("c p t -> p c t"),
                      reads=yTd_b, writes=[y_])

            seq = [(nb_, bi_) for nb_ in range(4) for bi_ in range(len(QB))]
            load_w(0)
            load_y(0, 0)
            for idx, (nb, bi) in enumerate(seq):
                wo = wob[nb % 2]
                t0, N = QB[bi]
                y = yb[idx % 2]
                if bi == 0 and nb + 1 < 4:
                    load_w(nb + 1)
                if idx + 1 < len(seq):
                    load_y(*seq[idx + 1])
                tiles = []
                for qs in range((N + 127) // 128):
                    qn = min(128, N - qs * 128)
                    tok0 = t0 + qs * 128
                    ti = tok0 // 128
                    hi = hb[cnt % 4]
                    hn = ho[cnt % 4]
                    cnt += 1
                    src_ = h_src(first, ti)
                    k.dma("sp", hi[0:qn, :], src_[:, nb * 512:(nb + 1) * 512], reads=[hd_b[ti]], writes=[hi])
                    tiles.append((qs, qn, tok0, ti, hi, hn))
                for (qs, qn, tok0, ti, hi, hn) in tiles:
                    p = nxt("ps", PSP[0])
                    k.mms([lambda e, c=c: e.matmul(p[0:qn, :], lhsT=y[:, c, qs * 128:qs * 128 + qn], rhs=wo[:, c, :],
                                                   start=(c == 0), stop=(c == 31)) for c in range(32)],
                          reads=[y, wo], writes=[p])
                    k.op("dve", lambda e: e.tensor_tensor(out=hn[0:qn, :], in0=p[0:qn, :], in1=hi[0:qn, :], op=ALU.add),
                         reads=[p, hi], writes=[hn])
                    k.dma("sp", hd[tok0:tok0 + qn, nb * 512:(nb + 1) * 512], hn[0:qn, :], reads=[hn],
                          writes=[hd_b[ti]])

        def phase_final(first, ls):
            gt = k.sb([128, D], F32, ls, "gt")
            hb = [k.sb([128, D], F32, ls, "hb") for _ in range(2)]
            ob = [k.sb([128, D], F32, ls, "ob") for _ in range(2)]
            junk = k.sb([128, D], BF16, ls, "junk")
            ssb = [k.sb([128, 2], F32, ls, "ss") for _ in range(2)]
            k.dma("sp", gt[:], nrm_d[4:5, :].partition_broadcast(128), writes=[gt])
            toks = []
            for ti, (s, n) in enumerate(TT[:16]):
                h = hb[ti % 2]
                o = ob[ti % 2]
                ss = ssb[ti % 2]
                k.dma("sp", h[0:n, :], h_src(first, ti), reads=[hd_b[ti]], writes=[h])
                k.op("act", lambda e: e.activation(out=junk[0:n, :], in_=h[0:n, :], func=AF.Square,
                                                   accum_out=ss[0:n, 0:1]), reads=[h], writes=[junk, ss])
                k.op("act", lambda e: e.activation(out=ss[0:n, 1:2], in_=ss[0:n, 0:1], func=AF.Ln, scale=1.0 / D,
                                                   bias=epsc[0:n, 0:1]), reads=[ss, epsc], writes=[ss])
                k.op("act", lambda e: e.activation(out=ss[0:n, 0:1], in_=ss[0:n, 1:2], func=AF.Exp, scale=-0.5),
                     reads=[ss], writes=[ss])
                k.op("dve", lambda e: e.scalar_tensor_tensor(out=o[0:n, :], in0=h[0:n, :], scalar=ss[0:n, 0:1],
                                                             in1=gt[0:n, :], op0=ALU.mult, op1=ALU.mult),
                     reads=[h, ss, gt], writes=[o])
                toks.append(k.dma("pool", out_d[s:s + n, :], o[0:n, :], reads=[o]))
            return toks

        first = True
        for (kind, widx, layer_idx) in layers:
            with ExitStack() as ls:
                xnT = k.sb([128, 16, L], BF16, ls, "xnT")
                with ExitStack() as ls2:
                    phase_norm(first, (0 if kind == "E" else 2) + widx, xnT, ls2)
                    k.barrier()
                with ExitStack() as ls2:
                    if kind == "O":
                        odd_mixer(widx, layer_idx, xnT, ls2)
                    else:
                        even_mixer_A(widx, xnT, ls2)
                    k.barrier()
                if kind == "E":
                    with ExitStack() as ls2:
                        even_mixer_B(widx, xnT, ls2)
                        k.barrier()
            with ExitStack() as ls:
                phase_out(first, woutc_d if kind == "O" else wouta_d, widx, ls)
                k.barrier()
            first = False
        out_toks = []
        with ExitStack() as ls:
            if do_final:
                out_toks = phase_final(first, ls)
            else:
                hb = [k.sb([128, D], F32, ls, "hb") for _ in range(2)]
                for ti, (s, n) in enumerate(TT):
                    h = hb[ti % 2]
                    k.dma("sp", h[0:n, :], h_src(first, ti), reads=[hd_b[ti]], writes=[h])
                    out_toks.append(k.dma("pool", out_d[s:s + n, :], h[0:n, :], reads=[h]))
            k.barrier()
        k.check_deadlock()
    return nc


def const_tables():
    i = np.arange(128, dtype=np.float32)[:, None]
    ident = np.eye(128, dtype=np.float32)
    dlin = (np.arange(512, dtype=np.float32)[None, :] - i).astype(np.float32)
    dabs = np.abs(np.arange(896, dtype=np.float32)[None, :] - i - 384).astype(np.float32)
    sl = np.array(slopes16(), dtype=np.float64)
    cb = (-(sl[:, None] * np.array(DELTAS, dtype=np.float64)[None, :])).reshape(1, -1)
    cb = np.repeat(cb, 128, axis=0).astype(np.float32)
    return {"ident": ident, "dlin": dlin, "dabs": dabs, "cbtab": np.ascontiguousarray(cb)}


def layout_winc(w):
    a = w.reshape(2, 2, 8, 128, 4, 16, 256)
    a = a.transpose(0, 5, 4, 1, 3, 2, 6)
    return np.ascontiguousarray(a).reshape(2 * 16 * 4 * 2 * 128, 2048)


def layout_wina(w):
    a = w.reshape(2, 16, 128, 120, 128)
    a = a.transpose(0, 3, 2, 1, 4)
    return np.ascontiguousarray(a).reshape(2 * 120 * 128, 2048)


def even_tables():
    i = np.arange(128, dtype=np.float32)[:, None]
    smask = np.ones((128, L), np.float32)
    smask[:, 0:2048:64] = 0.0
    smask[:, 2048] = 0.0
    j = np.arange(512)
    mA = ((j % 128) < 64).astype(np.float32)
    mab = np.concatenate([np.tile(mA[None], (128, 1)), np.tile((1 - mA)[None], (128, 1))], axis=1)
    s_ = np.arange(128)[:, None]; t_ = np.arange(128)[None, :]
    same = (s_ // 64) == (t_ // 64)
    mf = (same & (s_ <= t_)).astype(np.float32)
    mb_ = (same & (s_ >= t_)).astype(np.float32)
    mtri = np.concatenate([mf, mb_], axis=1)
    dw = np.abs(np.arange(1152, dtype=np.float32)[None, :] - i - 512)
    wabs = np.where(dw <= 128, dw, 1e9).astype(np.float32)
    mclip = np.minimum(np.arange(1024, dtype=np.float32)[None, :] - i + 16, 128.0).astype(np.float32)
    return {"smask": smask, "mab": np.ascontiguousarray(mab), "mtri": np.ascontiguousarray(mtri),
            "wabs": wabs, "mclip": mclip}


def layout_wout(w):
    a = w.reshape(2, 8, 4, 128, 4, 512)
    a = a.transpose(0, 4, 1, 3, 2, 5)
    return np.ascontiguousarray(a).reshape(2 * 4 * 8 * 128, 2048)


def run_layers(layers, do_final, inputs, ncores=8):
    out_rows = NX if do_final else L
    nc = build(layers, do_final, out_rows)
    f = lambda a: np.ascontiguousarray(np.asarray(a, dtype=np.float32))
    shared = dict(const_tables())
    shared["meta"] = f(inputs["meta_tokens"])
    shared["norms"] = np.concatenate([f(inputs["norm_a"]), f(inputs["norm_c"]), f(inputs["final_norm"])[None]], axis=0)
    if any(l[0] == "E" for l in layers):
        shared.update(even_tables())
        shared["wina"] = layout_wina(f(inputs["w_in_a"]))
        shared["wouta"] = layout_wout(f(inputs["w_out_a"]))
        shared["lbl"] = np.ascontiguousarray(f(inputs["hgrn_lb"]).reshape(2, 2, 16, 128).transpose(3, 0, 1, 2)).reshape(128, 64)
        shared["hnorm"] = f(inputs["hgrn_norm"])
        shared["sink"] = f(inputs["sink_logits"])
    if any(l[0] == "O" for l in layers):
        shared["winc"] = layout_winc(f(inputs["w_in_c"]))
        shared["woutc"] = layout_wout(f(inputs["w_out_c"]))
        shared["dlam"] = f(inputs["diff_lambda"]).reshape(2, 512)
        shared["dnorm"] = f(inputs["diff_norm"])
    x = f(inputs["x"])
    in_maps = []
    for b in range(ncores):
        m = dict(shared)
        m["x"] = x[b]
        in_maps.append(m)
    res = run_bass_kernel_spmd(nc, in_maps, core_ids=list(range(ncores)))
    return np.stack([r["out"] for r in res.results], axis=0)


def kernel(**inputs):
    layers = [("E", 0, 0), ("O", 0, 1), ("E", 1, 2), ("O", 1, 3)]
    return run_layers(layers, True, inputs)
```
